# Optimizing a Trainium2 kernel written in Bass

```python
import math
import jax, jax.numpy as jnp
from jax import lax
import numpy as np

D_MODEL = 1024
BATCH = 4
SEQ = 8192
DEPTH = 4

CHUNK = 64
QBLOCK = 128
D_FF = 4 * D_MODEL
EPS = 1e-6

A_HEADS = 8
A_KV_HEADS = 2
A_HEAD_DIM = 64
A_WINDOW = 128
A_WINDOW_CHUNKS = A_WINDOW // CHUNK
A_PREV_BLOCKS = -(-(A_WINDOW_CHUNKS * CHUNK) // QBLOCK)
B_HEADS = 8
B_Q_RANK = 256
B_KV_RANK = 128
B_NOPE_DIM = 64
B_ROPE_DIM = 32
B_V_DIM = 64
ROPE_THETA = 10000.0
C_HEADS = 8
C_HEAD_DIM = 64
D_HEADS = 4
D_HEAD_DIM = 64
D_V_DIM = 2 * D_HEAD_DIM

EVEN_SPLITS = [A_HEADS * A_HEAD_DIM, A_KV_HEADS * A_HEAD_DIM, A_KV_HEADS * A_HEAD_DIM,
               B_Q_RANK, B_KV_RANK, B_ROPE_DIM]
EVEN_IN = sum(EVEN_SPLITS)
EVEN_MIX = A_HEADS * A_HEAD_DIM + B_HEADS * B_V_DIM
ODD_SPLITS = [C_HEADS * C_HEAD_DIM] * 3 + [D_HEADS * 2 * D_HEAD_DIM] * 2 + [D_HEADS * D_V_DIM]
ODD_IN = sum(ODD_SPLITS)
ODD_MIX = C_HEADS * C_HEAD_DIM + D_HEADS * D_V_DIM
N_EVEN = (DEPTH + 1) // 2
N_ODD = DEPTH // 2

kernel_name = "hybrid_chunk_causal_encoder"


def rms_norm(x, g):
    xf = x.astype(jnp.float32)
    y = xf * lax.rsqrt(jnp.mean(xf * xf, axis=-1, keepdims=True) + EPS)
    return (y * g.astype(jnp.float32)).astype(x.dtype)


def head_rms(x, g):
    xf = x.astype(jnp.float32)
    return xf * lax.rsqrt(jnp.mean(xf * xf, axis=-1, keepdims=True) + EPS) * g.astype(jnp.float32)


def alibi_slopes(n):
    return 2.0 ** (-8.0 * jnp.arange(1, n + 1, dtype=jnp.float32) / n)


def apply_rope(x, positions):
    half = x.shape[-1] // 2
    inv = ROPE_THETA ** (-jnp.arange(half, dtype=jnp.float32) / half)
    ang = positions.astype(jnp.float32)[:, :, None, None] * inv
    cos, sin = jnp.cos(ang), jnp.sin(ang)
    x1, x2 = x[..., :half], x[..., half:]
    return jnp.concatenate([x1 * cos - x2 * sin, x1 * sin + x2 * cos], axis=-1)


def sweep_query_blocks(block_fn, *q_arrays):
    B, S = q_arrays[0].shape[:2]
    nb = S // QBLOCK
    xs = tuple(jnp.swapaxes(a.reshape((B, nb, QBLOCK) + a.shape[2:]), 0, 1) for a in q_arrays)
    out = lax.map(lambda t: block_fn(t[0], *t[1:]), (jnp.arange(nb),) + xs)
    return jnp.swapaxes(out, 0, 1).reshape((B, S) + out.shape[3:])


def banded_sink_attention(q, k, v, sinks):
    B, S, H, Dh = q.shape
    G = k.shape[2]
    R = H // G
    nb = S // QBLOCK
    qb = q.reshape(B, nb, QBLOCK, G, R, Dh)

    def band(t):
        tb = t.reshape(B, nb, QBLOCK, G, Dh)
        prev = [jnp.pad(tb, ((0, 0), (p, 0), (0, 0), (0, 0), (0, 0)))[:, :nb]
                for p in range(A_PREV_BLOCKS, 0, -1)]
        return jnp.concatenate(prev + [tb], axis=2)

    kk, vv = band(k), band(v)
    nk = kk.shape[2]
    blk = jnp.arange(nb)[:, None]
    qpos = blk * QBLOCK + jnp.arange(QBLOCK)[None]
    kpos = (blk - A_PREV_BLOCKS) * QBLOCK + jnp.arange(nk)[None]
    dchunk = qpos[:, :, None] // CHUNK - kpos[:, None, :] // CHUNK
    allowed = (kpos[:, None, :] >= 0) & (dchunk >= 0) & (dchunk <= A_WINDOW_CHUNKS)
    dist = jnp.abs(qpos[:, :, None] - kpos[:, None, :]).astype(jnp.float32)
    slopes = alibi_slopes(H).reshape(G, R)
    s = jnp.einsum('bnqgrd,bnkgd->bngrqk', qb, kk) * (Dh ** -0.5)
    s = s - slopes[None, None, :, :, None, None] * dist[None, :, None, None]
    s = jnp.where(allowed[None, :, None, None], s, -jnp.inf)
    sink = sinks.astype(jnp.float32).reshape(1, 1, G, R, 1, 1)
    m = jnp.maximum(jnp.max(s, axis=-1, keepdims=True), sink)
    p = jnp.exp(s - m)
    w = p / (jnp.sum(p, axis=-1, keepdims=True) + jnp.exp(sink - m))
    o = jnp.einsum('bngrqk,bnkgd->bnqgrd', w, vv)
    return o.reshape(B, S, H, Dh)


def chunk_causal_softmax_attention(q, k, v):
    S = q.shape[1]
    scale = q.shape[-1] ** -0.5
    kchunk = jnp.arange(S) // CHUNK

    def block(i, qb):
        qchunk = (i * QBLOCK + jnp.arange(QBLOCK)) // CHUNK
        allowed = kchunk[None, :] <= qchunk[:, None]
        s = jnp.einsum('bqhd,bkhd->bhqk', qb, k) * scale
        p = jax.nn.softmax(jnp.where(allowed, s, -jnp.inf), axis=-1)
        return jnp.einsum('bhqk,bkhd->bqhd', p, v)

    return sweep_query_blocks(block, q)


def stick_breaking_attention(q, k, v):
    S, Dh = q.shape[1], q.shape[-1]
    kpos = jnp.arange(S)

    def block(i, qb):
        qpos = i * QBLOCK + jnp.arange(QBLOCK)
        strict = kpos[None, :] < qpos[:, None]
        z = jnp.einsum('bqhd,bkhd->bhqk', qb, k) * (Dh ** -0.5)
        log1m = jnp.where(strict, -jax.nn.softplus(z), 0.0)
        between = lax.cumsum(log1m, axis=3, reverse=True) - log1m
        a = jnp.where(strict, jnp.exp(jax.nn.log_sigmoid(z) + between), 0.0)
        return jnp.einsum('bhqk,bkhd->bqhd', a, v)

    return sweep_query_blocks(block, q)


def differential_attention(q1, q2, k1, k2, v, lam):
    S, H, Dh = q1.shape[1], q1.shape[2], q1.shape[-1]
    scale = Dh ** -0.5
    kpos = jnp.arange(S)
    slopes = alibi_slopes(H)

    def block(i, q1b, q2b):
        qpos = i * QBLOCK + jnp.arange(QBLOCK)
        allowed = (kpos[None, :] // CHUNK) <= (qpos[:, None] // CHUNK)
        dist = jnp.abs(qpos[:, None] - kpos[None, :]).astype(jnp.float32)
        bias = -slopes[:, None, None] * dist[None]
        s1 = jnp.einsum('bqhd,bkhd->bhqk', q1b, k1) * scale + bias
        s2 = jnp.einsum('bqhd,bkhd->bhqk', q2b, k2) * scale + bias
        p1 = jax.nn.softmax(jnp.where(allowed, s1, -jnp.inf), axis=-1)
        p2 = jax.nn.softmax(jnp.where(allowed, s2, -jnp.inf), axis=-1)
        return jnp.einsum('bhqk,bkhd->bqhd', p1 - lam * p2, v)

    return sweep_query_blocks(block, q1, q2)


def even_mixer(h, positions, w_in, w_out, a_qg, a_kg, a_sinks, b_cqg, b_ckvg, b_w_uq, b_w_ukv, b_qg, b_kg):
    B, S, _ = h.shape
    proj = h @ w_in
    aq, ak, av, cq, ckv, krope = jnp.split(proj, np.cumsum(EVEN_SPLITS)[:-1].tolist(), axis=-1)
    qa = head_rms(aq.reshape(B, S, A_HEADS, A_HEAD_DIM), a_qg)
    ka = head_rms(ak.reshape(B, S, A_KV_HEADS, A_HEAD_DIM), a_kg)
    va = av.reshape(B, S, A_KV_HEADS, A_HEAD_DIM).astype(jnp.float32)
    oa = banded_sink_attention(qa, ka, va, a_sinks)
    cq = rms_norm(cq, b_cqg)
    ckv = rms_norm(ckv, b_ckvg)
    qb = (cq @ b_w_uq).reshape(B, S, B_HEADS, B_NOPE_DIM + B_ROPE_DIM)
    kv = (ckv @ b_w_ukv).reshape(B, S, B_HEADS, B_NOPE_DIM + B_V_DIM)
    kb = jnp.concatenate([kv[..., :B_NOPE_DIM],
                          jnp.broadcast_to(krope[:, :, None, :], (B, S, B_HEADS, B_ROPE_DIM))], axis=-1)
    vb = kv[..., B_NOPE_DIM:].astype(jnp.float32)
    qb = head_rms(qb, b_qg)
    kb = head_rms(kb, b_kg)
    qb = jnp.concatenate([qb[..., :B_NOPE_DIM], apply_rope(qb[..., B_NOPE_DIM:], positions)], axis=-1)
    kb = jnp.concatenate([kb[..., :B_NOPE_DIM], apply_rope(kb[..., B_NOPE_DIM:], positions)], axis=-1)
    ob = chunk_causal_softmax_attention(qb, kb, vb)
    mix = jnp.concatenate([oa.reshape(B, S, -1), ob.reshape(B, S, -1)], axis=-1).astype(h.dtype)
    return mix @ w_out


def odd_mixer(h, w_in, w_out, d_qg, d_kg, d_lam, d_subln, lambda_init):
    B, S, _ = h.shape
    proj = h @ w_in
    cq, ck, cv, dq, dk, dv = jnp.split(proj, np.cumsum(ODD_SPLITS)[:-1].tolist(), axis=-1)
    f32 = jnp.float32
    oc = stick_breaking_attention(cq.reshape(B, S, C_HEADS, C_HEAD_DIM).astype(f32),
                                  ck.reshape(B, S, C_HEADS, C_HEAD_DIM).astype(f32),
                                  cv.reshape(B, S, C_HEADS, C_HEAD_DIM).astype(f32))
    q = head_rms(dq.reshape(B, S, D_HEADS, 2, D_HEAD_DIM), d_qg)
    k = head_rms(dk.reshape(B, S, D_HEADS, 2, D_HEAD_DIM), d_kg)
    v = dv.reshape(B, S, D_HEADS, D_V_DIM).astype(f32)
    lf = d_lam.astype(f32)
    lam = jnp.exp(jnp.sum(lf[0] * lf[1])) - jnp.exp(jnp.sum(lf[2] * lf[3])) + lambda_init
    od = differential_attention(q[:, :, :, 0], q[:, :, :, 1], k[:, :, :, 0], k[:, :, :, 1], v, lam)
    od = head_rms(od, d_subln) * (1.0 - lambda_init)
    mix = jnp.concatenate([oc.reshape(B, S, -1), od.reshape(B, S, -1)], axis=-1).astype(h.dtype)
    return mix @ w_out


def squared_relu_mlp(h, w_up, w_down):
    return jnp.square(jax.nn.relu(h @ w_up)) @ w_down


def setup_inputs(seed: int = 0) -> dict:
    key = jax.random.key(seed)
    ks = jax.random.split(key, 23)
    f32 = jnp.float32

    def nrm(i, shape, scale):
        return jax.random.normal(ks[i], shape, f32) * scale

    def gain(i, shape):
        return 1.0 + 0.1 * jax.random.normal(ks[i], shape, f32)

    x = jax.random.normal(ks[0], (BATCH, SEQ, D_MODEL), f32)
    offset = jax.random.randint(ks[1], (BATCH, 1), 0, 4096, dtype=jnp.int32)
    positions = jnp.arange(SEQ, dtype=jnp.int32)[None, :] + offset
    return {
        "x": x,
        "positions": positions,
        "norm_mix_g": gain(2, (DEPTH, D_MODEL)),
        "norm_ffn_g": gain(3, (DEPTH, D_MODEL)),
        "mlp_w_up": nrm(4, (DEPTH, D_MODEL, D_FF), D_MODEL ** -0.5),
        "mlp_w_down": nrm(5, (DEPTH, D_FF, D_MODEL), D_FF ** -0.5),
        "ev_w_in": nrm(6, (N_EVEN, D_MODEL, EVEN_IN), D_MODEL ** -0.5),
        "ev_w_out": nrm(7, (N_EVEN, EVEN_MIX, D_MODEL), EVEN_MIX ** -0.5),
        "a_q_norm": gain(8, (N_EVEN, A_HEAD_DIM)),
        "a_k_norm": gain(9, (N_EVEN, A_HEAD_DIM)),
        "a_sinks": nrm(10, (N_EVEN, A_HEADS), 0.5),
        "b_cq_norm": gain(11, (N_EVEN, B_Q_RANK)),
        "b_ckv_norm": gain(12, (N_EVEN, B_KV_RANK)),
        "b_w_uq": nrm(13, (N_EVEN, B_Q_RANK, B_HEADS * (B_NOPE_DIM + B_ROPE_DIM)), B_Q_RANK ** -0.5),
        "b_w_ukv": nrm(14, (N_EVEN, B_KV_RANK, B_HEADS * (B_NOPE_DIM + B_V_DIM)), B_KV_RANK ** -0.5),
        "b_q_norm": gain(15, (N_EVEN, B_NOPE_DIM + B_ROPE_DIM)),
        "b_k_norm": gain(16, (N_EVEN, B_NOPE_DIM + B_ROPE_DIM)),
        "od_w_in": nrm(17, (N_ODD, D_MODEL, ODD_IN), D_MODEL ** -0.5),
        "od_w_out": nrm(18, (N_ODD, ODD_MIX, D_MODEL), ODD_MIX ** -0.5),
        "d_q_norm": gain(19, (N_ODD, 2, D_HEAD_DIM)),
        "d_k_norm": gain(20, (N_ODD, 2, D_HEAD_DIM)),
        "d_lambda": nrm(21, (N_ODD, 4, D_HEAD_DIM), 0.1),
        "d_subln": gain(22, (N_ODD, D_V_DIM)),
    }


def reference(x, positions, norm_mix_g, norm_ffn_g, mlp_w_up, mlp_w_down, ev_w_in, ev_w_out,
              a_q_norm, a_k_norm, a_sinks, b_cq_norm, b_ckv_norm, b_w_uq, b_w_ukv, b_q_norm, b_k_norm,
              od_w_in, od_w_out, d_q_norm, d_k_norm, d_lambda, d_subln):
    for layer in range(DEPTH):
        j = layer // 2
        h = rms_norm(x, norm_mix_g[layer])
        if layer % 2 == 0:
            mix = even_mixer(h, positions, ev_w_in[j], ev_w_out[j], a_q_norm[j], a_k_norm[j], a_sinks[j],
                             b_cq_norm[j], b_ckv_norm[j], b_w_uq[j], b_w_ukv[j], b_q_norm[j], b_k_norm[j])
        else:
            lambda_init = 0.8 - 0.6 * math.exp(-0.3 * layer)
            mix = odd_mixer(h, od_w_in[j], od_w_out[j], d_q_norm[j], d_k_norm[j], d_lambda[j], d_subln[j],
                            lambda_init)
        x = x + mix
        h = rms_norm(x, norm_ffn_g[layer])
        x = x + squared_relu_mlp(h, mlp_w_up[layer], mlp_w_down[layer])
    return x
```

```python
import math
import os
from contextlib import ExitStack

import numpy as np
import ml_dtypes
import concourse.bass as bass
import concourse.mybir as mybir
from concourse.bass_utils import run_bass_kernel_spmd

F32, BF16, I32 = mybir.dt.float32, mybir.dt.bfloat16, mybir.dt.int32
AF = mybir.ActivationFunctionType
ALU = mybir.AluOpType
AX = mybir.AxisListType
PE, ACT, DVE, POOL, SP = 0, 1, 2, 3, 4
EPS = 1e-6
DEPTH = 4
TWO_PI = 2.0 * math.pi

C_ONES, C_PERM, C_NEGU, C_NEGONES, C_DFB, C_DFC, C_DFD, C_SEL2, C_SEL4 = 0, 128, 256, 384, 512, 640, 768, 1280, 1312
NCM = 1328
F_SEL2T, F_SEL4T, F_KSELT, F_ABIAS, F_INVF = 0, 512, 768, 1024, 4096
NCF = 4097
NPC = 192


class Res:
    __slots__ = ("lw", "rd", "sem", "cnt")

    def __init__(self):
        self.lw = None
        self.rd = []
        self.sem = None
        self.cnt = 0


class Sched:
    def __init__(self, nc, stack):
        self.nc = nc
        self.stack = stack
        self.q = [[] for _ in range(5)]
        self.seq = [0] * 5
        self.esem = [stack.enter_context(nc.semaphore(f"es{i}")) for i in range(5)]
        self.waited = [dict() for _ in range(5)]
        self.dsems = []
        self.free_dsems = []
        self.semcnt = {}
        self.nds = 0

    def _waits(self, e, deps):
        best = {}
        for (sem, val, src) in deps:
            if e == PE and src == PE:
                continue
            k = id(sem)
            if self.waited[e].get(k, 0) >= val:
                continue
            if k not in best or best[k][1] < val:
                best[k] = (sem, val)
        out = []
        for k, (sem, val) in best.items():
            self.waited[e][k] = val
            out.append((sem, val))
        return out

    def _deps(self, r, w):
        deps = []
        for x in r:
            if x.lw is not None:
                deps.append(x.lw)
        for x in w:
            if x.lw is not None:
                deps.append(x.lw)
            deps.extend(x.rd)
        return deps

    def op(self, e, fn, r=(), w=()):
        waits = self._waits(e, self._deps(r, w))
        self.seq[e] += 1
        tok = (self.esem[e], self.seq[e], e)
        self.q[e].append((waits, fn, (self.esem[e], 1), True))
        for x in r:
            x.rd.append(tok)
        for x in w:
            x.lw = tok
            x.rd = []

    def dma(self, qe, fn, r=(), w=(), semres=None):
        waits = self._waits(qe, self._deps(r, w))
        sr = semres if semres is not None else (w[0] if w else r[0])
        if sr.sem is None:
            if self.free_dsems:
                sr.sem = self.free_dsems.pop()
            else:
                self.nds += 1
                sr.sem = self.stack.enter_context(self.nc.semaphore(f"ds{self.nds}"))
            sr.cnt = self.semcnt.get(id(sr.sem), 0)
            self.dsems.append(sr)
        sr.cnt += 16
        self.semcnt[id(sr.sem)] = sr.cnt
        tok = (sr.sem, sr.cnt, -1)
        self.q[qe].append((waits, fn, (sr.sem, 16), False))
        for x in r:
            x.rd.append(tok)
        for x in w:
            x.lw = tok
            x.rd = []

    def barrier(self):
        toks = [(self.esem[i], self.seq[i], i) for i in range(5) if self.seq[i] > 0]
        toks += [(sr.sem, sr.cnt, -1) for sr in self.dsems]
        for e in range(5):
            deps = [t for t in toks if t[2] != e]
            waits = self._waits(e, [(s, v, -2) for (s, v, _) in deps])
            if waits:
                self.q[e].append((waits, None, None, False))
        for sr in self.dsems:
            self.free_dsems.append(sr.sem)
            sr.sem = None
        self.dsems = []

    def emit(self, e, eng):
        for (waits, fn, inc, attach) in self.q[e]:
            if fn is None:
                for (sem, val) in waits:
                    eng.wait_ge(sem, val)
                continue
            if attach and waits:
                for (sem, val) in waits[:-1]:
                    eng.wait_ge(sem, val)
                ins = fn(eng)
                ins._wait_ge(*waits[-1])
            else:
                for (sem, val) in waits:
                    eng.wait_ge(sem, val)
                ins = fn(eng)
            ins.then_inc(inc[0], inc[1])


def pipeline(tiles, skews):
    T = len(tiles)
    mx = max(skews)
    for i in range(T + mx):
        for s, sk in enumerate(skews):
            t = i - sk
            if 0 <= t < T and tiles[t][s] is not None:
                tiles[t][s]()


class T:
    __slots__ = ("h", "res")

    def __init__(self, h):
        self.h = h
        self.res = Res()

    def __getitem__(self, k):
        return self.h[k]


def build(S):
    NG = S // 512
    NB = S // 128
    nc = bass.Bass("TRN2", target_bir_lowering=False)

    def din(name, shape, dt=F32):
        return nc.dram_tensor(name, list(shape), dt, kind="ExternalInput").ap()

    def dscr(name, shape, dt=BF16):
        return nc.dram_tensor(name, list(shape), dt).ap()

    xT = din("xT", [1024, S])
    posb = din("posb", [128, S], I32)
    pcols_d = din("pcols", [128, NPC])
    dlam_d = din("dlam", [128, 2, 256])
    cmat_d = din("cmat", [128, NCM], BF16)
    cf32_d = din("cf32", [128, NCF])
    daug_d = din("daug", [4, 2, 6, S], BF16)
    w_up_d = din("mlp_w_up", [4, 1024, 4096])
    w_down_d = din("mlp_w_down", [4, 4096, 1024])
    ev_in_d = din("ev_w_in", [2, 1024, 1184])
    ev_out_d = din("ev_w_out", [2, 1024, 1024])
    uq_d = din("b_w_uq", [2, 256, 768])
    ukv_d = din("b_w_ukv", [2, 128, 1024])
    od_in_d = din("od_w_in", [2, 1024, 3072])
    od_out_d = din("od_w_out", [2, 1024, 1024])
    yT = nc.dram_tensor("yT", [1024, S], F32, kind="ExternalOutput").ap()

    w_up = dscr("s_w_up", [4, 1024, 4096])
    w_down = dscr("s_w_down", [4, 4096, 1024])
    ev_in = dscr("s_ev_in", [2, 1024, 1184])
    ev_out = dscr("s_ev_out", [2, 1024, 1024])
    uq = dscr("s_uq", [2, 256, 768])
    ukv = dscr("s_ukv", [2, 128, 1024])
    od_in = dscr("s_od_in", [2, 1024, 3072])
    od_out = dscr("s_od_out", [2, 1024, 1024])
    xres = dscr("s_xres", [1024, S], F32)
    mixT = dscr("s_mix", [1024, S])
    costab = dscr("s_cos", [128, S], F32)
    sintab = dscr("s_sin", [128, S], F32)
    qa = dscr("s_qa", [8, 64, S])
    ka = dscr("s_ka", [2, 64, S])
    va = dscr("s_va", [2, 128, NB, 64])
    qb = dscr("s_qb", [8, 96, S])
    kb_ = dscr("s_kb", [8, 96, S])
    vb = dscr("s_vb", [8, 128, NB, 64])
    qc = dscr("s_qc", [8, 64, S])
    kc = dscr("s_kc", [8, 64, S])
    vc = dscr("s_vc", [8, 128, NB, 64])
    qd = dscr("s_qd", [4, 2, 70, S])
    kd = dscr("s_kd", [4, 2, 70, S])
    vd = dscr("s_vd", [4, 128, NB, 128])

    stack = ExitStack()
    sch = Sched(nc, stack)
    op, dma = sch.op, sch.dma

    nctr = [0]

    def sb(st, name, shape, dt):
        nctr[0] += 1
        return T(st.enter_context(nc.sbuf_tensor(f"t{nctr[0]}_{name}", list(shape), dt)))

    ps = [T(stack.enter_context(nc.psum_tensor(f"ps{i}", [128, 512], F32))) for i in range(8)]
    cmat = sb(stack, "cmat", [128, NCM], BF16)
    cf32 = sb(stack, "cf32", [128, NCF], F32)
    pcols = sb(stack, "pcols", [128, NPC], F32)
    dcol = sb(stack, "dcol", [128, 32], F32)
    epsc = sb(stack, "epsc", [128, 1], F32)
    onec = sb(stack, "onec", [128, 1], F32)

    ONES = lambda k=128, m=128: cmat[0:k, C_ONES:C_ONES + m]

    def act(out, in_, func, scale=1.0, bias=None):
        if bias is None:
            return lambda e: e.activation(out=out, in_=in_, func=func, scale=scale)
        return lambda e: e.activation(out=out, in_=in_, func=func, scale=scale, bias=bias)

    def mm(out, lhsT, rhs, start, stop):
        return lambda e: e.matmul(out, lhsT=lhsT, rhs=rhs, start=start, stop=stop)

    def tt(out, in0, in1, o):
        return lambda e: e.tensor_tensor(out=out, in0=in0, in1=in1, op=o)

    def ts(out, in0, s1, s2, o0, o1=None):
        if o1 is None:
            return lambda e: e.tensor_scalar(out=out, in0=in0, scalar1=s1, scalar2=None, op0=o0)
        return lambda e: e.tensor_scalar(out=out, in0=in0, scalar1=s1, scalar2=s2, op0=o0, op1=o1)

    def stt(out, in0, s, in1, o0, o1):
        return lambda e: e.scalar_tensor_tensor(out=out, in0=in0, scalar=s, in1=in1, op0=o0, op1=o1)

    def cp(out, in_):
        return lambda e: e.tensor_copy(out=out, in_=in_)

    def dm(out, in_):
        return lambda e: e.dma_start(out=out, in_=in_)

    def rstd_ops(dst, src_ps, n, d):
        op(ACT, act(dst[0:n, :], src_ps[0:n, :], AF.Ln, scale=1.0 / d, bias=epsc[0:n, :]), r=[src_ps.res, epsc.res], w=[dst.res])
        op(ACT, act(dst[0:n, :], dst[0:n, :], AF.Exp, scale=-0.5), r=[dst.res], w=[dst.res])

    with ExitStack() as st:
        dma(SP, dm(cmat[:, :], cmat_d), w=[cmat.res])
        dma(SP, dm(cf32[:, :], cf32_d), w=[cf32.res])
        dma(SP, dm(pcols[:, :], pcols_d), w=[pcols.res])
        op(DVE, lambda e: e.memset(epsc[:, :], EPS), w=[epsc.res])
        op(DVE, lambda e: e.memset(onec[:, :], 1.0), w=[onec.res])
        castres = Res()

        def cast2d(dst, src, rows, step=128):
            for r0 in range(0, rows, step):
                r1 = min(rows, r0 + step)
                dma(POOL, dm(dst[r0:r1, :], src[r0:r1, :]), r=[castres], semres=castres)

        for l in range(4):
            cast2d(w_up[l], w_up_d[l], 1024)
            cast2d(w_down[l], w_down_d[l], 4096, 256)
        for j in range(2):
            cast2d(ev_in[j], ev_in_d[j], 1024)
            cast2d(ev_out[j], ev_out_d[j], 1024)
            cast2d(od_in[j], od_in_d[j], 1024)
            cast2d(od_out[j], od_out_d[j], 1024)
            cast2d(uq[j], uq_d[j], 256)
            cast2d(ukv[j], ukv_d[j], 128)
        for h in range(4):
            for m in range(2):
                dma(SP, dm(qd[h, m, 64:70, :], daug_d[h, 0]), r=[castres], semres=castres)
                dma(SP, dm(kd[h, m, 64:70, :], daug_d[h, 1]), r=[castres], semres=castres)
        dl = sb(st, "dl", [128, 2, 256], F32)
        dlp = sb(st, "dlp", [128, 64], F32)
        dma(SP, dm(dl[:, :, :], dlam_d), w=[dl.res])
        for j in range(2):
            layer = 2 * j + 1
            lam_init = 0.8 - 0.6 * math.exp(-0.3 * layer)
            for t in range(2):
                op(DVE, tt(dlp[:, :], dl[:, j, 128 * t:128 * t + 64], dl[:, j, 128 * t + 64:128 * t + 128], ALU.mult), r=[dl.res], w=[dlp.res])
                op(DVE, lambda e, t=t, j=j: e.reduce_sum(out=dcol[:, 20 + t:21 + t], in_=dlp[:, :], axis=AX.X), r=[dlp.res], w=[dcol.res])
                op(ACT, act(dcol[:, 20 + t:21 + t], dcol[:, 20 + t:21 + t], AF.Exp), r=[dcol.res], w=[dcol.res])
            op(DVE, tt(dcol[:, 22:23], dcol[:, 21:22], dcol[:, 20:21], ALU.subtract), r=[dcol.res], w=[dcol.res])
            op(DVE, ts(dcol[:, j:j + 1], dcol[:, 22:23], -lam_init, None, ALU.add), r=[dcol.res], w=[dcol.res])
            op(DVE, ts(dcol[:, 2 + j:3 + j], pcols[:, 128 + 32 * j + 2:128 + 32 * j + 3], 1.0 - lam_init, None, ALU.mult), r=[pcols.res, dcol.res], w=[dcol.res])
        for j in range(2):
            op(ACT, act(dcol[:, 4 + 8 * j:12 + 8 * j], pcols[:, 64 + 32 * j + 9:64 + 32 * j + 17], AF.Exp), r=[pcols.res, dcol.res], w=[dcol.res])
        CH = min(S, 2048)
        posi = sb(st, "posi", [128, CH], I32)
        ang = sb(st, "ang", [128, CH], F32)
        tq = sb(st, "tq", [128, CH], F32)
        ki = sb(st, "ki", [128, CH], I32)
        kf = sb(st, "kf", [128, CH], F32)
        rr = sb(st, "rr", [128, CH], F32)
        mk = sb(st, "mk", [128, CH], F32)
        C1 = 6.28125
        C2 = TWO_PI - C1
        for c0 in range(0, S, CH):
            dma(SP, dm(posi[:, :], posb[:, c0:c0 + CH]), w=[posi.res])
            op(DVE, cp(ang[:, :], posi[:, :]), r=[posi.res], w=[ang.res])
            op(DVE, ts(ang[:, :], ang[:, :], cf32[:, F_INVF:F_INVF + 1], None, ALU.mult), r=[ang.res, cf32.res], w=[ang.res])
            for which, tab in ((0, sintab), (1, costab)):
                src = ang
                if which == 1:
                    op(DVE, ts(rr[:, :], ang[:, :], math.pi / 2, None, ALU.add), r=[ang.res], w=[rr.res])
                    src = rr
                op(DVE, ts(tq[:, :], src[:, :], 1.0 / TWO_PI, None, ALU.mult), r=[src.res], w=[tq.res])
                op(DVE, cp(ki[:, :], tq[:, :]), r=[tq.res], w=[ki.res])
                op(DVE, cp(kf[:, :], ki[:, :]), r=[ki.res], w=[kf.res])
                op(DVE, stt(rr[:, :], kf[:, :], -C1, src[:, :], ALU.mult, ALU.add), r=[kf.res, src.res], w=[rr.res])
                op(DVE, stt(rr[:, :], kf[:, :], -C2, rr[:, :], ALU.mult, ALU.add), r=[kf.res, rr.res], w=[rr.res])
                op(DVE, ts(mk[:, :], rr[:, :], math.pi, -TWO_PI, ALU.is_gt, ALU.mult), r=[rr.res], w=[mk.res])
                op(DVE, tt(rr[:, :], rr[:, :], mk[:, :], ALU.add), r=[rr.res, mk.res], w=[rr.res])
                op(DVE, ts(mk[:, :], rr[:, :], -math.pi, TWO_PI, ALU.is_lt, ALU.mult), r=[rr.res], w=[mk.res])
                op(DVE, tt(rr[:, :], rr[:, :], mk[:, :], ALU.add), r=[rr.res, mk.res], w=[rr.res])
                op(DVE, ts(rr[:, :], rr[:, :], math.pi, -math.pi, ALU.min, ALU.max), r=[rr.res], w=[rr.res])
                op(ACT, act(tq[:, :], rr[:, :], AF.Sin), r=[rr.res], w=[tq.res])
                dma(SP, dm(tab[:, c0:c0 + CH], tq[:, :]), r=[tq.res])
        sch.barrier()

    def wview(w2d, c0, ncol, nchunk=8):
        return w2d.rearrange("(i p) c -> p i c", p=128)[:, 0:nchunk, c0:c0 + ncol]

    class Ctx:
        pass

    def proj_phase(l):
        with ExitStack() as st:
            xs = sb(st, "xs", [128, 8, 512], F32)
            xn = sb(st, "xn", [128, 8, 512], BF16)
            sq = sb(st, "sq", [128, 8, 512], BF16)
            rstd = sb(st, "rstd", [128, 512], F32)
            wts = [sb(st, f"wt{i}", [128, 8, 512], BF16) for i in range(4)]
            wctr = [0]
            if l > 0:
                H = sb(st, "H", [128, 32, 512], BF16)
                mx = sb(st, "mx", [128, 8, 512], BF16)
                rl = [sb(st, f"rl{i}", [128, 512], BF16) for i in range(2)]
            if l < DEPTH:
                raws = [sb(st, f"raw{i}", [128, 512], F32) for i in range(8)]
                sqs = [sb(st, f"sqs{i}", [128, 512], BF16) for i in range(8)]
                outs = [sb(st, f"ob{i}", [128, 512], BF16) for i in range(4)]
                rc = sb(st, "rc", [8, 512], F32)
                vt = sb(st, "vt", [128, 4, 512], BF16)
                octr = [0]
                if l % 2 == 0:
                    cqn = sb(st, "cqn", [128, 2, 512], BF16)
                    ckvn = sb(st, "ckvn", [128, 512], BF16)
                    cost = sb(st, "cost", [128, 512], F32)
                    sint = sb(st, "sint", [128, 512], F32)
                    t1 = sb(st, "t1", [128, 512], F32)
                    t2 = sb(st, "t2", [128, 512], F32)
                    krr = sb(st, "krr", [32, 512], F32)
                    wsm = sb(st, "wsm", [128, 2, 768], BF16)
                    wkv = sb(st, "wkv", [128, 1024], BF16)
            pctr = [0]

            def nps():
                pctr[0] += 1
                return ps[pctr[0] % 4]

            def load_w(view, nchunk=8, ncol=512):
                wt = wts[wctr[0] % 4]
                wctr[0] += 1
                dma(SP, dm(wt[:, 0:nchunk, 0:ncol], view), w=[wt.res])
                return wt

            def norm(gbase):
                for c in range(8):
                    op(POOL, tt(sq[:, c, :], xs[:, c, :], xs[:, c, :], ALU.mult), r=[xs.res], w=[sq.res])
                p = nps()
                for c in range(8):
                    op(PE, mm(p[:, :], ONES(), sq[:, c, :], c == 0, c == 7), r=[sq.res, cmat.res], w=[p.res])
                rstd_ops(rstd, p, 128, 1024.0)
                for c in range(8):
                    op(DVE, stt(xn[:, c, :], xs[:, c, :], pcols[:, gbase + c:gbase + c + 1], rstd[:, :], ALU.mult, ALU.mult),
                       r=[xs.res, pcols.res, rstd.res], w=[xn.res])

            def outbuf():
                o = outs[octr[0] % 4]
                octr[0] += 1
                return o

            def proj_chunk(wt, col0, ncol, src, nchunk=8, srcidx=None):
                p = nps()
                for i in range(nchunk):
                    rhs = src[:, i, :] if srcidx is None else srcidx(i)
                    op(PE, mm(p[0:ncol, :], wt[:, i, col0:col0 + ncol], rhs, i == 0, i == nchunk - 1), r=[wt.res, src.res], w=[p.res])
                return p

            def vproj(wt, col0, ncol, src, nchunk, dst_fn, srcidx=None):
                for tb in range(4):
                    p = nps()
                    for i in range(nchunk):
                        lhs = src[:, i, tb * 128:(tb + 1) * 128] if srcidx is None else srcidx(i, tb)
                        op(PE, mm(p[:, 0:ncol], lhs, wt[:, i, col0:col0 + ncol], i == 0, i == nchunk - 1), r=[wt.res, src.res], w=[p.res])
                    op(ACT, act(vt[:, tb, 0:ncol], p[:, 0:ncol], AF.Copy), r=[p.res], w=[vt.res])
                dst_fn()

            for g in range(NG):
                gs = slice(g * 512, (g + 1) * 512)
                src_x = xT if l == 0 else xres
                dma(SP, dm(xs[:, :, :], src_x.rearrange("(c p) s -> p c s", p=128)[:, :, gs]), w=[xs.res])
                if l > 0:
                    lp = l - 1
                    jp = lp // 2
                    dma(SP, dm(mx[:, :, :], mixT.rearrange("(c p) s -> p c s", p=128)[:, :, gs]), w=[mx.res])
                    wout = (ev_out if lp % 2 == 0 else od_out)[jp]
                    for half in range(2):
                        wt = load_w(wview(wout, half * 512, 512))
                        for oc4 in range(4):
                            oc = half * 4 + oc4
                            p = proj_chunk(wt, oc4 * 128, 128, mx)
                            op(DVE, tt(xs[:, oc, :], p[:, :], xs[:, oc, :], ALU.add), r=[p.res, xs.res], w=[xs.res])
                    norm(lp * 16 + 8)
                    for fb in range(8):
                        wt = load_w(wview(w_up[lp], fb * 512, 512))
                        for f4 in range(4):
                            fc = fb * 4 + f4
                            p = proj_chunk(wt, f4 * 128, 128, xn)
                            r_ = rl[fc % 2]
                            op(DVE, ts(r_[:, :], p[:, :], 0.0, None, ALU.max), r=[p.res], w=[r_.res])
                            op(POOL, tt(H[:, fc, :], r_[:, :], r_[:, :], ALU.mult), r=[r_.res], w=[H.res])
                    for half in range(2):
                        accs = [ps[4 + i] for i in range(4)]
                        for fs in range(4):
                            view = w_down[lp].rearrange("(i p) c -> p i c", p=128)[:, fs * 8:(fs + 1) * 8, half * 512:(half + 1) * 512]
                            wt = load_w(view)
                            for oc4 in range(4):
                                for i in range(8):
                                    fc = fs * 8 + i
                                    op(PE, mm(accs[oc4][:, :], wt[:, i, oc4 * 128:(oc4 + 1) * 128], H[:, fc, :], fc == 0, fc == 31),
                                       r=[wt.res, H.res], w=[accs[oc4].res])
                        for oc4 in range(4):
                            oc = half * 4 + oc4
                            op(DVE, tt(xs[:, oc, :], accs[oc4][:, :], xs[:, oc, :], ALU.add), r=[accs[oc4].res, xs.res], w=[xs.res])
                if l == DEPTH:
                    dma(SP, dm(yT.rearrange("(c p) s -> p c s", p=128)[:, :, gs], xs[:, :, :]), r=[xs.res])
                    continue
                dma(SP, dm(xres.rearrange("(c p) s -> p c s", p=128)[:, :, gs], xs[:, :, :]), r=[xs.res])
                norm(l * 16)
                j = l // 2
                if l % 2 == 1:
                    pb = 128 + 32 * j
                    win = od_in[j]
                    for part, dst, scale in ((0, qc, 0.125), (1, kc, 1.0)):
                        wt = load_w(wview(win, part * 512, 512))
                        for c in range(4):
                            p = proj_chunk(wt, c * 128, 128, xn)
                            o = outbuf()
                            op(ACT, act(o[:, :], p[:, :], AF.Copy, scale=scale), r=[p.res], w=[o.res])
                            for hh in range(2):
                                dma(SP, dm(dst[2 * c + hh, :, gs], o[64 * hh:64 * hh + 64, :]), r=[o.res])
                    wt = load_w(wview(win, 1024, 512))

                    def st_cv():
                        for h in range(8):
                            dma(SP, dm(vc[h, :, 4 * g:4 * g + 4, :], vt[:, :, 64 * h:64 * h + 64]), r=[vt.res])
                    vproj(wt, 0, 512, xn, 8, st_cv)
                    for part, dst, gcol in ((3, qd, pb + 0), (4, kd, pb + 1)):
                        wt = load_w(wview(win, part * 512, 512))
                        pc = nps()
                        for c in range(4):
                            p = proj_chunk(wt, c * 128, 128, xn)
                            op(ACT, act(raws[c][:, :], p[:, :], AF.Copy), r=[p.res], w=[raws[c].res])
                            op(POOL, tt(sqs[c][:, :], raws[c][:, :], raws[c][:, :], ALU.mult), r=[raws[c].res], w=[sqs[c].res])
                        for c in range(4):
                            op(PE, mm(pc[0:8, :], cmat[:, C_SEL2 + 8 * c:C_SEL2 + 8 * c + 8], sqs[c][:, :], c == 0, c == 3), r=[cmat.res, sqs[c].res], w=[pc.res])
                        rstd_ops(rc, pc, 8, 64.0)
                        for c in range(4):
                            pbc = nps()
                            op(PE, mm(pbc[:, :], cf32[0:8, F_SEL2T + 128 * c:F_SEL2T + 128 * c + 128], rc[0:8, :], True, True), r=[cf32.res, rc.res], w=[pbc.res])
                            o = outbuf()
                            op(DVE, stt(o[:, :], raws[c][:, :], pcols[:, gcol:gcol + 1], pbc[:, :], ALU.mult, ALU.mult), r=[raws[c].res, pcols.res, pbc.res], w=[o.res])
                            for m in range(2):
                                dma(SP, dm(dst[c, m, 0:64, gs], o[64 * m:64 * m + 64, :]), r=[o.res])
                    wt = load_w(wview(win, 2560, 512))

                    def st_dv():
                        for h in range(4):
                            dma(SP, dm(vd[h, :, 4 * g:4 * g + 4, :], vt[:, :, 128 * h:128 * h + 128]), r=[vt.res])
                    vproj(wt, 0, 512, xn, 8, st_dv)
                else:
                    pb = 64 + 32 * j
                    win = ev_in[j]
                    wt = load_w(wview(win, 0, 512))
                    pc = nps()
                    for c in range(4):
                        p = proj_chunk(wt, c * 128, 128, xn)
                        op(ACT, act(raws[c][:, :], p[:, :], AF.Copy), r=[p.res], w=[raws[c].res])
                        op(POOL, tt(sqs[c][:, :], raws[c][:, :], raws[c][:, :], ALU.mult), r=[raws[c].res], w=[sqs[c].res])
                    for c in range(4):
                        op(PE, mm(pc[0:8, :], cmat[:, C_SEL2 + 8 * c:C_SEL2 + 8 * c + 8], sqs[c][:, :], c == 0, c == 3), r=[cmat.res, sqs[c].res], w=[pc.res])
                    rstd_ops(rc, pc, 8, 64.0)
                    for c in range(4):
                        pbc = nps()
                        op(PE, mm(pbc[:, :], cf32[0:8, F_SEL2T + 128 * c:F_SEL2T + 128 * c + 128], rc[0:8, :], True, True), r=[cf32.res, rc.res], w=[pbc.res])
                        o = outbuf()
                        op(DVE, stt(o[:, :], raws[c][:, :], pcols[:, pb:pb + 1], pbc[:, :], ALU.mult, ALU.mult), r=[raws[c].res, pcols.res, pbc.res], w=[o.res])
                        for hh in range(2):
                            dma(SP, dm(qa[2 * c + hh, :, gs], o[64 * hh:64 * hh + 64, :]), r=[o.res])
                    wt = load_w(wview(win, 512, 512))
                    p = proj_chunk(wt, 0, 128, xn)
                    op(ACT, act(raws[0][:, :], p[:, :], AF.Copy), r=[p.res], w=[raws[0].res])
                    op(POOL, tt(sqs[0][:, :], raws[0][:, :], raws[0][:, :], ALU.mult), r=[raws[0].res], w=[sqs[0].res])
                    pc = nps()
                    op(PE, mm(pc[0:8, :], cmat[:, C_SEL2:C_SEL2 + 8], sqs[0][:, :], True, True), r=[cmat.res, sqs[0].res], w=[pc.res])
                    rstd_ops(rc, pc, 8, 64.0)
                    pbc = nps()
                    op(PE, mm(pbc[:, :], cf32[0:8, F_SEL2T:F_SEL2T + 128], rc[0:8, :], True, True), r=[cf32.res, rc.res], w=[pbc.res])
                    o = outbuf()
                    op(DVE, stt(o[:, :], raws[0][:, :], pcols[:, pb + 1:pb + 2], pbc[:, :], ALU.mult, ALU.mult), r=[raws[0].res, pcols.res, pbc.res], w=[o.res])
                    for hh in range(2):
                        dma(SP, dm(ka[hh, :, gs], o[64 * hh:64 * hh + 64, :]), r=[o.res])

                    def st_av():
                        for h in range(2):
                            dma(SP, dm(va[h, :, 4 * g:4 * g + 4, :], vt[:, :, 64 * h:64 * h + 64]), r=[vt.res])
                    vproj(wt, 128, 128, xn, 8, st_av)
                    for c in range(2):
                        p = proj_chunk(wt, 256 + c * 128, 128, xn)
                        op(ACT, act(raws[c][:, :], p[:, :], AF.Copy), r=[p.res], w=[raws[c].res])
                        op(POOL, tt(sqs[c][:, :], raws[c][:, :], raws[c][:, :], ALU.mult), r=[raws[c].res], w=[sqs[c].res])
                    pc = nps()
                    for c in range(2):
                        op(PE, mm(pc[:, :], ONES(), sqs[c][:, :], c == 0, c == 1), r=[cmat.res, sqs[c].res], w=[pc.res])
                    rstd_ops(rstd, pc, 128, 256.0)
                    for c in range(2):
                        op(DVE, stt(cqn[:, c, :], raws[c][:, :], pcols[:, pb + 2 + c:pb + 3 + c], rstd[:, :], ALU.mult, ALU.mult), r=[raws[c].res, pcols.res, rstd.res], w=[cqn.res])
                    wt = load_w(wview(win, 1024, 160), 8, 160)
                    p = proj_chunk(wt, 0, 128, xn)
                    op(ACT, act(raws[0][:, :], p[:, :], AF.Copy), r=[p.res], w=[raws[0].res])
                    op(POOL, tt(sqs[0][:, :], raws[0][:, :], raws[0][:, :], ALU.mult), r=[raws[0].res], w=[sqs[0].res])
                    pc = nps()
                    op(PE, mm(pc[:, :], ONES(), sqs[0][:, :], True, True), r=[cmat.res, sqs[0].res], w=[pc.res])
                    rstd_ops(rstd, pc, 128, 128.0)
                    op(DVE, stt(ckvn[:, :], raws[0][:, :], pcols[:, pb + 4:pb + 5], rstd[:, :], ALU.mult, ALU.mult), r=[raws[0].res, pcols.res, rstd.res], w=[ckvn.res])
                    pkr = proj_chunk(wt, 128, 32, xn)
                    op(ACT, act(raws[6][0:32, :], pkr[0:32, :], AF.Copy), r=[pkr.res], w=[raws[6].res])
                    op(POOL, tt(sqs[6][0:32, :], raws[6][0:32, :], raws[6][0:32, :], ALU.mult), r=[raws[6].res], w=[sqs[6].res])
                    dma(SP, dm(cost[:, :], costab[:, gs]), w=[cost.res])
                    dma(SP, dm(sint[:, :], sintab[:, gs]), w=[sint.res])
                    op(DVE, ts(t1[0:32, :], raws[6][0:32, :], pcols[0:32, pb + 8:pb + 9], None, ALU.mult), r=[raws[6].res, pcols.res], w=[t1.res])
                    op(DVE, cp(sqs[7][0:32, :], t1[0:32, :]), r=[t1.res], w=[sqs[7].res])
                    prot = nps()
                    op(PE, mm(prot[0:32, :], cmat[0:32, C_PERM:C_PERM + 32], sqs[7][0:32, :], True, True), r=[cmat.res, sqs[7].res], w=[prot.res])
                    op(DVE, tt(t2[0:32, :], prot[0:32, :], sint[0:32, :], ALU.mult), r=[prot.res, sint.res], w=[t2.res])
                    op(DVE, tt(t1[0:32, :], t1[0:32, :], cost[0:32, :], ALU.mult), r=[t1.res, cost.res], w=[t1.res])
                    op(DVE, tt(krr[0:32, :], t1[0:32, :], t2[0:32, :], ALU.add), r=[t1.res, t2.res], w=[krr.res])
                    uqv = uq[j].rearrange("(i p) (h d) -> p i h d", p=128, d=96)
                    for i in range(2):
                        dma(SP, dm(wsm[:, i, 0:512].rearrange("p (h d) -> p h d", d=64), uqv[:, i, :, 0:64]), w=[wsm.res])
                        dma(SP, dm(wsm[:, i, 512:768].rearrange("p (h d) -> p h d", d=32), uqv[:, i, :, 64:96]), w=[wsm.res])
                    ukvv = ukv[j].rearrange("p (h d) -> p h d", d=128)
                    dma(SP, dm(wkv[:, 0:512].rearrange("p (h d) -> p h d", d=64), ukvv[:, :, 0:64]), w=[wkv.res])
                    dma(SP, dm(wkv[:, 512:1024].rearrange("p (h d) -> p h d", d=64), ukvv[:, :, 64:128]), w=[wkv.res])
                    for c in range(6):
                        p = nps()
                        for i in range(2):
                            op(PE, mm(p[:, :], wsm[:, i, c * 128:(c + 1) * 128], cqn[:, i, :], i == 0, i == 1), r=[wsm.res, cqn.res], w=[p.res])
                        op(ACT, act(raws[c][:, :], p[:, :], AF.Copy), r=[p.res], w=[raws[c].res])
                        op(POOL, tt(sqs[c][:, :], raws[c][:, :], raws[c][:, :], ALU.mult), r=[raws[c].res], w=[sqs[c].res])
                    pc = nps()
                    for c in range(6):
                        sel = cmat[:, C_SEL2 + 8 * c:C_SEL2 + 8 * c + 8] if c < 4 else cmat[:, C_SEL4 + 8 * (c - 4):C_SEL4 + 8 * (c - 4) + 8]
                        op(PE, mm(pc[0:8, :], sel, sqs[c][:, :], c == 0, c == 5), r=[cmat.res, sqs[c].res], w=[pc.res])
                    rstd_ops(rc, pc, 8, 96.0)
                    for c in range(6):
                        pbc = nps()
                        selT = cf32[0:8, F_SEL2T + 128 * c:F_SEL2T + 128 * c + 128] if c < 4 else cf32[0:8, F_SEL4T + 128 * (c - 4):F_SEL4T + 128 * (c - 4) + 128]
                        op(PE, mm(pbc[:, :], selT, rc[0:8, :], True, True), r=[cf32.res, rc.res], w=[pbc.res])
                        if c < 4:
                            o = outbuf()
                            op(DVE, stt(o[:, :], raws[c][:, :], pcols[:, pb + 5:pb + 6], pbc[:, :], ALU.mult, ALU.mult), r=[raws[c].res, pcols.res, pbc.res], w=[o.res])
                            for hh in range(2):
                                dma(SP, dm(qb[2 * c + hh, 0:64, gs], o[64 * hh:64 * hh + 64, :]), r=[o.res])
                        else:
                            op(DVE, stt(t1[:, :], raws[c][:, :], pcols[:, pb + 6:pb + 7], pbc[:, :], ALU.mult, ALU.mult), r=[raws[c].res, pcols.res, pbc.res], w=[t1.res])
                            op(POOL, cp(sqs[7][:, :], t1[:, :]), r=[t1.res], w=[sqs[7].res])
                            prot = nps()
                            op(PE, mm(prot[:, :], cmat[:, C_PERM:C_PERM + 128], sqs[7][:, :], True, True), r=[cmat.res, sqs[7].res], w=[prot.res])
                            op(DVE, tt(t2[:, :], prot[:, :], sint[:, :], ALU.mult), r=[prot.res, sint.res], w=[t2.res])
                            op(DVE, tt(t1[:, :], t1[:, :], cost[:, :], ALU.mult), r=[t1.res, cost.res], w=[t1.res])
                            o = outbuf()
                            op(DVE, tt(o[:, :], t1[:, :], t2[:, :], ALU.add), r=[t1.res, t2.res], w=[o.res])
                            for hh in range(4):
                                dma(SP, dm(qb[4 * (c - 4) + hh, 64:96, gs], o[32 * hh:32 * hh + 32, :]), r=[o.res])
                    for c in range(4):
                        p = nps()
                        op(PE, mm(p[:, :], wkv[:, c * 128:(c + 1) * 128], ckvn[:, :], True, True), r=[wkv.res, ckvn.res], w=[p.res])
                        op(ACT, act(raws[c][:, :], p[:, :], AF.Copy), r=[p.res], w=[raws[c].res])
                        op(POOL, tt(sqs[c][:, :], raws[c][:, :], raws[c][:, :], ALU.mult), r=[raws[c].res], w=[sqs[c].res])
                    pc = nps()
                    for c in range(4):
                        op(PE, mm(pc[0:8, :], cmat[:, C_SEL2 + 8 * c:C_SEL2 + 8 * c + 8], sqs[c][:, :], c == 0, False), r=[cmat.res, sqs[c].res], w=[pc.res])
                    op(PE, mm(pc[0:8, :], cmat[0:32, C_ONES:C_ONES + 8], sqs[6][0:32, :], False, True), r=[cmat.res, sqs[6].res], w=[pc.res])
                    rstd_ops(rc, pc, 8, 96.0)
                    for c in range(4):
                        pbc = nps()
                        op(PE, mm(pbc[:, :], cf32[0:8, F_SEL2T + 128 * c:F_SEL2T + 128 * c + 128], rc[0:8, :], True, True), r=[cf32.res, rc.res], w=[pbc.res])
                        o = outbuf()
                        op(DVE, stt(o[:, :], raws[c][:, :], pcols[:, pb + 7:pb + 8], pbc[:, :], ALU.mult, ALU.mult), r=[raws[c].res, pcols.res, pbc.res], w=[o.res])
                        for hh in range(2):
                            dma(SP, dm(kb_[2 * c + hh, 0:64, gs], o[64 * hh:64 * hh + 64, :]), r=[o.res])
                    for h in range(8):
                        pbc = nps()
                        op(PE, mm(pbc[0:32, :], cf32[0:8, F_KSELT + 32 * h:F_KSELT + 32 * h + 32], rc[0:8, :], True, True), r=[cf32.res, rc.res], w=[pbc.res])
                        o = outbuf()
                        op(DVE, tt(o[0:32, :], krr[0:32, :], pbc[0:32, :], ALU.mult), r=[krr.res, pbc.res], w=[o.res])
                        dma(SP, dm(kb_[h, 64:96, gs], o[0:32, :]), r=[o.res])
                    for tb in range(4):
                        p = nps()
                        op(PE, mm(p[:, :], ckvn[:, tb * 128:(tb + 1) * 128], wkv[:, 512:1024], True, True), r=[wkv.res, ckvn.res], w=[p.res])
                        op(ACT, act(vt[:, tb, :], p[:, :], AF.Copy), r=[p.res], w=[vt.res])
                    for h in range(8):
                        dma(SP, dm(vb[h, :, 4 * g:4 * g + 4, :], vt[:, :, 64 * h:64 * h + 64]), r=[vt.res])
        sch.barrier()

    def attn_even(j):
        pb = 64 + 32 * j
        with ExitStack() as st:
            KT = [sb(st, f"KT{i}", [128, S], BF16) for i in range(2)]
            V = [sb(st, f"V{i}", [128, NB, 64], BF16) for i in range(2)]
            Q = [sb(st, f"Q{i}", [128, 512], BF16) for i in range(3)]
            Pb = [sb(st, f"P{i}", [128, 512], BF16) for i in range(4)]
            sbb = [sb(st, f"sbb{i}", [128, 256], F32) for i in range(3)]
            rec = sb(st, "rec", [64, 512], F32)
            ob = [sb(st, f"oo{i}", [64, 512], BF16) for i in range(2)]
            O, L = ps[6], ps[7]
            qctr = [0]
            octr = [0]
            tctr = [0]
            for kvh in range(2):
                kt, v = KT[kvh], V[kvh]
                dma(SP, dm(kt[0:64, :], ka[kvh]), w=[kt.res])
                dma(SP, dm(v[:, :, :], va[kvh]), w=[v.res])
                for hq in range(4):
                    h = 4 * kvh + hq
                    for g in range(NG):
                        q = Q[qctr[0] % 3]
                        qctr[0] += 1
                        dma(SP, dm(q[0:64, :], qa[h, :, g * 512:(g + 1) * 512]), w=[q.res])
                        tiles = []
                        rels = list(range(-1, 4)) if g > 0 else list(range(0, 4))
                        for idx, rel in enumerate(rels):
                            kbi = 4 * g + rel
                            if rel < 0:
                                q0, n, boff = 0, 128, 0
                            else:
                                q0 = 128 * rel
                                n = min(256, 512 - q0)
                                boff = 128
                            t = tctr[0]
                            tctr[0] += 1
                            sp_, P_, sb_ = ps[t % 3], Pb[t % 4], sbb[t % 3]
                            first, last = idx == 0, idx == len(rels) - 1

                            def s1(kbi=kbi, q0=q0, n=n, sp_=sp_, q=q, kt=kt):
                                op(PE, mm(sp_[:, 0:n], kt[0:64, kbi * 128:(kbi + 1) * 128], q[0:64, q0:q0 + n], True, True), r=[kt.res, q.res], w=[sp_.res])

                            def s2(n=n, sp_=sp_, sb_=sb_, P_=P_, boff=boff, h=h):
                                op(DVE, stt(sb_[:, 0:n], sp_[:, 0:n], 0.125, cf32[:, F_ABIAS + 384 * h + boff:F_ABIAS + 384 * h + boff + n], ALU.mult, ALU.add),
                                   r=[sp_.res, cf32.res], w=[sb_.res])
                                op(ACT, act(P_[:, 0:n], sb_[:, 0:n], AF.Exp), r=[sb_.res], w=[P_.res])

                            def s3(kbi=kbi, q0=q0, n=n, P_=P_, v=v, first=first, last=last):
                                op(PE, mm(O[0:64, q0:q0 + n], v[:, kbi, :], P_[:, 0:n], first, last), r=[v.res, P_.res], w=[O.res])
                                op(PE, mm(L[0:64, q0:q0 + n], ONES(128, 64), P_[:, 0:n], first, last), r=[cmat.res, P_.res], w=[L.res])
                            tiles.append([s1, s2, s3])
                        pipeline(tiles, [0, 1, 2])
                        o = ob[octr[0] % 2]
                        octr[0] += 1
                        op(DVE, ts(rec[:, :], L[0:64, :], dcol[0:64, 4 + 8 * j + h:5 + 8 * j + h], None, ALU.add), r=[L.res, dcol.res], w=[rec.res])
                        op(DVE, lambda e: e.reciprocal(out=rec[:, :], in_=rec[:, :]), r=[rec.res], w=[rec.res])
                        op(DVE, tt(o[:, :], O[0:64, :], rec[:, :], ALU.mult), r=[O.res, rec.res], w=[o.res])
                        dma(SP, dm(mixT[64 * h:64 * h + 64, g * 512:(g + 1) * 512], o[:, :]), r=[o.res])
            scaleB = 96.0 ** -0.5
            for h in range(8):
                kt, v = KT[h % 2], V[h % 2]
                dma(SP, dm(kt[0:96, :], kb_[h]), w=[kt.res])
                dma(SP, dm(v[:, :, :], vb[h]), w=[v.res])
                for g in range(NG):
                    q = Q[qctr[0] % 3]
                    qctr[0] += 1
                    dma(SP, dm(q[0:96, :], qb[h, :, g * 512:(g + 1) * 512]), w=[q.res])
                    tiles = []
                    nkb = 4 * g + 4
                    for kbi in range(nkb):
                        rel = kbi - 4 * g
                        q0 = 128 * rel if rel > 0 else 0
                        n = 512 - q0
                        t = tctr[0]
                        tctr[0] += 1
                        sp_, P_ = ps[t % 3], Pb[t % 4]
                        first, last = kbi == 0, kbi == nkb - 1

                        def s1(kbi=kbi, q0=q0, n=n, sp_=sp_, q=q, kt=kt):
                            op(PE, mm(sp_[:, 0:n], kt[0:96, kbi * 128:(kbi + 1) * 128], q[0:96, q0:q0 + n], True, True), r=[kt.res, q.res], w=[sp_.res])

                        def s2(n=n, sp_=sp_, P_=P_, rel=rel):
                            op(ACT, act(P_[:, 0:n], sp_[:, 0:n], AF.Exp, scale=scaleB), r=[sp_.res], w=[P_.res])
                            if rel >= 0:
                                op(DVE, tt(P_[:, 0:128], P_[:, 0:128], cmat[:, C_DFB:C_DFB + 128], ALU.mult), r=[P_.res, cmat.res], w=[P_.res])

                        def s3(kbi=kbi, q0=q0, n=n, P_=P_, v=v, first=first, last=last):
                            op(PE, mm(O[0:64, q0:q0 + n], v[:, kbi, :], P_[:, 0:n], first, last), r=[v.res, P_.res], w=[O.res])
                            op(PE, mm(L[0:64, q0:q0 + n], ONES(128, 64), P_[:, 0:n], first, last), r=[cmat.res, P_.res], w=[L.res])
                        tiles.append([s1, s2, s3])
                    pipeline(tiles, [0, 1, 2])
                    o = ob[octr[0] % 2]
                    octr[0] += 1
                    op(DVE, lambda e: e.reciprocal(out=rec[:, :], in_=L[0:64, :]), r=[L.res], w=[rec.res])
                    op(DVE, tt(o[:, :], O[0:64, :], rec[:, :], ALU.mult), r=[O.res, rec.res], w=[o.res])
                    dma(SP, dm(mixT[512 + 64 * h:512 + 64 * h + 64, g * 512:(g + 1) * 512], o[:, :]), r=[o.res])
        sch.barrier()

    def attn_odd(j):
        with ExitStack() as st:
            KT = [sb(st, f"KT{i}", [128, S], BF16) for i in range(3)]
            V = [sb(st, f"V{i}", [128, NB, 128], BF16) for i in range(2)]
            Q = [sb(st, f"Q{i}", [128, 512], BF16) for i in range(4)]
            Pb = [sb(st, f"P{i}", [128, 512], BF16) for i in range(6)]
            ef = [sb(st, f"ef{i}", [128, 512], F32) for i in range(2)]
            spb = [sb(st, f"spb{i}", [128, 512], BF16) for i in range(3)]
            R32 = sb(st, "R32", [128, 512], F32)
            Rb = [sb(st, f"Rb{i}", [128, 512], BF16) for i in range(2)]
            ob = [sb(st, f"oo{i}", [128, 512], BF16) for i in range(2)]
            f1 = sb(st, "f1", [128, 512], F32)
            f2 = sb(st, "f2", [128, 512], F32)
            f3 = sb(st, "f3", [128, 512], F32)
            fsq = sb(st, "fsq", [128, 512], BF16)
            qctr = [0]
            octr = [0]
            tctr = [0]
            O = ps[7]
            for h in range(8):
                kt, v = KT[h % 2], V[h % 2]
                dma(SP, dm(kt[0:64, :], kc[h]), w=[kt.res])
                dma(SP, dm(v[:, :, 0:64], vc[h]), w=[v.res])
                for g in range(NG):
                    q = Q[qctr[0] % 4]
                    qctr[0] += 1
                    dma(SP, dm(q[0:64, :], qc[h, :, g * 512:(g + 1) * 512]), w=[q.res])
                    tiles = []
                    nkb = 4 * g + 4
                    order = list(range(nkb - 1, -1, -1))
                    for idx, kbi in enumerate(order):
                        rel = kbi - 4 * g
                        q0 = 128 * rel if rel > 0 else 0
                        n = 512 - q0
                        t = tctr[0]
                        tctr[0] += 1
                        zp, e_, s_, a_ = ps[t % 4], ef[t % 2], spb[t % 3], Pb[t % 4]
                        rb_r, rb_w = Rb[idx % 2], Rb[(idx + 1) % 2]
                        first, last = idx == 0, idx == len(order) - 1

                        def s1(kbi=kbi, q0=q0, n=n, zp=zp, q=q, kt=kt):
                            op(PE, mm(zp[:, q0:q0 + n], kt[0:64, kbi * 128:(kbi + 1) * 128], q[0:64, q0:q0 + n], True, False), r=[kt.res, q.res], w=[zp.res])

                        def s2(q0=q0, n=n, zp=zp, e_=e_, s_=s_, rel=rel):
                            op(ACT, act(e_[:, q0:q0 + n], zp[:, q0:q0 + n], AF.Exp), r=[zp.res], w=[e_.res])
                            op(ACT, act(s_[:, q0:q0 + n], e_[:, q0:q0 + n], AF.Ln, bias=onec[:, :]), r=[e_.res, onec.res], w=[s_.res])
                            if rel >= 0:
                                op(DVE, tt(s_[:, q0:q0 + 128], s_[:, q0:q0 + 128], cmat[:, C_DFC:C_DFC + 128], ALU.mult), r=[s_.res, cmat.res], w=[s_.res])

                        def s3(q0=q0, n=n, zp=zp, s_=s_, rb_r=rb_r, rb_w=rb_w, first=first, last=last):
                            op(PE, mm(zp[:, q0:q0 + n], cmat[:, C_NEGU:C_NEGU + 128], s_[:, q0:q0 + n], False, first), r=[cmat.res, s_.res], w=[zp.res])
                            if not first:
                                op(PE, mm(zp[:, q0:q0 + n], cmat[:, C_NEGONES:C_NEGONES + 128], rb_r[:, q0:q0 + n], False, True), r=[cmat.res, rb_r.res], w=[zp.res])
                            if not last:
                                if first:
                                    op(POOL, lambda e: e.memset(R32[:, :], 0.0), w=[R32.res])
                                op(POOL, tt(R32[:, q0:q0 + n], R32[:, q0:q0 + n], s_[:, q0:q0 + n], ALU.add), r=[R32.res, s_.res], w=[R32.res])
                                op(DVE, cp(rb_w[:, :], R32[:, :]), r=[R32.res], w=[rb_w.res])

                        def s4(q0=q0, n=n, zp=zp, a_=a_, rel=rel):
                            op(ACT, act(a_[:, q0:q0 + n], zp[:, q0:q0 + n], AF.Exp), r=[zp.res], w=[a_.res])
                            if rel >= 0:
                                op(DVE, tt(a_[:, q0:q0 + 128], a_[:, q0:q0 + 128], cmat[:, C_DFC:C_DFC + 128], ALU.mult), r=[a_.res, cmat.res], w=[a_.res])

                        def s5(kbi=kbi, q0=q0, n=n, a_=a_, v=v, first=first, last=last):
                            op(PE, mm(O[0:64, q0:q0 + n], v[:, kbi, 0:64], a_[:, q0:q0 + n], first, last), r=[v.res, a_.res], w=[O.res])
                        tiles.append([s1, s2, s3, s4, s5])
                    pipeline(tiles, [0, 1, 2, 2, 3])
                    o = ob[octr[0] % 2]
                    octr[0] += 1
                    op(DVE, cp(o[0:64, :], O[0:64, :]), r=[O.res], w=[o.res])
                    dma(SP, dm(mixT[64 * h:64 * h + 64, g * 512:(g + 1) * 512], o[0:64, :]), r=[o.res])
            O1, L1, O2, L2 = ps[4], ps[5], ps[6], ps[7]
            for h in range(4):
                k1, k2, v = KT[0], KT[1], V[h % 2]
                dma(SP, dm(k1[0:70, :], kd[h, 0]), w=[k1.res])
                dma(SP, dm(k2[0:70, :], kd[h, 1]), w=[k2.res])
                dma(SP, dm(v[:, :, :], vd[h]), w=[v.res])
                for g in range(NG):
                    q1 = Q[qctr[0] % 4]
                    q2 = Q[(qctr[0] + 1) % 4]
                    qctr[0] += 2
                    dma(SP, dm(q1[0:70, :], qd[h, 0, :, g * 512:(g + 1) * 512]), w=[q1.res])
                    dma(SP, dm(q2[0:70, :], qd[h, 1, :, g * 512:(g + 1) * 512]), w=[q2.res])
                    tiles = []
                    nkb = 4 * g + 4
                    for kbi in range(nkb):
                        rel = kbi - 4 * g
                        q0 = 128 * rel if rel > 0 else 0
                        n = 512 - q0
                        t = tctr[0]
                        tctr[0] += 1
                        sa, sb2 = ps[(2 * t) % 4], ps[(2 * t + 1) % 4]
                        Pa, Pb2 = Pb[(2 * t) % 6], Pb[(2 * t + 1) % 6]
                        first, last = kbi == 0, kbi == nkb - 1

                        def s1(kbi=kbi, q0=q0, n=n, sa=sa, sb2=sb2, q1=q1, q2=q2):
                            op(PE, mm(sa[:, 0:n], k1[0:70, kbi * 128:(kbi + 1) * 128], q1[0:70, q0:q0 + n], True, True), r=[k1.res, q1.res], w=[sa.res])
                            op(PE, mm(sb2[:, 0:n], k2[0:70, kbi * 128:(kbi + 1) * 128], q2[0:70, q0:q0 + n], True, True), r=[k2.res, q2.res], w=[sb2.res])

                        def s2(n=n, sa=sa, sb2=sb2, Pa=Pa, Pb2=Pb2, rel=rel, h=h):
                            for s_, p_ in ((sa, Pa), (sb2, Pb2)):
                                op(ACT, act(p_[:, 0:n], s_[:, 0:n], AF.Exp, scale=0.125), r=[s_.res], w=[p_.res])
                                if rel >= 0:
                                    op(DVE, tt(p_[:, 0:128], p_[:, 0:128], cmat[:, C_DFD + 128 * h:C_DFD + 128 * h + 128], ALU.mult), r=[p_.res, cmat.res], w=[p_.res])

                        def s3(kbi=kbi, q0=q0, n=n, Pa=Pa, Pb2=Pb2, v=v, first=first, last=last):
                            op(PE, mm(O1[:, q0:q0 + n], v[:, kbi, :], Pa[:, 0:n], first, last), r=[v.res, Pa.res], w=[O1.res])
                            op(PE, mm(L1[:, q0:q0 + n], ONES(), Pa[:, 0:n], first, last), r=[cmat.res, Pa.res], w=[L1.res])
                            op(PE, mm(O2[:, q0:q0 + n], v[:, kbi, :], Pb2[:, 0:n], first, last), r=[v.res, Pb2.res], w=[O2.res])
                            op(PE, mm(L2[:, q0:q0 + n], ONES(), Pb2[:, 0:n], first, last), r=[cmat.res, Pb2.res], w=[L2.res])
                        tiles.append([s1, s2, s3])
                    pipeline(tiles, [0, 1, 2])
                    op(DVE, lambda e: e.reciprocal(out=f1[:, :], in_=L1[:, :]), r=[L1.res], w=[f1.res])
                    op(DVE, tt(f1[:, :], O1[:, :], f1[:, :], ALU.mult), r=[O1.res, f1.res], w=[f1.res])
                    op(DVE, lambda e: e.reciprocal(out=f2[:, :], in_=L2[:, :]), r=[L2.res], w=[f2.res])
                    op(DVE, tt(f2[:, :], O2[:, :], f2[:, :], ALU.mult), r=[O2.res, f2.res], w=[f2.res])
                    op(DVE, stt(f3[:, :], f2[:, :], dcol[:, j:j + 1], f1[:, :], ALU.mult, ALU.add), r=[f2.res, dcol.res, f1.res], w=[f3.res])
                    op(POOL, tt(fsq[:, :], f3[:, :], f3[:, :], ALU.mult), r=[f3.res], w=[fsq.res])
                    pn = ps[(2 * tctr[0]) % 4]
                    op(PE, mm(pn[:, :], ONES(), fsq[:, :], True, True), r=[cmat.res, fsq.res], w=[pn.res])
                    rstd_ops(f1, pn, 128, 128.0)
                    o = ob[octr[0] % 2]
                    octr[0] += 1
                    op(DVE, stt(o[:, :], f3[:, :], dcol[:, 2 + j:3 + j], f1[:, :], ALU.mult, ALU.mult), r=[f3.res, dcol.res, f1.res], w=[o.res])
                    dma(SP, dm(mixT[512 + 128 * h:512 + 128 * h + 128, g * 512:(g + 1) * 512], o[:, :]), r=[o.res])
        sch.barrier()

    stop = int(os.environ.get("KSTOP", "99"))
    cnt = 0
    for l in range(DEPTH + 1):
        if cnt >= stop:
            break
        proj_phase(l)
        cnt += 1
        if l < DEPTH:
            if cnt >= stop:
                break
            if l % 2 == 0:
                attn_even(l // 2)
            else:
                attn_odd(l // 2)
            cnt += 1
    sch.barrier()

    with nc.Block() as block:
        @block.tensor
        def _(e):
            sch.emit(PE, e)

        @block.scalar
        def _(e):
            sch.emit(ACT, e)

        @block.vector
        def _(e):
            sch.emit(DVE, e)

        @block.gpsimd
        def _(e):
            sch.emit(POOL, e)

        @block.sync
        def _(e):
            sch.emit(SP, e)
    stack.close()
    return nc


def host_consts(S):
    bf = ml_dtypes.bfloat16
    cm = np.zeros((128, NCM), np.float32)
    cm[:, C_ONES:C_ONES + 128] = 1.0
    for m in range(128):
        if m % 32 < 16:
            cm[m + 16, C_PERM + m] = -1.0
        else:
            cm[m - 16, C_PERM + m] = 1.0
    jj, kk = np.meshgrid(np.arange(128), np.arange(128), indexing="ij")
    cm[:, C_NEGU:C_NEGU + 128] = -(jj >= kk).astype(np.float32)
    cm[:, C_NEGONES:C_NEGONES + 128] = -1.0
    k_, q_ = jj, kk
    cm[:, C_DFB:C_DFB + 128] = ((k_ // 64) <= (q_ // 64)).astype(np.float32)
    cm[:, C_DFC:C_DFC + 128] = (k_ < q_).astype(np.float32)
    for h in range(4):
        m = 2.0 ** (-8.0 * (h + 1) / 4)
        d = np.where(k_ <= q_, 1.0, np.where((k_ // 64) == (q_ // 64), np.exp(-2.0 * m * (k_ - q_)), 0.0))
        cm[:, C_DFD + 128 * h:C_DFD + 128 * h + 128] = d
    p = np.arange(128)
    for c in range(4):
        cm[p, C_SEL2 + 8 * c + 2 * c + p // 64] = 1.0
    for r in range(2):
        cm[p, C_SEL4 + 8 * r + 4 * r + p // 32] = 1.0
    cf = np.zeros((128, NCF), np.float32)
    for c in range(4):
        cf[2 * c + p // 64, F_SEL2T + 128 * c + p] = 1.0
    for r in range(2):
        cf[4 * r + p // 32, F_SEL4T + 128 * r + p] = 1.0
    for h in range(8):
        cf[h, F_KSELT + 32 * h:F_KSELT + 32 * h + 32] = 1.0
    for h in range(8):
        m = 2.0 ** (-8.0 * (h + 1) / 8)
        kpos = np.arange(128)[:, None] - 128
        qpos = np.arange(128)[None, :]
        dch = qpos // 64 - np.floor_divide(kpos, 64)
        ok = (dch >= 0) & (dch <= 2)
        cf[:, F_ABIAS + 384 * h:F_ABIAS + 384 * h + 128] = np.where(ok, -m * np.abs(qpos - kpos), -30000.0)
        kpos = np.arange(128)[:, None]
        qpos = np.arange(256)[None, :]
        dch = qpos // 64 - kpos // 64
        ok = (dch >= 0) & (dch <= 2)
        cf[:, F_ABIAS + 384 * h + 128:F_ABIAS + 384 * h + 384] = np.where(ok, -m * np.abs(qpos - kpos), -30000.0)
    half = 16
    inv = (np.float32(10000.0) ** (-np.arange(half, dtype=np.float32) / np.float32(half))).astype(np.float32)
    cf[:, F_INVF] = inv[p % 16]
    pos = np.arange(S)
    a, b, c = pos // 1024, (pos % 1024) // 32, pos % 32
    daug = np.zeros((4, 2, 6, S), np.float32)
    for h in range(4):
        m = 2.0 ** (-8.0 * (h + 1) / 4) * 8.0
        daug[h, 0, 0], daug[h, 0, 1], daug[h, 0, 2] = -m * 1024 * a, -m * 32 * b, -m * c
        daug[h, 0, 3:6] = 1.0
        daug[h, 1, 0:3] = 1.0
        daug[h, 1, 3], daug[h, 1, 4], daug[h, 1, 5] = m * 1024 * a, m * 32 * b, m * c
    return cm.astype(bf), cf, daug.astype(bf)


def host_pcols(inp):
    pc = np.zeros((128, NPC), np.float32)
    p = np.arange(128)
    for l in range(4):
        pc[:, l * 16:l * 16 + 8] = inp["norm_mix_g"][l].reshape(8, 128).T
        pc[:, l * 16 + 8:l * 16 + 16] = inp["norm_ffn_g"][l].reshape(8, 128).T
    for j in range(2):
        b = 64 + 32 * j
        pc[:, b + 0] = inp["a_q_norm"][j][p % 64]
        pc[:, b + 1] = inp["a_k_norm"][j][p % 64]
        pc[:, b + 2:b + 4] = inp["b_cq_norm"][j].reshape(2, 128).T
        pc[:, b + 4] = inp["b_ckv_norm"][j]
        pc[:, b + 5] = inp["b_q_norm"][j][p % 64]
        pc[:, b + 6] = inp["b_q_norm"][j][64 + p % 32]
        pc[:, b + 7] = inp["b_k_norm"][j][p % 64]
        pc[:, b + 8] = inp["b_k_norm"][j][64 + p % 32]
        pc[:, b + 9:b + 17] = inp["a_sinks"][j][None, :]
        b = 128 + 32 * j
        pc[:, b + 0] = inp["d_q_norm"][j].reshape(128)
        pc[:, b + 1] = inp["d_k_norm"][j].reshape(128)
        pc[:, b + 2] = inp["d_subln"][j]
    dlam = np.broadcast_to(inp["d_lambda"].reshape(1, 2, 256), (128, 2, 256)).astype(np.float32)
    return pc, np.ascontiguousarray(dlam)


_CACHE = {}


def kernel(**inputs):
    inp = {k: np.asarray(v) for k, v in inputs.items()}
    x = inp["x"]
    B, S, D = x.shape
    if S not in _CACHE:
        _CACHE[S] = build(S)
    nc = _CACHE[S]
    cm, cf, daug = host_consts(S)
    pc, dlam = host_pcols(inp)
    shared = {
        "pcols": pc, "dlam": dlam, "cmat": cm, "cf32": cf, "daug": daug,
        "mlp_w_up": inp["mlp_w_up"], "mlp_w_down": inp["mlp_w_down"],
        "ev_w_in": inp["ev_w_in"], "ev_w_out": inp["ev_w_out"],
        "b_w_uq": inp["b_w_uq"], "b_w_ukv": inp["b_w_ukv"],
        "od_w_in": inp["od_w_in"], "od_w_out": inp["od_w_out"],
    }
    in_maps = []
    for b in range(B):
        m = dict(shared)
        m["xT"] = np.ascontiguousarray(x[b].T)
        m["posb"] = np.ascontiguousarray(np.broadcast_to(inp["positions"][b][None, :], (128, S))).astype(np.int32)
        in_maps.append(m)
    res = run_bass_kernel_spmd(nc, in_maps, core_ids=list(range(B)))
    out = np.stack([np.ascontiguousarray(r["yT"].T) for r in res.results], axis=0)
    return out.astype(np.float32)
```

```python
import math
import os
from contextlib import ExitStack

import numpy as np
import ml_dtypes
import concourse.bass as bass
import concourse.mybir as mybir
from concourse.bass_utils import run_bass_kernel_spmd

F32, BF16, I32 = mybir.dt.float32, mybir.dt.bfloat16, mybir.dt.int32
AF = mybir.ActivationFunctionType
ALU = mybir.AluOpType
AX = mybir.AxisListType
PE, ACT, DVE, POOL, SP = 0, 1, 2, 3, 4
EPS = 1e-6
DEPTH = 4
TWO_PI = 2.0 * math.pi

C_ONES, C_PERM, C_NEGU, C_NEGONES, C_DFB, C_DFC, C_DFD, C_SEL2, C_SEL4 = 0, 128, 256, 384, 512, 640, 768, 1280, 1312
NCM = 1328
F_SEL2T, F_SEL4T, F_KSELT, F_ABIAS, F_INVF = 0, 512, 768, 1024, 4096
NCF = 4097
NPC = 192


class Res:
    __slots__ = ("lw", "rd", "sem", "cnt")

    def __init__(self):
        self.lw = None
        self.rd = []
        self.sem = None
        self.cnt = 0


class Sched:
    def __init__(self, nc, stack):
        self.nc = nc
        self.stack = stack
        self.q = [[] for _ in range(5)]
        self.seq = [0] * 5
        self.esem = [stack.enter_context(nc.semaphore(f"es{i}")) for i in range(5)]
        self.waited = [dict() for _ in range(5)]
        self.dsems = []
        self.free_dsems = []
        self.semcnt = {}
        self.nds = 0

    def _waits(self, e, deps):
        best = {}
        for (sem, val, src) in deps:
            if e == PE and src == PE:
                continue
            k = id(sem)
            if self.waited[e].get(k, 0) >= val:
                continue
            if k not in best or best[k][1] < val:
                best[k] = (sem, val)
        out = []
        for k, (sem, val) in best.items():
            self.waited[e][k] = val
            out.append((sem, val))
        return out

    def _deps(self, r, w, e=-9):
        deps = []
        for x in r:
            if x.lw is not None:
                deps.append(x.lw)
        for x in w:
            if x.lw is not None and x.lw[2] != e:
                deps.append(x.lw)
            for t in x.rd:
                if t[2] != e:
                    deps.append(t)
        return deps

    def op(self, e, fn, r=(), w=()):
        waits = self._waits(e, self._deps(r, w, e))
        self.seq[e] += 1
        tok = (self.esem[e], self.seq[e], e)
        self.q[e].append((waits, fn, (self.esem[e], 1), True))
        for x in r:
            x.rd.append(tok)
        for x in w:
            x.lw = tok
            x.rd = []

    def dma(self, qe, fn, r=(), w=(), semres=None):
        waits = self._waits(qe, self._deps(r, w))
        sr = semres if semres is not None else (w[0] if w else r[0])
        if sr.sem is None:
            if self.free_dsems:
                sr.sem = self.free_dsems.pop()
            else:
                self.nds += 1
                sr.sem = self.stack.enter_context(self.nc.semaphore(f"ds{self.nds}"))
            sr.cnt = self.semcnt.get(id(sr.sem), 0)
            self.dsems.append(sr)
        sr.cnt += 16
        self.semcnt[id(sr.sem)] = sr.cnt
        tok = (sr.sem, sr.cnt, -1)
        self.q[qe].append((waits, fn, (sr.sem, 16), False))
        for x in r:
            x.rd.append(tok)
        for x in w:
            x.lw = tok
            x.rd = []

    def barrier(self):
        toks = [(self.esem[i], self.seq[i], i) for i in range(5) if self.seq[i] > 0]
        toks += [(sr.sem, sr.cnt, -1) for sr in self.dsems]
        for e in range(5):
            deps = [t for t in toks if t[2] != e]
            waits = self._waits(e, [(s, v, -2) for (s, v, _) in deps])
            if waits:
                self.q[e].append((waits, None, None, False))
        for sr in self.dsems:
            self.free_dsems.append(sr.sem)
            sr.sem = None
        self.dsems = []

    def emit(self, e, eng):
        for (waits, fn, inc, attach) in self.q[e]:
            if fn is None:
                for (sem, val) in waits:
                    eng.wait_ge(sem, val)
                continue
            if attach and waits:
                for (sem, val) in waits[:-1]:
                    eng.wait_ge(sem, val)
                ins = fn(eng)
                ins._wait_ge(*waits[-1])
            else:
                for (sem, val) in waits:
                    eng.wait_ge(sem, val)
                ins = fn(eng)
            ins.then_inc(inc[0], inc[1])


def pipeline(tiles, skews):
    T = len(tiles)
    mx = max(skews)
    for i in range(T + mx):
        for s, sk in enumerate(skews):
            t = i - sk
            if 0 <= t < T and tiles[t][s] is not None:
                tiles[t][s]()


class T:
    __slots__ = ("h", "res")

    def __init__(self, h):
        self.h = h
        self.res = Res()

    def __getitem__(self, k):
        return self.h[k]


def build(S):
    NG = S // 512
    NB = S // 128
    nc = bass.Bass("TRN2", target_bir_lowering=False)

    def din(name, shape, dt=F32):
        return nc.dram_tensor(name, list(shape), dt, kind="ExternalInput").ap()

    def dscr(name, shape, dt=BF16):
        return nc.dram_tensor(name, list(shape), dt).ap()

    xT = din("xT", [1024, S])
    posb = din("posb", [128, S], I32)
    pcols_d = din("pcols", [128, NPC])
    dlam_d = din("dlam", [128, 2, 256])
    cmat_d = din("cmat", [128, NCM], BF16)
    cf32_d = din("cf32", [128, NCF])
    daug_d = din("daug", [4, 2, 6, S], BF16)
    w_up_d = din("mlp_w_up", [4, 1024, 4096])
    w_down_d = din("mlp_w_down", [4, 4096, 1024])
    ev_in_d = din("ev_w_in", [2, 1024, 1184])
    ev_out_d = din("ev_w_out", [2, 1024, 1024])
    uq_d = din("b_w_uq", [2, 256, 768])
    ukv_d = din("b_w_ukv", [2, 128, 1024])
    od_in_d = din("od_w_in", [2, 1024, 3072])
    od_out_d = din("od_w_out", [2, 1024, 1024])
    yT = nc.dram_tensor("yT", [1024, S], F32, kind="ExternalOutput").ap()

    w_up = dscr("s_w_up", [4, 1024, 4096])
    w_down = dscr("s_w_down", [4, 4096, 1024])
    ev_in = dscr("s_ev_in", [2, 1024, 1184])
    ev_out = dscr("s_ev_out", [2, 1024, 1024])
    uq = dscr("s_uq", [2, 256, 768])
    ukv = dscr("s_ukv", [2, 128, 1024])
    od_in = dscr("s_od_in", [2, 1024, 3072])
    od_out = dscr("s_od_out", [2, 1024, 1024])
    xres = dscr("s_xres", [1024, S], F32)
    mixT = dscr("s_mix", [1024, S])
    costab = dscr("s_cos", [128, S], F32)
    sintab = dscr("s_sin", [128, S], F32)
    qa = dscr("s_qa", [8, 64, S])
    ka = dscr("s_ka", [2, 64, S])
    va = dscr("s_va", [2, 128, NB, 64])
    qb = dscr("s_qb", [8, 96, S])
    kb_ = dscr("s_kb", [8, 96, S])
    vb = dscr("s_vb", [8, 128, NB, 64])
    qc = dscr("s_qc", [8, 64, S])
    kc = dscr("s_kc", [8, 64, S])
    vc = dscr("s_vc", [8, 128, NB, 64])
    qd = dscr("s_qd", [4, 2, 70, S])
    kd = dscr("s_kd", [4, 2, 70, S])
    vd = dscr("s_vd", [4, 128, NB, 128])

    stack = ExitStack()
    sch = Sched(nc, stack)
    op, dma = sch.op, sch.dma

    nctr = [0]

    def sb(st, name, shape, dt):
        nctr[0] += 1
        return T(st.enter_context(nc.sbuf_tensor(f"t{nctr[0]}_{name}", list(shape), dt)))

    ps = [T(stack.enter_context(nc.psum_tensor(f"ps{i}", [128, 512], F32))) for i in range(8)]
    cmat = sb(stack, "cmat", [128, NCM], BF16)
    cf32 = sb(stack, "cf32", [128, NCF], F32)
    pcols = sb(stack, "pcols", [128, NPC], F32)
    dcol = sb(stack, "dcol", [128, 32], F32)
    epsc = sb(stack, "epsc", [128, 1], F32)
    onec = sb(stack, "onec", [128, 1], F32)

    ONES = lambda k=128, m=128: cmat[0:k, C_ONES:C_ONES + m]

    def act(out, in_, func, scale=1.0, bias=None):
        if bias is None:
            return lambda e: e.activation(out=out, in_=in_, func=func, scale=scale)
        return lambda e: e.activation(out=out, in_=in_, func=func, scale=scale, bias=bias)

    def mm(out, lhsT, rhs, start, stop):
        return lambda e: e.matmul(out, lhsT=lhsT, rhs=rhs, start=start, stop=stop)

    def tt(out, in0, in1, o):
        return lambda e: e.tensor_tensor(out=out, in0=in0, in1=in1, op=o)

    def ts(out, in0, s1, s2, o0, o1=None):
        if o1 is None:
            return lambda e: e.tensor_scalar(out=out, in0=in0, scalar1=s1, scalar2=None, op0=o0)
        return lambda e: e.tensor_scalar(out=out, in0=in0, scalar1=s1, scalar2=s2, op0=o0, op1=o1)

    def stt(out, in0, s, in1, o0, o1):
        return lambda e: e.scalar_tensor_tensor(out=out, in0=in0, scalar=s, in1=in1, op0=o0, op1=o1)

    def cp(out, in_):
        return lambda e: e.tensor_copy(out=out, in_=in_)

    def dm(out, in_):
        return lambda e: e.dma_start(out=out, in_=in_)

    def rstd_ops(dst, src_ps, n, d):
        op(ACT, act(dst[0:n, :], src_ps[0:n, :], AF.Ln, scale=1.0 / d, bias=epsc[0:n, :]), r=[src_ps.res, epsc.res], w=[dst.res])
        op(ACT, act(dst[0:n, :], dst[0:n, :], AF.Exp, scale=-0.5), r=[dst.res], w=[dst.res])

    with ExitStack() as st:
        dma(SP, dm(cmat[:, :], cmat_d), w=[cmat.res])
        dma(SP, dm(cf32[:, :], cf32_d), w=[cf32.res])
        dma(SP, dm(pcols[:, :], pcols_d), w=[pcols.res])
        op(DVE, lambda e: e.memset(epsc[:, :], EPS), w=[epsc.res])
        op(DVE, lambda e: e.memset(onec[:, :], 1.0), w=[onec.res])
        castres = Res()

        def cast2d(dst, src, rows, step=128):
            for r0 in range(0, rows, step):
                r1 = min(rows, r0 + step)
                dma(POOL, dm(dst[r0:r1, :], src[r0:r1, :]), r=[castres], semres=castres)

        for l in range(4):
            cast2d(w_up[l], w_up_d[l], 1024)
            cast2d(w_down[l], w_down_d[l], 4096, 256)
        for j in range(2):
            cast2d(ev_in[j], ev_in_d[j], 1024)
            cast2d(ev_out[j], ev_out_d[j], 1024)
            cast2d(od_in[j], od_in_d[j], 1024)
            cast2d(od_out[j], od_out_d[j], 1024)
            cast2d(uq[j], uq_d[j], 256)
            cast2d(ukv[j], ukv_d[j], 128)
        for h in range(4):
            for m in range(2):
                dma(SP, dm(qd[h, m, 64:70, :], daug_d[h, 0]), r=[castres], semres=castres)
                dma(SP, dm(kd[h, m, 64:70, :], daug_d[h, 1]), r=[castres], semres=castres)
        dl = sb(st, "dl", [128, 2, 256], F32)
        dlp = sb(st, "dlp", [128, 64], F32)
        dma(SP, dm(dl[:, :, :], dlam_d), w=[dl.res])
        for j in range(2):
            layer = 2 * j + 1
            lam_init = 0.8 - 0.6 * math.exp(-0.3 * layer)
            for t in range(2):
                op(DVE, tt(dlp[:, :], dl[:, j, 128 * t:128 * t + 64], dl[:, j, 128 * t + 64:128 * t + 128], ALU.mult), r=[dl.res], w=[dlp.res])
                op(DVE, lambda e, t=t, j=j: e.reduce_sum(out=dcol[:, 20 + t:21 + t], in_=dlp[:, :], axis=AX.X), r=[dlp.res], w=[dcol.res])
                op(ACT, act(dcol[:, 20 + t:21 + t], dcol[:, 20 + t:21 + t], AF.Exp), r=[dcol.res], w=[dcol.res])
            op(DVE, tt(dcol[:, 22:23], dcol[:, 21:22], dcol[:, 20:21], ALU.subtract), r=[dcol.res], w=[dcol.res])
            op(DVE, ts(dcol[:, j:j + 1], dcol[:, 22:23], -lam_init, None, ALU.add), r=[dcol.res], w=[dcol.res])
            op(DVE, ts(dcol[:, 2 + j:3 + j], pcols[:, 128 + 32 * j + 2:128 + 32 * j + 3], 1.0 - lam_init, None, ALU.mult), r=[pcols.res, dcol.res], w=[dcol.res])
        for j in range(2):
            op(ACT, act(dcol[:, 4 + 8 * j:12 + 8 * j], pcols[:, 64 + 32 * j + 9:64 + 32 * j + 17], AF.Exp), r=[pcols.res, dcol.res], w=[dcol.res])
        CH = min(S, 2048)
        posi = sb(st, "posi", [128, CH], I32)
        ang = sb(st, "ang", [128, CH], F32)
        tq = sb(st, "tq", [128, CH], F32)
        ki = sb(st, "ki", [128, CH], I32)
        kf = sb(st, "kf", [128, CH], F32)
        rr = sb(st, "rr", [128, CH], F32)
        mk = sb(st, "mk", [128, CH], F32)
        C1 = 6.28125
        C2 = TWO_PI - C1
        for c0 in range(0, S, CH):
            dma(SP, dm(posi[:, :], posb[:, c0:c0 + CH]), w=[posi.res])
            op(DVE, cp(ang[:, :], posi[:, :]), r=[posi.res], w=[ang.res])
            op(DVE, ts(ang[:, :], ang[:, :], cf32[:, F_INVF:F_INVF + 1], None, ALU.mult), r=[ang.res, cf32.res], w=[ang.res])
            for which, tab in ((0, sintab), (1, costab)):
                src = ang
                if which == 1:
                    op(DVE, ts(rr[:, :], ang[:, :], math.pi / 2, None, ALU.add), r=[ang.res], w=[rr.res])
                    src = rr
                op(DVE, ts(tq[:, :], src[:, :], 1.0 / TWO_PI, None, ALU.mult), r=[src.res], w=[tq.res])
                op(DVE, cp(ki[:, :], tq[:, :]), r=[tq.res], w=[ki.res])
                op(DVE, cp(kf[:, :], ki[:, :]), r=[ki.res], w=[kf.res])
                op(DVE, stt(rr[:, :], kf[:, :], -C1, src[:, :], ALU.mult, ALU.add), r=[kf.res, src.res], w=[rr.res])
                op(DVE, stt(rr[:, :], kf[:, :], -C2, rr[:, :], ALU.mult, ALU.add), r=[kf.res, rr.res], w=[rr.res])
                op(DVE, ts(mk[:, :], rr[:, :], math.pi, -TWO_PI, ALU.is_gt, ALU.mult), r=[rr.res], w=[mk.res])
                op(DVE, tt(rr[:, :], rr[:, :], mk[:, :], ALU.add), r=[rr.res, mk.res], w=[rr.res])
                op(DVE, ts(mk[:, :], rr[:, :], -math.pi, TWO_PI, ALU.is_lt, ALU.mult), r=[rr.res], w=[mk.res])
                op(DVE, tt(rr[:, :], rr[:, :], mk[:, :], ALU.add), r=[rr.res, mk.res], w=[rr.res])
                op(DVE, ts(rr[:, :], rr[:, :], math.pi, -math.pi, ALU.min, ALU.max), r=[rr.res], w=[rr.res])
                op(ACT, act(tq[:, :], rr[:, :], AF.Sin), r=[rr.res], w=[tq.res])
                dma(SP, dm(tab[:, c0:c0 + CH], tq[:, :]), r=[tq.res])
        sch.barrier()

    def wview(w2d, c0, ncol, nchunk=8):
        return w2d.rearrange("(i p) c -> p i c", p=128)[:, 0:nchunk, c0:c0 + ncol]

    class Ctx:
        pass

    def proj_phase(l):
        with ExitStack() as st:
            xs = sb(st, "xs", [128, 8, 512], F32)
            xn = sb(st, "xn", [128, 8, 512], BF16)
            sq = sb(st, "sq", [128, 8, 512], BF16)
            rstd = sb(st, "rstd", [128, 512], F32)
            wts = [sb(st, f"wt{i}", [128, 8, 512], BF16) for i in range(4)]
            wctr = [0]
            if l > 0:
                H = sb(st, "H", [128, 32, 512], BF16)
                mx = sb(st, "mx", [128, 8, 512], BF16)
                rl = [sb(st, f"rl{i}", [128, 512], BF16) for i in range(2)]
            if l < DEPTH:
                raws = [sb(st, f"raw{i}", [128, 512], F32) for i in range(8)]
                sqs = [sb(st, f"sqs{i}", [128, 512], BF16) for i in range(8)]
                outs = [sb(st, f"ob{i}", [128, 512], BF16) for i in range(4)]
                rc = sb(st, "rc", [8, 512], F32)
                vt = sb(st, "vt", [128, 4, 512], BF16)
                octr = [0]
                if l % 2 == 0:
                    cqn = sb(st, "cqn", [128, 2, 512], BF16)
                    ckvn = sb(st, "ckvn", [128, 512], BF16)
                    cost = sb(st, "cost", [128, 512], F32)
                    sint = sb(st, "sint", [128, 512], F32)
                    t1 = sb(st, "t1", [128, 512], F32)
                    t2 = sb(st, "t2", [128, 512], F32)
                    krr = sb(st, "krr", [32, 512], F32)
                    wsm = sb(st, "wsm", [128, 2, 768], BF16)
                    wkv = sb(st, "wkv", [128, 1024], BF16)
            pctr = [0]
            STQ = ACT

            def nps():
                pctr[0] += 1
                return ps[pctr[0] % 4]

            def load_w(view, nchunk=8, ncol=512):
                wt = wts[wctr[0] % 4]
                wctr[0] += 1
                dma(SP, dm(wt[:, 0:nchunk, 0:ncol], view), w=[wt.res])
                return wt

            def norm(gbase):
                for c in range(8):
                    op(POOL, tt(sq[:, c, :], xs[:, c, :], xs[:, c, :], ALU.mult), r=[xs.res], w=[sq.res])
                p = nps()
                for c in range(8):
                    op(PE, mm(p[:, :], ONES(), sq[:, c, :], c == 0, c == 7), r=[sq.res, cmat.res], w=[p.res])
                rstd_ops(rstd, p, 128, 1024.0)
                for c in range(8):
                    op(DVE, stt(xn[:, c, :], xs[:, c, :], pcols[:, gbase + c:gbase + c + 1], rstd[:, :], ALU.mult, ALU.mult),
                       r=[xs.res, pcols.res, rstd.res], w=[xn.res])

            def outbuf():
                o = outs[octr[0] % 4]
                octr[0] += 1
                return o

            def proj_chunk(wt, col0, ncol, src, nchunk=8, srcidx=None):
                p = nps()
                for i in range(nchunk):
                    rhs = src[:, i, :] if srcidx is None else srcidx(i)
                    op(PE, mm(p[0:ncol, :], wt[:, i, col0:col0 + ncol], rhs, i == 0, i == nchunk - 1), r=[wt.res, src.res], w=[p.res])
                return p

            def vproj(wt, col0, ncol, src, nchunk, dst_fn, srcidx=None):
                for tb in range(4):
                    p = nps()
                    for i in range(nchunk):
                        lhs = src[:, i, tb * 128:(tb + 1) * 128] if srcidx is None else srcidx(i, tb)
                        op(PE, mm(p[:, 0:ncol], lhs, wt[:, i, col0:col0 + ncol], i == 0, i == nchunk - 1), r=[wt.res, src.res], w=[p.res])
                    op(ACT, act(vt[:, tb, 0:ncol], p[:, 0:ncol], AF.Copy), r=[p.res], w=[vt.res])
                dst_fn()

            for g in range(NG):
                gs = slice(g * 512, (g + 1) * 512)
                src_x = xT if l == 0 else xres
                dma(SP, dm(xs[:, :, :], src_x.rearrange("(c p) s -> p c s", p=128)[:, :, gs]), w=[xs.res])
                if l > 0:
                    lp = l - 1
                    jp = lp // 2
                    dma(SP, dm(mx[:, :, :], mixT.rearrange("(c p) s -> p c s", p=128)[:, :, gs]), w=[mx.res])
                    wout = (ev_out if lp % 2 == 0 else od_out)[jp]
                    for half in range(2):
                        wt = load_w(wview(wout, half * 512, 512))
                        for oc4 in range(4):
                            oc = half * 4 + oc4
                            p = proj_chunk(wt, oc4 * 128, 128, mx)
                            op(DVE, tt(xs[:, oc, :], p[:, :], xs[:, oc, :], ALU.add), r=[p.res, xs.res], w=[xs.res])
                    norm(lp * 16 + 8)
                    for fb in range(8):
                        wt = load_w(wview(w_up[lp], fb * 512, 512))
                        for f4 in range(4):
                            fc = fb * 4 + f4
                            p = proj_chunk(wt, f4 * 128, 128, xn)
                            r_ = rl[fc % 2]
                            op(DVE, ts(r_[:, :], p[:, :], 0.0, None, ALU.max), r=[p.res], w=[r_.res])
                            op(POOL, tt(H[:, fc, :], r_[:, :], r_[:, :], ALU.mult), r=[r_.res], w=[H.res])
                    for half in range(2):
                        accs = [ps[4 + i] for i in range(4)]
                        for fs in range(4):
                            view = w_down[lp].rearrange("(i p) c -> p i c", p=128)[:, fs * 8:(fs + 1) * 8, half * 512:(half + 1) * 512]
                            wt = load_w(view)
                            for oc4 in range(4):
                                for i in range(8):
                                    fc = fs * 8 + i
                                    op(PE, mm(accs[oc4][:, :], wt[:, i, oc4 * 128:(oc4 + 1) * 128], H[:, fc, :], fc == 0, fc == 31),
                                       r=[wt.res, H.res], w=[accs[oc4].res])
                        for oc4 in range(4):
                            oc = half * 4 + oc4
                            op(DVE, tt(xs[:, oc, :], accs[oc4][:, :], xs[:, oc, :], ALU.add), r=[accs[oc4].res, xs.res], w=[xs.res])
                if l == DEPTH:
                    dma(STQ, dm(yT.rearrange("(c p) s -> p c s", p=128)[:, :, gs], xs[:, :, :]), r=[xs.res])
                    continue
                dma(STQ, dm(xres.rearrange("(c p) s -> p c s", p=128)[:, :, gs], xs[:, :, :]), r=[xs.res])
                norm(l * 16)
                j = l // 2
                if l % 2 == 1:
                    pb = 128 + 32 * j
                    win = od_in[j]
                    for part, dst, scale in ((0, qc, 0.125), (1, kc, 1.0)):
                        wt = load_w(wview(win, part * 512, 512))
                        for c in range(4):
                            p = proj_chunk(wt, c * 128, 128, xn)
                            o = outbuf()
                            op(ACT, act(o[:, :], p[:, :], AF.Copy, scale=scale), r=[p.res], w=[o.res])
                            for hh in range(2):
                                dma(STQ, dm(dst[2 * c + hh, :, gs], o[64 * hh:64 * hh + 64, :]), r=[o.res])
                    wt = load_w(wview(win, 1024, 512))

                    def st_cv():
                        for h in range(8):
                            dma(STQ, dm(vc[h, :, 4 * g:4 * g + 4, :], vt[:, :, 64 * h:64 * h + 64]), r=[vt.res])
                    vproj(wt, 0, 512, xn, 8, st_cv)
                    for part, dst, gcol in ((3, qd, pb + 0), (4, kd, pb + 1)):
                        wt = load_w(wview(win, part * 512, 512))
                        pc = nps()
                        for c in range(4):
                            p = proj_chunk(wt, c * 128, 128, xn)
                            op(ACT, act(raws[c][:, :], p[:, :], AF.Copy), r=[p.res], w=[raws[c].res])
                            op(POOL, tt(sqs[c][:, :], raws[c][:, :], raws[c][:, :], ALU.mult), r=[raws[c].res], w=[sqs[c].res])
                        for c in range(4):
                            op(PE, mm(pc[0:8, :], cmat[:, C_SEL2 + 8 * c:C_SEL2 + 8 * c + 8], sqs[c][:, :], c == 0, c == 3), r=[cmat.res, sqs[c].res], w=[pc.res])
                        rstd_ops(rc, pc, 8, 64.0)
                        for c in range(4):
                            pbc = nps()
                            op(PE, mm(pbc[:, :], cf32[0:8, F_SEL2T + 128 * c:F_SEL2T + 128 * c + 128], rc[0:8, :], True, True), r=[cf32.res, rc.res], w=[pbc.res])
                            o = outbuf()
                            op(DVE, stt(o[:, :], raws[c][:, :], pcols[:, gcol:gcol + 1], pbc[:, :], ALU.mult, ALU.mult), r=[raws[c].res, pcols.res, pbc.res], w=[o.res])
                            for m in range(2):
                                dma(STQ, dm(dst[c, m, 0:64, gs], o[64 * m:64 * m + 64, :]), r=[o.res])
                    wt = load_w(wview(win, 2560, 512))

                    def st_dv():
                        for h in range(4):
                            dma(STQ, dm(vd[h, :, 4 * g:4 * g + 4, :], vt[:, :, 128 * h:128 * h + 128]), r=[vt.res])
                    vproj(wt, 0, 512, xn, 8, st_dv)
                else:
                    pb = 64 + 32 * j
                    win = ev_in[j]
                    wt = load_w(wview(win, 0, 512))
                    pc = nps()
                    for c in range(4):
                        p = proj_chunk(wt, c * 128, 128, xn)
                        op(ACT, act(raws[c][:, :], p[:, :], AF.Copy), r=[p.res], w=[raws[c].res])
                        op(POOL, tt(sqs[c][:, :], raws[c][:, :], raws[c][:, :], ALU.mult), r=[raws[c].res], w=[sqs[c].res])
                    for c in range(4):
                        op(PE, mm(pc[0:8, :], cmat[:, C_SEL2 + 8 * c:C_SEL2 + 8 * c + 8], sqs[c][:, :], c == 0, c == 3), r=[cmat.res, sqs[c].res], w=[pc.res])
                    rstd_ops(rc, pc, 8, 64.0)
                    for c in range(4):
                        pbc = nps()
                        op(PE, mm(pbc[:, :], cf32[0:8, F_SEL2T + 128 * c:F_SEL2T + 128 * c + 128], rc[0:8, :], True, True), r=[cf32.res, rc.res], w=[pbc.res])
                        o = outbuf()
                        op(DVE, stt(o[:, :], raws[c][:, :], pcols[:, pb:pb + 1], pbc[:, :], ALU.mult, ALU.mult), r=[raws[c].res, pcols.res, pbc.res], w=[o.res])
                        for hh in range(2):
                            dma(STQ, dm(qa[2 * c + hh, :, gs], o[64 * hh:64 * hh + 64, :]), r=[o.res])
                    wt = load_w(wview(win, 512, 512))
                    p = proj_chunk(wt, 0, 128, xn)
                    op(ACT, act(raws[0][:, :], p[:, :], AF.Copy), r=[p.res], w=[raws[0].res])
                    op(POOL, tt(sqs[0][:, :], raws[0][:, :], raws[0][:, :], ALU.mult), r=[raws[0].res], w=[sqs[0].res])
                    pc = nps()
                    op(PE, mm(pc[0:8, :], cmat[:, C_SEL2:C_SEL2 + 8], sqs[0][:, :], True, True), r=[cmat.res, sqs[0].res], w=[pc.res])
                    rstd_ops(rc, pc, 8, 64.0)
                    pbc = nps()
                    op(PE, mm(pbc[:, :], cf32[0:8, F_SEL2T:F_SEL2T + 128], rc[0:8, :], True, True), r=[cf32.res, rc.res], w=[pbc.res])
                    o = outbuf()
                    op(DVE, stt(o[:, :], raws[0][:, :], pcols[:, pb + 1:pb + 2], pbc[:, :], ALU.mult, ALU.mult), r=[raws[0].res, pcols.res, pbc.res], w=[o.res])
                    for hh in range(2):
                        dma(STQ, dm(ka[hh, :, gs], o[64 * hh:64 * hh + 64, :]), r=[o.res])

                    def st_av():
                        for h in range(2):
                            dma(STQ, dm(va[h, :, 4 * g:4 * g + 4, :], vt[:, :, 64 * h:64 * h + 64]), r=[vt.res])
                    vproj(wt, 128, 128, xn, 8, st_av)
                    for c in range(2):
                        p = proj_chunk(wt, 256 + c * 128, 128, xn)
                        op(ACT, act(raws[c][:, :], p[:, :], AF.Copy), r=[p.res], w=[raws[c].res])
                        op(POOL, tt(sqs[c][:, :], raws[c][:, :], raws[c][:, :], ALU.mult), r=[raws[c].res], w=[sqs[c].res])
                    pc = nps()
                    for c in range(2):
                        op(PE, mm(pc[:, :], ONES(), sqs[c][:, :], c == 0, c == 1), r=[cmat.res, sqs[c].res], w=[pc.res])
                    rstd_ops(rstd, pc, 128, 256.0)
                    for c in range(2):
                        op(DVE, stt(cqn[:, c, :], raws[c][:, :], pcols[:, pb + 2 + c:pb + 3 + c], rstd[:, :], ALU.mult, ALU.mult), r=[raws[c].res, pcols.res, rstd.res], w=[cqn.res])
                    wt = load_w(wview(win, 1024, 160), 8, 160)
                    p = proj_chunk(wt, 0, 128, xn)
                    op(ACT, act(raws[0][:, :], p[:, :], AF.Copy), r=[p.res], w=[raws[0].res])
                    op(POOL, tt(sqs[0][:, :], raws[0][:, :], raws[0][:, :], ALU.mult), r=[raws[0].res], w=[sqs[0].res])
                    pc = nps()
                    op(PE, mm(pc[:, :], ONES(), sqs[0][:, :], True, True), r=[cmat.res, sqs[0].res], w=[pc.res])
                    rstd_ops(rstd, pc, 128, 128.0)
                    op(DVE, stt(ckvn[:, :], raws[0][:, :], pcols[:, pb + 4:pb + 5], rstd[:, :], ALU.mult, ALU.mult), r=[raws[0].res, pcols.res, rstd.res], w=[ckvn.res])
                    pkr = proj_chunk(wt, 128, 32, xn)
                    op(ACT, act(raws[6][0:32, :], pkr[0:32, :], AF.Copy), r=[pkr.res], w=[raws[6].res])
                    op(POOL, tt(sqs[6][0:32, :], raws[6][0:32, :], raws[6][0:32, :], ALU.mult), r=[raws[6].res], w=[sqs[6].res])
                    dma(SP, dm(cost[:, :], costab[:, gs]), w=[cost.res])
                    dma(SP, dm(sint[:, :], sintab[:, gs]), w=[sint.res])
                    op(DVE, ts(t1[0:32, :], raws[6][0:32, :], pcols[0:32, pb + 8:pb + 9], None, ALU.mult), r=[raws[6].res, pcols.res], w=[t1.res])
                    op(DVE, cp(sqs[7][0:32, :], t1[0:32, :]), r=[t1.res], w=[sqs[7].res])
                    prot = nps()
                    op(PE, mm(prot[0:32, :], cmat[0:32, C_PERM:C_PERM + 32], sqs[7][0:32, :], True, True), r=[cmat.res, sqs[7].res], w=[prot.res])
                    op(DVE, tt(t2[0:32, :], prot[0:32, :], sint[0:32, :], ALU.mult), r=[prot.res, sint.res], w=[t2.res])
                    op(DVE, tt(t1[0:32, :], t1[0:32, :], cost[0:32, :], ALU.mult), r=[t1.res, cost.res], w=[t1.res])
                    op(DVE, tt(krr[0:32, :], t1[0:32, :], t2[0:32, :], ALU.add), r=[t1.res, t2.res], w=[krr.res])
                    uqv = uq[j].rearrange("(i p) (h d) -> p i h d", p=128, d=96)
                    for i in range(2):
                        dma(SP, dm(wsm[:, i, 0:512].rearrange("p (h d) -> p h d", d=64), uqv[:, i, :, 0:64]), w=[wsm.res])
                        dma(SP, dm(wsm[:, i, 512:768].rearrange("p (h d) -> p h d", d=32), uqv[:, i, :, 64:96]), w=[wsm.res])
                    ukvv = ukv[j].rearrange("p (h d) -> p h d", d=128)
                    dma(SP, dm(wkv[:, 0:512].rearrange("p (h d) -> p h d", d=64), ukvv[:, :, 0:64]), w=[wkv.res])
                    dma(SP, dm(wkv[:, 512:1024].rearrange("p (h d) -> p h d", d=64), ukvv[:, :, 64:128]), w=[wkv.res])
                    for c in range(6):
                        p = nps()
                        for i in range(2):
                            op(PE, mm(p[:, :], wsm[:, i, c * 128:(c + 1) * 128], cqn[:, i, :], i == 0, i == 1), r=[wsm.res, cqn.res], w=[p.res])
                        op(ACT, act(raws[c][:, :], p[:, :], AF.Copy), r=[p.res], w=[raws[c].res])
                        op(POOL, tt(sqs[c][:, :], raws[c][:, :], raws[c][:, :], ALU.mult), r=[raws[c].res], w=[sqs[c].res])
                    pc = nps()
                    for c in range(6):
                        sel = cmat[:, C_SEL2 + 8 * c:C_SEL2 + 8 * c + 8] if c < 4 else cmat[:, C_SEL4 + 8 * (c - 4):C_SEL4 + 8 * (c - 4) + 8]
                        op(PE, mm(pc[0:8, :], sel, sqs[c][:, :], c == 0, c == 5), r=[cmat.res, sqs[c].res], w=[pc.res])
                    rstd_ops(rc, pc, 8, 96.0)
                    for c in range(6):
                        pbc = nps()
                        selT = cf32[0:8, F_SEL2T + 128 * c:F_SEL2T + 128 * c + 128] if c < 4 else cf32[0:8, F_SEL4T + 128 * (c - 4):F_SEL4T + 128 * (c - 4) + 128]
                        op(PE, mm(pbc[:, :], selT, rc[0:8, :], True, True), r=[cf32.res, rc.res], w=[pbc.res])
                        if c < 4:
                            o = outbuf()
                            op(DVE, stt(o[:, :], raws[c][:, :], pcols[:, pb + 5:pb + 6], pbc[:, :], ALU.mult, ALU.mult), r=[raws[c].res, pcols.res, pbc.res], w=[o.res])
                            for hh in range(2):
                                dma(STQ, dm(qb[2 * c + hh, 0:64, gs], o[64 * hh:64 * hh + 64, :]), r=[o.res])
                        else:
                            op(DVE, stt(t1[:, :], raws[c][:, :], pcols[:, pb + 6:pb + 7], pbc[:, :], ALU.mult, ALU.mult), r=[raws[c].res, pcols.res, pbc.res], w=[t1.res])
                            op(POOL, cp(sqs[7][:, :], t1[:, :]), r=[t1.res], w=[sqs[7].res])
                            prot = nps()
                            op(PE, mm(prot[:, :], cmat[:, C_PERM:C_PERM + 128], sqs[7][:, :], True, True), r=[cmat.res, sqs[7].res], w=[prot.res])
                            op(DVE, tt(t2[:, :], prot[:, :], sint[:, :], ALU.mult), r=[prot.res, sint.res], w=[t2.res])
                            op(DVE, tt(t1[:, :], t1[:, :], cost[:, :], ALU.mult), r=[t1.res, cost.res], w=[t1.res])
                            o = outbuf()
                            op(DVE, tt(o[:, :], t1[:, :], t2[:, :], ALU.add), r=[t1.res, t2.res], w=[o.res])
                            for hh in range(4):
                                dma(STQ, dm(qb[4 * (c - 4) + hh, 64:96, gs], o[32 * hh:32 * hh + 32, :]), r=[o.res])
                    for c in range(4):
                        p = nps()
                        op(PE, mm(p[:, :], wkv[:, c * 128:(c + 1) * 128], ckvn[:, :], True, True), r=[wkv.res, ckvn.res], w=[p.res])
                        op(ACT, act(raws[c][:, :], p[:, :], AF.Copy), r=[p.res], w=[raws[c].res])
                        op(POOL, tt(sqs[c][:, :], raws[c][:, :], raws[c][:, :], ALU.mult), r=[raws[c].res], w=[sqs[c].res])
                    pc = nps()
                    for c in range(4):
                        op(PE, mm(pc[0:8, :], cmat[:, C_SEL2 + 8 * c:C_SEL2 + 8 * c + 8], sqs[c][:, :], c == 0, False), r=[cmat.res, sqs[c].res], w=[pc.res])
                    op(PE, mm(pc[0:8, :], cmat[0:32, C_ONES:C_ONES + 8], sqs[6][0:32, :], False, True), r=[cmat.res, sqs[6].res], w=[pc.res])
                    rstd_ops(rc, pc, 8, 96.0)
                    for c in range(4):
                        pbc = nps()
                        op(PE, mm(pbc[:, :], cf32[0:8, F_SEL2T + 128 * c:F_SEL2T + 128 * c + 128], rc[0:8, :], True, True), r=[cf32.res, rc.res], w=[pbc.res])
                        o = outbuf()
                        op(DVE, stt(o[:, :], raws[c][:, :], pcols[:, pb + 7:pb + 8], pbc[:, :], ALU.mult, ALU.mult), r=[raws[c].res, pcols.res, pbc.res], w=[o.res])
                        for hh in range(2):
                            dma(STQ, dm(kb_[2 * c + hh, 0:64, gs], o[64 * hh:64 * hh + 64, :]), r=[o.res])
                    for h in range(8):
                        pbc = nps()
                        op(PE, mm(pbc[0:32, :], cf32[0:8, F_KSELT + 32 * h:F_KSELT + 32 * h + 32], rc[0:8, :], True, True), r=[cf32.res, rc.res], w=[pbc.res])
                        o = outbuf()
                        op(DVE, tt(o[0:32, :], krr[0:32, :], pbc[0:32, :], ALU.mult), r=[krr.res, pbc.res], w=[o.res])
                        dma(STQ, dm(kb_[h, 64:96, gs], o[0:32, :]), r=[o.res])
                    for tb in range(4):
                        p = nps()
                        op(PE, mm(p[:, :], ckvn[:, tb * 128:(tb + 1) * 128], wkv[:, 512:1024], True, True), r=[wkv.res, ckvn.res], w=[p.res])
                        op(ACT, act(vt[:, tb, :], p[:, :], AF.Copy), r=[p.res], w=[vt.res])
                    for h in range(8):
                        dma(STQ, dm(vb[h, :, 4 * g:4 * g + 4, :], vt[:, :, 64 * h:64 * h + 64]), r=[vt.res])
        sch.barrier()

    def attn_even(j):
        pb = 64 + 32 * j
        with ExitStack() as st:
            KT = [sb(st, f"KT{i}", [128, S], BF16) for i in range(2)]
            V = [sb(st, f"V{i}", [128, NB, 64], BF16) for i in range(2)]
            Q = [sb(st, f"Q{i}", [128, 512], BF16) for i in range(3)]
            Pb = [sb(st, f"P{i}", [128, 512], BF16) for i in range(6)]
            sbb = [sb(st, f"sbb{i}", [128, 256], F32) for i in range(4)]
            rec = sb(st, "rec", [64, 512], F32)
            ob = [sb(st, f"oo{i}", [64, 512], BF16) for i in range(2)]
            O, L = ps[6], ps[7]
            qctr = [0]
            octr = [0]
            tctr = [0]
            for kvh in range(2):
                kt, v = KT[kvh], V[kvh]
                dma(SP, dm(kt[0:64, :], ka[kvh]), w=[kt.res])
                dma(SP, dm(v[:, :, :], va[kvh]), w=[v.res])
                for hq in range(4):
                    h = 4 * kvh + hq
                    for g in range(NG):
                        q = Q[qctr[0] % 3]
                        qctr[0] += 1
                        dma(SP, dm(q[0:64, :], qa[h, :, g * 512:(g + 1) * 512]), w=[q.res])
                        tiles = []
                        rels = list(range(-1, 4)) if g > 0 else list(range(0, 4))
                        for idx, rel in enumerate(rels):
                            kbi = 4 * g + rel
                            if rel < 0:
                                q0, n, boff = 0, 128, 0
                            else:
                                q0 = 128 * rel
                                n = min(256, 512 - q0)
                                boff = 128
                            t = tctr[0]
                            tctr[0] += 1
                            sp_, P_, sb_ = ps[t % 4], Pb[t % 6], sbb[t % 4]
                            first, last = idx == 0, idx == len(rels) - 1

                            def s1(kbi=kbi, q0=q0, n=n, sp_=sp_, q=q, kt=kt):
                                op(PE, mm(sp_[:, 0:n], kt[0:64, kbi * 128:(kbi + 1) * 128], q[0:64, q0:q0 + n], True, True), r=[kt.res, q.res], w=[sp_.res])

                            def s2(n=n, sp_=sp_, sb_=sb_, P_=P_, boff=boff, h=h):
                                op(DVE, stt(sb_[:, 0:n], sp_[:, 0:n], 0.125, cf32[:, F_ABIAS + 384 * h + boff:F_ABIAS + 384 * h + boff + n], ALU.mult, ALU.add),
                                   r=[sp_.res, cf32.res], w=[sb_.res])
                                op(ACT, act(P_[:, 0:n], sb_[:, 0:n], AF.Exp), r=[sb_.res], w=[P_.res])

                            def s3(kbi=kbi, q0=q0, n=n, P_=P_, v=v, first=first, last=last):
                                op(PE, mm(O[0:64, q0:q0 + n], v[:, kbi, :], P_[:, 0:n], first, last), r=[v.res, P_.res], w=[O.res])
                                op(PE, mm(L[0:64, q0:q0 + n], ONES(128, 64), P_[:, 0:n], first, last), r=[cmat.res, P_.res], w=[L.res])
                            tiles.append([s1, s2, s3])
                        pipeline(tiles, [0, 2, 4])
                        o = ob[octr[0] % 2]
                        octr[0] += 1
                        op(DVE, ts(rec[:, :], L[0:64, :], dcol[0:64, 4 + 8 * j + h:5 + 8 * j + h], None, ALU.add), r=[L.res, dcol.res], w=[rec.res])
                        op(DVE, lambda e: e.reciprocal(out=rec[:, :], in_=rec[:, :]), r=[rec.res], w=[rec.res])
                        op(DVE, tt(o[:, :], O[0:64, :], rec[:, :], ALU.mult), r=[O.res, rec.res], w=[o.res])
                        dma(SP, dm(mixT[64 * h:64 * h + 64, g * 512:(g + 1) * 512], o[:, :]), r=[o.res])
            scaleB = 96.0 ** -0.5
            for h in range(8):
                kt, v = KT[h % 2], V[h % 2]
                dma(SP, dm(kt[0:96, :], kb_[h]), w=[kt.res])
                dma(SP, dm(v[:, :, :], vb[h]), w=[v.res])
                for g in range(NG):
                    q = Q[qctr[0] % 3]
                    qctr[0] += 1
                    dma(SP, dm(q[0:96, :], qb[h, :, g * 512:(g + 1) * 512]), w=[q.res])
                    tiles = []
                    nkb = 4 * g + 4
                    for kbi in range(nkb):
                        rel = kbi - 4 * g
                        q0 = 128 * rel if rel > 0 else 0
                        n = 512 - q0
                        t = tctr[0]
                        tctr[0] += 1
                        sp_, P_ = ps[t % 4], Pb[t % 6]
                        first, last = kbi == 0, kbi == nkb - 1

                        def s1(kbi=kbi, q0=q0, n=n, sp_=sp_, q=q, kt=kt):
                            op(PE, mm(sp_[:, 0:n], kt[0:96, kbi * 128:(kbi + 1) * 128], q[0:96, q0:q0 + n], True, True), r=[kt.res, q.res], w=[sp_.res])

                        def s2(n=n, sp_=sp_, P_=P_, rel=rel):
                            op(ACT, act(P_[:, 0:n], sp_[:, 0:n], AF.Exp, scale=scaleB), r=[sp_.res], w=[P_.res])
                            if rel >= 0:
                                op(DVE, tt(P_[:, 0:128], P_[:, 0:128], cmat[:, C_DFB:C_DFB + 128], ALU.mult), r=[P_.res, cmat.res], w=[P_.res])

                        def s3(kbi=kbi, q0=q0, n=n, P_=P_, v=v, first=first, last=last):
                            op(PE, mm(O[0:64, q0:q0 + n], v[:, kbi, :], P_[:, 0:n], first, last), r=[v.res, P_.res], w=[O.res])
                            op(PE, mm(L[0:64, q0:q0 + n], ONES(128, 64), P_[:, 0:n], first, last), r=[cmat.res, P_.res], w=[L.res])
                        tiles.append([s1, s2, s3])
                    pipeline(tiles, [0, 2, 4])
                    o = ob[octr[0] % 2]
                    octr[0] += 1
                    op(DVE, lambda e: e.reciprocal(out=rec[:, :], in_=L[0:64, :]), r=[L.res], w=[rec.res])
                    op(DVE, tt(o[:, :], O[0:64, :], rec[:, :], ALU.mult), r=[O.res, rec.res], w=[o.res])
                    dma(SP, dm(mixT[512 + 64 * h:512 + 64 * h + 64, g * 512:(g + 1) * 512], o[:, :]), r=[o.res])
        sch.barrier()

    def attn_odd(j):
        with ExitStack() as st:
            KT = [sb(st, f"KT{i}", [128, S], BF16) for i in range(3)]
            V = [sb(st, f"V{i}", [128, NB, 128], BF16) for i in range(2)]
            Q = [sb(st, f"Q{i}", [128, 512], BF16) for i in range(4)]
            Pb = [sb(st, f"P{i}", [128, 512], BF16) for i in range(6)]
            ef = [sb(st, f"ef{i}", [128, 512], F32) for i in range(3)]
            spb = [sb(st, f"spb{i}", [128, 512], BF16) for i in range(3)]
            R32 = sb(st, "R32", [128, 512], F32)
            Rb = [sb(st, f"Rb{i}", [128, 512], BF16) for i in range(2)]
            ob = [sb(st, f"oo{i}", [128, 512], BF16) for i in range(2)]
            f1 = sb(st, "f1", [128, 512], F32)
            f2 = sb(st, "f2", [128, 512], F32)
            f3 = sb(st, "f3", [128, 512], F32)
            fsq = sb(st, "fsq", [128, 512], BF16)
            qctr = [0]
            octr = [0]
            tctr = [0]
            O = ps[7]
            for h in range(8):
                kt, v = KT[h % 2], V[h % 2]
                dma(SP, dm(kt[0:64, :], kc[h]), w=[kt.res])
                dma(SP, dm(v[:, :, 0:64], vc[h]), w=[v.res])
                for g in range(NG):
                    q = Q[qctr[0] % 4]
                    qctr[0] += 1
                    dma(SP, dm(q[0:64, :], qc[h, :, g * 512:(g + 1) * 512]), w=[q.res])
                    tiles = []
                    nkb = 4 * g + 4
                    order = list(range(nkb - 1, -1, -1))
                    for idx, kbi in enumerate(order):
                        rel = kbi - 4 * g
                        q0 = 128 * rel if rel > 0 else 0
                        n = 512 - q0
                        t = tctr[0]
                        tctr[0] += 1
                        zp, e_, s_, a_ = ps[t % 6], ef[t % 3], spb[t % 3], Pb[t % 4]
                        rb_r, rb_w = Rb[idx % 2], Rb[(idx + 1) % 2]
                        first, last = idx == 0, idx == len(order) - 1

                        def s1(kbi=kbi, q0=q0, n=n, zp=zp, q=q, kt=kt):
                            op(PE, mm(zp[:, q0:q0 + n], kt[0:64, kbi * 128:(kbi + 1) * 128], q[0:64, q0:q0 + n], True, False), r=[kt.res, q.res], w=[zp.res])

                        def s2a(q0=q0, n=n, zp=zp, e_=e_):
                            op(ACT, act(e_[:, q0:q0 + n], zp[:, q0:q0 + n], AF.Exp), r=[zp.res], w=[e_.res])

                        def s2(q0=q0, n=n, zp=zp, e_=e_, s_=s_, rel=rel):
                            op(ACT, act(s_[:, q0:q0 + n], e_[:, q0:q0 + n], AF.Ln, bias=onec[:, :]), r=[e_.res, onec.res], w=[s_.res])
                            if rel >= 0:
                                op(DVE, tt(s_[:, q0:q0 + 128], s_[:, q0:q0 + 128], cmat[:, C_DFC:C_DFC + 128], ALU.mult), r=[s_.res, cmat.res], w=[s_.res])

                        def s3(q0=q0, n=n, zp=zp, s_=s_, rb_r=rb_r, rb_w=rb_w, first=first, last=last):
                            op(PE, mm(zp[:, q0:q0 + n], cmat[:, C_NEGU:C_NEGU + 128], s_[:, q0:q0 + n], False, first), r=[cmat.res, s_.res], w=[zp.res])
                            if not first:
                                op(PE, mm(zp[:, q0:q0 + n], cmat[:, C_NEGONES:C_NEGONES + 128], rb_r[:, q0:q0 + n], False, True), r=[cmat.res, rb_r.res], w=[zp.res])
                            if not last:
                                if first:
                                    op(POOL, lambda e: e.memset(R32[:, :], 0.0), w=[R32.res])
                                op(POOL, tt(R32[:, q0:q0 + n], R32[:, q0:q0 + n], s_[:, q0:q0 + n], ALU.add), r=[R32.res, s_.res], w=[R32.res])
                                op(DVE, cp(rb_w[:, :], R32[:, :]), r=[R32.res], w=[rb_w.res])

                        def s4(q0=q0, n=n, zp=zp, a_=a_, rel=rel):
                            op(ACT, act(a_[:, q0:q0 + n], zp[:, q0:q0 + n], AF.Exp), r=[zp.res], w=[a_.res])
                            if rel >= 0:
                                op(DVE, tt(a_[:, q0:q0 + 128], a_[:, q0:q0 + 128], cmat[:, C_DFC:C_DFC + 128], ALU.mult), r=[a_.res, cmat.res], w=[a_.res])

                        def s5(kbi=kbi, q0=q0, n=n, a_=a_, v=v, first=first, last=last):
                            op(PE, mm(O[0:64, q0:q0 + n], v[:, kbi, 0:64], a_[:, q0:q0 + n], first, last), r=[v.res, a_.res], w=[O.res])
                        tiles.append([s1, s2a, s2, s3, s4, s5])
                    pipeline(tiles, [0, 1, 2, 3, 3, 4])
                    o = ob[octr[0] % 2]
                    octr[0] += 1
                    op(DVE, cp(o[0:64, :], O[0:64, :]), r=[O.res], w=[o.res])
                    dma(SP, dm(mixT[64 * h:64 * h + 64, g * 512:(g + 1) * 512], o[0:64, :]), r=[o.res])
            O1, L1, O2, L2 = ps[4], ps[5], ps[6], ps[7]
            for h in range(4):
                k1, k2, v = KT[0], KT[1], V[h % 2]
                dma(SP, dm(k1[0:70, :], kd[h, 0]), w=[k1.res])
                dma(SP, dm(k2[0:70, :], kd[h, 1]), w=[k2.res])
                dma(SP, dm(v[:, :, :], vd[h]), w=[v.res])
                for g in range(NG):
                    q1 = Q[qctr[0] % 4]
                    q2 = Q[(qctr[0] + 1) % 4]
                    qctr[0] += 2
                    dma(SP, dm(q1[0:70, :], qd[h, 0, :, g * 512:(g + 1) * 512]), w=[q1.res])
                    dma(SP, dm(q2[0:70, :], qd[h, 1, :, g * 512:(g + 1) * 512]), w=[q2.res])
                    tiles = []
                    nkb = 4 * g + 4
                    for kbi in range(nkb):
                        rel = kbi - 4 * g
                        q0 = 128 * rel if rel > 0 else 0
                        n = 512 - q0
                        t = tctr[0]
                        tctr[0] += 1
                        sa, sb2 = ps[(2 * t) % 4], ps[(2 * t + 1) % 4]
                        Pa, Pb2 = Pb[(2 * t) % 6], Pb[(2 * t + 1) % 6]
                        first, last = kbi == 0, kbi == nkb - 1

                        def s1(kbi=kbi, q0=q0, n=n, sa=sa, sb2=sb2, q1=q1, q2=q2):
                            op(PE, mm(sa[:, 0:n], k1[0:70, kbi * 128:(kbi + 1) * 128], q1[0:70, q0:q0 + n], True, True), r=[k1.res, q1.res], w=[sa.res])
                            op(PE, mm(sb2[:, 0:n], k2[0:70, kbi * 128:(kbi + 1) * 128], q2[0:70, q0:q0 + n], True, True), r=[k2.res, q2.res], w=[sb2.res])

                        def s2(n=n, sa=sa, sb2=sb2, Pa=Pa, Pb2=Pb2, rel=rel, h=h):
                            for s_, p_ in ((sa, Pa), (sb2, Pb2)):
                                op(ACT, act(p_[:, 0:n], s_[:, 0:n], AF.Exp, scale=0.125), r=[s_.res], w=[p_.res])
                                if rel >= 0:
                                    op(DVE, tt(p_[:, 0:128], p_[:, 0:128], cmat[:, C_DFD + 128 * h:C_DFD + 128 * h + 128], ALU.mult), r=[p_.res, cmat.res], w=[p_.res])

                        def s3(kbi=kbi, q0=q0, n=n, Pa=Pa, Pb2=Pb2, v=v, first=first, last=last):
                            op(PE, mm(O1[:, q0:q0 + n], v[:, kbi, :], Pa[:, 0:n], first, last), r=[v.res, Pa.res], w=[O1.res])
                            op(PE, mm(L1[:, q0:q0 + n], ONES(), Pa[:, 0:n], first, last), r=[cmat.res, Pa.res], w=[L1.res])
                            op(PE, mm(O2[:, q0:q0 + n], v[:, kbi, :], Pb2[:, 0:n], first, last), r=[v.res, Pb2.res], w=[O2.res])
                            op(PE, mm(L2[:, q0:q0 + n], ONES(), Pb2[:, 0:n], first, last), r=[cmat.res, Pb2.res], w=[L2.res])
                        tiles.append([s1, s2, s3])
                    pipeline(tiles, [0, 1, 2])
                    op(DVE, lambda e: e.reciprocal(out=f1[:, :], in_=L1[:, :]), r=[L1.res], w=[f1.res])
                    op(DVE, tt(f1[:, :], O1[:, :], f1[:, :], ALU.mult), r=[O1.res, f1.res], w=[f1.res])
                    op(DVE, lambda e: e.reciprocal(out=f2[:, :], in_=L2[:, :]), r=[L2.res], w=[f2.res])
                    op(DVE, tt(f2[:, :], O2[:, :], f2[:, :], ALU.mult), r=[O2.res, f2.res], w=[f2.res])
                    op(DVE, stt(f3[:, :], f2[:, :], dcol[:, j:j + 1], f1[:, :], ALU.mult, ALU.add), r=[f2.res, dcol.res, f1.res], w=[f3.res])
                    op(POOL, tt(fsq[:, :], f3[:, :], f3[:, :], ALU.mult), r=[f3.res], w=[fsq.res])
                    pn = ps[(2 * tctr[0]) % 4]
                    op(PE, mm(pn[:, :], ONES(), fsq[:, :], True, True), r=[cmat.res, fsq.res], w=[pn.res])
                    rstd_ops(f1, pn, 128, 128.0)
                    o = ob[octr[0] % 2]
                    octr[0] += 1
                    op(DVE, stt(o[:, :], f3[:, :], dcol[:, 2 + j:3 + j], f1[:, :], ALU.mult, ALU.mult), r=[f3.res, dcol.res, f1.res], w=[o.res])
                    dma(SP, dm(mixT[512 + 128 * h:512 + 128 * h + 128, g * 512:(g + 1) * 512], o[:, :]), r=[o.res])
        sch.barrier()

    stop = int(os.environ.get("KSTOP", "99"))
    cnt = 0
    for l in range(DEPTH + 1):
        if cnt >= stop:
            break
        proj_phase(l)
        cnt += 1
        if l < DEPTH:
            if cnt >= stop:
                break
            if l % 2 == 0:
                attn_even(l // 2)
            else:
                attn_odd(l // 2)
            cnt += 1
    sch.barrier()

    with nc.Block() as block:
        @block.tensor
        def _(e):
            sch.emit(PE, e)

        @block.scalar
        def _(e):
            sch.emit(ACT, e)

        @block.vector
        def _(e):
            sch.emit(DVE, e)

        @block.gpsimd
        def _(e):
            sch.emit(POOL, e)

        @block.sync
        def _(e):
            sch.emit(SP, e)
    stack.close()
    return nc


def host_consts(S):
    bf = ml_dtypes.bfloat16
    cm = np.zeros((128, NCM), np.float32)
    cm[:, C_ONES:C_ONES + 128] = 1.0
    for m in range(128):
        if m % 32 < 16:
            cm[m + 16, C_PERM + m] = -1.0
        else:
            cm[m - 16, C_PERM + m] = 1.0
    jj, kk = np.meshgrid(np.arange(128), np.arange(128), indexing="ij")
    cm[:, C_NEGU:C_NEGU + 128] = -(jj >= kk).astype(np.float32)
    cm[:, C_NEGONES:C_NEGONES + 128] = -1.0
    k_, q_ = jj, kk
    cm[:, C_DFB:C_DFB + 128] = ((k_ // 64) <= (q_ // 64)).astype(np.float32)
    cm[:, C_DFC:C_DFC + 128] = (k_ < q_).astype(np.float32)
    for h in range(4):
        m = 2.0 ** (-8.0 * (h + 1) / 4)
        d = np.where(k_ <= q_, 1.0, np.where((k_ // 64) == (q_ // 64), np.exp(-2.0 * m * (k_ - q_)), 0.0))
        cm[:, C_DFD + 128 * h:C_DFD + 128 * h + 128] = d
    p = np.arange(128)
    for c in range(4):
        cm[p, C_SEL2 + 8 * c + 2 * c + p // 64] = 1.0
    for r in range(2):
        cm[p, C_SEL4 + 8 * r + 4 * r + p // 32] = 1.0
    cf = np.zeros((128, NCF), np.float32)
    for c in range(4):
        cf[2 * c + p // 64, F_SEL2T + 128 * c + p] = 1.0
    for r in range(2):
        cf[4 * r + p // 32, F_SEL4T + 128 * r + p] = 1.0
    for h in range(8):
        cf[h, F_KSELT + 32 * h:F_KSELT + 32 * h + 32] = 1.0
    for h in range(8):
        m = 2.0 ** (-8.0 * (h + 1) / 8)
        kpos = np.arange(128)[:, None] - 128
        qpos = np.arange(128)[None, :]
        dch = qpos // 64 - np.floor_divide(kpos, 64)
        ok = (dch >= 0) & (dch <= 2)
        cf[:, F_ABIAS + 384 * h:F_ABIAS + 384 * h + 128] = np.where(ok, -m * np.abs(qpos - kpos), -30000.0)
        kpos = np.arange(128)[:, None]
        qpos = np.arange(256)[None, :]
        dch = qpos // 64 - kpos // 64
        ok = (dch >= 0) & (dch <= 2)
        cf[:, F_ABIAS + 384 * h + 128:F_ABIAS + 384 * h + 384] = np.where(ok, -m * np.abs(qpos - kpos), -30000.0)
    half = 16
    inv = (np.float32(10000.0) ** (-np.arange(half, dtype=np.float32) / np.float32(half))).astype(np.float32)
    cf[:, F_INVF] = inv[p % 16]
    pos = np.arange(S)
    a, b, c = pos // 1024, (pos % 1024) // 32, pos % 32
    daug = np.zeros((4, 2, 6, S), np.float32)
    for h in range(4):
        m = 2.0 ** (-8.0 * (h + 1) / 4) * 8.0
        daug[h, 0, 0], daug[h, 0, 1], daug[h, 0, 2] = -m * 1024 * a, -m * 32 * b, -m * c
        daug[h, 0, 3:6] = 1.0
        daug[h, 1, 0:3] = 1.0
        daug[h, 1, 3], daug[h, 1, 4], daug[h, 1, 5] = m * 1024 * a, m * 32 * b, m * c
    return cm.astype(bf), cf, daug.astype(bf)


def host_pcols(inp):
    pc = np.zeros((128, NPC), np.float32)
    p = np.arange(128)
    for l in range(4):
        pc[:, l * 16:l * 16 + 8] = inp["norm_mix_g"][l].reshape(8, 128).T
        pc[:, l * 16 + 8:l * 16 + 16] = inp["norm_ffn_g"][l].reshape(8, 128).T
    for j in range(2):
        b = 64 + 32 * j
        pc[:, b + 0] = inp["a_q_norm"][j][p % 64]
        pc[:, b + 1] = inp["a_k_norm"][j][p % 64]
        pc[:, b + 2:b + 4] = inp["b_cq_norm"][j].reshape(2, 128).T
        pc[:, b + 4] = inp["b_ckv_norm"][j]
        pc[:, b + 5] = inp["b_q_norm"][j][p % 64]
        pc[:, b + 6] = inp["b_q_norm"][j][64 + p % 32]
        pc[:, b + 7] = inp["b_k_norm"][j][p % 64]
        pc[:, b + 8] = inp["b_k_norm"][j][64 + p % 32]
        pc[:, b + 9:b + 17] = inp["a_sinks"][j][None, :]
        b = 128 + 32 * j
        pc[:, b + 0] = inp["d_q_norm"][j].reshape(128)
        pc[:, b + 1] = inp["d_k_norm"][j].reshape(128)
        pc[:, b + 2] = inp["d_subln"][j]
    dlam = np.broadcast_to(inp["d_lambda"].reshape(1, 2, 256), (128, 2, 256)).astype(np.float32)
    return pc, np.ascontiguousarray(dlam)


_CACHE = {}


def kernel(**inputs):
    inp = {k: np.asarray(v) for k, v in inputs.items()}
    x = inp["x"]
    B, S, D = x.shape
    if S not in _CACHE:
        _CACHE[S] = build(S)
    nc = _CACHE[S]
    cm, cf, daug = host_consts(S)
    pc, dlam = host_pcols(inp)
    shared = {
        "pcols": pc, "dlam": dlam, "cmat": cm, "cf32": cf, "daug": daug,
        "mlp_w_up": inp["mlp_w_up"], "mlp_w_down": inp["mlp_w_down"],
        "ev_w_in": inp["ev_w_in"], "ev_w_out": inp["ev_w_out"],
        "b_w_uq": inp["b_w_uq"], "b_w_ukv": inp["b_w_ukv"],
        "od_w_in": inp["od_w_in"], "od_w_out": inp["od_w_out"],
    }
    in_maps = []
    for b in range(B):
        m = dict(shared)
        m["xT"] = np.ascontiguousarray(x[b].T)
        m["posb"] = np.ascontiguousarray(np.broadcast_to(inp["positions"][b][None, :], (128, S))).astype(np.int32)
        in_maps.append(m)
    res = run_bass_kernel_spmd(nc, in_maps, core_ids=list(range(B)))
    out = np.stack([np.ascontiguousarray(r["yT"].T) for r in res.results], axis=0)
    return out.astype(np.float32)
```

```python
import math
import os
from contextlib import ExitStack

import numpy as np
import ml_dtypes
import concourse.bass as bass
import concourse.mybir as mybir
from concourse.bass_utils import run_bass_kernel_spmd

F32, BF16, I32 = mybir.dt.float32, mybir.dt.bfloat16, mybir.dt.int32
AF = mybir.ActivationFunctionType
ALU = mybir.AluOpType
AX = mybir.AxisListType
PE, ACT, DVE, POOL, SP = 0, 1, 2, 3, 4
EPS = 1e-6
DEPTH = 4
TWO_PI = 2.0 * math.pi

C_ONES, C_PERM, C_NEGU, C_NEGONES, C_DFB, C_DFC, C_DFD, C_SEL2, C_SEL4 = 0, 128, 256, 384, 512, 640, 768, 1280, 1312
NCM = 1328
F_SEL2T, F_SEL4T, F_KSELT, F_ABIAS, F_INVF, F_SHIFT = 0, 512, 768, 1024, 4096, 4097
NCF = 4161
NPC = 192


class Res:
    __slots__ = ("lw", "rd", "sem", "cnt")

    def __init__(self):
        self.lw = None
        self.rd = []
        self.sem = None
        self.cnt = 0


class Sched:
    def __init__(self, nc, stack):
        self.nc = nc
        self.stack = stack
        self.q = [[] for _ in range(5)]
        self.seq = [0] * 5
        self.esem = [stack.enter_context(nc.semaphore(f"es{i}")) for i in range(5)]
        self.waited = [dict() for _ in range(5)]
        self.dsems = []
        self.free_dsems = []
        self.semcnt = {}
        self.nds = 0

    def _waits(self, e, deps):
        best = {}
        for (sem, val, src) in deps:
            if e == PE and src == PE:
                continue
            k = id(sem)
            if self.waited[e].get(k, 0) >= val:
                continue
            if k not in best or best[k][1] < val:
                best[k] = (sem, val)
        out = []
        for k, (sem, val) in best.items():
            self.waited[e][k] = val
            out.append((sem, val))
        return out

    def _deps(self, r, w, e=-9):
        deps = []
        for x in r:
            if x.lw is not None:
                deps.append(x.lw)
        for x in w:
            if x.lw is not None and x.lw[2] != e:
                deps.append(x.lw)
            for t in x.rd:
                if t[2] != e:
                    deps.append(t)
        return deps

    def op(self, e, fn, r=(), w=()):
        waits = self._waits(e, self._deps(r, w, e))
        self.seq[e] += 1
        tok = (self.esem[e], self.seq[e], e)
        self.q[e].append((waits, fn, (self.esem[e], 1), True))
        for x in r:
            x.rd.append(tok)
        for x in w:
            x.lw = tok
            x.rd = []

    def dma(self, qe, fn, r=(), w=(), semres=None):
        waits = self._waits(qe, self._deps(r, w))
        sr = semres if semres is not None else (w[0] if w else r[0])
        if sr.sem is None:
            if self.free_dsems:
                sr.sem = self.free_dsems.pop()
            else:
                self.nds += 1
                sr.sem = self.stack.enter_context(self.nc.semaphore(f"ds{self.nds}"))
            sr.cnt = self.semcnt.get(id(sr.sem), 0)
            self.dsems.append(sr)
        sr.cnt += 16
        self.semcnt[id(sr.sem)] = sr.cnt
        tok = (sr.sem, sr.cnt, -1)
        self.q[qe].append((waits, fn, (sr.sem, 16), False))
        for x in r:
            x.rd.append(tok)
        for x in w:
            x.lw = tok
            x.rd = []

    def barrier(self):
        toks = [(self.esem[i], self.seq[i], i) for i in range(5) if self.seq[i] > 0]
        toks += [(sr.sem, sr.cnt, -1) for sr in self.dsems]
        for e in range(5):
            deps = [t for t in toks if t[2] != e]
            waits = self._waits(e, [(s, v, -2) for (s, v, _) in deps])
            if waits:
                self.q[e].append((waits, None, None, False))
        for sr in self.dsems:
            self.free_dsems.append(sr.sem)
            sr.sem = None
        self.dsems = []

    def emit(self, e, eng):
        for (waits, fn, inc, attach) in self.q[e]:
            if fn is None:
                for (sem, val) in waits:
                    eng.wait_ge(sem, val)
                continue
            if attach and waits:
                for (sem, val) in waits[:-1]:
                    eng.wait_ge(sem, val)
                ins = fn(eng)
                ins._wait_ge(*waits[-1])
            else:
                for (sem, val) in waits:
                    eng.wait_ge(sem, val)
                ins = fn(eng)
            ins.then_inc(inc[0], inc[1])


def pipeline(tiles, skews):
    T = len(tiles)
    mx = max(skews)
    for i in range(T + mx):
        for s, sk in enumerate(skews):
            t = i - sk
            if 0 <= t < T and tiles[t][s] is not None:
                tiles[t][s]()


class T:
    __slots__ = ("h", "res")

    def __init__(self, h):
        self.h = h
        self.res = Res()

    def __getitem__(self, k):
        return self.h[k]


def build(S):
    NG = S // 512
    NB = S // 128
    nc = bass.Bass("TRN2", target_bir_lowering=False)

    def din(name, shape, dt=F32):
        return nc.dram_tensor(name, list(shape), dt, kind="ExternalInput").ap()

    def dscr(name, shape, dt=BF16):
        return nc.dram_tensor(name, list(shape), dt).ap()

    xT = din("xT", [1024, S])
    posb = din("posb", [128, S], I32)
    pcols_d = din("pcols", [128, NPC])
    dlam_d = din("dlam", [128, 2, 256])
    cmat_d = din("cmat", [128, NCM], BF16)
    cf32_d = din("cf32", [128, NCF])
    daug_d = din("daug", [4, 2, 6, S], BF16)
    w_up_d = din("mlp_w_up", [4, 1024, 4096])
    w_down_d = din("mlp_w_down", [4, 4096, 1024])
    ev_in_d = din("ev_w_in", [2, 1024, 1184])
    ev_out_d = din("ev_w_out", [2, 1024, 1024])
    uq_d = din("b_w_uq", [2, 256, 768])
    ukv_d = din("b_w_ukv", [2, 128, 1024])
    od_in_d = din("od_w_in", [2, 1024, 3072])
    od_out_d = din("od_w_out", [2, 1024, 1024])
    yT = nc.dram_tensor("yT", [1024, S], F32, kind="ExternalOutput").ap()

    w_up = dscr("s_w_up", [4, 1024, 4096])
    w_down = dscr("s_w_down", [4, 4096, 1024])
    ev_in = dscr("s_ev_in", [2, 1024, 1184])
    ev_out = dscr("s_ev_out", [2, 1024, 1024])
    uq = dscr("s_uq", [2, 256, 768])
    ukv = dscr("s_ukv", [2, 128, 1024])
    od_in = dscr("s_od_in", [2, 1024, 3072])
    od_out = dscr("s_od_out", [2, 1024, 1024])
    xres = dscr("s_xres", [1024, S], F32)
    mixT = dscr("s_mix", [1024, S])
    costab = dscr("s_cos", [128, S], F32)
    sintab = dscr("s_sin", [128, S], F32)
    qa = dscr("s_qa", [8, 64, S])
    ka = dscr("s_ka", [2, 64, S])
    va = dscr("s_va", [2, 128, NB, 64])
    qb = dscr("s_qb", [8, 96, S])
    kb_ = dscr("s_kb", [8, 96, S])
    vb = dscr("s_vb", [8, 128, NB, 64])
    qc = dscr("s_qc", [8, 64, S])
    kc = dscr("s_kc", [8, 64, S])
    vc = dscr("s_vc", [8, 128, NB, 64])
    qd = dscr("s_qd", [4, 2, 70, S])
    kd = dscr("s_kd", [4, 2, 70, S])
    vd = dscr("s_vd", [4, 128, NB, 128])

    stack = ExitStack()
    sch = Sched(nc, stack)
    op, dma = sch.op, sch.dma

    nctr = [0]

    def sb(st, name, shape, dt):
        nctr[0] += 1
        return T(st.enter_context(nc.sbuf_tensor(f"t{nctr[0]}_{name}", list(shape), dt)))

    ps = [T(stack.enter_context(nc.psum_tensor(f"ps{i}", [128, 512], F32))) for i in range(8)]
    cmat = sb(stack, "cmat", [128, NCM], BF16)
    cf32 = sb(stack, "cf32", [128, NCF], F32)
    pcols = sb(stack, "pcols", [128, NPC], F32)
    dcol = sb(stack, "dcol", [128, 32], F32)
    epsc = sb(stack, "epsc", [128, 1], F32)
    onec = sb(stack, "onec", [128, 1], F32)

    ONES = lambda k=128, m=128: cmat[0:k, C_ONES:C_ONES + m]

    def act(out, in_, func, scale=1.0, bias=None):
        if bias is None:
            return lambda e: e.activation(out=out, in_=in_, func=func, scale=scale)
        return lambda e: e.activation(out=out, in_=in_, func=func, scale=scale, bias=bias)

    def mm(out, lhsT, rhs, start, stop):
        return lambda e: e.matmul(out, lhsT=lhsT, rhs=rhs, start=start, stop=stop)

    def tt(out, in0, in1, o):
        return lambda e: e.tensor_tensor(out=out, in0=in0, in1=in1, op=o)

    def ts(out, in0, s1, s2, o0, o1=None):
        if o1 is None:
            return lambda e: e.tensor_scalar(out=out, in0=in0, scalar1=s1, scalar2=None, op0=o0)
        return lambda e: e.tensor_scalar(out=out, in0=in0, scalar1=s1, scalar2=s2, op0=o0, op1=o1)

    def stt(out, in0, s, in1, o0, o1):
        return lambda e: e.scalar_tensor_tensor(out=out, in0=in0, scalar=s, in1=in1, op0=o0, op1=o1)

    def cp(out, in_):
        return lambda e: e.tensor_copy(out=out, in_=in_)

    def dm(out, in_):
        return lambda e: e.dma_start(out=out, in_=in_)

    def rstd_ops(dst, src_ps, n, d):
        op(ACT, act(dst[0:n, :], src_ps[0:n, :], AF.Ln, scale=1.0 / d, bias=epsc[0:n, :]), r=[src_ps.res, epsc.res], w=[dst.res])
        op(ACT, act(dst[0:n, :], dst[0:n, :], AF.Exp, scale=-0.5), r=[dst.res], w=[dst.res])

    with ExitStack() as st:
        dma(SP, dm(cmat[:, :], cmat_d), w=[cmat.res])
        dma(SP, dm(cf32[:, :], cf32_d), w=[cf32.res])
        dma(SP, dm(pcols[:, :], pcols_d), w=[pcols.res])
        op(DVE, lambda e: e.memset(epsc[:, :], EPS), w=[epsc.res])
        op(DVE, lambda e: e.memset(onec[:, :], 1.0), w=[onec.res])
        castres = Res()

        def cast2d(dst, src, rows, step=128):
            for r0 in range(0, rows, step):
                r1 = min(rows, r0 + step)
                dma(POOL, dm(dst[r0:r1, :], src[r0:r1, :]), r=[castres], semres=castres)

        for l in range(4):
            cast2d(w_up[l], w_up_d[l], 1024)
            cast2d(w_down[l], w_down_d[l], 4096, 256)
        for j in range(2):
            cast2d(ev_in[j], ev_in_d[j], 1024)
            cast2d(ev_out[j], ev_out_d[j], 1024)
            cast2d(od_in[j], od_in_d[j], 1024)
            cast2d(od_out[j], od_out_d[j], 1024)
            cast2d(uq[j], uq_d[j], 256)
            cast2d(ukv[j], ukv_d[j], 128)
        for h in range(4):
            for m in range(2):
                dma(SP, dm(qd[h, m, 64:70, :], daug_d[h, 0]), r=[castres], semres=castres)
                dma(SP, dm(kd[h, m, 64:70, :], daug_d[h, 1]), r=[castres], semres=castres)
        dl = sb(st, "dl", [128, 2, 256], F32)
        dlp = sb(st, "dlp", [128, 64], F32)
        dma(SP, dm(dl[:, :, :], dlam_d), w=[dl.res])
        for j in range(2):
            layer = 2 * j + 1
            lam_init = 0.8 - 0.6 * math.exp(-0.3 * layer)
            for t in range(2):
                op(DVE, tt(dlp[:, :], dl[:, j, 128 * t:128 * t + 64], dl[:, j, 128 * t + 64:128 * t + 128], ALU.mult), r=[dl.res], w=[dlp.res])
                op(DVE, lambda e, t=t, j=j: e.reduce_sum(out=dcol[:, 20 + t:21 + t], in_=dlp[:, :], axis=AX.X), r=[dlp.res], w=[dcol.res])
                op(ACT, act(dcol[:, 20 + t:21 + t], dcol[:, 20 + t:21 + t], AF.Exp), r=[dcol.res], w=[dcol.res])
            op(DVE, tt(dcol[:, 22:23], dcol[:, 21:22], dcol[:, 20:21], ALU.subtract), r=[dcol.res], w=[dcol.res])
            op(DVE, ts(dcol[:, j:j + 1], dcol[:, 22:23], -lam_init, None, ALU.add), r=[dcol.res], w=[dcol.res])
            op(DVE, ts(dcol[:, 2 + j:3 + j], pcols[:, 128 + 32 * j + 2:128 + 32 * j + 3], 1.0 - lam_init, None, ALU.mult), r=[pcols.res, dcol.res], w=[dcol.res])
        for j in range(2):
            op(ACT, act(dcol[:, 4 + 8 * j:12 + 8 * j], pcols[:, 64 + 32 * j + 9:64 + 32 * j + 17], AF.Exp), r=[pcols.res, dcol.res], w=[dcol.res])
        CH = min(S, 2048)
        posi = sb(st, "posi", [128, CH], I32)
        ang = sb(st, "ang", [128, CH], F32)
        tq = sb(st, "tq", [128, CH], F32)
        ki = sb(st, "ki", [128, CH], I32)
        kf = sb(st, "kf", [128, CH], F32)
        rr = sb(st, "rr", [128, CH], F32)
        mk = sb(st, "mk", [128, CH], F32)
        C1 = 6.28125
        C2 = TWO_PI - C1
        for c0 in range(0, S, CH):
            dma(SP, dm(posi[:, :], posb[:, c0:c0 + CH]), w=[posi.res])
            op(DVE, cp(ang[:, :], posi[:, :]), r=[posi.res], w=[ang.res])
            op(DVE, ts(ang[:, :], ang[:, :], cf32[:, F_INVF:F_INVF + 1], None, ALU.mult), r=[ang.res, cf32.res], w=[ang.res])
            for which, tab in ((0, sintab), (1, costab)):
                src = ang
                if which == 1:
                    op(DVE, ts(rr[:, :], ang[:, :], math.pi / 2, None, ALU.add), r=[ang.res], w=[rr.res])
                    src = rr
                op(DVE, ts(tq[:, :], src[:, :], 1.0 / TWO_PI, None, ALU.mult), r=[src.res], w=[tq.res])
                op(DVE, cp(ki[:, :], tq[:, :]), r=[tq.res], w=[ki.res])
                op(DVE, cp(kf[:, :], ki[:, :]), r=[ki.res], w=[kf.res])
                op(DVE, stt(rr[:, :], kf[:, :], -C1, src[:, :], ALU.mult, ALU.add), r=[kf.res, src.res], w=[rr.res])
                op(DVE, stt(rr[:, :], kf[:, :], -C2, rr[:, :], ALU.mult, ALU.add), r=[kf.res, rr.res], w=[rr.res])
                op(DVE, ts(mk[:, :], rr[:, :], math.pi, -TWO_PI, ALU.is_gt, ALU.mult), r=[rr.res], w=[mk.res])
                op(DVE, tt(rr[:, :], rr[:, :], mk[:, :], ALU.add), r=[rr.res, mk.res], w=[rr.res])
                op(DVE, ts(mk[:, :], rr[:, :], -math.pi, TWO_PI, ALU.is_lt, ALU.mult), r=[rr.res], w=[mk.res])
                op(DVE, tt(rr[:, :], rr[:, :], mk[:, :], ALU.add), r=[rr.res, mk.res], w=[rr.res])
                op(DVE, ts(rr[:, :], rr[:, :], math.pi, -math.pi, ALU.min, ALU.max), r=[rr.res], w=[rr.res])
                op(ACT, act(tq[:, :], rr[:, :], AF.Sin), r=[rr.res], w=[tq.res])
                dma(SP, dm(tab[:, c0:c0 + CH], tq[:, :]), r=[tq.res])
        sch.barrier()

    def wview(w2d, c0, ncol, nchunk=8):
        return w2d.rearrange("(i p) c -> p i c", p=128)[:, 0:nchunk, c0:c0 + ncol]

    class Ctx:
        pass

    def proj_phase(l):
        with ExitStack() as st:
            xs = sb(st, "xs", [128, 8, 512], F32)
            xn = sb(st, "xn", [128, 8, 512], BF16)
            sq = sb(st, "sq", [128, 8, 512], BF16)
            rstd = sb(st, "rstd", [128, 512], F32)
            wts = [sb(st, f"wt{i}", [128, 8, 512], BF16) for i in range(4)]
            wctr = [0]
            if l > 0:
                H = sb(st, "H", [128, 32, 512], BF16)
                mx = sb(st, "mx", [128, 8, 512], BF16)
                rl = [sb(st, f"rl{i}", [128, 512], BF16) for i in range(2)]
            if l < DEPTH:
                raws = [sb(st, f"raw{i}", [128, 512], F32) for i in range(8)]
                sqs = [sb(st, f"sqs{i}", [128, 512], BF16) for i in range(8)]
                outs = [sb(st, f"ob{i}", [128, 512], BF16) for i in range(4)]
                rc = sb(st, "rc", [8, 512], F32)
                vt = sb(st, "vt", [128, 4, 512], BF16)
                octr = [0]
                if l % 2 == 0:
                    cqn = sb(st, "cqn", [128, 2, 512], BF16)
                    ckvn = sb(st, "ckvn", [128, 512], BF16)
                    cost = sb(st, "cost", [128, 512], F32)
                    sint = sb(st, "sint", [128, 512], F32)
                    t1 = sb(st, "t1", [128, 512], F32)
                    t2 = sb(st, "t2", [128, 512], F32)
                    krr = sb(st, "krr", [32, 512], F32)
                    wsm = sb(st, "wsm", [128, 2, 768], BF16)
                    wkv = sb(st, "wkv", [128, 1024], BF16)
            pctr = [0]
            STQ = ACT

            def nps():
                pctr[0] += 1
                return ps[pctr[0] % 4]

            def load_w(view, nchunk=8, ncol=512):
                wt = wts[wctr[0] % 4]
                wctr[0] += 1
                dma(SP, dm(wt[:, 0:nchunk, 0:ncol], view), w=[wt.res])
                return wt

            def norm(gbase):
                for c in range(8):
                    op(POOL, tt(sq[:, c, :], xs[:, c, :], xs[:, c, :], ALU.mult), r=[xs.res], w=[sq.res])
                p = nps()
                for c in range(8):
                    op(PE, mm(p[:, :], ONES(), sq[:, c, :], c == 0, c == 7), r=[sq.res, cmat.res], w=[p.res])
                rstd_ops(rstd, p, 128, 1024.0)
                for c in range(8):
                    op(DVE, stt(xn[:, c, :], xs[:, c, :], pcols[:, gbase + c:gbase + c + 1], rstd[:, :], ALU.mult, ALU.mult),
                       r=[xs.res, pcols.res, rstd.res], w=[xn.res])

            def outbuf():
                o = outs[octr[0] % 4]
                octr[0] += 1
                return o

            def proj_chunk(wt, col0, ncol, src, nchunk=8, srcidx=None):
                p = nps()
                for i in range(nchunk):
                    rhs = src[:, i, :] if srcidx is None else srcidx(i)
                    op(PE, mm(p[0:ncol, :], wt[:, i, col0:col0 + ncol], rhs, i == 0, i == nchunk - 1), r=[wt.res, src.res], w=[p.res])
                return p

            def vproj(wt, col0, ncol, src, nchunk, dst_fn, srcidx=None):
                for tb in range(4):
                    p = nps()
                    for i in range(nchunk):
                        lhs = src[:, i, tb * 128:(tb + 1) * 128] if srcidx is None else srcidx(i, tb)
                        op(PE, mm(p[:, 0:ncol], lhs, wt[:, i, col0:col0 + ncol], i == 0, i == nchunk - 1), r=[wt.res, src.res], w=[p.res])
                    op(ACT, act(vt[:, tb, 0:ncol], p[:, 0:ncol], AF.Copy), r=[p.res], w=[vt.res])
                dst_fn()

            for g in range(NG):
                gs = slice(g * 512, (g + 1) * 512)
                src_x = xT if l == 0 else xres
                dma(SP, dm(xs[:, :, :], src_x.rearrange("(c p) s -> p c s", p=128)[:, :, gs]), w=[xs.res])
                if l > 0:
                    lp = l - 1
                    jp = lp // 2
                    dma(SP, dm(mx[:, :, :], mixT.rearrange("(c p) s -> p c s", p=128)[:, :, gs]), w=[mx.res])
                    wout = (ev_out if lp % 2 == 0 else od_out)[jp]
                    for half in range(2):
                        wt = load_w(wview(wout, half * 512, 512))
                        for oc4 in range(4):
                            oc = half * 4 + oc4
                            p = proj_chunk(wt, oc4 * 128, 128, mx)
                            op(DVE, tt(xs[:, oc, :], p[:, :], xs[:, oc, :], ALU.add), r=[p.res, xs.res], w=[xs.res])
                    norm(lp * 16 + 8)
                    for fb in range(8):
                        wt = load_w(wview(w_up[lp], fb * 512, 512))
                        for f4 in range(4):
                            fc = fb * 4 + f4
                            p = proj_chunk(wt, f4 * 128, 128, xn)
                            r_ = rl[fc % 2]
                            op(DVE, ts(r_[:, :], p[:, :], 0.0, None, ALU.max), r=[p.res], w=[r_.res])
                            op(POOL, tt(H[:, fc, :], r_[:, :], r_[:, :], ALU.mult), r=[r_.res], w=[H.res])
                    for half in range(2):
                        accs = [ps[4 + i] for i in range(4)]
                        for fs in range(4):
                            view = w_down[lp].rearrange("(i p) c -> p i c", p=128)[:, fs * 8:(fs + 1) * 8, half * 512:(half + 1) * 512]
                            wt = load_w(view)
                            for oc4 in range(4):
                                for i in range(8):
                                    fc = fs * 8 + i
                                    op(PE, mm(accs[oc4][:, :], wt[:, i, oc4 * 128:(oc4 + 1) * 128], H[:, fc, :], fc == 0, fc == 31),
                                       r=[wt.res, H.res], w=[accs[oc4].res])
                        for oc4 in range(4):
                            oc = half * 4 + oc4
                            op(DVE, tt(xs[:, oc, :], accs[oc4][:, :], xs[:, oc, :], ALU.add), r=[accs[oc4].res, xs.res], w=[xs.res])
                if l == DEPTH:
                    dma(STQ, dm(yT.rearrange("(c p) s -> p c s", p=128)[:, :, gs], xs[:, :, :]), r=[xs.res])
                    continue
                dma(STQ, dm(xres.rearrange("(c p) s -> p c s", p=128)[:, :, gs], xs[:, :, :]), r=[xs.res])
                norm(l * 16)
                j = l // 2
                if l % 2 == 1:
                    pb = 128 + 32 * j
                    win = od_in[j]
                    for part, dst, scale in ((0, qc, 0.125), (1, kc, 1.0)):
                        wt = load_w(wview(win, part * 512, 512))
                        for c in range(4):
                            p = proj_chunk(wt, c * 128, 128, xn)
                            o = outbuf()
                            op(ACT, act(o[:, :], p[:, :], AF.Copy, scale=scale), r=[p.res], w=[o.res])
                            for hh in range(2):
                                dma(STQ, dm(dst[2 * c + hh, :, gs], o[64 * hh:64 * hh + 64, :]), r=[o.res])
                    wt = load_w(wview(win, 1024, 512))

                    def st_cv():
                        for h in range(8):
                            dma(STQ, dm(vc[h, :, 4 * g:4 * g + 4, :], vt[:, :, 64 * h:64 * h + 64]), r=[vt.res])
                    vproj(wt, 0, 512, xn, 8, st_cv)
                    for part, dst, gcol in ((3, qd, pb + 0), (4, kd, pb + 1)):
                        wt = load_w(wview(win, part * 512, 512))
                        pc = nps()
                        for c in range(4):
                            p = proj_chunk(wt, c * 128, 128, xn)
                            op(ACT, act(raws[c][:, :], p[:, :], AF.Copy), r=[p.res], w=[raws[c].res])
                            op(POOL, tt(sqs[c][:, :], raws[c][:, :], raws[c][:, :], ALU.mult), r=[raws[c].res], w=[sqs[c].res])
                        for c in range(4):
                            op(PE, mm(pc[0:8, :], cmat[:, C_SEL2 + 8 * c:C_SEL2 + 8 * c + 8], sqs[c][:, :], c == 0, c == 3), r=[cmat.res, sqs[c].res], w=[pc.res])
                        rstd_ops(rc, pc, 8, 64.0)
                        for c in range(4):
                            pbc = nps()
                            op(PE, mm(pbc[:, :], cf32[0:8, F_SEL2T + 128 * c:F_SEL2T + 128 * c + 128], rc[0:8, :], True, True), r=[cf32.res, rc.res], w=[pbc.res])
                            o = outbuf()
                            op(DVE, stt(o[:, :], raws[c][:, :], pcols[:, gcol:gcol + 1], pbc[:, :], ALU.mult, ALU.mult), r=[raws[c].res, pcols.res, pbc.res], w=[o.res])
                            for m in range(2):
                                dma(STQ, dm(dst[c, m, 0:64, gs], o[64 * m:64 * m + 64, :]), r=[o.res])
                    wt = load_w(wview(win, 2560, 512))

                    def st_dv():
                        for h in range(4):
                            dma(STQ, dm(vd[h, :, 4 * g:4 * g + 4, :], vt[:, :, 128 * h:128 * h + 128]), r=[vt.res])
                    vproj(wt, 0, 512, xn, 8, st_dv)
                else:
                    pb = 64 + 32 * j
                    win = ev_in[j]
                    wt = load_w(wview(win, 0, 512))
                    pc = nps()
                    for c in range(4):
                        p = proj_chunk(wt, c * 128, 128, xn)
                        op(ACT, act(raws[c][:, :], p[:, :], AF.Copy), r=[p.res], w=[raws[c].res])
                        op(POOL, tt(sqs[c][:, :], raws[c][:, :], raws[c][:, :], ALU.mult), r=[raws[c].res], w=[sqs[c].res])
                    for c in range(4):
                        op(PE, mm(pc[0:8, :], cmat[:, C_SEL2 + 8 * c:C_SEL2 + 8 * c + 8], sqs[c][:, :], c == 0, c == 3), r=[cmat.res, sqs[c].res], w=[pc.res])
                    rstd_ops(rc, pc, 8, 64.0)
                    for c in range(4):
                        pbc = nps()
                        op(PE, mm(pbc[:, :], cf32[0:8, F_SEL2T + 128 * c:F_SEL2T + 128 * c + 128], rc[0:8, :], True, True), r=[cf32.res, rc.res], w=[pbc.res])
                        o = outbuf()
                        op(DVE, stt(o[:, :], raws[c][:, :], pcols[:, pb:pb + 1], pbc[:, :], ALU.mult, ALU.mult), r=[raws[c].res, pcols.res, pbc.res], w=[o.res])
                        for hh in range(2):
                            dma(STQ, dm(qa[2 * c + hh, :, gs], o[64 * hh:64 * hh + 64, :]), r=[o.res])
                    wt = load_w(wview(win, 512, 512))
                    p = proj_chunk(wt, 0, 128, xn)
                    op(ACT, act(raws[0][:, :], p[:, :], AF.Copy), r=[p.res], w=[raws[0].res])
                    op(POOL, tt(sqs[0][:, :], raws[0][:, :], raws[0][:, :], ALU.mult), r=[raws[0].res], w=[sqs[0].res])
                    pc = nps()
                    op(PE, mm(pc[0:8, :], cmat[:, C_SEL2:C_SEL2 + 8], sqs[0][:, :], True, True), r=[cmat.res, sqs[0].res], w=[pc.res])
                    rstd_ops(rc, pc, 8, 64.0)
                    pbc = nps()
                    op(PE, mm(pbc[:, :], cf32[0:8, F_SEL2T:F_SEL2T + 128], rc[0:8, :], True, True), r=[cf32.res, rc.res], w=[pbc.res])
                    o = outbuf()
                    op(DVE, stt(o[:, :], raws[0][:, :], pcols[:, pb + 1:pb + 2], pbc[:, :], ALU.mult, ALU.mult), r=[raws[0].res, pcols.res, pbc.res], w=[o.res])
                    for hh in range(2):
                        dma(STQ, dm(ka[hh, :, gs], o[64 * hh:64 * hh + 64, :]), r=[o.res])

                    def st_av():
                        for h in range(2):
                            dma(STQ, dm(va[h, :, 4 * g:4 * g + 4, :], vt[:, :, 64 * h:64 * h + 64]), r=[vt.res])
                    vproj(wt, 128, 128, xn, 8, st_av)
                    for c in range(2):
                        p = proj_chunk(wt, 256 + c * 128, 128, xn)
                        op(ACT, act(raws[c][:, :], p[:, :], AF.Copy), r=[p.res], w=[raws[c].res])
                        op(POOL, tt(sqs[c][:, :], raws[c][:, :], raws[c][:, :], ALU.mult), r=[raws[c].res], w=[sqs[c].res])
                    pc = nps()
                    for c in range(2):
                        op(PE, mm(pc[:, :], ONES(), sqs[c][:, :], c == 0, c == 1), r=[cmat.res, sqs[c].res], w=[pc.res])
                    rstd_ops(rstd, pc, 128, 256.0)
                    for c in range(2):
                        op(DVE, stt(cqn[:, c, :], raws[c][:, :], pcols[:, pb + 2 + c:pb + 3 + c], rstd[:, :], ALU.mult, ALU.mult), r=[raws[c].res, pcols.res, rstd.res], w=[cqn.res])
                    wt = load_w(wview(win, 1024, 160), 8, 160)
                    p = proj_chunk(wt, 0, 128, xn)
                    op(ACT, act(raws[0][:, :], p[:, :], AF.Copy), r=[p.res], w=[raws[0].res])
                    op(POOL, tt(sqs[0][:, :], raws[0][:, :], raws[0][:, :], ALU.mult), r=[raws[0].res], w=[sqs[0].res])
                    pc = nps()
                    op(PE, mm(pc[:, :], ONES(), sqs[0][:, :], True, True), r=[cmat.res, sqs[0].res], w=[pc.res])
                    rstd_ops(rstd, pc, 128, 128.0)
                    op(DVE, stt(ckvn[:, :], raws[0][:, :], pcols[:, pb + 4:pb + 5], rstd[:, :], ALU.mult, ALU.mult), r=[raws[0].res, pcols.res, rstd.res], w=[ckvn.res])
                    pkr = proj_chunk(wt, 128, 32, xn)
                    op(ACT, act(raws[6][0:32, :], pkr[0:32, :], AF.Copy), r=[pkr.res], w=[raws[6].res])
                    op(POOL, tt(sqs[6][0:32, :], raws[6][0:32, :], raws[6][0:32, :], ALU.mult), r=[raws[6].res], w=[sqs[6].res])
                    dma(SP, dm(cost[:, :], costab[:, gs]), w=[cost.res])
                    dma(SP, dm(sint[:, :], sintab[:, gs]), w=[sint.res])
                    op(DVE, ts(t1[0:32, :], raws[6][0:32, :], pcols[0:32, pb + 8:pb + 9], None, ALU.mult), r=[raws[6].res, pcols.res], w=[t1.res])
                    op(DVE, cp(sqs[7][0:32, :], t1[0:32, :]), r=[t1.res], w=[sqs[7].res])
                    prot = nps()
                    op(PE, mm(prot[0:32, :], cmat[0:32, C_PERM:C_PERM + 32], sqs[7][0:32, :], True, True), r=[cmat.res, sqs[7].res], w=[prot.res])
                    op(DVE, tt(t2[0:32, :], prot[0:32, :], sint[0:32, :], ALU.mult), r=[prot.res, sint.res], w=[t2.res])
                    op(DVE, tt(t1[0:32, :], t1[0:32, :], cost[0:32, :], ALU.mult), r=[t1.res, cost.res], w=[t1.res])
                    op(DVE, tt(krr[0:32, :], t1[0:32, :], t2[0:32, :], ALU.add), r=[t1.res, t2.res], w=[krr.res])
                    uqv = uq[j].rearrange("(i p) (h d) -> p i h d", p=128, d=96)
                    for i in range(2):
                        dma(SP, dm(wsm[:, i, 0:512].rearrange("p (h d) -> p h d", d=64), uqv[:, i, :, 0:64]), w=[wsm.res])
                        dma(SP, dm(wsm[:, i, 512:768].rearrange("p (h d) -> p h d", d=32), uqv[:, i, :, 64:96]), w=[wsm.res])
                    ukvv = ukv[j].rearrange("p (h d) -> p h d", d=128)
                    dma(SP, dm(wkv[:, 0:512].rearrange("p (h d) -> p h d", d=64), ukvv[:, :, 0:64]), w=[wkv.res])
                    dma(SP, dm(wkv[:, 512:1024].rearrange("p (h d) -> p h d", d=64), ukvv[:, :, 64:128]), w=[wkv.res])
                    for c in range(6):
                        p = nps()
                        for i in range(2):
                            op(PE, mm(p[:, :], wsm[:, i, c * 128:(c + 1) * 128], cqn[:, i, :], i == 0, i == 1), r=[wsm.res, cqn.res], w=[p.res])
                        op(ACT, act(raws[c][:, :], p[:, :], AF.Copy), r=[p.res], w=[raws[c].res])
                        op(POOL, tt(sqs[c][:, :], raws[c][:, :], raws[c][:, :], ALU.mult), r=[raws[c].res], w=[sqs[c].res])
                    pc = nps()
                    for c in range(6):
                        sel = cmat[:, C_SEL2 + 8 * c:C_SEL2 + 8 * c + 8] if c < 4 else cmat[:, C_SEL4 + 8 * (c - 4):C_SEL4 + 8 * (c - 4) + 8]
                        op(PE, mm(pc[0:8, :], sel, sqs[c][:, :], c == 0, c == 5), r=[cmat.res, sqs[c].res], w=[pc.res])
                    rstd_ops(rc, pc, 8, 96.0)
                    for c in range(6):
                        pbc = nps()
                        selT = cf32[0:8, F_SEL2T + 128 * c:F_SEL2T + 128 * c + 128] if c < 4 else cf32[0:8, F_SEL4T + 128 * (c - 4):F_SEL4T + 128 * (c - 4) + 128]
                        op(PE, mm(pbc[:, :], selT, rc[0:8, :], True, True), r=[cf32.res, rc.res], w=[pbc.res])
                        if c < 4:
                            o = outbuf()
                            op(DVE, stt(o[:, :], raws[c][:, :], pcols[:, pb + 5:pb + 6], pbc[:, :], ALU.mult, ALU.mult), r=[raws[c].res, pcols.res, pbc.res], w=[o.res])
                            for hh in range(2):
                                dma(STQ, dm(qb[2 * c + hh, 0:64, gs], o[64 * hh:64 * hh + 64, :]), r=[o.res])
                        else:
                            op(DVE, stt(t1[:, :], raws[c][:, :], pcols[:, pb + 6:pb + 7], pbc[:, :], ALU.mult, ALU.mult), r=[raws[c].res, pcols.res, pbc.res], w=[t1.res])
                            op(POOL, cp(sqs[7][:, :], t1[:, :]), r=[t1.res], w=[sqs[7].res])
                            prot = nps()
                            op(PE, mm(prot[:, :], cmat[:, C_PERM:C_PERM + 128], sqs[7][:, :], True, True), r=[cmat.res, sqs[7].res], w=[prot.res])
                            op(DVE, tt(t2[:, :], prot[:, :], sint[:, :], ALU.mult), r=[prot.res, sint.res], w=[t2.res])
                            op(DVE, tt(t1[:, :], t1[:, :], cost[:, :], ALU.mult), r=[t1.res, cost.res], w=[t1.res])
                            o = outbuf()
                            op(DVE, tt(o[:, :], t1[:, :], t2[:, :], ALU.add), r=[t1.res, t2.res], w=[o.res])
                            for hh in range(4):
                                dma(STQ, dm(qb[4 * (c - 4) + hh, 64:96, gs], o[32 * hh:32 * hh + 32, :]), r=[o.res])
                    for c in range(4):
                        p = nps()
                        op(PE, mm(p[:, :], wkv[:, c * 128:(c + 1) * 128], ckvn[:, :], True, True), r=[wkv.res, ckvn.res], w=[p.res])
                        op(ACT, act(raws[c][:, :], p[:, :], AF.Copy), r=[p.res], w=[raws[c].res])
                        op(POOL, tt(sqs[c][:, :], raws[c][:, :], raws[c][:, :], ALU.mult), r=[raws[c].res], w=[sqs[c].res])
                    pc = nps()
                    for c in range(4):
                        op(PE, mm(pc[0:8, :], cmat[:, C_SEL2 + 8 * c:C_SEL2 + 8 * c + 8], sqs[c][:, :], c == 0, False), r=[cmat.res, sqs[c].res], w=[pc.res])
                    op(PE, mm(pc[0:8, :], cmat[0:32, C_ONES:C_ONES + 8], sqs[6][0:32, :], False, True), r=[cmat.res, sqs[6].res], w=[pc.res])
                    rstd_ops(rc, pc, 8, 96.0)
                    for c in range(4):
                        pbc = nps()
                        op(PE, mm(pbc[:, :], cf32[0:8, F_SEL2T + 128 * c:F_SEL2T + 128 * c + 128], rc[0:8, :], True, True), r=[cf32.res, rc.res], w=[pbc.res])
                        o = outbuf()
                        op(DVE, stt(o[:, :], raws[c][:, :], pcols[:, pb + 7:pb + 8], pbc[:, :], ALU.mult, ALU.mult), r=[raws[c].res, pcols.res, pbc.res], w=[o.res])
                        for hh in range(2):
                            dma(STQ, dm(kb_[2 * c + hh, 0:64, gs], o[64 * hh:64 * hh + 64, :]), r=[o.res])
                    for h in range(8):
                        pbc = nps()
                        op(PE, mm(pbc[0:32, :], cf32[0:8, F_KSELT + 32 * h:F_KSELT + 32 * h + 32], rc[0:8, :], True, True), r=[cf32.res, rc.res], w=[pbc.res])
                        o = outbuf()
                        op(DVE, tt(o[0:32, :], krr[0:32, :], pbc[0:32, :], ALU.mult), r=[krr.res, pbc.res], w=[o.res])
                        dma(STQ, dm(kb_[h, 64:96, gs], o[0:32, :]), r=[o.res])
                    for tb in range(4):
                        p = nps()
                        op(PE, mm(p[:, :], ckvn[:, tb * 128:(tb + 1) * 128], wkv[:, 512:1024], True, True), r=[wkv.res, ckvn.res], w=[p.res])
                        op(ACT, act(vt[:, tb, :], p[:, :], AF.Copy), r=[p.res], w=[vt.res])
                    for h in range(8):
                        dma(STQ, dm(vb[h, :, 4 * g:4 * g + 4, :], vt[:, :, 64 * h:64 * h + 64]), r=[vt.res])
        sch.barrier()

    def attn_even(j):
        pb = 64 + 32 * j
        with ExitStack() as st:
            KT = [sb(st, f"KT{i}", [128, S], BF16) for i in range(2)]
            V = [sb(st, f"V{i}", [128, NB, 128], BF16) for i in range(2)]
            for v_ in V:
                op(POOL, lambda e, v_=v_: e.memset(v_[:, :, 64:128], 1.0), w=[v_.res])
            tmpf = [sb(st, f"tmpf{i}", [128, 512], F32) for i in range(2)]
            STQA = POOL
            Q = [sb(st, f"Q{i}", [128, 512], BF16) for i in range(3)]
            Pb = [sb(st, f"P{i}", [128, 512], BF16) for i in range(6)]
            sbb = [sb(st, f"sbb{i}", [128, 256], F32) for i in range(4)]
            rec = sb(st, "rec", [64, 512], F32)
            ob = [sb(st, f"oo{i}", [64, 512], BF16) for i in range(2)]
            O, L = ps[6], ps[7]
            qctr = [0]
            octr = [0]
            tctr = [0]
            for kvh in range(2):
                kt, v = KT[kvh], V[kvh]
                dma(SP, dm(kt[0:64, :], ka[kvh]), w=[kt.res])
                dma(SP, dm(v[:, :, 0:64], va[kvh]), w=[v.res])
                for hq in range(4):
                    h = 4 * kvh + hq
                    for g in range(NG):
                        q = Q[qctr[0] % 3]
                        qctr[0] += 1
                        dma(SP, dm(q[0:64, :], qa[h, :, g * 512:(g + 1) * 512]), w=[q.res])
                        tiles = []
                        rels = list(range(-1, 4)) if g > 0 else list(range(0, 4))
                        for idx, rel in enumerate(rels):
                            kbi = 4 * g + rel
                            if rel < 0:
                                q0, n, boff = 0, 128, 0
                            else:
                                q0 = 128 * rel
                                n = min(256, 512 - q0)
                                boff = 128
                            t = tctr[0]
                            tctr[0] += 1
                            sp_, P_, sb_ = ps[t % 4], Pb[t % 6], sbb[t % 4]
                            first, last = idx == 0, idx == len(rels) - 1

                            def s1(kbi=kbi, q0=q0, n=n, sp_=sp_, q=q, kt=kt):
                                op(PE, mm(sp_[:, 0:n], kt[0:64, kbi * 128:(kbi + 1) * 128], q[0:64, q0:q0 + n], True, True), r=[kt.res, q.res], w=[sp_.res])

                            def s2(n=n, sp_=sp_, sb_=sb_, P_=P_, boff=boff, h=h):
                                op(DVE, stt(sb_[:, 0:n], sp_[:, 0:n], 0.125, cf32[:, F_ABIAS + 384 * h + boff:F_ABIAS + 384 * h + boff + n], ALU.mult, ALU.add),
                                   r=[sp_.res, cf32.res], w=[sb_.res])
                                op(ACT, act(P_[:, 0:n], sb_[:, 0:n], AF.Exp), r=[sb_.res], w=[P_.res])

                            def s3(kbi=kbi, q0=q0, n=n, P_=P_, v=v, first=first, last=last):
                                op(PE, mm(O[0:64, q0:q0 + n], v[:, kbi, 0:64], P_[:, 0:n], first, last), r=[v.res, P_.res], w=[O.res])
                                op(PE, mm(L[0:64, q0:q0 + n], ONES(128, 64), P_[:, 0:n], first, last), r=[cmat.res, P_.res], w=[L.res])
                            tiles.append([s1, s2, s3])
                        pipeline(tiles, [0, 2, 4])
                        o = ob[octr[0] % 2]
                        octr[0] += 1
                        op(DVE, ts(rec[:, :], L[0:64, :], dcol[0:64, 4 + 8 * j + h:5 + 8 * j + h], None, ALU.add), r=[L.res, dcol.res], w=[rec.res])
                        op(DVE, lambda e: e.reciprocal(out=rec[:, :], in_=rec[:, :]), r=[rec.res], w=[rec.res])
                        op(DVE, tt(o[:, :], O[0:64, :], rec[:, :], ALU.mult), r=[O.res, rec.res], w=[o.res])
                        dma(STQA, dm(mixT[64 * h:64 * h + 64, g * 512:(g + 1) * 512], o[:, :]), r=[o.res])
            scaleB = 96.0 ** -0.5
            for h in range(8):
                kt, v = KT[h % 2], V[h % 2]
                dma(SP, dm(kt[0:96, :], kb_[h]), w=[kt.res])
                dma(SP, dm(v[:, :, 0:64], vb[h]), w=[v.res])
                for g in range(NG):
                    q = Q[qctr[0] % 3]
                    qctr[0] += 1
                    dma(SP, dm(q[0:96, :], qb[h, :, g * 512:(g + 1) * 512]), w=[q.res])
                    OL = ps[6 + (octr[0] % 2)]
                    tiles = []
                    nkb = 4 * g + 4
                    for kbi in range(nkb):
                        rel = kbi - 4 * g
                        q0 = 128 * rel if rel > 0 else 0
                        n = 512 - q0
                        t = tctr[0]
                        tctr[0] += 1
                        sp_, P_ = ps[t % 4], Pb[t % 6]
                        first, last = kbi == 0, kbi == nkb - 1

                        def s1(kbi=kbi, q0=q0, n=n, sp_=sp_, q=q, kt=kt):
                            op(PE, mm(sp_[:, 0:n], kt[0:96, kbi * 128:(kbi + 1) * 128], q[0:96, q0:q0 + n], True, True), r=[kt.res, q.res], w=[sp_.res])

                        def s2(n=n, sp_=sp_, P_=P_, rel=rel):
                            op(ACT, act(P_[:, 0:n], sp_[:, 0:n], AF.Exp, scale=scaleB), r=[sp_.res], w=[P_.res])
                            if rel >= 0:
                                op(DVE, tt(P_[:, 0:128], P_[:, 0:128], cmat[:, C_DFB:C_DFB + 128], ALU.mult), r=[P_.res, cmat.res], w=[P_.res])

                        def s3(kbi=kbi, q0=q0, n=n, P_=P_, v=v, first=first, last=last, OL=OL):
                            op(PE, mm(OL[:, q0:q0 + n], v[:, kbi, :], P_[:, 0:n], first, last), r=[v.res, P_.res], w=[OL.res])
                        tiles.append([s1, s2, s3])
                    pipeline(tiles, [0, 2, 4])
                    o = ob[octr[0] % 2]
                    tf = tmpf[octr[0] % 2]
                    octr[0] += 1
                    op(DVE, cp(tf[:, :], OL[:, :]), r=[OL.res], w=[tf.res])
                    op(PE, mm(ps[5][0:64, :], cf32[:, F_SHIFT:F_SHIFT + 64], tf[:, :], True, True), r=[cf32.res, tf.res], w=[ps[5].res])
                    op(DVE, lambda e: e.reciprocal(out=rec[:, :], in_=ps[5][0:64, :]), r=[ps[5].res], w=[rec.res])
                    op(DVE, tt(o[:, :], tf[0:64, :], rec[:, :], ALU.mult), r=[tf.res, rec.res], w=[o.res])
                    dma(STQA, dm(mixT[512 + 64 * h:512 + 64 * h + 64, g * 512:(g + 1) * 512], o[:, :]), r=[o.res])
        sch.barrier()

    def attn_odd(j):
        with ExitStack() as st:
            KT = [sb(st, f"KT{i}", [128, S], BF16) for i in range(3)]
            V = [sb(st, f"V{i}", [128, NB, 128], BF16) for i in range(2)]
            Q = [sb(st, f"Q{i}", [128, 512], BF16) for i in range(4)]
            Pb = [sb(st, f"P{i}", [128, 512], BF16) for i in range(6)]
            ef = [sb(st, f"ef{i}", [128, 512], F32) for i in range(3)]
            spb = [sb(st, f"spb{i}", [128, 512], BF16) for i in range(3)]
            R32 = sb(st, "R32", [128, 512], F32)
            Rb = [sb(st, f"Rb{i}", [128, 512], BF16) for i in range(2)]
            ob = [sb(st, f"oo{i}", [128, 512], BF16) for i in range(2)]
            f1 = sb(st, "f1", [128, 512], F32)
            f2 = sb(st, "f2", [128, 512], F32)
            f3 = sb(st, "f3", [128, 512], F32)
            fsq = sb(st, "fsq", [128, 512], BF16)
            qctr = [0]
            octr = [0]
            tctr = [0]
            O = ps[7]
            for h in range(8):
                kt, v = KT[h % 2], V[h % 2]
                dma(SP, dm(kt[0:64, :], kc[h]), w=[kt.res])
                dma(SP, dm(v[:, :, 0:64], vc[h]), w=[v.res])
                for g in range(NG):
                    q = Q[qctr[0] % 4]
                    qctr[0] += 1
                    dma(SP, dm(q[0:64, :], qc[h, :, g * 512:(g + 1) * 512]), w=[q.res])
                    tiles = []
                    nkb = 4 * g + 4
                    order = list(range(nkb - 1, -1, -1))
                    for idx, kbi in enumerate(order):
                        rel = kbi - 4 * g
                        q0 = 128 * rel if rel > 0 else 0
                        n = 512 - q0
                        t = tctr[0]
                        tctr[0] += 1
                        zp, e_, s_, a_ = ps[t % 6], ef[t % 3], spb[t % 3], Pb[t % 4]
                        rb_r, rb_w = Rb[idx % 2], Rb[(idx + 1) % 2]
                        first, last = idx == 0, idx == len(order) - 1

                        def s1(kbi=kbi, q0=q0, n=n, zp=zp, q=q, kt=kt):
                            op(PE, mm(zp[:, q0:q0 + n], kt[0:64, kbi * 128:(kbi + 1) * 128], q[0:64, q0:q0 + n], True, False), r=[kt.res, q.res], w=[zp.res])

                        def s2a(q0=q0, n=n, zp=zp, e_=e_):
                            op(ACT, act(e_[:, q0:q0 + n], zp[:, q0:q0 + n], AF.Exp), r=[zp.res], w=[e_.res])

                        def s2(q0=q0, n=n, zp=zp, e_=e_, s_=s_, rel=rel):
                            op(ACT, act(s_[:, q0:q0 + n], e_[:, q0:q0 + n], AF.Ln, bias=onec[:, :]), r=[e_.res, onec.res], w=[s_.res])
                            if rel >= 0:
                                op(DVE, tt(s_[:, q0:q0 + 128], s_[:, q0:q0 + 128], cmat[:, C_DFC:C_DFC + 128], ALU.mult), r=[s_.res, cmat.res], w=[s_.res])

                        def s3(q0=q0, n=n, zp=zp, s_=s_, rb_r=rb_r, rb_w=rb_w, first=first, last=last):
                            op(PE, mm(zp[:, q0:q0 + n], cmat[:, C_NEGU:C_NEGU + 128], s_[:, q0:q0 + n], False, first), r=[cmat.res, s_.res], w=[zp.res])
                            if not first:
                                op(PE, mm(zp[:, q0:q0 + n], cmat[:, C_NEGONES:C_NEGONES + 128], rb_r[:, q0:q0 + n], False, True), r=[cmat.res, rb_r.res], w=[zp.res])
                            if not last:
                                if first:
                                    op(POOL, lambda e: e.memset(R32[:, :], 0.0), w=[R32.res])
                                op(POOL, tt(R32[:, q0:q0 + n], R32[:, q0:q0 + n], s_[:, q0:q0 + n], ALU.add), r=[R32.res, s_.res], w=[R32.res])
                                op(DVE, cp(rb_w[:, :], R32[:, :]), r=[R32.res], w=[rb_w.res])

                        def s4(q0=q0, n=n, zp=zp, a_=a_, rel=rel):
                            op(ACT, act(a_[:, q0:q0 + n], zp[:, q0:q0 + n], AF.Exp), r=[zp.res], w=[a_.res])
                            if rel >= 0:
                                op(DVE, tt(a_[:, q0:q0 + 128], a_[:, q0:q0 + 128], cmat[:, C_DFC:C_DFC + 128], ALU.mult), r=[a_.res, cmat.res], w=[a_.res])

                        def s5(kbi=kbi, q0=q0, n=n, a_=a_, v=v, first=first, last=last):
                            op(PE, mm(O[0:64, q0:q0 + n], v[:, kbi, 0:64], a_[:, q0:q0 + n], first, last), r=[v.res, a_.res], w=[O.res])
                        tiles.append([s1, s2a, s2, s3, s4, s5])
                    pipeline(tiles, [0, 1, 2, 3, 3, 4])
                    o = ob[octr[0] % 2]
                    octr[0] += 1
                    op(DVE, cp(o[0:64, :], O[0:64, :]), r=[O.res], w=[o.res])
                    dma(POOL, dm(mixT[64 * h:64 * h + 64, g * 512:(g + 1) * 512], o[0:64, :]), r=[o.res])
            O1, L1, O2, L2 = ps[4], ps[5], ps[6], ps[7]
            for h in range(4):
                k1, k2, v = KT[0], KT[1], V[h % 2]
                dma(SP, dm(k1[0:70, :], kd[h, 0]), w=[k1.res])
                dma(SP, dm(k2[0:70, :], kd[h, 1]), w=[k2.res])
                dma(SP, dm(v[:, :, :], vd[h]), w=[v.res])
                for g in range(NG):
                    q1 = Q[qctr[0] % 4]
                    q2 = Q[(qctr[0] + 1) % 4]
                    qctr[0] += 2
                    dma(SP, dm(q1[0:70, :], qd[h, 0, :, g * 512:(g + 1) * 512]), w=[q1.res])
                    dma(SP, dm(q2[0:70, :], qd[h, 1, :, g * 512:(g + 1) * 512]), w=[q2.res])
                    tiles = []
                    nkb = 4 * g + 4
                    for kbi in range(nkb):
                        rel = kbi - 4 * g
                        q0 = 128 * rel if rel > 0 else 0
                        n = 512 - q0
                        t = tctr[0]
                        tctr[0] += 1
                        sa, sb2 = ps[(2 * t) % 4], ps[(2 * t + 1) % 4]
                        Pa, Pb2 = Pb[(2 * t) % 6], Pb[(2 * t + 1) % 6]
                        first, last = kbi == 0, kbi == nkb - 1

                        def s1(kbi=kbi, q0=q0, n=n, sa=sa, sb2=sb2, q1=q1, q2=q2):
                            op(PE, mm(sa[:, 0:n], k1[0:70, kbi * 128:(kbi + 1) * 128], q1[0:70, q0:q0 + n], True, True), r=[k1.res, q1.res], w=[sa.res])
                            op(PE, mm(sb2[:, 0:n], k2[0:70, kbi * 128:(kbi + 1) * 128], q2[0:70, q0:q0 + n], True, True), r=[k2.res, q2.res], w=[sb2.res])

                        def s2(n=n, sa=sa, sb2=sb2, Pa=Pa, Pb2=Pb2, rel=rel, h=h):
                            for s_, p_ in ((sa, Pa), (sb2, Pb2)):
                                op(ACT, act(p_[:, 0:n], s_[:, 0:n], AF.Exp, scale=0.125), r=[s_.res], w=[p_.res])
                                if rel >= 0:
                                    op(DVE, tt(p_[:, 0:128], p_[:, 0:128], cmat[:, C_DFD + 128 * h:C_DFD + 128 * h + 128], ALU.mult), r=[p_.res, cmat.res], w=[p_.res])

                        def s3(kbi=kbi, q0=q0, n=n, Pa=Pa, Pb2=Pb2, v=v, first=first, last=last):
                            op(PE, mm(O1[:, q0:q0 + n], v[:, kbi, :], Pa[:, 0:n], first, last), r=[v.res, Pa.res], w=[O1.res])
                            op(PE, mm(L1[:, q0:q0 + n], ONES(), Pa[:, 0:n], first, last), r=[cmat.res, Pa.res], w=[L1.res])
                            op(PE, mm(O2[:, q0:q0 + n], v[:, kbi, :], Pb2[:, 0:n], first, last), r=[v.res, Pb2.res], w=[O2.res])
                            op(PE, mm(L2[:, q0:q0 + n], ONES(), Pb2[:, 0:n], first, last), r=[cmat.res, Pb2.res], w=[L2.res])
                        tiles.append([s1, s2, s3])
                    pipeline(tiles, [0, 1, 2])
                    op(DVE, lambda e: e.reciprocal(out=f1[:, :], in_=L1[:, :]), r=[L1.res], w=[f1.res])
                    op(DVE, tt(f1[:, :], O1[:, :], f1[:, :], ALU.mult), r=[O1.res, f1.res], w=[f1.res])
                    op(DVE, lambda e: e.reciprocal(out=f2[:, :], in_=L2[:, :]), r=[L2.res], w=[f2.res])
                    op(DVE, tt(f2[:, :], O2[:, :], f2[:, :], ALU.mult), r=[O2.res, f2.res], w=[f2.res])
                    op(DVE, stt(f3[:, :], f2[:, :], dcol[:, j:j + 1], f1[:, :], ALU.mult, ALU.add), r=[f2.res, dcol.res, f1.res], w=[f3.res])
                    op(POOL, tt(fsq[:, :], f3[:, :], f3[:, :], ALU.mult), r=[f3.res], w=[fsq.res])
                    pn = ps[(2 * tctr[0]) % 4]
                    op(PE, mm(pn[:, :], ONES(), fsq[:, :], True, True), r=[cmat.res, fsq.res], w=[pn.res])
                    rstd_ops(f1, pn, 128, 128.0)
                    o = ob[octr[0] % 2]
                    octr[0] += 1
                    op(DVE, stt(o[:, :], f3[:, :], dcol[:, 2 + j:3 + j], f1[:, :], ALU.mult, ALU.mult), r=[f3.res, dcol.res, f1.res], w=[o.res])
                    dma(POOL, dm(mixT[512 + 128 * h:512 + 128 * h + 128, g * 512:(g + 1) * 512], o[:, :]), r=[o.res])
        sch.barrier()

    stop = int(os.environ.get("KSTOP", "99"))
    cnt = 0
    for l in range(DEPTH + 1):
        if cnt >= stop:
            break
        proj_phase(l)
        cnt += 1
        if l < DEPTH:
            if cnt >= stop:
                break
            if l % 2 == 0:
                attn_even(l // 2)
            else:
                attn_odd(l // 2)
            cnt += 1
    sch.barrier()

    with nc.Block() as block:
        @block.tensor
        def _(e):
            sch.emit(PE, e)

        @block.scalar
        def _(e):
            sch.emit(ACT, e)

        @block.vector
        def _(e):
            sch.emit(DVE, e)

        @block.gpsimd
        def _(e):
            sch.emit(POOL, e)

        @block.sync
        def _(e):
            sch.emit(SP, e)
    stack.close()
    return nc


def host_consts(S):
    bf = ml_dtypes.bfloat16
    cm = np.zeros((128, NCM), np.float32)
    cm[:, C_ONES:C_ONES + 128] = 1.0
    for m in range(128):
        if m % 32 < 16:
            cm[m + 16, C_PERM + m] = -1.0
        else:
            cm[m - 16, C_PERM + m] = 1.0
    jj, kk = np.meshgrid(np.arange(128), np.arange(128), indexing="ij")
    cm[:, C_NEGU:C_NEGU + 128] = -(jj >= kk).astype(np.float32)
    cm[:, C_NEGONES:C_NEGONES + 128] = -1.0
    k_, q_ = jj, kk
    cm[:, C_DFB:C_DFB + 128] = ((k_ // 64) <= (q_ // 64)).astype(np.float32)
    cm[:, C_DFC:C_DFC + 128] = (k_ < q_).astype(np.float32)
    for h in range(4):
        m = 2.0 ** (-8.0 * (h + 1) / 4)
        d = np.where(k_ <= q_, 1.0, np.where((k_ // 64) == (q_ // 64), np.exp(-2.0 * m * (k_ - q_)), 0.0))
        cm[:, C_DFD + 128 * h:C_DFD + 128 * h + 128] = d
    p = np.arange(128)
    for c in range(4):
        cm[p, C_SEL2 + 8 * c + 2 * c + p // 64] = 1.0
    for r in range(2):
        cm[p, C_SEL4 + 8 * r + 4 * r + p // 32] = 1.0
    cf = np.zeros((128, NCF), np.float32)
    for c in range(4):
        cf[2 * c + p // 64, F_SEL2T + 128 * c + p] = 1.0
    for r in range(2):
        cf[4 * r + p // 32, F_SEL4T + 128 * r + p] = 1.0
    for h in range(8):
        cf[h, F_KSELT + 32 * h:F_KSELT + 32 * h + 32] = 1.0
    for h in range(8):
        m = 2.0 ** (-8.0 * (h + 1) / 8)
        kpos = np.arange(128)[:, None] - 128
        qpos = np.arange(128)[None, :]
        dch = qpos // 64 - np.floor_divide(kpos, 64)
        ok = (dch >= 0) & (dch <= 2)
        cf[:, F_ABIAS + 384 * h:F_ABIAS + 384 * h + 128] = np.where(ok, -m * np.abs(qpos - kpos), -30000.0)
        kpos = np.arange(128)[:, None]
        qpos = np.arange(256)[None, :]
        dch = qpos // 64 - kpos // 64
        ok = (dch >= 0) & (dch <= 2)
        cf[:, F_ABIAS + 384 * h + 128:F_ABIAS + 384 * h + 384] = np.where(ok, -m * np.abs(qpos - kpos), -30000.0)
    half = 16
    inv = (np.float32(10000.0) ** (-np.arange(half, dtype=np.float32) / np.float32(half))).astype(np.float32)
    cf[:, F_INVF] = inv[p % 16]
    for m in range(64):
        cf[64 + m, F_SHIFT + m] = 1.0
    pos = np.arange(S)
    a, b, c = pos // 1024, (pos % 1024) // 32, pos % 32
    daug = np.zeros((4, 2, 6, S), np.float32)
    for h in range(4):
        m = 2.0 ** (-8.0 * (h + 1) / 4) * 8.0
        daug[h, 0, 0], daug[h, 0, 1], daug[h, 0, 2] = -m * 1024 * a, -m * 32 * b, -m * c
        daug[h, 0, 3:6] = 1.0
        daug[h, 1, 0:3] = 1.0
        daug[h, 1, 3], daug[h, 1, 4], daug[h, 1, 5] = m * 1024 * a, m * 32 * b, m * c
    return cm.astype(bf), cf, daug.astype(bf)


def host_pcols(inp):
    pc = np.zeros((128, NPC), np.float32)
    p = np.arange(128)
    for l in range(4):
        pc[:, l * 16:l * 16 + 8] = inp["norm_mix_g"][l].reshape(8, 128).T
        pc[:, l * 16 + 8:l * 16 + 16] = inp["norm_ffn_g"][l].reshape(8, 128).T
    for j in range(2):
        b = 64 + 32 * j
        pc[:, b + 0] = inp["a_q_norm"][j][p % 64]
        pc[:, b + 1] = inp["a_k_norm"][j][p % 64]
        pc[:, b + 2:b + 4] = inp["b_cq_norm"][j].reshape(2, 128).T
        pc[:, b + 4] = inp["b_ckv_norm"][j]
        pc[:, b + 5] = inp["b_q_norm"][j][p % 64]
        pc[:, b + 6] = inp["b_q_norm"][j][64 + p % 32]
        pc[:, b + 7] = inp["b_k_norm"][j][p % 64]
        pc[:, b + 8] = inp["b_k_norm"][j][64 + p % 32]
        pc[:, b + 9:b + 17] = inp["a_sinks"][j][None, :]
        b = 128 + 32 * j
        pc[:, b + 0] = inp["d_q_norm"][j].reshape(128)
        pc[:, b + 1] = inp["d_k_norm"][j].reshape(128)
        pc[:, b + 2] = inp["d_subln"][j]
    dlam = np.broadcast_to(inp["d_lambda"].reshape(1, 2, 256), (128, 2, 256)).astype(np.float32)
    return pc, np.ascontiguousarray(dlam)


_CACHE = {}


def kernel(**inputs):
    inp = {k: np.asarray(v) for k, v in inputs.items()}
    x = inp["x"]
    B, S, D = x.shape
    if S not in _CACHE:
        _CACHE[S] = build(S)
    nc = _CACHE[S]
    cm, cf, daug = host_consts(S)
    pc, dlam = host_pcols(inp)
    shared = {
        "pcols": pc, "dlam": dlam, "cmat": cm, "cf32": cf, "daug": daug,
        "mlp_w_up": inp["mlp_w_up"], "mlp_w_down": inp["mlp_w_down"],
        "ev_w_in": inp["ev_w_in"], "ev_w_out": inp["ev_w_out"],
        "b_w_uq": inp["b_w_uq"], "b_w_ukv": inp["b_w_ukv"],
        "od_w_in": inp["od_w_in"], "od_w_out": inp["od_w_out"],
    }
    in_maps = []
    for b in range(B):
        m = dict(shared)
        m["xT"] = np.ascontiguousarray(x[b].T)
        m["posb"] = np.ascontiguousarray(np.broadcast_to(inp["positions"][b][None, :], (128, S))).astype(np.int32)
        in_maps.append(m)
    res = run_bass_kernel_spmd(nc, in_maps, core_ids=list(range(B)))
    out = np.stack([np.ascontiguousarray(r["yT"].T) for r in res.results], axis=0)
    return out.astype(np.float32)
```

```python
import math
import os
from contextlib import ExitStack

import numpy as np
import ml_dtypes
import concourse.bass as bass
import concourse.mybir as mybir
from concourse.bass_utils import run_bass_kernel_spmd

F32, BF16, I32 = mybir.dt.float32, mybir.dt.bfloat16, mybir.dt.int32
AF = mybir.ActivationFunctionType
ALU = mybir.AluOpType
AX = mybir.AxisListType
PE, ACT, DVE, POOL, SP = 0, 1, 2, 3, 4
EPS = 1e-6
DEPTH = 4
TWO_PI = 2.0 * math.pi

C_ONES, C_PERM, C_NEGU, C_NEGONES, C_DFB, C_DFC, C_DFD, C_SEL2, C_SEL4 = 0, 128, 256, 384, 512, 640, 768, 1280, 1312
NCM = 1328
F_SEL2T, F_SEL4T, F_KSELT, F_ABIAS, F_INVF, F_SHIFT = 0, 512, 768, 1024, 4096, 4097
NCF = 4161
NPC = 192


class Res:
    __slots__ = ("lw", "rd", "sem", "cnt")

    def __init__(self):
        self.lw = None
        self.rd = []
        self.sem = None
        self.cnt = 0


class Sched:
    def __init__(self, nc, stack):
        self.nc = nc
        self.stack = stack
        self.q = [[] for _ in range(5)]
        self.seq = [0] * 5
        self.esem = [stack.enter_context(nc.semaphore(f"es{i}")) for i in range(5)]
        self.waited = [dict() for _ in range(5)]
        self.dsems = []
        self.free_dsems = []
        self.semcnt = {}
        self.nds = 0

    def _waits(self, e, deps):
        best = {}
        for (sem, val, src) in deps:
            if e == PE and src == PE:
                continue
            k = id(sem)
            if self.waited[e].get(k, 0) >= val:
                continue
            if k not in best or best[k][1] < val:
                best[k] = (sem, val)
        out = []
        for k, (sem, val) in best.items():
            self.waited[e][k] = val
            out.append((sem, val))
        return out

    def _deps(self, r, w, e=-9):
        deps = []
        for x in r:
            if x.lw is not None:
                deps.append(x.lw)
        for x in w:
            if x.lw is not None and x.lw[2] != e:
                deps.append(x.lw)
            for t in x.rd:
                if t[2] != e:
                    deps.append(t)
        return deps

    def op(self, e, fn, r=(), w=()):
        waits = self._waits(e, self._deps(r, w, e))
        self.seq[e] += 1
        tok = (self.esem[e], self.seq[e], e)
        self.q[e].append((waits, fn, (self.esem[e], 1), True))
        for x in r:
            x.rd.append(tok)
        for x in w:
            x.lw = tok
            x.rd = []

    def dma(self, qe, fn, r=(), w=(), semres=None):
        waits = self._waits(qe, self._deps(r, w))
        sr = semres if semres is not None else (w[0] if w else r[0])
        if sr.sem is None:
            if self.free_dsems:
                sr.sem = self.free_dsems.pop()
            else:
                self.nds += 1
                sr.sem = self.stack.enter_context(self.nc.semaphore(f"ds{self.nds}"))
            sr.cnt = self.semcnt.get(id(sr.sem), 0)
            self.dsems.append(sr)
        sr.cnt += 16
        self.semcnt[id(sr.sem)] = sr.cnt
        tok = (sr.sem, sr.cnt, -1)
        self.q[qe].append((waits, fn, (sr.sem, 16), False))
        for x in r:
            x.rd.append(tok)
        for x in w:
            x.lw = tok
            x.rd = []

    def barrier(self):
        toks = [(self.esem[i], self.seq[i], i) for i in range(5) if self.seq[i] > 0]
        toks += [(sr.sem, sr.cnt, -1) for sr in self.dsems]
        for e in range(5):
            deps = [t for t in toks if t[2] != e]
            waits = self._waits(e, [(s, v, -2) for (s, v, _) in deps])
            if waits:
                self.q[e].append((waits, None, None, False))
        for sr in self.dsems:
            self.free_dsems.append(sr.sem)
            sr.sem = None
        self.dsems = []

    def emit(self, e, eng):
        for (waits, fn, inc, attach) in self.q[e]:
            if fn is None:
                for (sem, val) in waits:
                    eng.wait_ge(sem, val)
                continue
            if attach and waits:
                for (sem, val) in waits[:-1]:
                    eng.wait_ge(sem, val)
                ins = fn(eng)
                ins._wait_ge(*waits[-1])
            else:
                for (sem, val) in waits:
                    eng.wait_ge(sem, val)
                ins = fn(eng)
            ins.then_inc(inc[0], inc[1])


def pipeline(tiles, skews):
    T = len(tiles)
    mx = max(skews)
    for i in range(T + mx):
        for s, sk in enumerate(skews):
            t = i - sk
            if 0 <= t < T and tiles[t][s] is not None:
                tiles[t][s]()


class T:
    __slots__ = ("h", "res")

    def __init__(self, h):
        self.h = h
        self.res = Res()

    def __getitem__(self, k):
        return self.h[k]


def build(S):
    NG = S // 512
    NB = S // 128
    nc = bass.Bass("TRN2", target_bir_lowering=False)

    def din(name, shape, dt=F32):
        return nc.dram_tensor(name, list(shape), dt, kind="ExternalInput").ap()

    def dscr(name, shape, dt=BF16):
        return nc.dram_tensor(name, list(shape), dt).ap()

    xT = din("xT", [1024, S])
    posb = din("posb", [128, S], I32)
    pcols_d = din("pcols", [128, NPC])
    dlam_d = din("dlam", [128, 2, 256])
    cmat_d = din("cmat", [128, NCM], BF16)
    cf32_d = din("cf32", [128, NCF])
    daug_d = din("daug", [4, 2, 6, S], BF16)
    w_up_d = din("mlp_w_up", [4, 1024, 4096])
    w_down_d = din("mlp_w_down", [4, 4096, 1024])
    ev_in_d = din("ev_w_in", [2, 1024, 1184])
    ev_out_d = din("ev_w_out", [2, 1024, 1024])
    uq_d = din("b_w_uq", [2, 256, 768])
    ukv_d = din("b_w_ukv", [2, 128, 1024])
    od_in_d = din("od_w_in", [2, 1024, 3072])
    od_out_d = din("od_w_out", [2, 1024, 1024])
    yT = nc.dram_tensor("yT", [1024, S], F32, kind="ExternalOutput").ap()

    w_up = dscr("s_w_up", [4, 1024, 4096])
    w_down = dscr("s_w_down", [4, 4096, 1024])
    ev_in = dscr("s_ev_in", [2, 1024, 1184])
    ev_out = dscr("s_ev_out", [2, 1024, 1024])
    uq = dscr("s_uq", [2, 256, 768])
    ukv = dscr("s_ukv", [2, 128, 1024])
    od_in = dscr("s_od_in", [2, 1024, 3072])
    od_out = dscr("s_od_out", [2, 1024, 1024])
    xres = dscr("s_xres", [1024, S], F32)
    mixT = dscr("s_mix", [1024, S])
    costab = dscr("s_cos", [128, S], F32)
    sintab = dscr("s_sin", [128, S], F32)
    qa = dscr("s_qa", [8, 64, S])
    ka = dscr("s_ka", [2, 64, S])
    va = dscr("s_va", [2, 128, NB, 64])
    qb = dscr("s_qb", [8, 96, S])
    kb_ = dscr("s_kb", [8, 96, S])
    vb = dscr("s_vb", [8, 128, NB, 64])
    qc = dscr("s_qc", [8, 64, S])
    kc = dscr("s_kc", [8, 64, S])
    vc = dscr("s_vc", [8, 128, NB, 64])
    qd = dscr("s_qd", [4, 2, 70, S])
    kd = dscr("s_kd", [4, 2, 70, S])
    vd = dscr("s_vd", [4, 128, NB, 128])

    stack = ExitStack()
    sch = Sched(nc, stack)
    op, dma = sch.op, sch.dma

    nctr = [0]

    def sb(st, name, shape, dt):
        nctr[0] += 1
        return T(st.enter_context(nc.sbuf_tensor(f"t{nctr[0]}_{name}", list(shape), dt)))

    ps = [T(stack.enter_context(nc.psum_tensor(f"ps{i}", [128, 512], F32))) for i in range(8)]
    cmat = sb(stack, "cmat", [128, NCM], BF16)
    cf32 = sb(stack, "cf32", [128, NCF], F32)
    pcols = sb(stack, "pcols", [128, NPC], F32)
    dcol = sb(stack, "dcol", [128, 32], F32)
    epsc = sb(stack, "epsc", [128, 1], F32)
    onec = sb(stack, "onec", [128, 1], F32)

    ONES = lambda k=128, m=128: cmat[0:k, C_ONES:C_ONES + m]

    def act(out, in_, func, scale=1.0, bias=None):
        if bias is None:
            return lambda e: e.activation(out=out, in_=in_, func=func, scale=scale)
        return lambda e: e.activation(out=out, in_=in_, func=func, scale=scale, bias=bias)

    def mm(out, lhsT, rhs, start, stop):
        return lambda e: e.matmul(out, lhsT=lhsT, rhs=rhs, start=start, stop=stop)

    def tt(out, in0, in1, o):
        return lambda e: e.tensor_tensor(out=out, in0=in0, in1=in1, op=o)

    def ts(out, in0, s1, s2, o0, o1=None):
        if o1 is None:
            return lambda e: e.tensor_scalar(out=out, in0=in0, scalar1=s1, scalar2=None, op0=o0)
        return lambda e: e.tensor_scalar(out=out, in0=in0, scalar1=s1, scalar2=s2, op0=o0, op1=o1)

    def stt(out, in0, s, in1, o0, o1):
        return lambda e: e.scalar_tensor_tensor(out=out, in0=in0, scalar=s, in1=in1, op0=o0, op1=o1)

    def cp(out, in_):
        return lambda e: e.tensor_copy(out=out, in_=in_)

    def dm(out, in_):
        return lambda e: e.dma_start(out=out, in_=in_)

    def rstd_ops(dst, src_ps, n, d):
        op(ACT, act(dst[0:n, :], src_ps[0:n, :], AF.Ln, scale=1.0 / d, bias=epsc[0:n, :]), r=[src_ps.res, epsc.res], w=[dst.res])
        op(ACT, act(dst[0:n, :], dst[0:n, :], AF.Exp, scale=-0.5), r=[dst.res], w=[dst.res])

    with ExitStack() as st:
        dma(SP, dm(cmat[:, :], cmat_d), w=[cmat.res])
        dma(SP, dm(cf32[:, :], cf32_d), w=[cf32.res])
        dma(SP, dm(pcols[:, :], pcols_d), w=[pcols.res])
        op(DVE, lambda e: e.memset(epsc[:, :], EPS), w=[epsc.res])
        op(DVE, lambda e: e.memset(onec[:, :], 1.0), w=[onec.res])
        castres = Res()

        def cast2d(dst, src, rows, step=128):
            for r0 in range(0, rows, step):
                r1 = min(rows, r0 + step)
                dma(POOL, dm(dst[r0:r1, :], src[r0:r1, :]), r=[castres], semres=castres)

        for l in range(4):
            cast2d(w_up[l], w_up_d[l], 1024)
            cast2d(w_down[l], w_down_d[l], 4096, 256)
        for j in range(2):
            cast2d(ev_in[j], ev_in_d[j], 1024)
            cast2d(ev_out[j], ev_out_d[j], 1024)
            cast2d(od_in[j], od_in_d[j], 1024)
            cast2d(od_out[j], od_out_d[j], 1024)
            cast2d(uq[j], uq_d[j], 256)
            cast2d(ukv[j], ukv_d[j], 128)
        for h in range(4):
            for m in range(2):
                dma(SP, dm(qd[h, m, 64:70, :], daug_d[h, 0]), r=[castres], semres=castres)
                dma(SP, dm(kd[h, m, 64:70, :], daug_d[h, 1]), r=[castres], semres=castres)
        dl = sb(st, "dl", [128, 2, 256], F32)
        dlp = sb(st, "dlp", [128, 64], F32)
        dma(SP, dm(dl[:, :, :], dlam_d), w=[dl.res])
        for j in range(2):
            layer = 2 * j + 1
            lam_init = 0.8 - 0.6 * math.exp(-0.3 * layer)
            for t in range(2):
                op(DVE, tt(dlp[:, :], dl[:, j, 128 * t:128 * t + 64], dl[:, j, 128 * t + 64:128 * t + 128], ALU.mult), r=[dl.res], w=[dlp.res])
                op(DVE, lambda e, t=t, j=j: e.reduce_sum(out=dcol[:, 20 + t:21 + t], in_=dlp[:, :], axis=AX.X), r=[dlp.res], w=[dcol.res])
                op(ACT, act(dcol[:, 20 + t:21 + t], dcol[:, 20 + t:21 + t], AF.Exp), r=[dcol.res], w=[dcol.res])
            op(DVE, tt(dcol[:, 22:23], dcol[:, 21:22], dcol[:, 20:21], ALU.subtract), r=[dcol.res], w=[dcol.res])
            op(DVE, ts(dcol[:, j:j + 1], dcol[:, 22:23], -lam_init, None, ALU.add), r=[dcol.res], w=[dcol.res])
            op(DVE, ts(dcol[:, 2 + j:3 + j], pcols[:, 128 + 32 * j + 2:128 + 32 * j + 3], 1.0 - lam_init, None, ALU.mult), r=[pcols.res, dcol.res], w=[dcol.res])
        for j in range(2):
            op(ACT, act(dcol[:, 4 + 8 * j:12 + 8 * j], pcols[:, 64 + 32 * j + 9:64 + 32 * j + 17], AF.Exp), r=[pcols.res, dcol.res], w=[dcol.res])
        CH = min(S, 2048)
        posi = sb(st, "posi", [128, CH], I32)
        ang = sb(st, "ang", [128, CH], F32)
        tq = sb(st, "tq", [128, CH], F32)
        ki = sb(st, "ki", [128, CH], I32)
        kf = sb(st, "kf", [128, CH], F32)
        rr = sb(st, "rr", [128, CH], F32)
        mk = sb(st, "mk", [128, CH], F32)
        C1 = 6.28125
        C2 = TWO_PI - C1
        for c0 in range(0, S, CH):
            dma(SP, dm(posi[:, :], posb[:, c0:c0 + CH]), w=[posi.res])
            op(DVE, cp(ang[:, :], posi[:, :]), r=[posi.res], w=[ang.res])
            op(DVE, ts(ang[:, :], ang[:, :], cf32[:, F_INVF:F_INVF + 1], None, ALU.mult), r=[ang.res, cf32.res], w=[ang.res])
            for which, tab in ((0, sintab), (1, costab)):
                src = ang
                if which == 1:
                    op(DVE, ts(rr[:, :], ang[:, :], math.pi / 2, None, ALU.add), r=[ang.res], w=[rr.res])
                    src = rr
                op(DVE, ts(tq[:, :], src[:, :], 1.0 / TWO_PI, None, ALU.mult), r=[src.res], w=[tq.res])
                op(DVE, cp(ki[:, :], tq[:, :]), r=[tq.res], w=[ki.res])
                op(DVE, cp(kf[:, :], ki[:, :]), r=[ki.res], w=[kf.res])
                op(DVE, stt(rr[:, :], kf[:, :], -C1, src[:, :], ALU.mult, ALU.add), r=[kf.res, src.res], w=[rr.res])
                op(DVE, stt(rr[:, :], kf[:, :], -C2, rr[:, :], ALU.mult, ALU.add), r=[kf.res, rr.res], w=[rr.res])
                op(DVE, ts(mk[:, :], rr[:, :], math.pi, -TWO_PI, ALU.is_gt, ALU.mult), r=[rr.res], w=[mk.res])
                op(DVE, tt(rr[:, :], rr[:, :], mk[:, :], ALU.add), r=[rr.res, mk.res], w=[rr.res])
                op(DVE, ts(mk[:, :], rr[:, :], -math.pi, TWO_PI, ALU.is_lt, ALU.mult), r=[rr.res], w=[mk.res])
                op(DVE, tt(rr[:, :], rr[:, :], mk[:, :], ALU.add), r=[rr.res, mk.res], w=[rr.res])
                op(DVE, ts(rr[:, :], rr[:, :], math.pi, -math.pi, ALU.min, ALU.max), r=[rr.res], w=[rr.res])
                op(ACT, act(tq[:, :], rr[:, :], AF.Sin), r=[rr.res], w=[tq.res])
                dma(SP, dm(tab[:, c0:c0 + CH], tq[:, :]), r=[tq.res])
        sch.barrier()

    def wview(w2d, c0, ncol, nchunk=8):
        return w2d.rearrange("(i p) c -> p i c", p=128)[:, 0:nchunk, c0:c0 + ncol]

    class Ctx:
        pass

    def proj_phase(l):
        with ExitStack() as st:
            xs = sb(st, "xs", [128, 8, 512], F32)
            xn = sb(st, "xn", [128, 8, 512], BF16)
            sq = sb(st, "sq", [128, 8, 512], BF16)
            sqr = [Res() for _ in range(8)]
            rstd = sb(st, "rstd", [128, 512], F32)
            wts = [sb(st, f"wt{i}", [128, 8, 512], BF16) for i in range(4)]
            wctr = [0]
            if l > 0:
                H = sb(st, "H", [128, 32, 512], BF16)
                mx = sb(st, "mx", [128, 8, 512], BF16)
                rl = [sb(st, f"rl{i}", [128, 512], BF16) for i in range(2)]
            if l < DEPTH:
                raws = [sb(st, f"raw{i}", [128, 512], F32) for i in range(8)]
                sqs = [sb(st, f"sqs{i}", [128, 512], BF16) for i in range(8)]
                outs = [sb(st, f"ob{i}", [128, 512], BF16) for i in range(4)]
                rc = sb(st, "rc", [8, 512], F32)
                vt = sb(st, "vt", [128, 4, 512], BF16)
                octr = [0]
                if l % 2 == 0:
                    cqn = sb(st, "cqn", [128, 2, 512], BF16)
                    ckvn = sb(st, "ckvn", [128, 512], BF16)
                    cost = sb(st, "cost", [128, 512], F32)
                    sint = sb(st, "sint", [128, 512], F32)
                    t1 = sb(st, "t1", [128, 512], F32)
                    t2 = sb(st, "t2", [128, 512], F32)
                    krr = sb(st, "krr", [32, 512], F32)
                    wsm = sb(st, "wsm", [128, 2, 768], BF16)
                    wkv = sb(st, "wkv", [128, 1024], BF16)
            pctr = [0]
            STQ = ACT

            def nps():
                pctr[0] += 1
                return ps[pctr[0] % 4]

            def load_w(view, nchunk=8, ncol=512):
                wt = wts[wctr[0] % 4]
                wctr[0] += 1
                dma(SP, dm(wt[:, 0:nchunk, 0:ncol], view), w=[wt.res])
                return wt

            def norm(gbase):
                for c in range(8):
                    op(POOL if c % 2 == 0 else DVE, tt(sq[:, c, :], xs[:, c, :], xs[:, c, :], ALU.mult), r=[xs.res], w=[sqr[c]])
                p = nps()
                for c in range(8):
                    op(PE, mm(p[:, :], ONES(), sq[:, c, :], c == 0, c == 7), r=[sqr[c], cmat.res], w=[p.res])
                rstd_ops(rstd, p, 128, 1024.0)
                for c in range(8):
                    op(DVE, stt(xn[:, c, :], xs[:, c, :], pcols[:, gbase + c:gbase + c + 1], rstd[:, :], ALU.mult, ALU.mult),
                       r=[xs.res, pcols.res, rstd.res], w=[xn.res])

            def outbuf():
                o = outs[octr[0] % 4]
                octr[0] += 1
                return o

            def proj_chunk(wt, col0, ncol, src, nchunk=8, srcidx=None):
                p = nps()
                for i in range(nchunk):
                    rhs = src[:, i, :] if srcidx is None else srcidx(i)
                    op(PE, mm(p[0:ncol, :], wt[:, i, col0:col0 + ncol], rhs, i == 0, i == nchunk - 1), r=[wt.res, src.res], w=[p.res])
                return p

            def vproj(wt, col0, ncol, src, nchunk, dst_fn, srcidx=None):
                for tb in range(4):
                    p = nps()
                    for i in range(nchunk):
                        lhs = src[:, i, tb * 128:(tb + 1) * 128] if srcidx is None else srcidx(i, tb)
                        op(PE, mm(p[:, 0:ncol], lhs, wt[:, i, col0:col0 + ncol], i == 0, i == nchunk - 1), r=[wt.res, src.res], w=[p.res])
                    op(ACT, act(vt[:, tb, 0:ncol], p[:, 0:ncol], AF.Copy), r=[p.res], w=[vt.res])
                dst_fn()

            for g in range(NG):
                gs = slice(g * 512, (g + 1) * 512)
                src_x = xT if l == 0 else xres
                dma(SP, dm(xs[:, :, :], src_x.rearrange("(c p) s -> p c s", p=128)[:, :, gs]), w=[xs.res])
                if l > 0:
                    lp = l - 1
                    jp = lp // 2
                    dma(SP, dm(mx[:, :, :], mixT.rearrange("(c p) s -> p c s", p=128)[:, :, gs]), w=[mx.res])
                    wout = (ev_out if lp % 2 == 0 else od_out)[jp]
                    for half in range(2):
                        wt = load_w(wview(wout, half * 512, 512))
                        for oc4 in range(4):
                            oc = half * 4 + oc4
                            p = proj_chunk(wt, oc4 * 128, 128, mx)
                            op(DVE, tt(xs[:, oc, :], p[:, :], xs[:, oc, :], ALU.add), r=[p.res, xs.res], w=[xs.res])
                    norm(lp * 16 + 8)
                    for fb in range(8):
                        wt = load_w(wview(w_up[lp], fb * 512, 512))
                        for f4 in range(4):
                            fc = fb * 4 + f4
                            p = proj_chunk(wt, f4 * 128, 128, xn)
                            r_ = rl[fc % 2]
                            op(DVE, ts(r_[:, :], p[:, :], 0.0, None, ALU.max), r=[p.res], w=[r_.res])
                            op(POOL, tt(H[:, fc, :], r_[:, :], r_[:, :], ALU.mult), r=[r_.res], w=[H.res])
                    for half in range(2):
                        accs = [ps[4 + i] for i in range(4)]
                        for fs in range(4):
                            view = w_down[lp].rearrange("(i p) c -> p i c", p=128)[:, fs * 8:(fs + 1) * 8, half * 512:(half + 1) * 512]
                            wt = load_w(view)
                            for oc4 in range(4):
                                for i in range(8):
                                    fc = fs * 8 + i
                                    op(PE, mm(accs[oc4][:, :], wt[:, i, oc4 * 128:(oc4 + 1) * 128], H[:, fc, :], fc == 0, fc == 31),
                                       r=[wt.res, H.res], w=[accs[oc4].res])
                        for oc4 in range(4):
                            oc = half * 4 + oc4
                            op(DVE, tt(xs[:, oc, :], accs[oc4][:, :], xs[:, oc, :], ALU.add), r=[accs[oc4].res, xs.res], w=[xs.res])
                if l == DEPTH:
                    dma(STQ, dm(yT.rearrange("(c p) s -> p c s", p=128)[:, :, gs], xs[:, :, :]), r=[xs.res])
                    continue
                dma(STQ, dm(xres.rearrange("(c p) s -> p c s", p=128)[:, :, gs], xs[:, :, :]), r=[xs.res])
                norm(l * 16)
                j = l // 2
                if l % 2 == 1:
                    pb = 128 + 32 * j
                    win = od_in[j]
                    for part, dst, scale in ((0, qc, 0.125), (1, kc, 1.0)):
                        wt = load_w(wview(win, part * 512, 512))
                        for c in range(4):
                            p = proj_chunk(wt, c * 128, 128, xn)
                            o = outbuf()
                            op(ACT, act(o[:, :], p[:, :], AF.Copy, scale=scale), r=[p.res], w=[o.res])
                            for hh in range(2):
                                dma(STQ, dm(dst[2 * c + hh, :, gs], o[64 * hh:64 * hh + 64, :]), r=[o.res])
                    wt = load_w(wview(win, 1024, 512))

                    def st_cv():
                        for h in range(8):
                            dma(STQ, dm(vc[h, :, 4 * g:4 * g + 4, :], vt[:, :, 64 * h:64 * h + 64]), r=[vt.res])
                    vproj(wt, 0, 512, xn, 8, st_cv)
                    for part, dst, gcol in ((3, qd, pb + 0), (4, kd, pb + 1)):
                        wt = load_w(wview(win, part * 512, 512))
                        pc = nps()
                        for c in range(4):
                            p = proj_chunk(wt, c * 128, 128, xn)
                            op(ACT, act(raws[c][:, :], p[:, :], AF.Copy), r=[p.res], w=[raws[c].res])
                            op(POOL, tt(sqs[c][:, :], raws[c][:, :], raws[c][:, :], ALU.mult), r=[raws[c].res], w=[sqs[c].res])
                        for c in range(4):
                            op(PE, mm(pc[0:8, :], cmat[:, C_SEL2 + 8 * c:C_SEL2 + 8 * c + 8], sqs[c][:, :], c == 0, c == 3), r=[cmat.res, sqs[c].res], w=[pc.res])
                        rstd_ops(rc, pc, 8, 64.0)
                        for c in range(4):
                            pbc = nps()
                            op(PE, mm(pbc[:, :], cf32[0:8, F_SEL2T + 128 * c:F_SEL2T + 128 * c + 128], rc[0:8, :], True, True), r=[cf32.res, rc.res], w=[pbc.res])
                            o = outbuf()
                            op(DVE, stt(o[:, :], raws[c][:, :], pcols[:, gcol:gcol + 1], pbc[:, :], ALU.mult, ALU.mult), r=[raws[c].res, pcols.res, pbc.res], w=[o.res])
                            for m in range(2):
                                dma(STQ, dm(dst[c, m, 0:64, gs], o[64 * m:64 * m + 64, :]), r=[o.res])
                    wt = load_w(wview(win, 2560, 512))

                    def st_dv():
                        for h in range(4):
                            dma(STQ, dm(vd[h, :, 4 * g:4 * g + 4, :], vt[:, :, 128 * h:128 * h + 128]), r=[vt.res])
                    vproj(wt, 0, 512, xn, 8, st_dv)
                else:
                    pb = 64 + 32 * j
                    win = ev_in[j]
                    wt = load_w(wview(win, 0, 512))
                    pc = nps()
                    for c in range(4):
                        p = proj_chunk(wt, c * 128, 128, xn)
                        op(ACT, act(raws[c][:, :], p[:, :], AF.Copy), r=[p.res], w=[raws[c].res])
                        op(POOL, tt(sqs[c][:, :], raws[c][:, :], raws[c][:, :], ALU.mult), r=[raws[c].res], w=[sqs[c].res])
                    for c in range(4):
                        op(PE, mm(pc[0:8, :], cmat[:, C_SEL2 + 8 * c:C_SEL2 + 8 * c + 8], sqs[c][:, :], c == 0, c == 3), r=[cmat.res, sqs[c].res], w=[pc.res])
                    rstd_ops(rc, pc, 8, 64.0)
                    for c in range(4):
                        pbc = nps()
                        op(PE, mm(pbc[:, :], cf32[0:8, F_SEL2T + 128 * c:F_SEL2T + 128 * c + 128], rc[0:8, :], True, True), r=[cf32.res, rc.res], w=[pbc.res])
                        o = outbuf()
                        op(DVE, stt(o[:, :], raws[c][:, :], pcols[:, pb:pb + 1], pbc[:, :], ALU.mult, ALU.mult), r=[raws[c].res, pcols.res, pbc.res], w=[o.res])
                        for hh in range(2):
                            dma(STQ, dm(qa[2 * c + hh, :, gs], o[64 * hh:64 * hh + 64, :]), r=[o.res])
                    wt = load_w(wview(win, 512, 512))
                    p = proj_chunk(wt, 0, 128, xn)
                    op(ACT, act(raws[0][:, :], p[:, :], AF.Copy), r=[p.res], w=[raws[0].res])
                    op(POOL, tt(sqs[0][:, :], raws[0][:, :], raws[0][:, :], ALU.mult), r=[raws[0].res], w=[sqs[0].res])
                    pc = nps()
                    op(PE, mm(pc[0:8, :], cmat[:, C_SEL2:C_SEL2 + 8], sqs[0][:, :], True, True), r=[cmat.res, sqs[0].res], w=[pc.res])
                    rstd_ops(rc, pc, 8, 64.0)
                    pbc = nps()
                    op(PE, mm(pbc[:, :], cf32[0:8, F_SEL2T:F_SEL2T + 128], rc[0:8, :], True, True), r=[cf32.res, rc.res], w=[pbc.res])
                    o = outbuf()
                    op(DVE, stt(o[:, :], raws[0][:, :], pcols[:, pb + 1:pb + 2], pbc[:, :], ALU.mult, ALU.mult), r=[raws[0].res, pcols.res, pbc.res], w=[o.res])
                    for hh in range(2):
                        dma(STQ, dm(ka[hh, :, gs], o[64 * hh:64 * hh + 64, :]), r=[o.res])

                    def st_av():
                        for h in range(2):
                            dma(STQ, dm(va[h, :, 4 * g:4 * g + 4, :], vt[:, :, 64 * h:64 * h + 64]), r=[vt.res])
                    vproj(wt, 128, 128, xn, 8, st_av)
                    for c in range(2):
                        p = proj_chunk(wt, 256 + c * 128, 128, xn)
                        op(ACT, act(raws[c][:, :], p[:, :], AF.Copy), r=[p.res], w=[raws[c].res])
                        op(POOL, tt(sqs[c][:, :], raws[c][:, :], raws[c][:, :], ALU.mult), r=[raws[c].res], w=[sqs[c].res])
                    pc = nps()
                    for c in range(2):
                        op(PE, mm(pc[:, :], ONES(), sqs[c][:, :], c == 0, c == 1), r=[cmat.res, sqs[c].res], w=[pc.res])
                    rstd_ops(rstd, pc, 128, 256.0)
                    for c in range(2):
                        op(DVE, stt(cqn[:, c, :], raws[c][:, :], pcols[:, pb + 2 + c:pb + 3 + c], rstd[:, :], ALU.mult, ALU.mult), r=[raws[c].res, pcols.res, rstd.res], w=[cqn.res])
                    wt = load_w(wview(win, 1024, 160), 8, 160)
                    p = proj_chunk(wt, 0, 128, xn)
                    op(ACT, act(raws[0][:, :], p[:, :], AF.Copy), r=[p.res], w=[raws[0].res])
                    op(POOL, tt(sqs[0][:, :], raws[0][:, :], raws[0][:, :], ALU.mult), r=[raws[0].res], w=[sqs[0].res])
                    pc = nps()
                    op(PE, mm(pc[:, :], ONES(), sqs[0][:, :], True, True), r=[cmat.res, sqs[0].res], w=[pc.res])
                    rstd_ops(rstd, pc, 128, 128.0)
                    op(DVE, stt(ckvn[:, :], raws[0][:, :], pcols[:, pb + 4:pb + 5], rstd[:, :], ALU.mult, ALU.mult), r=[raws[0].res, pcols.res, rstd.res], w=[ckvn.res])
                    pkr = proj_chunk(wt, 128, 32, xn)
                    op(ACT, act(raws[6][0:32, :], pkr[0:32, :], AF.Copy), r=[pkr.res], w=[raws[6].res])
                    op(POOL, tt(sqs[6][0:32, :], raws[6][0:32, :], raws[6][0:32, :], ALU.mult), r=[raws[6].res], w=[sqs[6].res])
                    dma(SP, dm(cost[:, :], costab[:, gs]), w=[cost.res])
                    dma(SP, dm(sint[:, :], sintab[:, gs]), w=[sint.res])
                    op(DVE, ts(t1[0:32, :], raws[6][0:32, :], pcols[0:32, pb + 8:pb + 9], None, ALU.mult), r=[raws[6].res, pcols.res], w=[t1.res])
                    op(DVE, cp(sqs[7][0:32, :], t1[0:32, :]), r=[t1.res], w=[sqs[7].res])
                    prot = nps()
                    op(PE, mm(prot[0:32, :], cmat[0:32, C_PERM:C_PERM + 32], sqs[7][0:32, :], True, True), r=[cmat.res, sqs[7].res], w=[prot.res])
                    op(DVE, tt(t2[0:32, :], prot[0:32, :], sint[0:32, :], ALU.mult), r=[prot.res, sint.res], w=[t2.res])
                    op(DVE, tt(t1[0:32, :], t1[0:32, :], cost[0:32, :], ALU.mult), r=[t1.res, cost.res], w=[t1.res])
                    op(DVE, tt(krr[0:32, :], t1[0:32, :], t2[0:32, :], ALU.add), r=[t1.res, t2.res], w=[krr.res])
                    uqv = uq[j].rearrange("(i p) (h d) -> p i h d", p=128, d=96)
                    for i in range(2):
                        dma(SP, dm(wsm[:, i, 0:512].rearrange("p (h d) -> p h d", d=64), uqv[:, i, :, 0:64]), w=[wsm.res])
                        dma(SP, dm(wsm[:, i, 512:768].rearrange("p (h d) -> p h d", d=32), uqv[:, i, :, 64:96]), w=[wsm.res])
                    ukvv = ukv[j].rearrange("p (h d) -> p h d", d=128)
                    dma(SP, dm(wkv[:, 0:512].rearrange("p (h d) -> p h d", d=64), ukvv[:, :, 0:64]), w=[wkv.res])
                    dma(SP, dm(wkv[:, 512:1024].rearrange("p (h d) -> p h d", d=64), ukvv[:, :, 64:128]), w=[wkv.res])
                    for c in range(6):
                        p = nps()
                        for i in range(2):
                            op(PE, mm(p[:, :], wsm[:, i, c * 128:(c + 1) * 128], cqn[:, i, :], i == 0, i == 1), r=[wsm.res, cqn.res], w=[p.res])
                        op(ACT, act(raws[c][:, :], p[:, :], AF.Copy), r=[p.res], w=[raws[c].res])
                        op(POOL, tt(sqs[c][:, :], raws[c][:, :], raws[c][:, :], ALU.mult), r=[raws[c].res], w=[sqs[c].res])
                    pc = nps()
                    for c in range(6):
                        sel = cmat[:, C_SEL2 + 8 * c:C_SEL2 + 8 * c + 8] if c < 4 else cmat[:, C_SEL4 + 8 * (c - 4):C_SEL4 + 8 * (c - 4) + 8]
                        op(PE, mm(pc[0:8, :], sel, sqs[c][:, :], c == 0, c == 5), r=[cmat.res, sqs[c].res], w=[pc.res])
                    rstd_ops(rc, pc, 8, 96.0)
                    for c in range(6):
                        pbc = nps()
                        selT = cf32[0:8, F_SEL2T + 128 * c:F_SEL2T + 128 * c + 128] if c < 4 else cf32[0:8, F_SEL4T + 128 * (c - 4):F_SEL4T + 128 * (c - 4) + 128]
                        op(PE, mm(pbc[:, :], selT, rc[0:8, :], True, True), r=[cf32.res, rc.res], w=[pbc.res])
                        if c < 4:
                            o = outbuf()
                            op(DVE, stt(o[:, :], raws[c][:, :], pcols[:, pb + 5:pb + 6], pbc[:, :], ALU.mult, ALU.mult), r=[raws[c].res, pcols.res, pbc.res], w=[o.res])
                            for hh in range(2):
                                dma(STQ, dm(qb[2 * c + hh, 0:64, gs], o[64 * hh:64 * hh + 64, :]), r=[o.res])
                        else:
                            op(DVE, stt(t1[:, :], raws[c][:, :], pcols[:, pb + 6:pb + 7], pbc[:, :], ALU.mult, ALU.mult), r=[raws[c].res, pcols.res, pbc.res], w=[t1.res])
                            op(POOL, cp(sqs[7][:, :], t1[:, :]), r=[t1.res], w=[sqs[7].res])
                            prot = nps()
                            op(PE, mm(prot[:, :], cmat[:, C_PERM:C_PERM + 128], sqs[7][:, :], True, True), r=[cmat.res, sqs[7].res], w=[prot.res])
                            op(DVE, tt(t2[:, :], prot[:, :], sint[:, :], ALU.mult), r=[prot.res, sint.res], w=[t2.res])
                            op(DVE, tt(t1[:, :], t1[:, :], cost[:, :], ALU.mult), r=[t1.res, cost.res], w=[t1.res])
                            o = outbuf()
                            op(DVE, tt(o[:, :], t1[:, :], t2[:, :], ALU.add), r=[t1.res, t2.res], w=[o.res])
                            for hh in range(4):
                                dma(STQ, dm(qb[4 * (c - 4) + hh, 64:96, gs], o[32 * hh:32 * hh + 32, :]), r=[o.res])
                    for c in range(4):
                        p = nps()
                        op(PE, mm(p[:, :], wkv[:, c * 128:(c + 1) * 128], ckvn[:, :], True, True), r=[wkv.res, ckvn.res], w=[p.res])
                        op(ACT, act(raws[c][:, :], p[:, :], AF.Copy), r=[p.res], w=[raws[c].res])
                        op(POOL, tt(sqs[c][:, :], raws[c][:, :], raws[c][:, :], ALU.mult), r=[raws[c].res], w=[sqs[c].res])
                    pc = nps()
                    for c in range(4):
                        op(PE, mm(pc[0:8, :], cmat[:, C_SEL2 + 8 * c:C_SEL2 + 8 * c + 8], sqs[c][:, :], c == 0, False), r=[cmat.res, sqs[c].res], w=[pc.res])
                    op(PE, mm(pc[0:8, :], cmat[0:32, C_ONES:C_ONES + 8], sqs[6][0:32, :], False, True), r=[cmat.res, sqs[6].res], w=[pc.res])
                    rstd_ops(rc, pc, 8, 96.0)
                    for c in range(4):
                        pbc = nps()
                        op(PE, mm(pbc[:, :], cf32[0:8, F_SEL2T + 128 * c:F_SEL2T + 128 * c + 128], rc[0:8, :], True, True), r=[cf32.res, rc.res], w=[pbc.res])
                        o = outbuf()
                        op(DVE, stt(o[:, :], raws[c][:, :], pcols[:, pb + 7:pb + 8], pbc[:, :], ALU.mult, ALU.mult), r=[raws[c].res, pcols.res, pbc.res], w=[o.res])
                        for hh in range(2):
                            dma(STQ, dm(kb_[2 * c + hh, 0:64, gs], o[64 * hh:64 * hh + 64, :]), r=[o.res])
                    for h in range(8):
                        pbc = nps()
                        op(PE, mm(pbc[0:32, :], cf32[0:8, F_KSELT + 32 * h:F_KSELT + 32 * h + 32], rc[0:8, :], True, True), r=[cf32.res, rc.res], w=[pbc.res])
                        o = outbuf()
                        op(DVE, tt(o[0:32, :], krr[0:32, :], pbc[0:32, :], ALU.mult), r=[krr.res, pbc.res], w=[o.res])
                        dma(STQ, dm(kb_[h, 64:96, gs], o[0:32, :]), r=[o.res])
                    for tb in range(4):
                        p = nps()
                        op(PE, mm(p[:, :], ckvn[:, tb * 128:(tb + 1) * 128], wkv[:, 512:1024], True, True), r=[wkv.res, ckvn.res], w=[p.res])
                        op(ACT, act(vt[:, tb, :], p[:, :], AF.Copy), r=[p.res], w=[vt.res])
                    for h in range(8):
                        dma(STQ, dm(vb[h, :, 4 * g:4 * g + 4, :], vt[:, :, 64 * h:64 * h + 64]), r=[vt.res])
        sch.barrier()

    def attn_even(j):
        pb = 64 + 32 * j
        with ExitStack() as st:
            KT = [sb(st, f"KT{i}", [128, S], BF16) for i in range(2)]
            V = [sb(st, f"V{i}", [128, NB, 128], BF16) for i in range(2)]
            for v_ in V:
                op(POOL, lambda e, v_=v_: e.memset(v_[:, :, 64:128], 1.0), w=[v_.res])
            tmpf = [sb(st, f"tmpf{i}", [128, 512], F32) for i in range(2)]
            STQA = POOL
            Q = [sb(st, f"Q{i}", [128, 512], BF16) for i in range(3)]
            Pb = [sb(st, f"P{i}", [128, 512], BF16) for i in range(6)]
            sbb = [sb(st, f"sbb{i}", [128, 256], F32) for i in range(4)]
            rec = sb(st, "rec", [64, 512], F32)
            ob = [sb(st, f"oo{i}", [64, 512], BF16) for i in range(2)]
            O, L = ps[6], ps[7]
            qctr = [0]
            octr = [0]
            tctr = [0]
            for kvh in range(2):
                kt, v = KT[kvh], V[kvh]
                dma(SP, dm(kt[0:64, :], ka[kvh]), w=[kt.res])
                dma(SP, dm(v[:, :, 0:64], va[kvh]), w=[v.res])
                for hq in range(4):
                    h = 4 * kvh + hq
                    for g in range(NG):
                        q = Q[qctr[0] % 3]
                        qctr[0] += 1
                        dma(SP, dm(q[0:64, :], qa[h, :, g * 512:(g + 1) * 512]), w=[q.res])
                        tiles = []
                        rels = list(range(-1, 4)) if g > 0 else list(range(0, 4))
                        for idx, rel in enumerate(rels):
                            kbi = 4 * g + rel
                            if rel < 0:
                                q0, n, boff = 0, 128, 0
                            else:
                                q0 = 128 * rel
                                n = min(256, 512 - q0)
                                boff = 128
                            t = tctr[0]
                            tctr[0] += 1
                            sp_, P_, sb_ = ps[t % 4], Pb[t % 6], sbb[t % 4]
                            first, last = idx == 0, idx == len(rels) - 1

                            def s1(kbi=kbi, q0=q0, n=n, sp_=sp_, q=q, kt=kt):
                                op(PE, mm(sp_[:, 0:n], kt[0:64, kbi * 128:(kbi + 1) * 128], q[0:64, q0:q0 + n], True, True), r=[kt.res, q.res], w=[sp_.res])

                            def s2(n=n, sp_=sp_, sb_=sb_, P_=P_, boff=boff, h=h):
                                op(DVE, stt(sb_[:, 0:n], sp_[:, 0:n], 0.125, cf32[:, F_ABIAS + 384 * h + boff:F_ABIAS + 384 * h + boff + n], ALU.mult, ALU.add),
                                   r=[sp_.res, cf32.res], w=[sb_.res])
                                op(ACT, act(P_[:, 0:n], sb_[:, 0:n], AF.Exp), r=[sb_.res], w=[P_.res])

                            def s3(kbi=kbi, q0=q0, n=n, P_=P_, v=v, first=first, last=last):
                                op(PE, mm(O[0:64, q0:q0 + n], v[:, kbi, 0:64], P_[:, 0:n], first, last), r=[v.res, P_.res], w=[O.res])
                                op(PE, mm(L[0:64, q0:q0 + n], ONES(128, 64), P_[:, 0:n], first, last), r=[cmat.res, P_.res], w=[L.res])
                            tiles.append([s1, s2, s3])
                        pipeline(tiles, [0, 2, 4])
                        o = ob[octr[0] % 2]
                        octr[0] += 1
                        op(DVE, ts(rec[:, :], L[0:64, :], dcol[0:64, 4 + 8 * j + h:5 + 8 * j + h], None, ALU.add), r=[L.res, dcol.res], w=[rec.res])
                        op(DVE, lambda e: e.reciprocal(out=rec[:, :], in_=rec[:, :]), r=[rec.res], w=[rec.res])
                        op(DVE, tt(o[:, :], O[0:64, :], rec[:, :], ALU.mult), r=[O.res, rec.res], w=[o.res])
                        dma(STQA, dm(mixT[64 * h:64 * h + 64, g * 512:(g + 1) * 512], o[:, :]), r=[o.res])
            scaleB = 96.0 ** -0.5
            for h in range(8):
                kt, v = KT[h % 2], V[h % 2]
                dma(SP, dm(kt[0:96, :], kb_[h]), w=[kt.res])
                dma(SP, dm(v[:, :, 0:64], vb[h]), w=[v.res])
                for g in range(NG):
                    q = Q[qctr[0] % 3]
                    qctr[0] += 1
                    dma(SP, dm(q[0:96, :], qb[h, :, g * 512:(g + 1) * 512]), w=[q.res])
                    OL = ps[6 + (octr[0] % 2)]
                    tiles = []
                    nkb = 4 * g + 4
                    for kbi in range(nkb):
                        rel = kbi - 4 * g
                        q0 = 128 * rel if rel > 0 else 0
                        n = 512 - q0
                        t = tctr[0]
                        tctr[0] += 1
                        sp_, P_ = ps[t % 4], Pb[t % 6]
                        first, last = kbi == 0, kbi == nkb - 1

                        def s1(kbi=kbi, q0=q0, n=n, sp_=sp_, q=q, kt=kt):
                            op(PE, mm(sp_[:, 0:n], kt[0:96, kbi * 128:(kbi + 1) * 128], q[0:96, q0:q0 + n], True, True), r=[kt.res, q.res], w=[sp_.res])

                        def s2(n=n, sp_=sp_, P_=P_, rel=rel):
                            op(ACT, act(P_[:, 0:n], sp_[:, 0:n], AF.Exp, scale=scaleB), r=[sp_.res], w=[P_.res])
                            if rel >= 0:
                                op(DVE, tt(P_[:, 0:128], P_[:, 0:128], cmat[:, C_DFB:C_DFB + 128], ALU.mult), r=[P_.res, cmat.res], w=[P_.res])

                        def s3(kbi=kbi, q0=q0, n=n, P_=P_, v=v, first=first, last=last, OL=OL):
                            op(PE, mm(OL[:, q0:q0 + n], v[:, kbi, :], P_[:, 0:n], first, last), r=[v.res, P_.res], w=[OL.res])
                        tiles.append([s1, s2, s3])
                    pipeline(tiles, [0, 2, 4])
                    o = ob[octr[0] % 2]
                    tf = tmpf[octr[0] % 2]
                    octr[0] += 1
                    op(DVE, cp(tf[:, :], OL[:, :]), r=[OL.res], w=[tf.res])
                    op(PE, mm(ps[5][0:64, :], cf32[:, F_SHIFT:F_SHIFT + 64], tf[:, :], True, True), r=[cf32.res, tf.res], w=[ps[5].res])
                    op(DVE, lambda e: e.reciprocal(out=rec[:, :], in_=ps[5][0:64, :]), r=[ps[5].res], w=[rec.res])
                    op(DVE, tt(o[:, :], tf[0:64, :], rec[:, :], ALU.mult), r=[tf.res, rec.res], w=[o.res])
                    dma(STQA, dm(mixT[512 + 64 * h:512 + 64 * h + 64, g * 512:(g + 1) * 512], o[:, :]), r=[o.res])
        sch.barrier()

    def attn_odd(j):
        with ExitStack() as st:
            KT = [sb(st, f"KT{i}", [128, S], BF16) for i in range(3)]
            V = [sb(st, f"V{i}", [128, NB, 128], BF16) for i in range(2)]
            Q = [sb(st, f"Q{i}", [128, 512], BF16) for i in range(4)]
            Pb = [sb(st, f"P{i}", [128, 512], BF16) for i in range(6)]
            ef = [sb(st, f"ef{i}", [128, 512], F32) for i in range(3)]
            spb = [sb(st, f"spb{i}", [128, 512], BF16) for i in range(3)]
            R32 = sb(st, "R32", [128, 512], F32)
            Rb = [sb(st, f"Rb{i}", [128, 512], BF16) for i in range(2)]
            ob = [sb(st, f"oo{i}", [128, 512], BF16) for i in range(2)]
            f1 = sb(st, "f1", [128, 512], F32)
            f2 = sb(st, "f2", [128, 512], F32)
            f3 = sb(st, "f3", [128, 512], F32)
            fsq = sb(st, "fsq", [128, 512], BF16)
            qctr = [0]
            octr = [0]
            tctr = [0]
            O = ps[7]
            for h in range(8):
                kt, v = KT[h % 2], V[h % 2]
                dma(SP, dm(kt[0:64, :], kc[h]), w=[kt.res])
                dma(SP, dm(v[:, :, 0:64], vc[h]), w=[v.res])
                for g in range(NG):
                    q = Q[qctr[0] % 4]
                    qctr[0] += 1
                    dma(SP, dm(q[0:64, :], qc[h, :, g * 512:(g + 1) * 512]), w=[q.res])
                    tiles = []
                    nkb = 4 * g + 4
                    order = list(range(nkb - 1, -1, -1))
                    for idx, kbi in enumerate(order):
                        rel = kbi - 4 * g
                        q0 = 128 * rel if rel > 0 else 0
                        n = 512 - q0
                        t = tctr[0]
                        tctr[0] += 1
                        zp, e_, s_, a_ = ps[t % 6], ef[t % 3], spb[t % 3], Pb[t % 4]
                        rb_r, rb_w = Rb[idx % 2], Rb[(idx + 1) % 2]
                        first, last = idx == 0, idx == len(order) - 1

                        def s1(kbi=kbi, q0=q0, n=n, zp=zp, q=q, kt=kt):
                            op(PE, mm(zp[:, q0:q0 + n], kt[0:64, kbi * 128:(kbi + 1) * 128], q[0:64, q0:q0 + n], True, False), r=[kt.res, q.res], w=[zp.res])

                        def s2a(q0=q0, n=n, zp=zp, e_=e_):
                            op(ACT, act(e_[:, q0:q0 + n], zp[:, q0:q0 + n], AF.Exp), r=[zp.res], w=[e_.res])

                        def s2(q0=q0, n=n, zp=zp, e_=e_, s_=s_, rel=rel):
                            op(ACT, act(s_[:, q0:q0 + n], e_[:, q0:q0 + n], AF.Ln, bias=1.0), r=[e_.res], w=[s_.res])
                            if rel >= 0:
                                op(DVE, tt(s_[:, q0:q0 + 128], s_[:, q0:q0 + 128], cmat[:, C_DFC:C_DFC + 128], ALU.mult), r=[s_.res, cmat.res], w=[s_.res])

                        def s3(q0=q0, n=n, zp=zp, s_=s_, rb_r=rb_r, rb_w=rb_w, first=first, last=last):
                            op(PE, mm(zp[:, q0:q0 + n], cmat[:, C_NEGU:C_NEGU + 128], s_[:, q0:q0 + n], False, first), r=[cmat.res, s_.res], w=[zp.res])
                            if not first:
                                op(PE, mm(zp[:, q0:q0 + n], cmat[:, C_NEGONES:C_NEGONES + 128], rb_r[:, q0:q0 + n], False, True), r=[cmat.res, rb_r.res], w=[zp.res])
                            if not last:
                                if first:
                                    op(POOL, lambda e: e.memset(R32[:, :], 0.0), w=[R32.res])
                                op(POOL, tt(R32[:, q0:q0 + n], R32[:, q0:q0 + n], s_[:, q0:q0 + n], ALU.add), r=[R32.res, s_.res], w=[R32.res])
                                op(DVE, cp(rb_w[:, :], R32[:, :]), r=[R32.res], w=[rb_w.res])

                        def s4(q0=q0, n=n, zp=zp, a_=a_, rel=rel):
                            op(ACT, act(a_[:, q0:q0 + n], zp[:, q0:q0 + n], AF.Exp), r=[zp.res], w=[a_.res])
                            if rel >= 0:
                                op(DVE, tt(a_[:, q0:q0 + 128], a_[:, q0:q0 + 128], cmat[:, C_DFC:C_DFC + 128], ALU.mult), r=[a_.res, cmat.res], w=[a_.res])

                        def s5(kbi=kbi, q0=q0, n=n, a_=a_, v=v, first=first, last=last):
                            op(PE, mm(O[0:64, q0:q0 + n], v[:, kbi, 0:64], a_[:, q0:q0 + n], first, last), r=[v.res, a_.res], w=[O.res])
                        tiles.append([s1, s2a, s2, s3, s4, s5])
                    pipeline(tiles, [0, 1, 2, 3, 4, 5])
                    o = ob[octr[0] % 2]
                    octr[0] += 1
                    op(DVE, cp(o[0:64, :], O[0:64, :]), r=[O.res], w=[o.res])
                    dma(POOL, dm(mixT[64 * h:64 * h + 64, g * 512:(g + 1) * 512], o[0:64, :]), r=[o.res])
            O1, L1, O2, L2 = ps[4], ps[5], ps[6], ps[7]
            for h in range(4):
                k1, k2, v = KT[0], KT[1], V[h % 2]
                dma(SP, dm(k1[0:70, :], kd[h, 0]), w=[k1.res])
                dma(SP, dm(k2[0:70, :], kd[h, 1]), w=[k2.res])
                dma(SP, dm(v[:, :, :], vd[h]), w=[v.res])
                for g in range(NG):
                    q1 = Q[qctr[0] % 4]
                    q2 = Q[(qctr[0] + 1) % 4]
                    qctr[0] += 2
                    dma(SP, dm(q1[0:70, :], qd[h, 0, :, g * 512:(g + 1) * 512]), w=[q1.res])
                    dma(SP, dm(q2[0:70, :], qd[h, 1, :, g * 512:(g + 1) * 512]), w=[q2.res])
                    tiles = []
                    nkb = 4 * g + 4
                    for kbi in range(nkb):
                        rel = kbi - 4 * g
                        q0 = 128 * rel if rel > 0 else 0
                        n = 512 - q0
                        t = tctr[0]
                        tctr[0] += 1
                        sa, sb2 = ps[(2 * t) % 4], ps[(2 * t + 1) % 4]
                        Pa, Pb2 = Pb[(2 * t) % 6], Pb[(2 * t + 1) % 6]
                        first, last = kbi == 0, kbi == nkb - 1

                        def s1(kbi=kbi, q0=q0, n=n, sa=sa, sb2=sb2, q1=q1, q2=q2):
                            op(PE, mm(sa[:, 0:n], k1[0:70, kbi * 128:(kbi + 1) * 128], q1[0:70, q0:q0 + n], True, True), r=[k1.res, q1.res], w=[sa.res])
                            op(PE, mm(sb2[:, 0:n], k2[0:70, kbi * 128:(kbi + 1) * 128], q2[0:70, q0:q0 + n], True, True), r=[k2.res, q2.res], w=[sb2.res])

                        def s2(n=n, sa=sa, sb2=sb2, Pa=Pa, Pb2=Pb2, rel=rel, h=h):
                            for s_, p_ in ((sa, Pa), (sb2, Pb2)):
                                op(ACT, act(p_[:, 0:n], s_[:, 0:n], AF.Exp, scale=0.125), r=[s_.res], w=[p_.res])
                                if rel >= 0:
                                    op(DVE, tt(p_[:, 0:128], p_[:, 0:128], cmat[:, C_DFD + 128 * h:C_DFD + 128 * h + 128], ALU.mult), r=[p_.res, cmat.res], w=[p_.res])

                        def s3(kbi=kbi, q0=q0, n=n, Pa=Pa, Pb2=Pb2, v=v, first=first, last=last):
                            op(PE, mm(O1[:, q0:q0 + n], v[:, kbi, :], Pa[:, 0:n], first, last), r=[v.res, Pa.res], w=[O1.res])
                            op(PE, mm(L1[:, q0:q0 + n], ONES(), Pa[:, 0:n], first, last), r=[cmat.res, Pa.res], w=[L1.res])
                            op(PE, mm(O2[:, q0:q0 + n], v[:, kbi, :], Pb2[:, 0:n], first, last), r=[v.res, Pb2.res], w=[O2.res])
                            op(PE, mm(L2[:, q0:q0 + n], ONES(), Pb2[:, 0:n], first, last), r=[cmat.res, Pb2.res], w=[L2.res])
                        tiles.append([s1, s2, s3])
                    pipeline(tiles, [0, 1, 2])
                    op(DVE, lambda e: e.reciprocal(out=f1[:, :], in_=L1[:, :]), r=[L1.res], w=[f1.res])
                    op(DVE, tt(f1[:, :], O1[:, :], f1[:, :], ALU.mult), r=[O1.res, f1.res], w=[f1.res])
                    op(DVE, lambda e: e.reciprocal(out=f2[:, :], in_=L2[:, :]), r=[L2.res], w=[f2.res])
                    op(DVE, tt(f2[:, :], O2[:, :], f2[:, :], ALU.mult), r=[O2.res, f2.res], w=[f2.res])
                    op(DVE, stt(f3[:, :], f2[:, :], dcol[:, j:j + 1], f1[:, :], ALU.mult, ALU.add), r=[f2.res, dcol.res, f1.res], w=[f3.res])
                    op(POOL, tt(fsq[:, :], f3[:, :], f3[:, :], ALU.mult), r=[f3.res], w=[fsq.res])
                    pn = ps[(2 * tctr[0]) % 4]
                    op(PE, mm(pn[:, :], ONES(), fsq[:, :], True, True), r=[cmat.res, fsq.res], w=[pn.res])
                    rstd_ops(f1, pn, 128, 128.0)
                    o = ob[octr[0] % 2]
                    octr[0] += 1
                    op(DVE, stt(o[:, :], f3[:, :], dcol[:, 2 + j:3 + j], f1[:, :], ALU.mult, ALU.mult), r=[f3.res, dcol.res, f1.res], w=[o.res])
                    dma(POOL, dm(mixT[512 + 128 * h:512 + 128 * h + 128, g * 512:(g + 1) * 512], o[:, :]), r=[o.res])
        sch.barrier()

    stop = int(os.environ.get("KSTOP", "99"))
    cnt = 0
    for l in range(DEPTH + 1):
        if cnt >= stop:
            break
        proj_phase(l)
        cnt += 1
        if l < DEPTH:
            if cnt >= stop:
                break
            if l % 2 == 0:
                attn_even(l // 2)
            else:
                attn_odd(l // 2)
            cnt += 1
    sch.barrier()

    with nc.Block() as block:
        @block.tensor
        def _(e):
            sch.emit(PE, e)

        @block.scalar
        def _(e):
            sch.emit(ACT, e)

        @block.vector
        def _(e):
            sch.emit(DVE, e)

        @block.gpsimd
        def _(e):
            sch.emit(POOL, e)

        @block.sync
        def _(e):
            sch.emit(SP, e)
    stack.close()
    return nc


def host_consts(S):
    bf = ml_dtypes.bfloat16
    cm = np.zeros((128, NCM), np.float32)
    cm[:, C_ONES:C_ONES + 128] = 1.0
    for m in range(128):
        if m % 32 < 16:
            cm[m + 16, C_PERM + m] = -1.0
        else:
            cm[m - 16, C_PERM + m] = 1.0
    jj, kk = np.meshgrid(np.arange(128), np.arange(128), indexing="ij")
    cm[:, C_NEGU:C_NEGU + 128] = -(jj >= kk).astype(np.float32)
    cm[:, C_NEGONES:C_NEGONES + 128] = -1.0
    k_, q_ = jj, kk
    cm[:, C_DFB:C_DFB + 128] = ((k_ // 64) <= (q_ // 64)).astype(np.float32)
    cm[:, C_DFC:C_DFC + 128] = (k_ < q_).astype(np.float32)
    for h in range(4):
        m = 2.0 ** (-8.0 * (h + 1) / 4)
        d = np.where(k_ <= q_, 1.0, np.where((k_ // 64) == (q_ // 64), np.exp(-2.0 * m * (k_ - q_)), 0.0))
        cm[:, C_DFD + 128 * h:C_DFD + 128 * h + 128] = d
    p = np.arange(128)
    for c in range(4):
        cm[p, C_SEL2 + 8 * c + 2 * c + p // 64] = 1.0
    for r in range(2):
        cm[p, C_SEL4 + 8 * r + 4 * r + p // 32] = 1.0
    cf = np.zeros((128, NCF), np.float32)
    for c in range(4):
        cf[2 * c + p // 64, F_SEL2T + 128 * c + p] = 1.0
    for r in range(2):
        cf[4 * r + p // 32, F_SEL4T + 128 * r + p] = 1.0
    for h in range(8):
        cf[h, F_KSELT + 32 * h:F_KSELT + 32 * h + 32] = 1.0
    for h in range(8):
        m = 2.0 ** (-8.0 * (h + 1) / 8)
        kpos = np.arange(128)[:, None] - 128
        qpos = np.arange(128)[None, :]
        dch = qpos // 64 - np.floor_divide(kpos, 64)
        ok = (dch >= 0) & (dch <= 2)
        cf[:, F_ABIAS + 384 * h:F_ABIAS + 384 * h + 128] = np.where(ok, -m * np.abs(qpos - kpos), -30000.0)
        kpos = np.arange(128)[:, None]
        qpos = np.arange(256)[None, :]
        dch = qpos // 64 - kpos // 64
        ok = (dch >= 0) & (dch <= 2)
        cf[:, F_ABIAS + 384 * h + 128:F_ABIAS + 384 * h + 384] = np.where(ok, -m * np.abs(qpos - kpos), -30000.0)
    half = 16
    inv = (np.float32(10000.0) ** (-np.arange(half, dtype=np.float32) / np.float32(half))).astype(np.float32)
    cf[:, F_INVF] = inv[p % 16]
    for m in range(64):
        cf[64 + m, F_SHIFT + m] = 1.0
    pos = np.arange(S)
    a, b, c = pos // 1024, (pos % 1024) // 32, pos % 32
    daug = np.zeros((4, 2, 6, S), np.float32)
    for h in range(4):
        m = 2.0 ** (-8.0 * (h + 1) / 4) * 8.0
        daug[h, 0, 0], daug[h, 0, 1], daug[h, 0, 2] = -m * 1024 * a, -m * 32 * b, -m * c
        daug[h, 0, 3:6] = 1.0
        daug[h, 1, 0:3] = 1.0
        daug[h, 1, 3], daug[h, 1, 4], daug[h, 1, 5] = m * 1024 * a, m * 32 * b, m * c
    return cm.astype(bf), cf, daug.astype(bf)


def host_pcols(inp):
    pc = np.zeros((128, NPC), np.float32)
    p = np.arange(128)
    for l in range(4):
        pc[:, l * 16:l * 16 + 8] = inp["norm_mix_g"][l].reshape(8, 128).T
        pc[:, l * 16 + 8:l * 16 + 16] = inp["norm_ffn_g"][l].reshape(8, 128).T
    for j in range(2):
        b = 64 + 32 * j
        pc[:, b + 0] = inp["a_q_norm"][j][p % 64]
        pc[:, b + 1] = inp["a_k_norm"][j][p % 64]
        pc[:, b + 2:b + 4] = inp["b_cq_norm"][j].reshape(2, 128).T
        pc[:, b + 4] = inp["b_ckv_norm"][j]
        pc[:, b + 5] = inp["b_q_norm"][j][p % 64]
        pc[:, b + 6] = inp["b_q_norm"][j][64 + p % 32]
        pc[:, b + 7] = inp["b_k_norm"][j][p % 64]
        pc[:, b + 8] = inp["b_k_norm"][j][64 + p % 32]
        pc[:, b + 9:b + 17] = inp["a_sinks"][j][None, :]
        b = 128 + 32 * j
        pc[:, b + 0] = inp["d_q_norm"][j].reshape(128)
        pc[:, b + 1] = inp["d_k_norm"][j].reshape(128)
        pc[:, b + 2] = inp["d_subln"][j]
    dlam = np.broadcast_to(inp["d_lambda"].reshape(1, 2, 256), (128, 2, 256)).astype(np.float32)
    return pc, np.ascontiguousarray(dlam)


_CACHE = {}


def kernel(**inputs):
    inp = {k: np.asarray(v) for k, v in inputs.items()}
    x = inp["x"]
    B, S, D = x.shape
    if S not in _CACHE:
        _CACHE[S] = build(S)
    nc = _CACHE[S]
    cm, cf, daug = host_consts(S)
    pc, dlam = host_pcols(inp)
    shared = {
        "pcols": pc, "dlam": dlam, "cmat": cm, "cf32": cf, "daug": daug,
        "mlp_w_up": inp["mlp_w_up"], "mlp_w_down": inp["mlp_w_down"],
        "ev_w_in": inp["ev_w_in"], "ev_w_out": inp["ev_w_out"],
        "b_w_uq": inp["b_w_uq"], "b_w_ukv": inp["b_w_ukv"],
        "od_w_in": inp["od_w_in"], "od_w_out": inp["od_w_out"],
    }
    in_maps = []
    for b in range(B):
        m = dict(shared)
        m["xT"] = np.ascontiguousarray(x[b].T)
        m["posb"] = np.ascontiguousarray(np.broadcast_to(inp["positions"][b][None, :], (128, S))).astype(np.int32)
        in_maps.append(m)
    res = run_bass_kernel_spmd(nc, in_maps, core_ids=list(range(B)))
    out = np.stack([np.ascontiguousarray(r["yT"].T) for r in res.results], axis=0)
    return out.astype(np.float32)
```

```python
import math
import os
from contextlib import ExitStack

import numpy as np
import ml_dtypes
import concourse.bass as bass
import concourse.mybir as mybir
from concourse.bass_utils import run_bass_kernel_spmd

F32, BF16, I32 = mybir.dt.float32, mybir.dt.bfloat16, mybir.dt.int32
AF = mybir.ActivationFunctionType
ALU = mybir.AluOpType
AX = mybir.AxisListType
PE, ACT, DVE, POOL, SP = 0, 1, 2, 3, 4
EPS = 1e-6
DEPTH = 4
TWO_PI = 2.0 * math.pi

C_ONES, C_PERM, C_NEGU, C_NEGONES, C_DFB, C_DFC, C_DFD, C_SEL2, C_SEL4 = 0, 128, 256, 384, 512, 640, 768, 1280, 1312
NCM = 1328
F_SEL2T, F_SEL4T, F_KSELT, F_ABIAS, F_INVF, F_SHIFT, F_ONES = 0, 512, 768, 1024, 4096, 4097, 4161
NCF = 4289
NPC = 192


class Res:
    __slots__ = ("lw", "rd", "sem", "cnt")

    def __init__(self):
        self.lw = None
        self.rd = []
        self.sem = None
        self.cnt = 0


class Sched:
    def __init__(self, nc, stack):
        self.nc = nc
        self.stack = stack
        self.q = [[] for _ in range(5)]
        self.seq = [0] * 5
        self.esem = [stack.enter_context(nc.semaphore(f"es{i}")) for i in range(5)]
        self.waited = [dict() for _ in range(5)]
        self.dsems = []
        self.free_dsems = []
        self.semcnt = {}
        self.nds = 0

    def _waits(self, e, deps):
        best = {}
        for (sem, val, src) in deps:
            if e == PE and src == PE:
                continue
            k = id(sem)
            if self.waited[e].get(k, 0) >= val:
                continue
            if k not in best or best[k][1] < val:
                best[k] = (sem, val)
        out = []
        for k, (sem, val) in best.items():
            self.waited[e][k] = val
            out.append((sem, val))
        return out

    def _deps(self, r, w, e=-9):
        deps = []
        for x in r:
            if x.lw is not None:
                deps.append(x.lw)
        for x in w:
            if x.lw is not None and x.lw[2] != e:
                deps.append(x.lw)
            for t in x.rd:
                if t[2] != e:
                    deps.append(t)
        return deps

    def op(self, e, fn, r=(), w=()):
        waits = self._waits(e, self._deps(r, w, e))
        self.seq[e] += 1
        tok = (self.esem[e], self.seq[e], e)
        self.q[e].append((waits, fn, (self.esem[e], 1), True))
        for x in r:
            x.rd.append(tok)
        for x in w:
            x.lw = tok
            x.rd = []

    def dma(self, qe, fn, r=(), w=(), semres=None):
        waits = self._waits(qe, self._deps(r, w))
        sr = semres if semres is not None else (w[0] if w else r[0])
        if sr.sem is None:
            if self.free_dsems:
                sr.sem = self.free_dsems.pop()
            else:
                self.nds += 1
                sr.sem = self.stack.enter_context(self.nc.semaphore(f"ds{self.nds}"))
            sr.cnt = self.semcnt.get(id(sr.sem), 0)
            self.dsems.append(sr)
        sr.cnt += 16
        self.semcnt[id(sr.sem)] = sr.cnt
        tok = (sr.sem, sr.cnt, -1)
        self.q[qe].append((waits, fn, (sr.sem, 16), False))
        for x in r:
            x.rd.append(tok)
        for x in w:
            x.lw = tok
            x.rd = []

    def barrier(self):
        toks = [(self.esem[i], self.seq[i], i) for i in range(5) if self.seq[i] > 0]
        toks += [(sr.sem, sr.cnt, -1) for sr in self.dsems]
        for e in range(5):
            deps = [t for t in toks if t[2] != e]
            waits = self._waits(e, [(s, v, -2) for (s, v, _) in deps])
            if waits:
                self.q[e].append((waits, None, None, False))
        for sr in self.dsems:
            self.free_dsems.append(sr.sem)
            sr.sem = None
        self.dsems = []

    def emit(self, e, eng):
        for (waits, fn, inc, attach) in self.q[e]:
            if fn is None:
                for (sem, val) in waits:
                    eng.wait_ge(sem, val)
                continue
            if attach and waits:
                for (sem, val) in waits[:-1]:
                    eng.wait_ge(sem, val)
                ins = fn(eng)
                ins._wait_ge(*waits[-1])
            else:
                for (sem, val) in waits:
                    eng.wait_ge(sem, val)
                ins = fn(eng)
            ins.then_inc(inc[0], inc[1])


def pipeline(tiles, skews):
    T = len(tiles)
    mx = max(skews)
    for i in range(T + mx):
        for s, sk in enumerate(skews):
            t = i - sk
            if 0 <= t < T and tiles[t][s] is not None:
                tiles[t][s]()


class T:
    __slots__ = ("h", "res")

    def __init__(self, h):
        self.h = h
        self.res = Res()

    def __getitem__(self, k):
        return self.h[k]


class PsView:
    __slots__ = ("h", "off", "w", "res")

    def __init__(self, h, off, w):
        self.h, self.off, self.w = h, off, w
        self.res = Res()

    def __getitem__(self, k):
        rs, cs = k
        a = 0 if cs.start is None else cs.start
        b = self.w if cs.stop is None else cs.stop
        return self.h[rs, self.off + a:self.off + b]


def build(S):
    NG = S // 512
    NB = S // 128
    nc = bass.Bass("TRN2", target_bir_lowering=False)

    def din(name, shape, dt=F32):
        return nc.dram_tensor(name, list(shape), dt, kind="ExternalInput").ap()

    def dscr(name, shape, dt=BF16):
        return nc.dram_tensor(name, list(shape), dt).ap()

    xT = din("xT", [1024, S])
    posb = din("posb", [128, S], I32)
    pcols_d = din("pcols", [128, NPC])
    dlam_d = din("dlam", [128, 2, 256])
    cmat_d = din("cmat", [128, NCM], BF16)
    cf32_d = din("cf32", [128, NCF])
    daug_d = din("daug", [4, 2, 6, S], BF16)
    w_up_d = din("mlp_w_up", [4, 1024, 4096])
    w_down_d = din("mlp_w_down", [4, 4096, 1024])
    ev_in_d = din("ev_w_in", [2, 1024, 1184])
    ev_out_d = din("ev_w_out", [2, 1024, 1024])
    uq_d = din("b_w_uq", [2, 256, 768])
    ukv_d = din("b_w_ukv", [2, 128, 1024])
    od_in_d = din("od_w_in", [2, 1024, 3072])
    od_out_d = din("od_w_out", [2, 1024, 1024])
    yT = nc.dram_tensor("yT", [1024, S], F32, kind="ExternalOutput").ap()

    w_up = dscr("s_w_up", [4, 1024, 4096])
    w_down = dscr("s_w_down", [4, 4096, 1024])
    ev_in = dscr("s_ev_in", [2, 1024, 1184])
    ev_out = dscr("s_ev_out", [2, 1024, 1024])
    uq = dscr("s_uq", [2, 256, 768])
    ukv = dscr("s_ukv", [2, 128, 1024])
    od_in = dscr("s_od_in", [2, 1024, 3072])
    od_out = dscr("s_od_out", [2, 1024, 1024])
    xres = dscr("s_xres", [1024, S], F32)
    mixT = dscr("s_mix", [1024, S])
    costab = dscr("s_cos", [128, S], F32)
    sintab = dscr("s_sin", [128, S], F32)
    qa = dscr("s_qa", [8, 64, S])
    ka = dscr("s_ka", [2, 64, S])
    va = dscr("s_va", [2, 128, NB, 64])
    qb = dscr("s_qb", [8, 96, S])
    kb_ = dscr("s_kb", [8, 96, S])
    vb = dscr("s_vb", [8, 128, NB, 64])
    qc = dscr("s_qc", [8, 64, S])
    kc = dscr("s_kc", [8, 64, S])
    vc = dscr("s_vc", [8, 128, NB, 64])
    qd = dscr("s_qd", [4, 2, 70, S])
    kd = dscr("s_kd", [4, 2, 70, S])
    vd = dscr("s_vd", [4, 128, NB, 128])

    stack = ExitStack()
    sch = Sched(nc, stack)
    op, dma = sch.op, sch.dma

    nctr = [0]

    def sb(st, name, shape, dt):
        nctr[0] += 1
        return T(st.enter_context(nc.sbuf_tensor(f"t{nctr[0]}_{name}", list(shape), dt)))

    psall = stack.enter_context(nc.psum_tensor("psall", [128, 4096], F32))
    ps = [PsView(psall, 512 * i, 512) for i in range(8)]
    cmat = sb(stack, "cmat", [128, NCM], BF16)
    cf32 = sb(stack, "cf32", [128, NCF], F32)
    pcols = sb(stack, "pcols", [128, NPC], F32)
    dcol = sb(stack, "dcol", [128, 32], F32)
    epsc = sb(stack, "epsc", [128, 1], F32)
    onec = sb(stack, "onec", [128, 1], F32)

    ONES = lambda k=128, m=128: cmat[0:k, C_ONES:C_ONES + m]

    def act(out, in_, func, scale=1.0, bias=None):
        if bias is None:
            return lambda e: e.activation(out=out, in_=in_, func=func, scale=scale)
        return lambda e: e.activation(out=out, in_=in_, func=func, scale=scale, bias=bias)

    def mm(out, lhsT, rhs, start, stop):
        return lambda e: e.matmul(out, lhsT=lhsT, rhs=rhs, start=start, stop=stop)

    def tt(out, in0, in1, o):
        return lambda e: e.tensor_tensor(out=out, in0=in0, in1=in1, op=o)

    def ts(out, in0, s1, s2, o0, o1=None):
        if o1 is None:
            return lambda e: e.tensor_scalar(out=out, in0=in0, scalar1=s1, scalar2=None, op0=o0)
        return lambda e: e.tensor_scalar(out=out, in0=in0, scalar1=s1, scalar2=s2, op0=o0, op1=o1)

    def stt(out, in0, s, in1, o0, o1):
        return lambda e: e.scalar_tensor_tensor(out=out, in0=in0, scalar=s, in1=in1, op0=o0, op1=o1)

    def cp(out, in_):
        return lambda e: e.tensor_copy(out=out, in_=in_)

    def dm(out, in_):
        return lambda e: e.dma_start(out=out, in_=in_)

    def rstd_ops(dst, src_ps, n, d):
        op(ACT, act(dst[0:n, :], src_ps[0:n, :], AF.Ln, scale=1.0 / d, bias=epsc[0:n, :]), r=[src_ps.res, epsc.res], w=[dst.res])
        op(ACT, act(dst[0:n, :], dst[0:n, :], AF.Exp, scale=-0.5), r=[dst.res], w=[dst.res])

    with ExitStack() as st:
        dma(SP, dm(cmat[:, :], cmat_d), w=[cmat.res])
        dma(SP, dm(cf32[:, :], cf32_d), w=[cf32.res])
        dma(SP, dm(pcols[:, :], pcols_d), w=[pcols.res])
        op(DVE, lambda e: e.memset(epsc[:, :], EPS), w=[epsc.res])
        op(DVE, lambda e: e.memset(onec[:, :], 1.0), w=[onec.res])
        castres = Res()

        def cast2d(dst, src, rows, step=128):
            for r0 in range(0, rows, step):
                r1 = min(rows, r0 + step)
                dma(POOL, dm(dst[r0:r1, :], src[r0:r1, :]), r=[castres], semres=castres)

        for l in range(4):
            cast2d(w_up[l], w_up_d[l], 1024)
            cast2d(w_down[l], w_down_d[l], 4096, 256)
        for j in range(2):
            cast2d(ev_in[j], ev_in_d[j], 1024)
            cast2d(ev_out[j], ev_out_d[j], 1024)
            cast2d(od_in[j], od_in_d[j], 1024)
            cast2d(od_out[j], od_out_d[j], 1024)
            cast2d(uq[j], uq_d[j], 256)
            cast2d(ukv[j], ukv_d[j], 128)
        for h in range(4):
            for m in range(2):
                dma(SP, dm(qd[h, m, 64:70, :], daug_d[h, 0]), r=[castres], semres=castres)
                dma(SP, dm(kd[h, m, 64:70, :], daug_d[h, 1]), r=[castres], semres=castres)
        dl = sb(st, "dl", [128, 2, 256], F32)
        dlp = sb(st, "dlp", [128, 64], F32)
        dma(SP, dm(dl[:, :, :], dlam_d), w=[dl.res])
        for j in range(2):
            layer = 2 * j + 1
            lam_init = 0.8 - 0.6 * math.exp(-0.3 * layer)
            for t in range(2):
                op(DVE, tt(dlp[:, :], dl[:, j, 128 * t:128 * t + 64], dl[:, j, 128 * t + 64:128 * t + 128], ALU.mult), r=[dl.res], w=[dlp.res])
                op(DVE, lambda e, t=t, j=j: e.reduce_sum(out=dcol[:, 20 + t:21 + t], in_=dlp[:, :], axis=AX.X), r=[dlp.res], w=[dcol.res])
                op(ACT, act(dcol[:, 20 + t:21 + t], dcol[:, 20 + t:21 + t], AF.Exp), r=[dcol.res], w=[dcol.res])
            op(DVE, tt(dcol[:, 22:23], dcol[:, 21:22], dcol[:, 20:21], ALU.subtract), r=[dcol.res], w=[dcol.res])
            op(DVE, ts(dcol[:, j:j + 1], dcol[:, 22:23], -lam_init, None, ALU.add), r=[dcol.res], w=[dcol.res])
            op(DVE, ts(dcol[:, 2 + j:3 + j], pcols[:, 128 + 32 * j + 2:128 + 32 * j + 3], 1.0 - lam_init, None, ALU.mult), r=[pcols.res, dcol.res], w=[dcol.res])
        for j in range(2):
            op(ACT, act(dcol[:, 4 + 8 * j:12 + 8 * j], pcols[:, 64 + 32 * j + 9:64 + 32 * j + 17], AF.Exp), r=[pcols.res, dcol.res], w=[dcol.res])
        CH = min(S, 2048)
        posi = sb(st, "posi", [128, CH], I32)
        ang = sb(st, "ang", [128, CH], F32)
        tq = sb(st, "tq", [128, CH], F32)
        ki = sb(st, "ki", [128, CH], I32)
        kf = sb(st, "kf", [128, CH], F32)
        rr = sb(st, "rr", [128, CH], F32)
        mk = sb(st, "mk", [128, CH], F32)
        C1 = 6.28125
        C2 = TWO_PI - C1
        for c0 in range(0, S, CH):
            dma(SP, dm(posi[:, :], posb[:, c0:c0 + CH]), w=[posi.res])
            op(DVE, cp(ang[:, :], posi[:, :]), r=[posi.res], w=[ang.res])
            op(DVE, ts(ang[:, :], ang[:, :], cf32[:, F_INVF:F_INVF + 1], None, ALU.mult), r=[ang.res, cf32.res], w=[ang.res])
            for which, tab in ((0, sintab), (1, costab)):
                src = ang
                if which == 1:
                    op(DVE, ts(rr[:, :], ang[:, :], math.pi / 2, None, ALU.add), r=[ang.res], w=[rr.res])
                    src = rr
                op(DVE, ts(tq[:, :], src[:, :], 1.0 / TWO_PI, None, ALU.mult), r=[src.res], w=[tq.res])
                op(DVE, cp(ki[:, :], tq[:, :]), r=[tq.res], w=[ki.res])
                op(DVE, cp(kf[:, :], ki[:, :]), r=[ki.res], w=[kf.res])
                op(DVE, stt(rr[:, :], kf[:, :], -C1, src[:, :], ALU.mult, ALU.add), r=[kf.res, src.res], w=[rr.res])
                op(DVE, stt(rr[:, :], kf[:, :], -C2, rr[:, :], ALU.mult, ALU.add), r=[kf.res, rr.res], w=[rr.res])
                op(DVE, ts(mk[:, :], rr[:, :], math.pi, -TWO_PI, ALU.is_gt, ALU.mult), r=[rr.res], w=[mk.res])
                op(DVE, tt(rr[:, :], rr[:, :], mk[:, :], ALU.add), r=[rr.res, mk.res], w=[rr.res])
                op(DVE, ts(mk[:, :], rr[:, :], -math.pi, TWO_PI, ALU.is_lt, ALU.mult), r=[rr.res], w=[mk.res])
                op(DVE, tt(rr[:, :], rr[:, :], mk[:, :], ALU.add), r=[rr.res, mk.res], w=[rr.res])
                op(DVE, ts(rr[:, :], rr[:, :], math.pi, -math.pi, ALU.min, ALU.max), r=[rr.res], w=[rr.res])
                op(ACT, act(tq[:, :], rr[:, :], AF.Sin), r=[rr.res], w=[tq.res])
                dma(SP, dm(tab[:, c0:c0 + CH], tq[:, :]), r=[tq.res])
        sch.barrier()

    def wview(w2d, c0, ncol, nchunk=8):
        return w2d.rearrange("(i p) c -> p i c", p=128)[:, 0:nchunk, c0:c0 + ncol]

    class Ctx:
        pass

    def proj_phase(l):
        with ExitStack() as st:
            xs = sb(st, "xs", [128, 8, 512], F32)
            xn = sb(st, "xn", [128, 8, 512], BF16)
            sq = sb(st, "sq", [128, 8, 512], BF16)
            sqr = [Res() for _ in range(8)]
            xsr = [Res() for _ in range(8)]
            xnr = [Res() for _ in range(8)]
            rstd = sb(st, "rstd", [128, 512], F32)
            wts = [sb(st, f"wt{i}", [128, 8, 512], BF16) for i in range(4)]
            wctr = [0]
            if l > 0:
                H = sb(st, "H", [128, 32, 512], BF16)
                mx = sb(st, "mx", [128, 8, 512], BF16)
                rl = [sb(st, f"rl{i}", [128, 512], BF16) for i in range(2)]
            if l < DEPTH:
                raws = [sb(st, f"raw{i}", [128, 512], F32) for i in range(8)]
                sqs = [sb(st, f"sqs{i}", [128, 512], BF16) for i in range(8)]
                outs = [sb(st, f"ob{i}", [128, 512], BF16) for i in range(4)]
                rc = sb(st, "rc", [8, 512], F32)
                vt = sb(st, "vt", [128, 4, 512], BF16)
                octr = [0]
                if l % 2 == 0:
                    cqn = sb(st, "cqn", [128, 2, 512], BF16)
                    ckvn = sb(st, "ckvn", [128, 512], BF16)
                    cost = sb(st, "cost", [128, 512], F32)
                    sint = sb(st, "sint", [128, 512], F32)
                    t1 = sb(st, "t1", [128, 512], F32)
                    t2 = sb(st, "t2", [128, 512], F32)
                    krr = sb(st, "krr", [32, 512], F32)
                    wsm = sb(st, "wsm", [128, 2, 768], BF16)
                    wkv = sb(st, "wkv", [128, 1024], BF16)
            pctr = [0]
            STQ = ACT

            def nps():
                pctr[0] += 1
                return ps[pctr[0] % 4]

            def load_w(view, nchunk=8, ncol=512):
                wt = wts[wctr[0] % 4]
                wctr[0] += 1
                dma(SP, dm(wt[:, 0:nchunk, 0:ncol], view), w=[wt.res])
                return wt

            def square(c):
                op(POOL if c % 2 == 0 else DVE, tt(sq[:, c, :], xs[:, c, :], xs[:, c, :], ALU.mult), r=[xsr[c]], w=[sqr[c]])

            def norm(gbase, do_sq=True):
                if do_sq:
                    for c in range(8):
                        square(c)
                p = nps()
                for c in range(8):
                    op(PE, mm(p[:, :], ONES(), sq[:, c, :], c == 0, c == 7), r=[sqr[c], cmat.res], w=[p.res])
                rstd_ops(rstd, p, 128, 1024.0)
                for c in range(8):
                    op(DVE, stt(xn[:, c, :], xs[:, c, :], pcols[:, gbase + c:gbase + c + 1], rstd[:, :], ALU.mult, ALU.mult),
                       r=[xsr[c], pcols.res, rstd.res], w=[xnr[c]])

            def outbuf():
                o = outs[octr[0] % 4]
                octr[0] += 1
                return o

            def proj_chunk(wt, col0, ncol, src, nchunk=8, srcidx=None):
                p = nps()
                for i in range(nchunk):
                    rhs = src[:, i, :] if srcidx is None else srcidx(i)
                    op(PE, mm(p[0:ncol, :], wt[:, i, col0:col0 + ncol], rhs, i == 0, i == nchunk - 1), r=[wt.res, xnr[i] if src is xn else src.res], w=[p.res])
                return p

            def vproj(wt, col0, ncol, src, nchunk, dst_fn, srcidx=None):
                for tb in range(4):
                    p = nps()
                    for i in range(nchunk):
                        lhs = src[:, i, tb * 128:(tb + 1) * 128] if srcidx is None else srcidx(i, tb)
                        op(PE, mm(p[:, 0:ncol], lhs, wt[:, i, col0:col0 + ncol], i == 0, i == nchunk - 1), r=[wt.res, xnr[i] if src is xn else src.res], w=[p.res])
                    op(ACT, act(vt[:, tb, 0:ncol], p[:, 0:ncol], AF.Copy), r=[p.res], w=[vt.res])
                dst_fn()

            for g in range(NG):
                gs = slice(g * 512, (g + 1) * 512)
                src_x = xT if l == 0 else xres
                dma(SP, dm(xs[:, :, :], src_x.rearrange("(c p) s -> p c s", p=128)[:, :, gs]), w=xsr)
                if l > 0:
                    lp = l - 1
                    jp = lp // 2
                    dma(SP, dm(mx[:, :, :], mixT.rearrange("(c p) s -> p c s", p=128)[:, :, gs]), w=[mx.res])
                    wout = (ev_out if lp % 2 == 0 else od_out)[jp]
                    for half in range(2):
                        wt = load_w(wview(wout, half * 512, 512))
                        for oc4 in range(4):
                            oc = half * 4 + oc4
                            p = proj_chunk(wt, oc4 * 128, 128, mx)
                            op(DVE, tt(xs[:, oc, :], p[:, :], xs[:, oc, :], ALU.add), r=[p.res, xsr[oc]], w=[xsr[oc]])
                            square(oc)
                    norm(lp * 16 + 8, do_sq=False)
                    for fb in range(8):
                        wt = load_w(wview(w_up[lp], fb * 512, 512))
                        for f4 in range(4):
                            fc = fb * 4 + f4
                            p = proj_chunk(wt, f4 * 128, 128, xn)
                            r_ = rl[fc % 2]
                            op(DVE, ts(r_[:, :], p[:, :], 0.0, None, ALU.max), r=[p.res], w=[r_.res])
                            op(POOL, tt(H[:, fc, :], r_[:, :], r_[:, :], ALU.mult), r=[r_.res], w=[H.res])
                    for half in range(2):
                        accs = [ps[4 + i] for i in range(4)]
                        for fs in range(4):
                            view = w_down[lp].rearrange("(i p) c -> p i c", p=128)[:, fs * 8:(fs + 1) * 8, half * 512:(half + 1) * 512]
                            wt = load_w(view)
                            for oc4 in range(4):
                                for i in range(8):
                                    fc = fs * 8 + i
                                    op(PE, mm(accs[oc4][:, :], wt[:, i, oc4 * 128:(oc4 + 1) * 128], H[:, fc, :], fc == 0, fc == 31),
                                       r=[wt.res, H.res], w=[accs[oc4].res])
                        for oc4 in range(4):
                            oc = half * 4 + oc4
                            op(DVE, tt(xs[:, oc, :], accs[oc4][:, :], xs[:, oc, :], ALU.add), r=[accs[oc4].res, xsr[oc]], w=[xsr[oc]])
                            if l < DEPTH:
                                square(oc)
                if l == DEPTH:
                    dma(STQ, dm(yT.rearrange("(c p) s -> p c s", p=128)[:, :, gs], xs[:, :, :]), r=xsr)
                    continue
                dma(STQ, dm(xres.rearrange("(c p) s -> p c s", p=128)[:, :, gs], xs[:, :, :]), r=xsr)
                norm(l * 16, do_sq=(l == 0))
                j = l // 2
                if l % 2 == 1:
                    pb = 128 + 32 * j
                    win = od_in[j]
                    for part, dst, scale in ((0, qc, 0.125), (1, kc, 1.0)):
                        wt = load_w(wview(win, part * 512, 512))
                        for c in range(4):
                            p = proj_chunk(wt, c * 128, 128, xn)
                            o = outbuf()
                            op(ACT, act(o[:, :], p[:, :], AF.Copy, scale=scale), r=[p.res], w=[o.res])
                            for hh in range(2):
                                dma(STQ, dm(dst[2 * c + hh, :, gs], o[64 * hh:64 * hh + 64, :]), r=[o.res])
                    wt = load_w(wview(win, 1024, 512))

                    def st_cv():
                        for h in range(8):
                            dma(STQ, dm(vc[h, :, 4 * g:4 * g + 4, :], vt[:, :, 64 * h:64 * h + 64]), r=[vt.res])
                    vproj(wt, 0, 512, xn, 8, st_cv)
                    for part, dst, gcol in ((3, qd, pb + 0), (4, kd, pb + 1)):
                        wt = load_w(wview(win, part * 512, 512))
                        pc = nps()
                        for c in range(4):
                            p = proj_chunk(wt, c * 128, 128, xn)
                            op(ACT, act(raws[c][:, :], p[:, :], AF.Copy), r=[p.res], w=[raws[c].res])
                            op(POOL, tt(sqs[c][:, :], raws[c][:, :], raws[c][:, :], ALU.mult), r=[raws[c].res], w=[sqs[c].res])
                        for c in range(4):
                            op(PE, mm(pc[0:8, :], cmat[:, C_SEL2 + 8 * c:C_SEL2 + 8 * c + 8], sqs[c][:, :], c == 0, c == 3), r=[cmat.res, sqs[c].res], w=[pc.res])
                        rstd_ops(rc, pc, 8, 64.0)
                        for c in range(4):
                            pbc = nps()
                            op(PE, mm(pbc[:, :], cf32[0:8, F_SEL2T + 128 * c:F_SEL2T + 128 * c + 128], rc[0:8, :], True, True), r=[cf32.res, rc.res], w=[pbc.res])
                            o = outbuf()
                            op(DVE, stt(o[:, :], raws[c][:, :], pcols[:, gcol:gcol + 1], pbc[:, :], ALU.mult, ALU.mult), r=[raws[c].res, pcols.res, pbc.res], w=[o.res])
                            for m in range(2):
                                dma(STQ, dm(dst[c, m, 0:64, gs], o[64 * m:64 * m + 64, :]), r=[o.res])
                    wt = load_w(wview(win, 2560, 512))

                    def st_dv():
                        for h in range(4):
                            dma(STQ, dm(vd[h, :, 4 * g:4 * g + 4, :], vt[:, :, 128 * h:128 * h + 128]), r=[vt.res])
                    vproj(wt, 0, 512, xn, 8, st_dv)
                else:
                    pb = 64 + 32 * j
                    win = ev_in[j]
                    wt = load_w(wview(win, 0, 512))
                    pc = nps()
                    for c in range(4):
                        p = proj_chunk(wt, c * 128, 128, xn)
                        op(ACT, act(raws[c][:, :], p[:, :], AF.Copy), r=[p.res], w=[raws[c].res])
                        op(POOL, tt(sqs[c][:, :], raws[c][:, :], raws[c][:, :], ALU.mult), r=[raws[c].res], w=[sqs[c].res])
                    for c in range(4):
                        op(PE, mm(pc[0:8, :], cmat[:, C_SEL2 + 8 * c:C_SEL2 + 8 * c + 8], sqs[c][:, :], c == 0, c == 3), r=[cmat.res, sqs[c].res], w=[pc.res])
                    rstd_ops(rc, pc, 8, 64.0)
                    for c in range(4):
                        pbc = nps()
                        op(PE, mm(pbc[:, :], cf32[0:8, F_SEL2T + 128 * c:F_SEL2T + 128 * c + 128], rc[0:8, :], True, True), r=[cf32.res, rc.res], w=[pbc.res])
                        o = outbuf()
                        op(DVE, stt(o[:, :], raws[c][:, :], pcols[:, pb:pb + 1], pbc[:, :], ALU.mult, ALU.mult), r=[raws[c].res, pcols.res, pbc.res], w=[o.res])
                        for hh in range(2):
                            dma(STQ, dm(qa[2 * c + hh, :, gs], o[64 * hh:64 * hh + 64, :]), r=[o.res])
                    wt = load_w(wview(win, 512, 512))
                    p = proj_chunk(wt, 0, 128, xn)
                    op(ACT, act(raws[0][:, :], p[:, :], AF.Copy), r=[p.res], w=[raws[0].res])
                    op(POOL, tt(sqs[0][:, :], raws[0][:, :], raws[0][:, :], ALU.mult), r=[raws[0].res], w=[sqs[0].res])
                    pc = nps()
                    op(PE, mm(pc[0:8, :], cmat[:, C_SEL2:C_SEL2 + 8], sqs[0][:, :], True, True), r=[cmat.res, sqs[0].res], w=[pc.res])
                    rstd_ops(rc, pc, 8, 64.0)
                    pbc = nps()
                    op(PE, mm(pbc[:, :], cf32[0:8, F_SEL2T:F_SEL2T + 128], rc[0:8, :], True, True), r=[cf32.res, rc.res], w=[pbc.res])
                    o = outbuf()
                    op(DVE, stt(o[:, :], raws[0][:, :], pcols[:, pb + 1:pb + 2], pbc[:, :], ALU.mult, ALU.mult), r=[raws[0].res, pcols.res, pbc.res], w=[o.res])
                    for hh in range(2):
                        dma(STQ, dm(ka[hh, :, gs], o[64 * hh:64 * hh + 64, :]), r=[o.res])

                    def st_av():
                        for h in range(2):
                            dma(STQ, dm(va[h, :, 4 * g:4 * g + 4, :], vt[:, :, 64 * h:64 * h + 64]), r=[vt.res])
                    vproj(wt, 128, 128, xn, 8, st_av)
                    for c in range(2):
                        p = proj_chunk(wt, 256 + c * 128, 128, xn)
                        op(ACT, act(raws[c][:, :], p[:, :], AF.Copy), r=[p.res], w=[raws[c].res])
                        op(POOL, tt(sqs[c][:, :], raws[c][:, :], raws[c][:, :], ALU.mult), r=[raws[c].res], w=[sqs[c].res])
                    pc = nps()
                    for c in range(2):
                        op(PE, mm(pc[:, :], ONES(), sqs[c][:, :], c == 0, c == 1), r=[cmat.res, sqs[c].res], w=[pc.res])
                    rstd_ops(rstd, pc, 128, 256.0)
                    for c in range(2):
                        op(DVE, stt(cqn[:, c, :], raws[c][:, :], pcols[:, pb + 2 + c:pb + 3 + c], rstd[:, :], ALU.mult, ALU.mult), r=[raws[c].res, pcols.res, rstd.res], w=[cqn.res])
                    wt = load_w(wview(win, 1024, 160), 8, 160)
                    p = proj_chunk(wt, 0, 128, xn)
                    op(ACT, act(raws[0][:, :], p[:, :], AF.Copy), r=[p.res], w=[raws[0].res])
                    op(POOL, tt(sqs[0][:, :], raws[0][:, :], raws[0][:, :], ALU.mult), r=[raws[0].res], w=[sqs[0].res])
                    pc = nps()
                    op(PE, mm(pc[:, :], ONES(), sqs[0][:, :], True, True), r=[cmat.res, sqs[0].res], w=[pc.res])
                    rstd_ops(rstd, pc, 128, 128.0)
                    op(DVE, stt(ckvn[:, :], raws[0][:, :], pcols[:, pb + 4:pb + 5], rstd[:, :], ALU.mult, ALU.mult), r=[raws[0].res, pcols.res, rstd.res], w=[ckvn.res])
                    pkr = proj_chunk(wt, 128, 32, xn)
                    op(ACT, act(raws[6][0:32, :], pkr[0:32, :], AF.Copy), r=[pkr.res], w=[raws[6].res])
                    op(POOL, tt(sqs[6][0:32, :], raws[6][0:32, :], raws[6][0:32, :], ALU.mult), r=[raws[6].res], w=[sqs[6].res])
                    dma(SP, dm(cost[:, :], costab[:, gs]), w=[cost.res])
                    dma(SP, dm(sint[:, :], sintab[:, gs]), w=[sint.res])
                    op(DVE, ts(t1[0:32, :], raws[6][0:32, :], pcols[0:32, pb + 8:pb + 9], None, ALU.mult), r=[raws[6].res, pcols.res], w=[t1.res])
                    op(DVE, cp(sqs[7][0:32, :], t1[0:32, :]), r=[t1.res], w=[sqs[7].res])
                    prot = nps()
                    op(PE, mm(prot[0:32, :], cmat[0:32, C_PERM:C_PERM + 32], sqs[7][0:32, :], True, True), r=[cmat.res, sqs[7].res], w=[prot.res])
                    op(DVE, tt(t2[0:32, :], prot[0:32, :], sint[0:32, :], ALU.mult), r=[prot.res, sint.res], w=[t2.res])
                    op(DVE, tt(t1[0:32, :], t1[0:32, :], cost[0:32, :], ALU.mult), r=[t1.res, cost.res], w=[t1.res])
                    op(DVE, tt(krr[0:32, :], t1[0:32, :], t2[0:32, :], ALU.add), r=[t1.res, t2.res], w=[krr.res])
                    uqv = uq[j].rearrange("(i p) (h d) -> p i h d", p=128, d=96)
                    for i in range(2):
                        dma(SP, dm(wsm[:, i, 0:512].rearrange("p (h d) -> p h d", d=64), uqv[:, i, :, 0:64]), w=[wsm.res])
                        dma(SP, dm(wsm[:, i, 512:768].rearrange("p (h d) -> p h d", d=32), uqv[:, i, :, 64:96]), w=[wsm.res])
                    ukvv = ukv[j].rearrange("p (h d) -> p h d", d=128)
                    dma(SP, dm(wkv[:, 0:512].rearrange("p (h d) -> p h d", d=64), ukvv[:, :, 0:64]), w=[wkv.res])
                    dma(SP, dm(wkv[:, 512:1024].rearrange("p (h d) -> p h d", d=64), ukvv[:, :, 64:128]), w=[wkv.res])
                    for c in range(6):
                        p = nps()
                        for i in range(2):
                            op(PE, mm(p[:, :], wsm[:, i, c * 128:(c + 1) * 128], cqn[:, i, :], i == 0, i == 1), r=[wsm.res, cqn.res], w=[p.res])
                        op(ACT, act(raws[c][:, :], p[:, :], AF.Copy), r=[p.res], w=[raws[c].res])
                        op(POOL, tt(sqs[c][:, :], raws[c][:, :], raws[c][:, :], ALU.mult), r=[raws[c].res], w=[sqs[c].res])
                    pc = nps()
                    for c in range(6):
                        sel = cmat[:, C_SEL2 + 8 * c:C_SEL2 + 8 * c + 8] if c < 4 else cmat[:, C_SEL4 + 8 * (c - 4):C_SEL4 + 8 * (c - 4) + 8]
                        op(PE, mm(pc[0:8, :], sel, sqs[c][:, :], c == 0, c == 5), r=[cmat.res, sqs[c].res], w=[pc.res])
                    rstd_ops(rc, pc, 8, 96.0)
                    for c in range(6):
                        pbc = nps()
                        selT = cf32[0:8, F_SEL2T + 128 * c:F_SEL2T + 128 * c + 128] if c < 4 else cf32[0:8, F_SEL4T + 128 * (c - 4):F_SEL4T + 128 * (c - 4) + 128]
                        op(PE, mm(pbc[:, :], selT, rc[0:8, :], True, True), r=[cf32.res, rc.res], w=[pbc.res])
                        if c < 4:
                            o = outbuf()
                            op(DVE, stt(o[:, :], raws[c][:, :], pcols[:, pb + 5:pb + 6], pbc[:, :], ALU.mult, ALU.mult), r=[raws[c].res, pcols.res, pbc.res], w=[o.res])
                            for hh in range(2):
                                dma(STQ, dm(qb[2 * c + hh, 0:64, gs], o[64 * hh:64 * hh + 64, :]), r=[o.res])
                        else:
                            op(DVE, stt(t1[:, :], raws[c][:, :], pcols[:, pb + 6:pb + 7], pbc[:, :], ALU.mult, ALU.mult), r=[raws[c].res, pcols.res, pbc.res], w=[t1.res])
                            op(POOL, cp(sqs[7][:, :], t1[:, :]), r=[t1.res], w=[sqs[7].res])
                            prot = nps()
                            op(PE, mm(prot[:, :], cmat[:, C_PERM:C_PERM + 128], sqs[7][:, :], True, True), r=[cmat.res, sqs[7].res], w=[prot.res])
                            op(DVE, tt(t2[:, :], prot[:, :], sint[:, :], ALU.mult), r=[prot.res, sint.res], w=[t2.res])
                            op(DVE, tt(t1[:, :], t1[:, :], cost[:, :], ALU.mult), r=[t1.res, cost.res], w=[t1.res])
                            o = outbuf()
                            op(DVE, tt(o[:, :], t1[:, :], t2[:, :], ALU.add), r=[t1.res, t2.res], w=[o.res])
                            for hh in range(4):
                                dma(STQ, dm(qb[4 * (c - 4) + hh, 64:96, gs], o[32 * hh:32 * hh + 32, :]), r=[o.res])
                    for c in range(4):
                        p = nps()
                        op(PE, mm(p[:, :], wkv[:, c * 128:(c + 1) * 128], ckvn[:, :], True, True), r=[wkv.res, ckvn.res], w=[p.res])
                        op(ACT, act(raws[c][:, :], p[:, :], AF.Copy), r=[p.res], w=[raws[c].res])
                        op(POOL, tt(sqs[c][:, :], raws[c][:, :], raws[c][:, :], ALU.mult), r=[raws[c].res], w=[sqs[c].res])
                    pc = nps()
                    for c in range(4):
                        op(PE, mm(pc[0:8, :], cmat[:, C_SEL2 + 8 * c:C_SEL2 + 8 * c + 8], sqs[c][:, :], c == 0, False), r=[cmat.res, sqs[c].res], w=[pc.res])
                    op(PE, mm(pc[0:8, :], cmat[0:32, C_ONES:C_ONES + 8], sqs[6][0:32, :], False, True), r=[cmat.res, sqs[6].res], w=[pc.res])
                    rstd_ops(rc, pc, 8, 96.0)
                    for c in range(4):
                        pbc = nps()
                        op(PE, mm(pbc[:, :], cf32[0:8, F_SEL2T + 128 * c:F_SEL2T + 128 * c + 128], rc[0:8, :], True, True), r=[cf32.res, rc.res], w=[pbc.res])
                        o = outbuf()
                        op(DVE, stt(o[:, :], raws[c][:, :], pcols[:, pb + 7:pb + 8], pbc[:, :], ALU.mult, ALU.mult), r=[raws[c].res, pcols.res, pbc.res], w=[o.res])
                        for hh in range(2):
                            dma(STQ, dm(kb_[2 * c + hh, 0:64, gs], o[64 * hh:64 * hh + 64, :]), r=[o.res])
                    for h in range(8):
                        pbc = nps()
                        op(PE, mm(pbc[0:32, :], cf32[0:8, F_KSELT + 32 * h:F_KSELT + 32 * h + 32], rc[0:8, :], True, True), r=[cf32.res, rc.res], w=[pbc.res])
                        o = outbuf()
                        op(DVE, tt(o[0:32, :], krr[0:32, :], pbc[0:32, :], ALU.mult), r=[krr.res, pbc.res], w=[o.res])
                        dma(STQ, dm(kb_[h, 64:96, gs], o[0:32, :]), r=[o.res])
                    for tb in range(4):
                        p = nps()
                        op(PE, mm(p[:, :], ckvn[:, tb * 128:(tb + 1) * 128], wkv[:, 512:1024], True, True), r=[wkv.res, ckvn.res], w=[p.res])
                        op(ACT, act(vt[:, tb, :], p[:, :], AF.Copy), r=[p.res], w=[vt.res])
                    for h in range(8):
                        dma(STQ, dm(vb[h, :, 4 * g:4 * g + 4, :], vt[:, :, 64 * h:64 * h + 64]), r=[vt.res])
        sch.barrier()

    def attn_even(j):
        pb = 64 + 32 * j
        with ExitStack() as st:
            KT = [sb(st, f"KT{i}", [128, S], BF16) for i in range(2)]
            V = [sb(st, f"V{i}", [128, NB, 128], BF16) for i in range(2)]
            for v_ in V:
                op(POOL, lambda e, v_=v_: e.memset(v_[:, :, 64:128], 1.0), w=[v_.res])
            tmpf = [sb(st, f"tmpf{i}", [128, 512], F32) for i in range(2)]
            STQA = POOL
            Q = [sb(st, f"Q{i}", [128, 512], BF16) for i in range(3)]
            Pb = [sb(st, f"P{i}", [128, 512], BF16) for i in range(6)]
            sbb = [sb(st, f"sbb{i}", [128, 256], F32) for i in range(4)]
            rec = sb(st, "rec", [64, 512], F32)
            ob = [sb(st, f"oo{i}", [64, 512], BF16) for i in range(2)]
            O, L = ps[6], ps[7]
            qctr = [0]
            octr = [0]
            tctr = [0]
            for kvh in range(2):
                kt, v = KT[kvh], V[kvh]
                dma(SP, dm(kt[0:64, :], ka[kvh]), w=[kt.res])
                dma(SP, dm(v[:, :, 0:64], va[kvh]), w=[v.res])
                for hq in range(4):
                    h = 4 * kvh + hq
                    for g in range(NG):
                        q = Q[qctr[0] % 3]
                        qctr[0] += 1
                        dma(SP, dm(q[0:64, :], qa[h, :, g * 512:(g + 1) * 512]), w=[q.res])
                        tiles = []
                        rels = list(range(-1, 4)) if g > 0 else list(range(0, 4))
                        for idx, rel in enumerate(rels):
                            kbi = 4 * g + rel
                            if rel < 0:
                                q0, n, boff = 0, 128, 0
                            else:
                                q0 = 128 * rel
                                n = min(256, 512 - q0)
                                boff = 128
                            t = tctr[0]
                            tctr[0] += 1
                            sp_, P_, sb_ = ps[t % 4], Pb[t % 6], sbb[t % 4]
                            first, last = idx == 0, idx == len(rels) - 1

                            def s1(kbi=kbi, q0=q0, n=n, sp_=sp_, q=q, kt=kt):
                                op(PE, mm(sp_[:, 0:n], kt[0:64, kbi * 128:(kbi + 1) * 128], q[0:64, q0:q0 + n], True, True), r=[kt.res, q.res], w=[sp_.res])

                            def s2(n=n, sp_=sp_, sb_=sb_, P_=P_, boff=boff, h=h):
                                op(DVE, stt(sb_[:, 0:n], sp_[:, 0:n], 0.125, cf32[:, F_ABIAS + 384 * h + boff:F_ABIAS + 384 * h + boff + n], ALU.mult, ALU.add),
                                   r=[sp_.res, cf32.res], w=[sb_.res])
                                op(ACT, act(P_[:, 0:n], sb_[:, 0:n], AF.Exp), r=[sb_.res], w=[P_.res])

                            def s3(kbi=kbi, q0=q0, n=n, P_=P_, v=v, first=first, last=last):
                                op(PE, mm(O[0:64, q0:q0 + n], v[:, kbi, 0:64], P_[:, 0:n], first, last), r=[v.res, P_.res], w=[O.res])
                                op(PE, mm(L[0:64, q0:q0 + n], ONES(128, 64), P_[:, 0:n], first, last), r=[cmat.res, P_.res], w=[L.res])
                            tiles.append([s1, s2, s3])
                        pipeline(tiles, [0, 2, 4])
                        o = ob[octr[0] % 2]
                        octr[0] += 1
                        op(DVE, ts(rec[:, :], L[0:64, :], dcol[0:64, 4 + 8 * j + h:5 + 8 * j + h], None, ALU.add), r=[L.res, dcol.res], w=[rec.res])
                        op(DVE, lambda e: e.reciprocal(out=rec[:, :], in_=rec[:, :]), r=[rec.res], w=[rec.res])
                        op(DVE, tt(o[:, :], O[0:64, :], rec[:, :], ALU.mult), r=[O.res, rec.res], w=[o.res])
                        dma(STQA, dm(mixT[64 * h:64 * h + 64, g * 512:(g + 1) * 512], o[:, :]), r=[o.res])
            scaleB = 96.0 ** -0.5
            for h in range(8):
                kt, v = KT[h % 2], V[h % 2]
                dma(SP, dm(kt[0:96, :], kb_[h]), w=[kt.res])
                dma(SP, dm(v[:, :, 0:64], vb[h]), w=[v.res])
                for g in range(NG):
                    q = Q[qctr[0] % 3]
                    qctr[0] += 1
                    dma(SP, dm(q[0:96, :], qb[h, :, g * 512:(g + 1) * 512]), w=[q.res])
                    OL = ps[6 + (octr[0] % 2)]
                    tiles = []
                    nkb = 4 * g + 4
                    for kbi in range(nkb):
                        rel = kbi - 4 * g
                        q0 = 128 * rel if rel > 0 else 0
                        n = 512 - q0
                        t = tctr[0]
                        tctr[0] += 1
                        sp_, P_ = ps[t % 4], Pb[t % 6]
                        first, last = kbi == 0, kbi == nkb - 1

                        def s1(kbi=kbi, q0=q0, n=n, sp_=sp_, q=q, kt=kt):
                            op(PE, mm(sp_[:, 0:n], kt[0:96, kbi * 128:(kbi + 1) * 128], q[0:96, q0:q0 + n], True, True), r=[kt.res, q.res], w=[sp_.res])

                        def s2(n=n, sp_=sp_, P_=P_, rel=rel):
                            op(ACT, act(P_[:, 0:n], sp_[:, 0:n], AF.Exp, scale=scaleB), r=[sp_.res], w=[P_.res])
                            if rel >= 0:
                                op(DVE, tt(P_[:, 0:128], P_[:, 0:128], cmat[:, C_DFB:C_DFB + 128], ALU.mult), r=[P_.res, cmat.res], w=[P_.res])

                        def s3(kbi=kbi, q0=q0, n=n, P_=P_, v=v, first=first, last=last, OL=OL):
                            op(PE, mm(OL[:, q0:q0 + n], v[:, kbi, :], P_[:, 0:n], first, last), r=[v.res, P_.res], w=[OL.res])
                        tiles.append([s1, s2, s3])
                    pipeline(tiles, [0, 2, 4])
                    o = ob[octr[0] % 2]
                    tf = tmpf[octr[0] % 2]
                    octr[0] += 1
                    op(DVE, cp(tf[:, :], OL[:, :]), r=[OL.res], w=[tf.res])
                    op(PE, mm(ps[5][0:64, :], cf32[:, F_SHIFT:F_SHIFT + 64], tf[:, :], True, True), r=[cf32.res, tf.res], w=[ps[5].res])
                    op(DVE, lambda e: e.reciprocal(out=rec[:, :], in_=ps[5][0:64, :]), r=[ps[5].res], w=[rec.res])
                    op(DVE, tt(o[:, :], tf[0:64, :], rec[:, :], ALU.mult), r=[tf.res, rec.res], w=[o.res])
                    dma(STQA, dm(mixT[512 + 64 * h:512 + 64 * h + 64, g * 512:(g + 1) * 512], o[:, :]), r=[o.res])
        sch.barrier()

    def attn_odd(j):
        with ExitStack() as st:
            KT = [sb(st, f"KT{i}", [128, S], BF16) for i in range(3)]
            V = [sb(st, f"V{i}", [128, NB, 128], BF16) for i in range(2)]
            Q = [sb(st, f"Q{i}", [128, 512], BF16) for i in range(4)]
            Pb = [sb(st, f"P{i}", [128, 512], BF16) for i in range(6)]
            ef = [sb(st, f"ef{i}", [128, 512], F32) for i in range(3)]
            spb = [sb(st, f"spb{i}", [128, 512], BF16) for i in range(3)]
            R32 = sb(st, "R32", [128, 512], F32)
            Rb = [sb(st, f"Rb{i}", [128, 512], BF16) for i in range(2)]
            ob = [sb(st, f"oo{i}", [128, 512], BF16) for i in range(2)]
            f1 = sb(st, "f1", [128, 512], F32)
            f2 = sb(st, "f2", [128, 512], F32)
            f3 = sb(st, "f3", [128, 512], F32)
            fsq = sb(st, "fsq", [128, 512], BF16)
            qctr = [0]
            octr = [0]
            tctr = [0]
            zw = [PsView(psall, 1024 * k, 1024) for k in range(3)]
            Ow = PsView(psall, 3072, 1024)
            efw = [sb(st, f"efw{i}", [128, 1024], F32) for i in range(2)]
            spw = [sb(st, f"spw{i}", [128, 1024], BF16) for i in range(3)]
            aw = [sb(st, f"aw{i}", [128, 1024], BF16) for i in range(3)]
            R32w = sb(st, "R32w", [128, 1024], F32)
            Rbw = [sb(st, f"Rbw{i}", [128, 1024], BF16) for i in range(2)]
            qws = [sb(st, f"qw{i}", [64, 1024], BF16) for i in range(2)]
            obw = [sb(st, f"obw{i}", [64, 1024], BF16) for i in range(2)]
            for h in range(8):
                kt, v = KT[h % 2], V[h % 2]
                dma(SP, dm(kt[0:64, :], kc[h]), w=[kt.res])
                dma(SP, dm(v[:, :, 0:64], vc[h]), w=[v.res])
                for G in range(NG // 2):
                    g = 2 * G
                    q = qws[qctr[0] % 2]
                    qctr[0] += 1
                    dma(SP, dm(q[0:64, :], qc[h, :, g * 512:(g + 2) * 512]), w=[q.res])
                    tiles = []
                    nkb = 4 * g + 8
                    order = list(range(nkb - 1, -1, -1))
                    for idx, kbi in enumerate(order):
                        if kbi >= 4 * g + 4:
                            c0, diag = 512 + 128 * (kbi - 4 * g - 4), True
                        elif kbi >= 4 * g:
                            c0, diag = 128 * (kbi - 4 * g), True
                        else:
                            c0, diag = 0, False
                        pieces = ([(c0, 512)] if c0 < 512 else []) + [(max(c0, 512), 1024)]
                        t = tctr[0]
                        tctr[0] += 1
                        zp, e_, s_, a_ = zw[t % 3], efw[t % 2], spw[t % 3], aw[t % 3]
                        rb_r, rb_w = Rbw[idx % 2], Rbw[(idx + 1) % 2]
                        first, last = idx == 0, idx == len(order) - 1
                        firstA = kbi == 4 * g + 3

                        def s1(kbi=kbi, pieces=pieces, zp=zp, q=q, kt=kt):
                            for (a_c, b_c) in pieces:
                                op(PE, mm(zp[:, a_c:b_c], kt[0:64, kbi * 128:(kbi + 1) * 128], q[0:64, a_c:b_c], True, False), r=[kt.res, q.res], w=[zp.res])

                        def s2a(c0=c0, zp=zp, e_=e_):
                            op(ACT, act(e_[:, c0:1024], zp[:, c0:1024], AF.Exp), r=[zp.res], w=[e_.res])

                        def s2(c0=c0, e_=e_, s_=s_, diag=diag):
                            op(ACT, act(s_[:, c0:1024], e_[:, c0:1024], AF.Ln, bias=onec[:, :]), r=[e_.res, onec.res], w=[s_.res])
                            if diag:
                                op(DVE, tt(s_[:, c0:c0 + 128], s_[:, c0:c0 + 128], cmat[:, C_DFC:C_DFC + 128], ALU.mult), r=[s_.res, cmat.res], w=[s_.res])

                        def s3(c0=c0, pieces=pieces, zp=zp, s_=s_, rb_r=rb_r, rb_w=rb_w, first=first, last=last):
                            for (a_c, b_c) in pieces:
                                op(PE, mm(zp[:, a_c:b_c], cmat[:, C_NEGU:C_NEGU + 128], s_[:, a_c:b_c], False, first), r=[cmat.res, s_.res], w=[zp.res])
                                if not first:
                                    op(PE, mm(zp[:, a_c:b_c], cmat[:, C_NEGONES:C_NEGONES + 128], rb_r[:, a_c:b_c], False, True), r=[cmat.res, rb_r.res], w=[zp.res])
                            if not last:
                                if first:
                                    op(POOL, lambda e: e.memset(R32w[:, :], 0.0), w=[R32w.res])
                                op(POOL, tt(R32w[:, c0:1024], R32w[:, c0:1024], s_[:, c0:1024], ALU.add), r=[R32w.res, s_.res], w=[R32w.res])
                                op(DVE, cp(rb_w[:, :], R32w[:, :]), r=[R32w.res], w=[rb_w.res])

                        def s4(c0=c0, zp=zp, a_=a_, diag=diag):
                            op(ACT, act(a_[:, c0:1024], zp[:, c0:1024], AF.Exp), r=[zp.res], w=[a_.res])
                            if diag:
                                op(DVE, tt(a_[:, c0:c0 + 128], a_[:, c0:c0 + 128], cmat[:, C_DFC:C_DFC + 128], ALU.mult), r=[a_.res, cmat.res], w=[a_.res])

                        def s5(kbi=kbi, pieces=pieces, a_=a_, v=v, first=first, firstA=firstA, last=last):
                            for (a_c, b_c) in pieces:
                                st_ = firstA if a_c < 512 else first
                                op(PE, mm(Ow[0:64, a_c:b_c], v[:, kbi, 0:64], a_[:, a_c:b_c], st_, last), r=[v.res, a_.res], w=[Ow.res])
                        tiles.append([s1, s2a, s2, s3, s4, s5])
                    pipeline(tiles, [0, 1, 1, 2, 2, 3])
                    o = obw[octr[0] % 2]
                    octr[0] += 1
                    op(DVE, cp(o[0:64, :], Ow[0:64, :]), r=[Ow.res], w=[o.res])
                    dma(POOL, dm(mixT[64 * h:64 * h + 64, g * 512:(g + 2) * 512], o[0:64, :]), r=[o.res])
            sch.barrier()
            O1, O2 = ps[6], ps[7]
            Lacc = [sb(st, f"Lacc{i}", [128, 512], F32) for i in range(2)]
            for h in range(4):
                k1, k2, v = KT[0], KT[1], V[h % 2]
                dma(SP, dm(k1[0:70, :], kd[h, 0]), w=[k1.res])
                dma(SP, dm(k2[0:70, :], kd[h, 1]), w=[k2.res])
                dma(SP, dm(v[:, :, :], vd[h]), w=[v.res])
                for g in range(NG):
                    q1 = Q[qctr[0] % 4]
                    q2 = Q[(qctr[0] + 1) % 4]
                    qctr[0] += 2
                    dma(SP, dm(q1[0:70, :], qd[h, 0, :, g * 512:(g + 1) * 512]), w=[q1.res])
                    dma(SP, dm(q2[0:70, :], qd[h, 1, :, g * 512:(g + 1) * 512]), w=[q2.res])
                    op(DVE, lambda e: e.memset(Lacc[0][:, :], 0.0), w=[Lacc[0].res])
                    op(POOL, lambda e: e.memset(Lacc[1][:, :], 0.0), w=[Lacc[1].res])
                    tiles = []
                    nkb = 4 * g + 4
                    for kbi in range(nkb):
                        rel = kbi - 4 * g
                        q0 = 128 * rel if rel > 0 else 0
                        n = 512 - q0
                        t = tctr[0]
                        tctr[0] += 1
                        sa, sb2 = ps[(2 * t) % 6], ps[(2 * t + 1) % 6]
                        Pa, Pb2 = Pb[(2 * t) % 6], Pb[(2 * t + 1) % 6]
                        first, last = kbi == 0, kbi == nkb - 1

                        def s1(kbi=kbi, q0=q0, n=n, sa=sa, sb2=sb2, q1=q1, q2=q2):
                            op(PE, mm(sa[:, 0:n], k1[0:70, kbi * 128:(kbi + 1) * 128], q1[0:70, q0:q0 + n], True, True), r=[k1.res, q1.res], w=[sa.res])
                            op(PE, mm(sb2[:, 0:n], k2[0:70, kbi * 128:(kbi + 1) * 128], q2[0:70, q0:q0 + n], True, True), r=[k2.res, q2.res], w=[sb2.res])

                        def s2(n=n, q0=q0, sa=sa, sb2=sb2, Pa=Pa, Pb2=Pb2, rel=rel, h=h):
                            for s_, p_, eng, la in ((sa, Pa, DVE, Lacc[0]), (sb2, Pb2, POOL, Lacc[1])):
                                op(ACT, act(p_[:, 0:n], s_[:, 0:n], AF.Exp, scale=0.125), r=[s_.res], w=[p_.res])
                                if rel >= 0:
                                    op(DVE, tt(p_[:, 0:128], p_[:, 0:128], cmat[:, C_DFD + 128 * h:C_DFD + 128 * h + 128], ALU.mult), r=[p_.res, cmat.res], w=[p_.res])
                                op(eng, tt(la[:, q0:q0 + n], la[:, q0:q0 + n], p_[:, 0:n], ALU.add), r=[la.res, p_.res], w=[la.res])

                        def s3(kbi=kbi, q0=q0, n=n, Pa=Pa, Pb2=Pb2, v=v, first=first, last=last):
                            op(PE, mm(O1[:, q0:q0 + n], v[:, kbi, :], Pa[:, 0:n], first, last), r=[v.res, Pa.res], w=[O1.res])
                            op(PE, mm(O2[:, q0:q0 + n], v[:, kbi, :], Pb2[:, 0:n], first, last), r=[v.res, Pb2.res], w=[O2.res])
                        tiles.append([s1, s2, s3])
                    pipeline(tiles, [0, 1, 2])
                    L1, L2 = ps[(2 * tctr[0]) % 6], ps[(2 * tctr[0] + 1) % 6]
                    op(PE, mm(L1[:, :], cf32[:, F_ONES:F_ONES + 128], Lacc[0][:, :], True, True), r=[cf32.res, Lacc[0].res], w=[L1.res])
                    op(PE, mm(L2[:, :], cf32[:, F_ONES:F_ONES + 128], Lacc[1][:, :], True, True), r=[cf32.res, Lacc[1].res], w=[L2.res])
                    op(DVE, lambda e, L1=L1: e.reciprocal(out=f1[:, :], in_=L1[:, :]), r=[L1.res], w=[f1.res])
                    op(DVE, tt(f1[:, :], O1[:, :], f1[:, :], ALU.mult), r=[O1.res, f1.res], w=[f1.res])
                    op(DVE, lambda e, L2=L2: e.reciprocal(out=f2[:, :], in_=L2[:, :]), r=[L2.res], w=[f2.res])
                    op(DVE, tt(f2[:, :], O2[:, :], f2[:, :], ALU.mult), r=[O2.res, f2.res], w=[f2.res])
                    op(DVE, stt(f3[:, :], f2[:, :], dcol[:, j:j + 1], f1[:, :], ALU.mult, ALU.add), r=[f2.res, dcol.res, f1.res], w=[f3.res])
                    op(POOL, tt(fsq[:, :], f3[:, :], f3[:, :], ALU.mult), r=[f3.res], w=[fsq.res])
                    pn = ps[(2 * tctr[0] + 2) % 6]
                    op(PE, mm(pn[:, :], ONES(), fsq[:, :], True, True), r=[cmat.res, fsq.res], w=[pn.res])
                    rstd_ops(f1, pn, 128, 128.0)
                    o = ob[octr[0] % 2]
                    octr[0] += 1
                    op(DVE, stt(o[:, :], f3[:, :], dcol[:, 2 + j:3 + j], f1[:, :], ALU.mult, ALU.mult), r=[f3.res, dcol.res, f1.res], w=[o.res])
                    dma(POOL, dm(mixT[512 + 128 * h:512 + 128 * h + 128, g * 512:(g + 1) * 512], o[:, :]), r=[o.res])
        sch.barrier()

    stop = int(os.environ.get("KSTOP", "99"))
    cnt = 0
    for l in range(DEPTH + 1):
        if cnt >= stop:
            break
        proj_phase(l)
        cnt += 1
        if l < DEPTH:
            if cnt >= stop:
                break
            if l % 2 == 0:
                attn_even(l // 2)
            else:
                attn_odd(l // 2)
            cnt += 1
    sch.barrier()

    with nc.Block() as block:
        @block.tensor
        def _(e):
            sch.emit(PE, e)

        @block.scalar
        def _(e):
            sch.emit(ACT, e)

        @block.vector
        def _(e):
            sch.emit(DVE, e)

        @block.gpsimd
        def _(e):
            sch.emit(POOL, e)

        @block.sync
        def _(e):
            sch.emit(SP, e)
    stack.close()
    return nc


def host_consts(S):
    bf = ml_dtypes.bfloat16
    cm = np.zeros((128, NCM), np.float32)
    cm[:, C_ONES:C_ONES + 128] = 1.0
    for m in range(128):
        if m % 32 < 16:
            cm[m + 16, C_PERM + m] = -1.0
        else:
            cm[m - 16, C_PERM + m] = 1.0
    jj, kk = np.meshgrid(np.arange(128), np.arange(128), indexing="ij")
    cm[:, C_NEGU:C_NEGU + 128] = -(jj >= kk).astype(np.float32)
    cm[:, C_NEGONES:C_NEGONES + 128] = -1.0
    k_, q_ = jj, kk
    cm[:, C_DFB:C_DFB + 128] = ((k_ // 64) <= (q_ // 64)).astype(np.float32)
    cm[:, C_DFC:C_DFC + 128] = (k_ < q_).astype(np.float32)
    for h in range(4):
        m = 2.0 ** (-8.0 * (h + 1) / 4)
        d = np.where(k_ <= q_, 1.0, np.where((k_ // 64) == (q_ // 64), np.exp(-2.0 * m * (k_ - q_)), 0.0))
        cm[:, C_DFD + 128 * h:C_DFD + 128 * h + 128] = d
    p = np.arange(128)
    for c in range(4):
        cm[p, C_SEL2 + 8 * c + 2 * c + p // 64] = 1.0
    for r in range(2):
        cm[p, C_SEL4 + 8 * r + 4 * r + p // 32] = 1.0
    cf = np.zeros((128, NCF), np.float32)
    for c in range(4):
        cf[2 * c + p // 64, F_SEL2T + 128 * c + p] = 1.0
    for r in range(2):
        cf[4 * r + p // 32, F_SEL4T + 128 * r + p] = 1.0
    for h in range(8):
        cf[h, F_KSELT + 32 * h:F_KSELT + 32 * h + 32] = 1.0
    for h in range(8):
        m = 2.0 ** (-8.0 * (h + 1) / 8)
        kpos = np.arange(128)[:, None] - 128
        qpos = np.arange(128)[None, :]
        dch = qpos // 64 - np.floor_divide(kpos, 64)
        ok = (dch >= 0) & (dch <= 2)
        cf[:, F_ABIAS + 384 * h:F_ABIAS + 384 * h + 128] = np.where(ok, -m * np.abs(qpos - kpos), -30000.0)
        kpos = np.arange(128)[:, None]
        qpos = np.arange(256)[None, :]
        dch = qpos // 64 - kpos // 64
        ok = (dch >= 0) & (dch <= 2)
        cf[:, F_ABIAS + 384 * h + 128:F_ABIAS + 384 * h + 384] = np.where(ok, -m * np.abs(qpos - kpos), -30000.0)
    half = 16
    inv = (np.float32(10000.0) ** (-np.arange(half, dtype=np.float32) / np.float32(half))).astype(np.float32)
    cf[:, F_INVF] = inv[p % 16]
    for m in range(64):
        cf[64 + m, F_SHIFT + m] = 1.0
    cf[:, F_ONES:F_ONES + 128] = 1.0
    pos = np.arange(S)
    a, b, c = pos // 1024, (pos % 1024) // 32, pos % 32
    daug = np.zeros((4, 2, 6, S), np.float32)
    for h in range(4):
        m = 2.0 ** (-8.0 * (h + 1) / 4) * 8.0
        daug[h, 0, 0], daug[h, 0, 1], daug[h, 0, 2] = -m * 1024 * a, -m * 32 * b, -m * c
        daug[h, 0, 3:6] = 1.0
        daug[h, 1, 0:3] = 1.0
        daug[h, 1, 3], daug[h, 1, 4], daug[h, 1, 5] = m * 1024 * a, m * 32 * b, m * c
    return cm.astype(bf), cf, daug.astype(bf)


def host_pcols(inp):
    pc = np.zeros((128, NPC), np.float32)
    p = np.arange(128)
    for l in range(4):
        pc[:, l * 16:l * 16 + 8] = inp["norm_mix_g"][l].reshape(8, 128).T
        pc[:, l * 16 + 8:l * 16 + 16] = inp["norm_ffn_g"][l].reshape(8, 128).T
    for j in range(2):
        b = 64 + 32 * j
        pc[:, b + 0] = inp["a_q_norm"][j][p % 64]
        pc[:, b + 1] = inp["a_k_norm"][j][p % 64]
        pc[:, b + 2:b + 4] = inp["b_cq_norm"][j].reshape(2, 128).T
        pc[:, b + 4] = inp["b_ckv_norm"][j]
        pc[:, b + 5] = inp["b_q_norm"][j][p % 64]
        pc[:, b + 6] = inp["b_q_norm"][j][64 + p % 32]
        pc[:, b + 7] = inp["b_k_norm"][j][p % 64]
        pc[:, b + 8] = inp["b_k_norm"][j][64 + p % 32]
        pc[:, b + 9:b + 17] = inp["a_sinks"][j][None, :]
        b = 128 + 32 * j
        pc[:, b + 0] = inp["d_q_norm"][j].reshape(128)
        pc[:, b + 1] = inp["d_k_norm"][j].reshape(128)
        pc[:, b + 2] = inp["d_subln"][j]
    dlam = np.broadcast_to(inp["d_lambda"].reshape(1, 2, 256), (128, 2, 256)).astype(np.float32)
    return pc, np.ascontiguousarray(dlam)


_CACHE = {}


def kernel(**inputs):
    inp = {k: np.asarray(v) for k, v in inputs.items()}
    x = inp["x"]
    B, S, D = x.shape
    if S not in _CACHE:
        _CACHE[S] = build(S)
    nc = _CACHE[S]
    cm, cf, daug = host_consts(S)
    pc, dlam = host_pcols(inp)
    shared = {
        "pcols": pc, "dlam": dlam, "cmat": cm, "cf32": cf, "daug": daug,
        "mlp_w_up": inp["mlp_w_up"], "mlp_w_down": inp["mlp_w_down"],
        "ev_w_in": inp["ev_w_in"], "ev_w_out": inp["ev_w_out"],
        "b_w_uq": inp["b_w_uq"], "b_w_ukv": inp["b_w_ukv"],
        "od_w_in": inp["od_w_in"], "od_w_out": inp["od_w_out"],
    }
    in_maps = []
    for b in range(B):
        m = dict(shared)
        m["xT"] = np.ascontiguousarray(x[b].T)
        m["posb"] = np.ascontiguousarray(np.broadcast_to(inp["positions"][b][None, :], (128, S))).astype(np.int32)
        in_maps.append(m)
    res = run_bass_kernel_spmd(nc, in_maps, core_ids=list(range(B)))
    out = np.stack([np.ascontiguousarray(r["yT"].T) for r in res.results], axis=0)
    return out.astype(np.float32)
```

```python
import math
import os
from contextlib import ExitStack

import numpy as np
import ml_dtypes
import concourse.bass as bass
import concourse.mybir as mybir
from concourse.bass_utils import run_bass_kernel_spmd

F32, BF16, I32 = mybir.dt.float32, mybir.dt.bfloat16, mybir.dt.int32
AF = mybir.ActivationFunctionType
ALU = mybir.AluOpType
AX = mybir.AxisListType
PE, ACT, DVE, POOL, SP = 0, 1, 2, 3, 4
EPS = 1e-6
DEPTH = 4
TWO_PI = 2.0 * math.pi

C_ONES, C_PERM, C_NEGU, C_NEGONES, C_DFB, C_DFC, C_DFD, C_SEL2, C_SEL4 = 0, 128, 256, 384, 512, 640, 768, 1280, 1312
NCM = 1328
F_SEL2T, F_SEL4T, F_KSELT, F_ABIAS, F_INVF, F_SHIFT, F_ONES = 0, 512, 768, 1024, 4096, 4097, 4161
NCF = 4289
NPC = 192


class Res:
    __slots__ = ("lw", "rd", "sem", "cnt", "ssem", "scnt")

    def __init__(self):
        self.lw = None
        self.rd = []
        self.sem = None
        self.cnt = 0
        self.ssem = None
        self.scnt = 0


class Sched:
    def __init__(self, nc, stack):
        self.nc = nc
        self.stack = stack
        self.q = [[] for _ in range(5)]
        self.seq = [0] * 5
        self.esem = [stack.enter_context(nc.semaphore(f"es{i}")) for i in range(5)]
        self.waited = [dict() for _ in range(5)]
        self.dsems = []
        self.free_dsems = []
        self.free_ssems = []
        self.ssems = []
        self.semcnt = {}
        self.nds = 0

    def _waits(self, e, deps):
        best = {}
        for (sem, val, src) in deps:
            if e == PE and src == PE:
                continue
            k = id(sem)
            if self.waited[e].get(k, 0) >= val:
                continue
            if k not in best or best[k][1] < val:
                best[k] = (sem, val)
        out = []
        for k, (sem, val) in best.items():
            self.waited[e][k] = val
            out.append((sem, val))
        return out

    def _deps(self, r, w, e=-9):
        deps = []
        for x in r:
            if x.lw is not None:
                deps.append(x.lw)
        for x in w:
            if x.lw is not None and x.lw[2] != e:
                deps.append(x.lw)
            for t in x.rd:
                if t[2] != e:
                    deps.append(t)
        return deps

    def op(self, e, fn, r=(), w=()):
        waits = self._waits(e, self._deps(r, w, e))
        self.seq[e] += 1
        tok = (self.esem[e], self.seq[e], e)
        self.q[e].append((waits, fn, (self.esem[e], 1), True))
        for x in r:
            x.rd.append(tok)
        for x in w:
            x.lw = tok
            x.rd = []

    def dma(self, qe, fn, r=(), w=(), semres=None):
        waits = self._waits(qe, self._deps(r, w))
        sr = semres if semres is not None else (w[0] if w else r[0])
        sw = qe == POOL
        sa, ca = ("ssem", "scnt") if sw else ("sem", "cnt")
        free = self.free_ssems if sw else self.free_dsems
        if getattr(sr, sa) is None:
            if free:
                sem = free.pop()
            else:
                self.nds += 1
                sem = self.stack.enter_context(self.nc.semaphore(f"ds{self.nds}"))
            setattr(sr, sa, sem)
            setattr(sr, ca, self.semcnt.get(id(sem), 0))
            (self.ssems if sw else self.dsems).append(sr)
        sem = getattr(sr, sa)
        cnt = getattr(sr, ca) + 16
        setattr(sr, ca, cnt)
        self.semcnt[id(sem)] = cnt
        tok = (sem, cnt, -1)
        self.q[qe].append((waits, fn, (sem, 16), False))
        for x in r:
            x.rd.append(tok)
        for x in w:
            x.lw = tok
            x.rd = []

    def barrier(self):
        toks = [(self.esem[i], self.seq[i], i) for i in range(5) if self.seq[i] > 0]
        toks += [(sr.sem, sr.cnt, -1) for sr in self.dsems]
        toks += [(sr.ssem, sr.scnt, -1) for sr in self.ssems]
        for e in range(5):
            deps = [t for t in toks if t[2] != e]
            waits = self._waits(e, [(s, v, -2) for (s, v, _) in deps])
            if waits:
                self.q[e].append((waits, None, None, False))
        for sr in self.dsems:
            self.free_dsems.append(sr.sem)
            sr.sem = None
        self.dsems = []
        for sr in self.ssems:
            self.free_ssems.append(sr.ssem)
            sr.ssem = None
        self.ssems = []

    def emit(self, e, eng):
        for (waits, fn, inc, attach) in self.q[e]:
            if fn is None:
                for (sem, val) in waits:
                    eng.wait_ge(sem, val)
                continue
            if attach and waits:
                for (sem, val) in waits[:-1]:
                    eng.wait_ge(sem, val)
                ins = fn(eng)
                ins._wait_ge(*waits[-1])
            else:
                for (sem, val) in waits:
                    eng.wait_ge(sem, val)
                ins = fn(eng)
            ins.then_inc(inc[0], inc[1])


def pipeline(tiles, skews):
    T = len(tiles)
    mx = max(skews)
    for i in range(T + mx):
        for s, sk in enumerate(skews):
            t = i - sk
            if 0 <= t < T and tiles[t][s] is not None:
                tiles[t][s]()


class T:
    __slots__ = ("h", "res")

    def __init__(self, h):
        self.h = h
        self.res = Res()

    def __getitem__(self, k):
        return self.h[k]


class PsView:
    __slots__ = ("h", "off", "w", "res")

    def __init__(self, h, off, w):
        self.h, self.off, self.w = h, off, w
        self.res = Res()

    def __getitem__(self, k):
        rs, cs = k
        a = 0 if cs.start is None else cs.start
        b = self.w if cs.stop is None else cs.stop
        return self.h[rs, self.off + a:self.off + b]


def build(S):
    NG = S // 512
    NB = S // 128
    nc = bass.Bass("TRN2", target_bir_lowering=False)

    def din(name, shape, dt=F32):
        return nc.dram_tensor(name, list(shape), dt, kind="ExternalInput").ap()

    def dscr(name, shape, dt=BF16):
        return nc.dram_tensor(name, list(shape), dt).ap()

    xT = din("xT", [1024, S])
    posb = din("posb", [128, S], I32)
    pcols_d = din("pcols", [128, NPC])
    dlam_d = din("dlam", [128, 2, 256])
    cmat_d = din("cmat", [128, NCM], BF16)
    cf32_d = din("cf32", [128, NCF])
    daug_d = din("daug", [4, 2, 6, S], BF16)
    w_up_d = din("mlp_w_up", [4, 1024, 4096])
    w_down_d = din("mlp_w_down", [4, 4096, 1024])
    ev_in_d = din("ev_w_in", [2, 1024, 1184])
    ev_out_d = din("ev_w_out", [2, 1024, 1024])
    uq_d = din("b_w_uq", [2, 256, 768])
    ukv_d = din("b_w_ukv", [2, 128, 1024])
    od_in_d = din("od_w_in", [2, 1024, 3072])
    od_out_d = din("od_w_out", [2, 1024, 1024])
    yT = nc.dram_tensor("yT", [1024, S], F32, kind="ExternalOutput").ap()

    w_up = dscr("s_w_up", [4, 1024, 4096])
    w_down = dscr("s_w_down", [4, 4096, 1024])
    ev_in = dscr("s_ev_in", [2, 1024, 1184])
    ev_out = dscr("s_ev_out", [2, 1024, 1024])
    uq = dscr("s_uq", [2, 256, 768])
    ukv = dscr("s_ukv", [2, 128, 1024])
    od_in = dscr("s_od_in", [2, 1024, 3072])
    od_out = dscr("s_od_out", [2, 1024, 1024])
    xres = dscr("s_xres", [1024, S], F32)
    mixT = dscr("s_mix", [1024, S])
    costab = dscr("s_cos", [128, S], F32)
    sintab = dscr("s_sin", [128, S], F32)
    qa = dscr("s_qa", [8, 64, S])
    ka = dscr("s_ka", [2, 64, S])
    va = dscr("s_va", [2, 128, NB, 64])
    qb = dscr("s_qb", [8, 96, S])
    kb_ = dscr("s_kb", [8, 96, S])
    vb = dscr("s_vb", [8, 128, NB, 64])
    qc = dscr("s_qc", [8, 64, S])
    kc = dscr("s_kc", [8, 64, S])
    vc = dscr("s_vc", [8, 128, NB, 64])
    qd = dscr("s_qd", [4, 2, 70, S])
    kd = dscr("s_kd", [4, 2, 70, S])
    vd = dscr("s_vd", [4, 128, NB, 128])

    stack = ExitStack()
    sch = Sched(nc, stack)
    op, dma = sch.op, sch.dma

    nctr = [0]

    def sb(st, name, shape, dt):
        nctr[0] += 1
        return T(st.enter_context(nc.sbuf_tensor(f"t{nctr[0]}_{name}", list(shape), dt)))

    psall = stack.enter_context(nc.psum_tensor("psall", [128, 4096], F32))
    ps = [PsView(psall, 512 * i, 512) for i in range(8)]
    cmat = sb(stack, "cmat", [128, NCM], BF16)
    cf32 = sb(stack, "cf32", [128, NCF], F32)
    pcols = sb(stack, "pcols", [128, NPC], F32)
    dcol = sb(stack, "dcol", [128, 32], F32)
    epsc = sb(stack, "epsc", [128, 1], F32)
    onec = sb(stack, "onec", [128, 1], F32)

    ONES = lambda k=128, m=128: cmat[0:k, C_ONES:C_ONES + m]

    def act(out, in_, func, scale=1.0, bias=None):
        if bias is None:
            return lambda e: e.activation(out=out, in_=in_, func=func, scale=scale)
        return lambda e: e.activation(out=out, in_=in_, func=func, scale=scale, bias=bias)

    def mm(out, lhsT, rhs, start, stop):
        return lambda e: e.matmul(out, lhsT=lhsT, rhs=rhs, start=start, stop=stop)

    def tt(out, in0, in1, o):
        return lambda e: e.tensor_tensor(out=out, in0=in0, in1=in1, op=o)

    def ts(out, in0, s1, s2, o0, o1=None):
        if o1 is None:
            return lambda e: e.tensor_scalar(out=out, in0=in0, scalar1=s1, scalar2=None, op0=o0)
        return lambda e: e.tensor_scalar(out=out, in0=in0, scalar1=s1, scalar2=s2, op0=o0, op1=o1)

    def stt(out, in0, s, in1, o0, o1):
        return lambda e: e.scalar_tensor_tensor(out=out, in0=in0, scalar=s, in1=in1, op0=o0, op1=o1)

    def cp(out, in_):
        return lambda e: e.tensor_copy(out=out, in_=in_)

    def dm(out, in_):
        return lambda e: e.dma_start(out=out, in_=in_)

    def rstd_ops(dst, src_ps, n, d):
        op(ACT, act(dst[0:n, :], src_ps[0:n, :], AF.Ln, scale=1.0 / d, bias=epsc[0:n, :]), r=[src_ps.res, epsc.res], w=[dst.res])
        op(ACT, act(dst[0:n, :], dst[0:n, :], AF.Exp, scale=-0.5), r=[dst.res], w=[dst.res])

    with ExitStack() as st:
        dma(SP, dm(cmat[:, :], cmat_d), w=[cmat.res])
        dma(SP, dm(cf32[:, :], cf32_d), w=[cf32.res])
        dma(SP, dm(pcols[:, :], pcols_d), w=[pcols.res])
        op(DVE, lambda e: e.memset(epsc[:, :], EPS), w=[epsc.res])
        op(DVE, lambda e: e.memset(onec[:, :], 1.0), w=[onec.res])
        castres = Res()

        def cast2d(dst, src, rows, step=128):
            for r0 in range(0, rows, step):
                r1 = min(rows, r0 + step)
                dma(POOL, dm(dst[r0:r1, :], src[r0:r1, :]), r=[castres], semres=castres)

        for l in range(4):
            cast2d(w_up[l], w_up_d[l], 1024)
            cast2d(w_down[l], w_down_d[l], 4096, 256)
        for j in range(2):
            cast2d(ev_in[j], ev_in_d[j], 1024)
            cast2d(ev_out[j], ev_out_d[j], 1024)
            cast2d(od_in[j], od_in_d[j], 1024)
            cast2d(od_out[j], od_out_d[j], 1024)
            cast2d(uq[j], uq_d[j], 256)
            cast2d(ukv[j], ukv_d[j], 128)
        augres = Res()
        for h in range(4):
            for m in range(2):
                dma(SP, dm(qd[h, m, 64:70, :], daug_d[h, 0]), r=[augres], semres=augres)
                dma(SP, dm(kd[h, m, 64:70, :], daug_d[h, 1]), r=[augres], semres=augres)
        dl = sb(st, "dl", [128, 2, 256], F32)
        dlp = sb(st, "dlp", [128, 64], F32)
        dma(SP, dm(dl[:, :, :], dlam_d), w=[dl.res])
        for j in range(2):
            layer = 2 * j + 1
            lam_init = 0.8 - 0.6 * math.exp(-0.3 * layer)
            for t in range(2):
                op(DVE, tt(dlp[:, :], dl[:, j, 128 * t:128 * t + 64], dl[:, j, 128 * t + 64:128 * t + 128], ALU.mult), r=[dl.res], w=[dlp.res])
                op(DVE, lambda e, t=t, j=j: e.reduce_sum(out=dcol[:, 20 + t:21 + t], in_=dlp[:, :], axis=AX.X), r=[dlp.res], w=[dcol.res])
                op(ACT, act(dcol[:, 20 + t:21 + t], dcol[:, 20 + t:21 + t], AF.Exp), r=[dcol.res], w=[dcol.res])
            op(DVE, tt(dcol[:, 22:23], dcol[:, 21:22], dcol[:, 20:21], ALU.subtract), r=[dcol.res], w=[dcol.res])
            op(DVE, ts(dcol[:, j:j + 1], dcol[:, 22:23], -lam_init, None, ALU.add), r=[dcol.res], w=[dcol.res])
            op(DVE, ts(dcol[:, 2 + j:3 + j], pcols[:, 128 + 32 * j + 2:128 + 32 * j + 3], 1.0 - lam_init, None, ALU.mult), r=[pcols.res, dcol.res], w=[dcol.res])
        for j in range(2):
            op(ACT, act(dcol[:, 4 + 8 * j:12 + 8 * j], pcols[:, 64 + 32 * j + 9:64 + 32 * j + 17], AF.Exp), r=[pcols.res, dcol.res], w=[dcol.res])
        CH = min(S, 2048)
        posi = sb(st, "posi", [128, CH], I32)
        ang = sb(st, "ang", [128, CH], F32)
        tq = sb(st, "tq", [128, CH], F32)
        ki = sb(st, "ki", [128, CH], I32)
        kf = sb(st, "kf", [128, CH], F32)
        rr = sb(st, "rr", [128, CH], F32)
        mk = sb(st, "mk", [128, CH], F32)
        C1 = 6.28125
        C2 = TWO_PI - C1
        for c0 in range(0, S, CH):
            dma(SP, dm(posi[:, :], posb[:, c0:c0 + CH]), w=[posi.res])
            op(DVE, cp(ang[:, :], posi[:, :]), r=[posi.res], w=[ang.res])
            op(DVE, ts(ang[:, :], ang[:, :], cf32[:, F_INVF:F_INVF + 1], None, ALU.mult), r=[ang.res, cf32.res], w=[ang.res])
            for which, tab in ((0, sintab), (1, costab)):
                src = ang
                if which == 1:
                    op(DVE, ts(rr[:, :], ang[:, :], math.pi / 2, None, ALU.add), r=[ang.res], w=[rr.res])
                    src = rr
                op(DVE, ts(tq[:, :], src[:, :], 1.0 / TWO_PI, None, ALU.mult), r=[src.res], w=[tq.res])
                op(DVE, cp(ki[:, :], tq[:, :]), r=[tq.res], w=[ki.res])
                op(DVE, cp(kf[:, :], ki[:, :]), r=[ki.res], w=[kf.res])
                op(DVE, stt(rr[:, :], kf[:, :], -C1, src[:, :], ALU.mult, ALU.add), r=[kf.res, src.res], w=[rr.res])
                op(DVE, stt(rr[:, :], kf[:, :], -C2, rr[:, :], ALU.mult, ALU.add), r=[kf.res, rr.res], w=[rr.res])
                op(DVE, ts(mk[:, :], rr[:, :], math.pi, -TWO_PI, ALU.is_gt, ALU.mult), r=[rr.res], w=[mk.res])
                op(DVE, tt(rr[:, :], rr[:, :], mk[:, :], ALU.add), r=[rr.res, mk.res], w=[rr.res])
                op(DVE, ts(mk[:, :], rr[:, :], -math.pi, TWO_PI, ALU.is_lt, ALU.mult), r=[rr.res], w=[mk.res])
                op(DVE, tt(rr[:, :], rr[:, :], mk[:, :], ALU.add), r=[rr.res, mk.res], w=[rr.res])
                op(DVE, ts(rr[:, :], rr[:, :], math.pi, -math.pi, ALU.min, ALU.max), r=[rr.res], w=[rr.res])
                op(ACT, act(tq[:, :], rr[:, :], AF.Sin), r=[rr.res], w=[tq.res])
                dma(SP, dm(tab[:, c0:c0 + CH], tq[:, :]), r=[tq.res])
        sch.barrier()

    def wview(w2d, c0, ncol, nchunk=8):
        return w2d.rearrange("(i p) c -> p i c", p=128)[:, 0:nchunk, c0:c0 + ncol]

    class Ctx:
        pass

    def proj_phase(l):
        with ExitStack() as st:
            xs = sb(st, "xs", [128, 8, 512], F32)
            xn = sb(st, "xn", [128, 8, 512], BF16)
            sq = sb(st, "sq", [128, 8, 512], BF16)
            sqr = [Res() for _ in range(8)]
            xsr = [Res() for _ in range(8)]
            xnr = [Res() for _ in range(8)]
            rstd = sb(st, "rstd", [128, 512], F32)
            wts = [sb(st, f"wt{i}", [128, 8, 512], BF16) for i in range(4)]
            wctr = [0]
            if l > 0:
                H = sb(st, "H", [128, 32, 512], BF16)
                mx = sb(st, "mx", [128, 8, 512], BF16)
                rl = [sb(st, f"rl{i}", [128, 512], BF16) for i in range(2)]
            if l < DEPTH:
                raws = [sb(st, f"raw{i}", [128, 512], F32) for i in range(8)]
                sqs = [sb(st, f"sqs{i}", [128, 512], BF16) for i in range(8)]
                outs = [sb(st, f"ob{i}", [128, 512], BF16) for i in range(4)]
                rc = sb(st, "rc", [8, 512], F32)
                vt = sb(st, "vt", [128, 4, 512], BF16)
                octr = [0]
                if l % 2 == 0:
                    cqn = sb(st, "cqn", [128, 2, 512], BF16)
                    ckvn = sb(st, "ckvn", [128, 512], BF16)
                    cost = sb(st, "cost", [128, 512], F32)
                    sint = sb(st, "sint", [128, 512], F32)
                    t1 = sb(st, "t1", [128, 512], F32)
                    t2 = sb(st, "t2", [128, 512], F32)
                    krr = sb(st, "krr", [32, 512], F32)
                    wsm = sb(st, "wsm", [128, 2, 768], BF16)
                    wkv = sb(st, "wkv", [128, 1024], BF16)
            pctr = [0]
            STQ = ACT

            def nps():
                pctr[0] += 1
                return ps[pctr[0] % 4]

            def load_w(view, nchunk=8, ncol=512):
                wt = wts[wctr[0] % 4]
                wctr[0] += 1
                dma(SP, dm(wt[:, 0:nchunk, 0:ncol], view), w=[wt.res])
                return wt

            def square(c):
                op(POOL if c % 2 == 0 else DVE, tt(sq[:, c, :], xs[:, c, :], xs[:, c, :], ALU.mult), r=[xsr[c]], w=[sqr[c]])

            def norm(gbase, do_sq=True):
                if do_sq:
                    for c in range(8):
                        square(c)
                p = nps()
                for c in range(8):
                    op(PE, mm(p[:, :], ONES(), sq[:, c, :], c == 0, c == 7), r=[sqr[c], cmat.res], w=[p.res])
                rstd_ops(rstd, p, 128, 1024.0)
                for c in range(8):
                    op(DVE, stt(xn[:, c, :], xs[:, c, :], pcols[:, gbase + c:gbase + c + 1], rstd[:, :], ALU.mult, ALU.mult),
                       r=[xsr[c], pcols.res, rstd.res], w=[xnr[c]])

            def outbuf():
                o = outs[octr[0] % 4]
                octr[0] += 1
                return o

            def proj_chunk(wt, col0, ncol, src, nchunk=8, srcidx=None):
                p = nps()
                for i in range(nchunk):
                    rhs = src[:, i, :] if srcidx is None else srcidx(i)
                    op(PE, mm(p[0:ncol, :], wt[:, i, col0:col0 + ncol], rhs, i == 0, i == nchunk - 1), r=[wt.res, xnr[i] if src is xn else src.res], w=[p.res])
                return p

            def vproj(wt, col0, ncol, src, nchunk, dst_fn, srcidx=None):
                for tb in range(4):
                    p = nps()
                    for i in range(nchunk):
                        lhs = src[:, i, tb * 128:(tb + 1) * 128] if srcidx is None else srcidx(i, tb)
                        op(PE, mm(p[:, 0:ncol], lhs, wt[:, i, col0:col0 + ncol], i == 0, i == nchunk - 1), r=[wt.res, xnr[i] if src is xn else src.res], w=[p.res])
                    op(ACT, act(vt[:, tb, 0:ncol], p[:, 0:ncol], AF.Copy), r=[p.res], w=[vt.res])
                dst_fn()

            for g in range(NG):
                gs = slice(g * 512, (g + 1) * 512)
                src_x = xT if l == 0 else xres
                dma(SP, dm(xs[:, :, :], src_x.rearrange("(c p) s -> p c s", p=128)[:, :, gs]), w=xsr)
                if l > 0:
                    lp = l - 1
                    jp = lp // 2
                    dma(SP, dm(mx[:, :, :], mixT.rearrange("(c p) s -> p c s", p=128)[:, :, gs]), w=[mx.res])
                    wout = (ev_out if lp % 2 == 0 else od_out)[jp]
                    for half in range(2):
                        wt = load_w(wview(wout, half * 512, 512))
                        for oc4 in range(4):
                            oc = half * 4 + oc4
                            p = proj_chunk(wt, oc4 * 128, 128, mx)
                            op(DVE, tt(xs[:, oc, :], p[:, :], xs[:, oc, :], ALU.add), r=[p.res, xsr[oc]], w=[xsr[oc]])
                            square(oc)
                    norm(lp * 16 + 8, do_sq=False)
                    for fb in range(8):
                        wt = load_w(wview(w_up[lp], fb * 512, 512))
                        for f4 in range(4):
                            fc = fb * 4 + f4
                            p = proj_chunk(wt, f4 * 128, 128, xn)
                            r_ = rl[fc % 2]
                            op(DVE, ts(r_[:, :], p[:, :], 0.0, None, ALU.max), r=[p.res], w=[r_.res])
                            op(POOL, tt(H[:, fc, :], r_[:, :], r_[:, :], ALU.mult), r=[r_.res], w=[H.res])
                    for half in range(2):
                        accs = [ps[4 + i] for i in range(4)]
                        for fs in range(4):
                            view = w_down[lp].rearrange("(i p) c -> p i c", p=128)[:, fs * 8:(fs + 1) * 8, half * 512:(half + 1) * 512]
                            wt = load_w(view)
                            for oc4 in range(4):
                                for i in range(8):
                                    fc = fs * 8 + i
                                    op(PE, mm(accs[oc4][:, :], wt[:, i, oc4 * 128:(oc4 + 1) * 128], H[:, fc, :], fc == 0, fc == 31),
                                       r=[wt.res, H.res], w=[accs[oc4].res])
                        for oc4 in range(4):
                            oc = half * 4 + oc4
                            op(DVE, tt(xs[:, oc, :], accs[oc4][:, :], xs[:, oc, :], ALU.add), r=[accs[oc4].res, xsr[oc]], w=[xsr[oc]])
                            if l < DEPTH:
                                square(oc)
                if l == DEPTH:
                    dma(STQ, dm(yT.rearrange("(c p) s -> p c s", p=128)[:, :, gs], xs[:, :, :]), r=xsr)
                    continue
                dma(STQ, dm(xres.rearrange("(c p) s -> p c s", p=128)[:, :, gs], xs[:, :, :]), r=xsr)
                norm(l * 16, do_sq=(l == 0))
                j = l // 2
                if l % 2 == 1:
                    pb = 128 + 32 * j
                    win = od_in[j]
                    for part, dst, scale in ((0, qc, 0.125), (1, kc, 1.0)):
                        wt = load_w(wview(win, part * 512, 512))
                        for c in range(4):
                            p = proj_chunk(wt, c * 128, 128, xn)
                            o = outbuf()
                            op(ACT, act(o[:, :], p[:, :], AF.Copy, scale=scale), r=[p.res], w=[o.res])
                            for hh in range(2):
                                dma(STQ, dm(dst[2 * c + hh, :, gs], o[64 * hh:64 * hh + 64, :]), r=[o.res])
                    wt = load_w(wview(win, 1024, 512))

                    def st_cv():
                        for h in range(8):
                            dma(STQ, dm(vc[h, :, 4 * g:4 * g + 4, :], vt[:, :, 64 * h:64 * h + 64]), r=[vt.res])
                    vproj(wt, 0, 512, xn, 8, st_cv)
                    for part, dst, gcol in ((3, qd, pb + 0), (4, kd, pb + 1)):
                        wt = load_w(wview(win, part * 512, 512))
                        pc = nps()
                        for c in range(4):
                            p = proj_chunk(wt, c * 128, 128, xn)
                            op(ACT, act(raws[c][:, :], p[:, :], AF.Copy), r=[p.res], w=[raws[c].res])
                            op(POOL, tt(sqs[c][:, :], raws[c][:, :], raws[c][:, :], ALU.mult), r=[raws[c].res], w=[sqs[c].res])
                        for c in range(4):
                            op(PE, mm(pc[0:8, :], cmat[:, C_SEL2 + 8 * c:C_SEL2 + 8 * c + 8], sqs[c][:, :], c == 0, c == 3), r=[cmat.res, sqs[c].res], w=[pc.res])
                        rstd_ops(rc, pc, 8, 64.0)
                        for c in range(4):
                            pbc = nps()
                            op(PE, mm(pbc[:, :], cf32[0:8, F_SEL2T + 128 * c:F_SEL2T + 128 * c + 128], rc[0:8, :], True, True), r=[cf32.res, rc.res], w=[pbc.res])
                            o = outbuf()
                            op(DVE, stt(o[:, :], raws[c][:, :], pcols[:, gcol:gcol + 1], pbc[:, :], ALU.mult, ALU.mult), r=[raws[c].res, pcols.res, pbc.res], w=[o.res])
                            for m in range(2):
                                dma(STQ, dm(dst[c, m, 0:64, gs], o[64 * m:64 * m + 64, :]), r=[o.res])
                    wt = load_w(wview(win, 2560, 512))

                    def st_dv():
                        for h in range(4):
                            dma(STQ, dm(vd[h, :, 4 * g:4 * g + 4, :], vt[:, :, 128 * h:128 * h + 128]), r=[vt.res])
                    vproj(wt, 0, 512, xn, 8, st_dv)
                else:
                    pb = 64 + 32 * j
                    win = ev_in[j]
                    wt = load_w(wview(win, 0, 512))
                    pc = nps()
                    for c in range(4):
                        p = proj_chunk(wt, c * 128, 128, xn)
                        op(ACT, act(raws[c][:, :], p[:, :], AF.Copy), r=[p.res], w=[raws[c].res])
                        op(POOL, tt(sqs[c][:, :], raws[c][:, :], raws[c][:, :], ALU.mult), r=[raws[c].res], w=[sqs[c].res])
                    for c in range(4):
                        op(PE, mm(pc[0:8, :], cmat[:, C_SEL2 + 8 * c:C_SEL2 + 8 * c + 8], sqs[c][:, :], c == 0, c == 3), r=[cmat.res, sqs[c].res], w=[pc.res])
                    rstd_ops(rc, pc, 8, 64.0)
                    for c in range(4):
                        pbc = nps()
                        op(PE, mm(pbc[:, :], cf32[0:8, F_SEL2T + 128 * c:F_SEL2T + 128 * c + 128], rc[0:8, :], True, True), r=[cf32.res, rc.res], w=[pbc.res])
                        o = outbuf()
                        op(DVE, stt(o[:, :], raws[c][:, :], pcols[:, pb:pb + 1], pbc[:, :], ALU.mult, ALU.mult), r=[raws[c].res, pcols.res, pbc.res], w=[o.res])
                        for hh in range(2):
                            dma(STQ, dm(qa[2 * c + hh, :, gs], o[64 * hh:64 * hh + 64, :]), r=[o.res])
                    wt = load_w(wview(win, 512, 512))
                    p = proj_chunk(wt, 0, 128, xn)
                    op(ACT, act(raws[0][:, :], p[:, :], AF.Copy), r=[p.res], w=[raws[0].res])
                    op(POOL, tt(sqs[0][:, :], raws[0][:, :], raws[0][:, :], ALU.mult), r=[raws[0].res], w=[sqs[0].res])
                    pc = nps()
                    op(PE, mm(pc[0:8, :], cmat[:, C_SEL2:C_SEL2 + 8], sqs[0][:, :], True, True), r=[cmat.res, sqs[0].res], w=[pc.res])
                    rstd_ops(rc, pc, 8, 64.0)
                    pbc = nps()
                    op(PE, mm(pbc[:, :], cf32[0:8, F_SEL2T:F_SEL2T + 128], rc[0:8, :], True, True), r=[cf32.res, rc.res], w=[pbc.res])
                    o = outbuf()
                    op(DVE, stt(o[:, :], raws[0][:, :], pcols[:, pb + 1:pb + 2], pbc[:, :], ALU.mult, ALU.mult), r=[raws[0].res, pcols.res, pbc.res], w=[o.res])
                    for hh in range(2):
                        dma(STQ, dm(ka[hh, :, gs], o[64 * hh:64 * hh + 64, :]), r=[o.res])

                    def st_av():
                        for h in range(2):
                            dma(STQ, dm(va[h, :, 4 * g:4 * g + 4, :], vt[:, :, 64 * h:64 * h + 64]), r=[vt.res])
                    vproj(wt, 128, 128, xn, 8, st_av)
                    for c in range(2):
                        p = proj_chunk(wt, 256 + c * 128, 128, xn)
                        op(ACT, act(raws[c][:, :], p[:, :], AF.Copy), r=[p.res], w=[raws[c].res])
                        op(POOL, tt(sqs[c][:, :], raws[c][:, :], raws[c][:, :], ALU.mult), r=[raws[c].res], w=[sqs[c].res])
                    pc = nps()
                    for c in range(2):
                        op(PE, mm(pc[:, :], ONES(), sqs[c][:, :], c == 0, c == 1), r=[cmat.res, sqs[c].res], w=[pc.res])
                    rstd_ops(rstd, pc, 128, 256.0)
                    for c in range(2):
                        op(DVE, stt(cqn[:, c, :], raws[c][:, :], pcols[:, pb + 2 + c:pb + 3 + c], rstd[:, :], ALU.mult, ALU.mult), r=[raws[c].res, pcols.res, rstd.res], w=[cqn.res])
                    wt = load_w(wview(win, 1024, 160), 8, 160)
                    p = proj_chunk(wt, 0, 128, xn)
                    op(ACT, act(raws[0][:, :], p[:, :], AF.Copy), r=[p.res], w=[raws[0].res])
                    op(POOL, tt(sqs[0][:, :], raws[0][:, :], raws[0][:, :], ALU.mult), r=[raws[0].res], w=[sqs[0].res])
                    pc = nps()
                    op(PE, mm(pc[:, :], ONES(), sqs[0][:, :], True, True), r=[cmat.res, sqs[0].res], w=[pc.res])
                    rstd_ops(rstd, pc, 128, 128.0)
                    op(DVE, stt(ckvn[:, :], raws[0][:, :], pcols[:, pb + 4:pb + 5], rstd[:, :], ALU.mult, ALU.mult), r=[raws[0].res, pcols.res, rstd.res], w=[ckvn.res])
                    pkr = proj_chunk(wt, 128, 32, xn)
                    op(ACT, act(raws[6][0:32, :], pkr[0:32, :], AF.Copy), r=[pkr.res], w=[raws[6].res])
                    op(POOL, tt(sqs[6][0:32, :], raws[6][0:32, :], raws[6][0:32, :], ALU.mult), r=[raws[6].res], w=[sqs[6].res])
                    dma(SP, dm(cost[:, :], costab[:, gs]), w=[cost.res])
                    dma(SP, dm(sint[:, :], sintab[:, gs]), w=[sint.res])
                    op(DVE, ts(t1[0:32, :], raws[6][0:32, :], pcols[0:32, pb + 8:pb + 9], None, ALU.mult), r=[raws[6].res, pcols.res], w=[t1.res])
                    op(DVE, cp(sqs[7][0:32, :], t1[0:32, :]), r=[t1.res], w=[sqs[7].res])
                    prot = nps()
                    op(PE, mm(prot[0:32, :], cmat[0:32, C_PERM:C_PERM + 32], sqs[7][0:32, :], True, True), r=[cmat.res, sqs[7].res], w=[prot.res])
                    op(DVE, tt(t2[0:32, :], prot[0:32, :], sint[0:32, :], ALU.mult), r=[prot.res, sint.res], w=[t2.res])
                    op(DVE, tt(t1[0:32, :], t1[0:32, :], cost[0:32, :], ALU.mult), r=[t1.res, cost.res], w=[t1.res])
                    op(DVE, tt(krr[0:32, :], t1[0:32, :], t2[0:32, :], ALU.add), r=[t1.res, t2.res], w=[krr.res])
                    uqv = uq[j].rearrange("(i p) (h d) -> p i h d", p=128, d=96)
                    for i in range(2):
                        dma(SP, dm(wsm[:, i, 0:512].rearrange("p (h d) -> p h d", d=64), uqv[:, i, :, 0:64]), w=[wsm.res])
                        dma(SP, dm(wsm[:, i, 512:768].rearrange("p (h d) -> p h d", d=32), uqv[:, i, :, 64:96]), w=[wsm.res])
                    ukvv = ukv[j].rearrange("p (h d) -> p h d", d=128)
                    dma(SP, dm(wkv[:, 0:512].rearrange("p (h d) -> p h d", d=64), ukvv[:, :, 0:64]), w=[wkv.res])
                    dma(SP, dm(wkv[:, 512:1024].rearrange("p (h d) -> p h d", d=64), ukvv[:, :, 64:128]), w=[wkv.res])
                    for c in range(6):
                        p = nps()
                        for i in range(2):
                            op(PE, mm(p[:, :], wsm[:, i, c * 128:(c + 1) * 128], cqn[:, i, :], i == 0, i == 1), r=[wsm.res, cqn.res], w=[p.res])
                        op(ACT, act(raws[c][:, :], p[:, :], AF.Copy), r=[p.res], w=[raws[c].res])
                        op(POOL, tt(sqs[c][:, :], raws[c][:, :], raws[c][:, :], ALU.mult), r=[raws[c].res], w=[sqs[c].res])
                    pc = nps()
                    for c in range(6):
                        sel = cmat[:, C_SEL2 + 8 * c:C_SEL2 + 8 * c + 8] if c < 4 else cmat[:, C_SEL4 + 8 * (c - 4):C_SEL4 + 8 * (c - 4) + 8]
                        op(PE, mm(pc[0:8, :], sel, sqs[c][:, :], c == 0, c == 5), r=[cmat.res, sqs[c].res], w=[pc.res])
                    rstd_ops(rc, pc, 8, 96.0)
                    for c in range(6):
                        pbc = nps()
                        selT = cf32[0:8, F_SEL2T + 128 * c:F_SEL2T + 128 * c + 128] if c < 4 else cf32[0:8, F_SEL4T + 128 * (c - 4):F_SEL4T + 128 * (c - 4) + 128]
                        op(PE, mm(pbc[:, :], selT, rc[0:8, :], True, True), r=[cf32.res, rc.res], w=[pbc.res])
                        if c < 4:
                            o = outbuf()
                            op(DVE, stt(o[:, :], raws[c][:, :], pcols[:, pb + 5:pb + 6], pbc[:, :], ALU.mult, ALU.mult), r=[raws[c].res, pcols.res, pbc.res], w=[o.res])
                            for hh in range(2):
                                dma(STQ, dm(qb[2 * c + hh, 0:64, gs], o[64 * hh:64 * hh + 64, :]), r=[o.res])
                        else:
                            op(DVE, stt(t1[:, :], raws[c][:, :], pcols[:, pb + 6:pb + 7], pbc[:, :], ALU.mult, ALU.mult), r=[raws[c].res, pcols.res, pbc.res], w=[t1.res])
                            op(POOL, cp(sqs[7][:, :], t1[:, :]), r=[t1.res], w=[sqs[7].res])
                            prot = nps()
                            op(PE, mm(prot[:, :], cmat[:, C_PERM:C_PERM + 128], sqs[7][:, :], True, True), r=[cmat.res, sqs[7].res], w=[prot.res])
                            op(DVE, tt(t2[:, :], prot[:, :], sint[:, :], ALU.mult), r=[prot.res, sint.res], w=[t2.res])
                            op(DVE, tt(t1[:, :], t1[:, :], cost[:, :], ALU.mult), r=[t1.res, cost.res], w=[t1.res])
                            o = outbuf()
                            op(DVE, tt(o[:, :], t1[:, :], t2[:, :], ALU.add), r=[t1.res, t2.res], w=[o.res])
                            for hh in range(4):
                                dma(STQ, dm(qb[4 * (c - 4) + hh, 64:96, gs], o[32 * hh:32 * hh + 32, :]), r=[o.res])
                    for c in range(4):
                        p = nps()
                        op(PE, mm(p[:, :], wkv[:, c * 128:(c + 1) * 128], ckvn[:, :], True, True), r=[wkv.res, ckvn.res], w=[p.res])
                        op(ACT, act(raws[c][:, :], p[:, :], AF.Copy), r=[p.res], w=[raws[c].res])
                        op(POOL, tt(sqs[c][:, :], raws[c][:, :], raws[c][:, :], ALU.mult), r=[raws[c].res], w=[sqs[c].res])
                    pc = nps()
                    for c in range(4):
                        op(PE, mm(pc[0:8, :], cmat[:, C_SEL2 + 8 * c:C_SEL2 + 8 * c + 8], sqs[c][:, :], c == 0, False), r=[cmat.res, sqs[c].res], w=[pc.res])
                    op(PE, mm(pc[0:8, :], cmat[0:32, C_ONES:C_ONES + 8], sqs[6][0:32, :], False, True), r=[cmat.res, sqs[6].res], w=[pc.res])
                    rstd_ops(rc, pc, 8, 96.0)
                    for c in range(4):
                        pbc = nps()
                        op(PE, mm(pbc[:, :], cf32[0:8, F_SEL2T + 128 * c:F_SEL2T + 128 * c + 128], rc[0:8, :], True, True), r=[cf32.res, rc.res], w=[pbc.res])
                        o = outbuf()
                        op(DVE, stt(o[:, :], raws[c][:, :], pcols[:, pb + 7:pb + 8], pbc[:, :], ALU.mult, ALU.mult), r=[raws[c].res, pcols.res, pbc.res], w=[o.res])
                        for hh in range(2):
                            dma(STQ, dm(kb_[2 * c + hh, 0:64, gs], o[64 * hh:64 * hh + 64, :]), r=[o.res])
                    for h in range(8):
                        pbc = nps()
                        op(PE, mm(pbc[0:32, :], cf32[0:8, F_KSELT + 32 * h:F_KSELT + 32 * h + 32], rc[0:8, :], True, True), r=[cf32.res, rc.res], w=[pbc.res])
                        o = outbuf()
                        op(DVE, tt(o[0:32, :], krr[0:32, :], pbc[0:32, :], ALU.mult), r=[krr.res, pbc.res], w=[o.res])
                        dma(STQ, dm(kb_[h, 64:96, gs], o[0:32, :]), r=[o.res])
                    for tb in range(4):
                        p = nps()
                        op(PE, mm(p[:, :], ckvn[:, tb * 128:(tb + 1) * 128], wkv[:, 512:1024], True, True), r=[wkv.res, ckvn.res], w=[p.res])
                        op(ACT, act(vt[:, tb, :], p[:, :], AF.Copy), r=[p.res], w=[vt.res])
                    for h in range(8):
                        dma(STQ, dm(vb[h, :, 4 * g:4 * g + 4, :], vt[:, :, 64 * h:64 * h + 64]), r=[vt.res])
        sch.barrier()

    def attn_even(j):
        pb = 64 + 32 * j
        with ExitStack() as st:
            KT = [sb(st, f"KT{i}", [128, S], BF16) for i in range(2)]
            V = [sb(st, f"V{i}", [128, NB, 128], BF16) for i in range(2)]
            for v_ in V:
                op(POOL, lambda e, v_=v_: e.memset(v_[:, :, 64:128], 1.0), w=[v_.res])
            tmpf = [sb(st, f"tmpf{i}", [128, 512], F32) for i in range(2)]
            STQA = POOL
            Q = [sb(st, f"Q{i}", [128, 512], BF16) for i in range(3)]
            Pb = [sb(st, f"P{i}", [128, 512], BF16) for i in range(6)]
            sbb = [sb(st, f"sbb{i}", [128, 256], F32) for i in range(4)]
            rec = sb(st, "rec", [64, 512], F32)
            ob = [sb(st, f"oo{i}", [64, 512], BF16) for i in range(2)]
            O, L = ps[6], ps[7]
            qctr = [0]
            octr = [0]
            tctr = [0]
            for kvh in range(2):
                kt, v = KT[kvh], V[kvh]
                dma(SP, dm(kt[0:64, :], ka[kvh]), w=[kt.res])
                dma(SP, dm(v[:, :, 0:64], va[kvh]), w=[v.res])
                for hq in range(4):
                    h = 4 * kvh + hq
                    for g in range(NG):
                        q = Q[qctr[0] % 3]
                        qctr[0] += 1
                        dma(SP, dm(q[0:64, :], qa[h, :, g * 512:(g + 1) * 512]), w=[q.res])
                        tiles = []
                        rels = list(range(-1, 4)) if g > 0 else list(range(0, 4))
                        for idx, rel in enumerate(rels):
                            kbi = 4 * g + rel
                            if rel < 0:
                                q0, n, boff = 0, 128, 0
                            else:
                                q0 = 128 * rel
                                n = min(256, 512 - q0)
                                boff = 128
                            t = tctr[0]
                            tctr[0] += 1
                            sp_, P_, sb_ = ps[t % 4], Pb[t % 6], sbb[t % 4]
                            first, last = idx == 0, idx == len(rels) - 1

                            def s1(kbi=kbi, q0=q0, n=n, sp_=sp_, q=q, kt=kt):
                                op(PE, mm(sp_[:, 0:n], kt[0:64, kbi * 128:(kbi + 1) * 128], q[0:64, q0:q0 + n], True, True), r=[kt.res, q.res], w=[sp_.res])

                            def s2(n=n, sp_=sp_, sb_=sb_, P_=P_, boff=boff, h=h):
                                op(DVE, stt(sb_[:, 0:n], sp_[:, 0:n], 0.125, cf32[:, F_ABIAS + 384 * h + boff:F_ABIAS + 384 * h + boff + n], ALU.mult, ALU.add),
                                   r=[sp_.res, cf32.res], w=[sb_.res])
                                op(ACT, act(P_[:, 0:n], sb_[:, 0:n], AF.Exp), r=[sb_.res], w=[P_.res])

                            def s3(kbi=kbi, q0=q0, n=n, P_=P_, v=v, first=first, last=last):
                                op(PE, mm(O[0:64, q0:q0 + n], v[:, kbi, 0:64], P_[:, 0:n], first, last), r=[v.res, P_.res], w=[O.res])
                                op(PE, mm(L[0:64, q0:q0 + n], ONES(128, 64), P_[:, 0:n], first, last), r=[cmat.res, P_.res], w=[L.res])
                            tiles.append([s1, s2, s3])
                        pipeline(tiles, [0, 2, 4])
                        o = ob[octr[0] % 2]
                        octr[0] += 1
                        op(DVE, ts(rec[:, :], L[0:64, :], dcol[0:64, 4 + 8 * j + h:5 + 8 * j + h], None, ALU.add), r=[L.res, dcol.res], w=[rec.res])
                        op(DVE, lambda e: e.reciprocal(out=rec[:, :], in_=rec[:, :]), r=[rec.res], w=[rec.res])
                        op(DVE, tt(o[:, :], O[0:64, :], rec[:, :], ALU.mult), r=[O.res, rec.res], w=[o.res])
                        dma(STQA, dm(mixT[64 * h:64 * h + 64, g * 512:(g + 1) * 512], o[:, :]), r=[o.res])
            scaleB = 96.0 ** -0.5
            for h in range(8):
                kt, v = KT[h % 2], V[h % 2]
                dma(SP, dm(kt[0:96, :], kb_[h]), w=[kt.res])
                dma(SP, dm(v[:, :, 0:64], vb[h]), w=[v.res])
                for g in range(NG):
                    q = Q[qctr[0] % 3]
                    qctr[0] += 1
                    dma(SP, dm(q[0:96, :], qb[h, :, g * 512:(g + 1) * 512]), w=[q.res])
                    OL = ps[6 + (octr[0] % 2)]
                    tiles = []
                    nkb = 4 * g + 4
                    for kbi in range(nkb):
                        rel = kbi - 4 * g
                        q0 = 128 * rel if rel > 0 else 0
                        n = 512 - q0
                        t = tctr[0]
                        tctr[0] += 1
                        sp_, P_ = ps[t % 4], Pb[t % 6]
                        first, last = kbi == 0, kbi == nkb - 1

                        def s1(kbi=kbi, q0=q0, n=n, sp_=sp_, q=q, kt=kt):
                            op(PE, mm(sp_[:, 0:n], kt[0:96, kbi * 128:(kbi + 1) * 128], q[0:96, q0:q0 + n], True, True), r=[kt.res, q.res], w=[sp_.res])

                        def s2(n=n, sp_=sp_, P_=P_, rel=rel):
                            op(ACT, act(P_[:, 0:n], sp_[:, 0:n], AF.Exp, scale=scaleB), r=[sp_.res], w=[P_.res])
                            if rel >= 0:
                                op(DVE, tt(P_[:, 0:128], P_[:, 0:128], cmat[:, C_DFB:C_DFB + 128], ALU.mult), r=[P_.res, cmat.res], w=[P_.res])

                        def s3(kbi=kbi, q0=q0, n=n, P_=P_, v=v, first=first, last=last, OL=OL):
                            op(PE, mm(OL[:, q0:q0 + n], v[:, kbi, :], P_[:, 0:n], first, last), r=[v.res, P_.res], w=[OL.res])
                        tiles.append([s1, s2, s3])
                    pipeline(tiles, [0, 2, 4])
                    o = ob[octr[0] % 2]
                    tf = tmpf[octr[0] % 2]
                    octr[0] += 1
                    op(DVE, cp(tf[:, :], OL[:, :]), r=[OL.res], w=[tf.res])
                    op(PE, mm(ps[5][0:64, :], cf32[:, F_SHIFT:F_SHIFT + 64], tf[:, :], True, True), r=[cf32.res, tf.res], w=[ps[5].res])
                    op(DVE, lambda e: e.reciprocal(out=rec[:, :], in_=ps[5][0:64, :]), r=[ps[5].res], w=[rec.res])
                    op(DVE, tt(o[:, :], tf[0:64, :], rec[:, :], ALU.mult), r=[tf.res, rec.res], w=[o.res])
                    dma(STQA, dm(mixT[512 + 64 * h:512 + 64 * h + 64, g * 512:(g + 1) * 512], o[:, :]), r=[o.res])
        sch.barrier()

    def attn_odd(j):
        with ExitStack() as st:
            KT = [sb(st, f"KT{i}", [128, S], BF16) for i in range(3)]
            V = [sb(st, f"V{i}", [128, NB, 128], BF16) for i in range(2)]
            Q = [sb(st, f"Q{i}", [128, 512], BF16) for i in range(4)]
            Pb = [sb(st, f"P{i}", [128, 512], BF16) for i in range(6)]
            ef = [sb(st, f"ef{i}", [128, 512], F32) for i in range(3)]
            spb = [sb(st, f"spb{i}", [128, 512], BF16) for i in range(3)]
            R32 = sb(st, "R32", [128, 512], F32)
            Rb = [sb(st, f"Rb{i}", [128, 512], BF16) for i in range(2)]
            ob = [sb(st, f"oo{i}", [128, 512], BF16) for i in range(2)]
            f1 = sb(st, "f1", [128, 512], F32)
            f2 = sb(st, "f2", [128, 512], F32)
            f3 = sb(st, "f3", [128, 512], F32)
            fsq = sb(st, "fsq", [128, 512], BF16)
            qctr = [0]
            octr = [0]
            tctr = [0]
            zw = [PsView(psall, 1024 * k, 1024) for k in range(3)]
            Ow = PsView(psall, 3072, 1024)
            efw = [sb(st, f"efw{i}", [128, 1024], F32) for i in range(2)]
            spw = [sb(st, f"spw{i}", [128, 1024], BF16) for i in range(3)]
            aw = [sb(st, f"aw{i}", [128, 1024], BF16) for i in range(3)]
            R32w = sb(st, "R32w", [128, 1024], F32)
            Rbw = [sb(st, f"Rbw{i}", [128, 1024], BF16) for i in range(2)]
            qws = [sb(st, f"qw{i}", [64, 1024], BF16) for i in range(2)]
            obw = [sb(st, f"obw{i}", [64, 1024], BF16) for i in range(2)]
            for h in range(8):
                kt, v = KT[h % 2], V[h % 2]
                dma(SP, dm(kt[0:64, :], kc[h]), w=[kt.res])
                dma(SP, dm(v[:, :, 0:64], vc[h]), w=[v.res])
                for G in range(NG // 2):
                    g = 2 * G
                    q = qws[qctr[0] % 2]
                    qctr[0] += 1
                    dma(SP, dm(q[0:64, :], qc[h, :, g * 512:(g + 2) * 512]), w=[q.res])
                    tiles = []
                    nkb = 4 * g + 8
                    order = list(range(nkb - 1, -1, -1))
                    for idx, kbi in enumerate(order):
                        if kbi >= 4 * g + 4:
                            c0, diag = 512 + 128 * (kbi - 4 * g - 4), True
                        elif kbi >= 4 * g:
                            c0, diag = 128 * (kbi - 4 * g), True
                        else:
                            c0, diag = 0, False
                        pieces = ([(c0, 512)] if c0 < 512 else []) + [(max(c0, 512), 1024)]
                        t = tctr[0]
                        tctr[0] += 1
                        zp, e_, s_, a_ = zw[t % 3], efw[t % 2], spw[t % 3], aw[t % 3]
                        rb_r, rb_w = Rbw[idx % 2], Rbw[(idx + 1) % 2]
                        first, last = idx == 0, idx == len(order) - 1
                        firstA = kbi == 4 * g + 3

                        def s1(kbi=kbi, pieces=pieces, zp=zp, q=q, kt=kt):
                            for (a_c, b_c) in pieces:
                                op(PE, mm(zp[:, a_c:b_c], kt[0:64, kbi * 128:(kbi + 1) * 128], q[0:64, a_c:b_c], True, False), r=[kt.res, q.res], w=[zp.res])

                        def s2a(c0=c0, zp=zp, e_=e_):
                            op(ACT, act(e_[:, c0:1024], zp[:, c0:1024], AF.Exp), r=[zp.res], w=[e_.res])

                        def s2(c0=c0, e_=e_, s_=s_, diag=diag):
                            op(ACT, act(s_[:, c0:1024], e_[:, c0:1024], AF.Ln, bias=onec[:, :]), r=[e_.res, onec.res], w=[s_.res])
                            if diag:
                                op(DVE, tt(s_[:, c0:c0 + 128], s_[:, c0:c0 + 128], cmat[:, C_DFC:C_DFC + 128], ALU.mult), r=[s_.res, cmat.res], w=[s_.res])

                        def s3(c0=c0, pieces=pieces, zp=zp, s_=s_, rb_r=rb_r, rb_w=rb_w, first=first, last=last):
                            for (a_c, b_c) in pieces:
                                op(PE, mm(zp[:, a_c:b_c], cmat[:, C_NEGU:C_NEGU + 128], s_[:, a_c:b_c], False, first), r=[cmat.res, s_.res], w=[zp.res])
                                if not first:
                                    op(PE, mm(zp[:, a_c:b_c], cmat[:, C_NEGONES:C_NEGONES + 128], rb_r[:, a_c:b_c], False, True), r=[cmat.res, rb_r.res], w=[zp.res])
                            if not last:
                                if first:
                                    op(POOL, lambda e: e.memset(R32w[:, :], 0.0), w=[R32w.res])
                                op(POOL, tt(R32w[:, c0:1024], R32w[:, c0:1024], s_[:, c0:1024], ALU.add), r=[R32w.res, s_.res], w=[R32w.res])
                                op(DVE, cp(rb_w[:, :], R32w[:, :]), r=[R32w.res], w=[rb_w.res])

                        def s4(c0=c0, zp=zp, a_=a_, diag=diag):
                            op(ACT, act(a_[:, c0:1024], zp[:, c0:1024], AF.Exp), r=[zp.res], w=[a_.res])
                            if diag:
                                op(DVE, tt(a_[:, c0:c0 + 128], a_[:, c0:c0 + 128], cmat[:, C_DFC:C_DFC + 128], ALU.mult), r=[a_.res, cmat.res], w=[a_.res])

                        def s5(kbi=kbi, pieces=pieces, a_=a_, v=v, first=first, firstA=firstA, last=last):
                            for (a_c, b_c) in pieces:
                                st_ = firstA if a_c < 512 else first
                                op(PE, mm(Ow[0:64, a_c:b_c], v[:, kbi, 0:64], a_[:, a_c:b_c], st_, last), r=[v.res, a_.res], w=[Ow.res])
                        tiles.append([s1, s2a, s2, s3, s4, s5])
                    pipeline(tiles, [0, 1, 1, 2, 2, 3])
                    o = obw[octr[0] % 2]
                    octr[0] += 1
                    op(DVE, cp(o[0:64, :], Ow[0:64, :]), r=[Ow.res], w=[o.res])
                    dma(POOL, dm(mixT[64 * h:64 * h + 64, g * 512:(g + 2) * 512], o[0:64, :]), r=[o.res])
            sch.barrier()
            O1, O2 = ps[6], ps[7]
            Lacc = [sb(st, f"Lacc{i}", [128, 512], F32) for i in range(2)]
            for h in range(4):
                k1, k2, v = KT[0], KT[1], V[h % 2]
                dma(SP, dm(k1[0:70, :], kd[h, 0]), w=[k1.res])
                dma(SP, dm(k2[0:70, :], kd[h, 1]), w=[k2.res])
                dma(SP, dm(v[:, :, :], vd[h]), w=[v.res])
                for g in range(NG):
                    q1 = Q[qctr[0] % 4]
                    q2 = Q[(qctr[0] + 1) % 4]
                    qctr[0] += 2
                    dma(SP, dm(q1[0:70, :], qd[h, 0, :, g * 512:(g + 1) * 512]), w=[q1.res])
                    dma(SP, dm(q2[0:70, :], qd[h, 1, :, g * 512:(g + 1) * 512]), w=[q2.res])
                    op(DVE, lambda e: e.memset(Lacc[0][:, :], 0.0), w=[Lacc[0].res])
                    op(POOL, lambda e: e.memset(Lacc[1][:, :], 0.0), w=[Lacc[1].res])
                    tiles = []
                    nkb = 4 * g + 4
                    for kbi in range(nkb):
                        rel = kbi - 4 * g
                        q0 = 128 * rel if rel > 0 else 0
                        n = 512 - q0
                        t = tctr[0]
                        tctr[0] += 1
                        sa, sb2 = ps[(2 * t) % 6], ps[(2 * t + 1) % 6]
                        Pa, Pb2 = Pb[(2 * t) % 6], Pb[(2 * t + 1) % 6]
                        first, last = kbi == 0, kbi == nkb - 1

                        def s1(kbi=kbi, q0=q0, n=n, sa=sa, sb2=sb2, q1=q1, q2=q2):
                            op(PE, mm(sa[:, 0:n], k1[0:70, kbi * 128:(kbi + 1) * 128], q1[0:70, q0:q0 + n], True, True), r=[k1.res, q1.res], w=[sa.res])
                            op(PE, mm(sb2[:, 0:n], k2[0:70, kbi * 128:(kbi + 1) * 128], q2[0:70, q0:q0 + n], True, True), r=[k2.res, q2.res], w=[sb2.res])

                        def s2(n=n, q0=q0, sa=sa, sb2=sb2, Pa=Pa, Pb2=Pb2, rel=rel, h=h):
                            for s_, p_, eng, la in ((sa, Pa, DVE, Lacc[0]), (sb2, Pb2, POOL, Lacc[1])):
                                op(ACT, act(p_[:, 0:n], s_[:, 0:n], AF.Exp, scale=0.125), r=[s_.res], w=[p_.res])
                                if rel >= 0:
                                    op(DVE, tt(p_[:, 0:128], p_[:, 0:128], cmat[:, C_DFD + 128 * h:C_DFD + 128 * h + 128], ALU.mult), r=[p_.res, cmat.res], w=[p_.res])
                                op(eng, tt(la[:, q0:q0 + n], la[:, q0:q0 + n], p_[:, 0:n], ALU.add), r=[la.res, p_.res], w=[la.res])

                        def s3(kbi=kbi, q0=q0, n=n, Pa=Pa, Pb2=Pb2, v=v, first=first, last=last):
                            op(PE, mm(O1[:, q0:q0 + n], v[:, kbi, :], Pa[:, 0:n], first, last), r=[v.res, Pa.res], w=[O1.res])
                            op(PE, mm(O2[:, q0:q0 + n], v[:, kbi, :], Pb2[:, 0:n], first, last), r=[v.res, Pb2.res], w=[O2.res])
                        tiles.append([s1, s2, s3])
                    pipeline(tiles, [0, 1, 2])
                    L1, L2 = ps[(2 * tctr[0]) % 6], ps[(2 * tctr[0] + 1) % 6]
                    op(PE, mm(L1[:, :], cf32[:, F_ONES:F_ONES + 128], Lacc[0][:, :], True, True), r=[cf32.res, Lacc[0].res], w=[L1.res])
                    op(PE, mm(L2[:, :], cf32[:, F_ONES:F_ONES + 128], Lacc[1][:, :], True, True), r=[cf32.res, Lacc[1].res], w=[L2.res])
                    op(DVE, lambda e, L1=L1: e.reciprocal(out=f1[:, :], in_=L1[:, :]), r=[L1.res], w=[f1.res])
                    op(DVE, tt(f1[:, :], O1[:, :], f1[:, :], ALU.mult), r=[O1.res, f1.res], w=[f1.res])
                    op(DVE, lambda e, L2=L2: e.reciprocal(out=f2[:, :], in_=L2[:, :]), r=[L2.res], w=[f2.res])
                    op(DVE, tt(f2[:, :], O2[:, :], f2[:, :], ALU.mult), r=[O2.res, f2.res], w=[f2.res])
                    op(DVE, stt(f3[:, :], f2[:, :], dcol[:, j:j + 1], f1[:, :], ALU.mult, ALU.add), r=[f2.res, dcol.res, f1.res], w=[f3.res])
                    op(POOL, tt(fsq[:, :], f3[:, :], f3[:, :], ALU.mult), r=[f3.res], w=[fsq.res])
                    pn = ps[(2 * tctr[0] + 2) % 6]
                    op(PE, mm(pn[:, :], ONES(), fsq[:, :], True, True), r=[cmat.res, fsq.res], w=[pn.res])
                    rstd_ops(f1, pn, 128, 128.0)
                    o = ob[octr[0] % 2]
                    octr[0] += 1
                    op(DVE, stt(o[:, :], f3[:, :], dcol[:, 2 + j:3 + j], f1[:, :], ALU.mult, ALU.mult), r=[f3.res, dcol.res, f1.res], w=[o.res])
                    dma(POOL, dm(mixT[512 + 128 * h:512 + 128 * h + 128, g * 512:(g + 1) * 512], o[:, :]), r=[o.res])
        sch.barrier()

    stop = int(os.environ.get("KSTOP", "99"))
    cnt = 0
    for l in range(DEPTH + 1):
        if cnt >= stop:
            break
        proj_phase(l)
        cnt += 1
        if l < DEPTH:
            if cnt >= stop:
                break
            if l % 2 == 0:
                attn_even(l // 2)
            else:
                attn_odd(l // 2)
            cnt += 1
    sch.barrier()

    with nc.Block() as block:
        @block.tensor
        def _(e):
            sch.emit(PE, e)

        @block.scalar
        def _(e):
            sch.emit(ACT, e)

        @block.vector
        def _(e):
            sch.emit(DVE, e)

        @block.gpsimd
        def _(e):
            sch.emit(POOL, e)

        @block.sync
        def _(e):
            sch.emit(SP, e)
    stack.close()
    return nc


def host_consts(S):
    bf = ml_dtypes.bfloat16
    cm = np.zeros((128, NCM), np.float32)
    cm[:, C_ONES:C_ONES + 128] = 1.0
    for m in range(128):
        if m % 32 < 16:
            cm[m + 16, C_PERM + m] = -1.0
        else:
            cm[m - 16, C_PERM + m] = 1.0
    jj, kk = np.meshgrid(np.arange(128), np.arange(128), indexing="ij")
    cm[:, C_NEGU:C_NEGU + 128] = -(jj >= kk).astype(np.float32)
    cm[:, C_NEGONES:C_NEGONES + 128] = -1.0
    k_, q_ = jj, kk
    cm[:, C_DFB:C_DFB + 128] = ((k_ // 64) <= (q_ // 64)).astype(np.float32)
    cm[:, C_DFC:C_DFC + 128] = (k_ < q_).astype(np.float32)
    for h in range(4):
        m = 2.0 ** (-8.0 * (h + 1) / 4)
        d = np.where(k_ <= q_, 1.0, np.where((k_ // 64) == (q_ // 64), np.exp(-2.0 * m * (k_ - q_)), 0.0))
        cm[:, C_DFD + 128 * h:C_DFD + 128 * h + 128] = d
    p = np.arange(128)
    for c in range(4):
        cm[p, C_SEL2 + 8 * c + 2 * c + p // 64] = 1.0
    for r in range(2):
        cm[p, C_SEL4 + 8 * r + 4 * r + p // 32] = 1.0
    cf = np.zeros((128, NCF), np.float32)
    for c in range(4):
        cf[2 * c + p // 64, F_SEL2T + 128 * c + p] = 1.0
    for r in range(2):
        cf[4 * r + p // 32, F_SEL4T + 128 * r + p] = 1.0
    for h in range(8):
        cf[h, F_KSELT + 32 * h:F_KSELT + 32 * h + 32] = 1.0
    for h in range(8):
        m = 2.0 ** (-8.0 * (h + 1) / 8)
        kpos = np.arange(128)[:, None] - 128
        qpos = np.arange(128)[None, :]
        dch = qpos // 64 - np.floor_divide(kpos, 64)
        ok = (dch >= 0) & (dch <= 2)
        cf[:, F_ABIAS + 384 * h:F_ABIAS + 384 * h + 128] = np.where(ok, -m * np.abs(qpos - kpos), -30000.0)
        kpos = np.arange(128)[:, None]
        qpos = np.arange(256)[None, :]
        dch = qpos // 64 - kpos // 64
        ok = (dch >= 0) & (dch <= 2)
        cf[:, F_ABIAS + 384 * h + 128:F_ABIAS + 384 * h + 384] = np.where(ok, -m * np.abs(qpos - kpos), -30000.0)
    half = 16
    inv = (np.float32(10000.0) ** (-np.arange(half, dtype=np.float32) / np.float32(half))).astype(np.float32)
    cf[:, F_INVF] = inv[p % 16]
    for m in range(64):
        cf[64 + m, F_SHIFT + m] = 1.0
    cf[:, F_ONES:F_ONES + 128] = 1.0
    pos = np.arange(S)
    a, b, c = pos // 1024, (pos % 1024) // 32, pos % 32
    daug = np.zeros((4, 2, 6, S), np.float32)
    for h in range(4):
        m = 2.0 ** (-8.0 * (h + 1) / 4) * 8.0
        daug[h, 0, 0], daug[h, 0, 1], daug[h, 0, 2] = -m * 1024 * a, -m * 32 * b, -m * c
        daug[h, 0, 3:6] = 1.0
        daug[h, 1, 0:3] = 1.0
        daug[h, 1, 3], daug[h, 1, 4], daug[h, 1, 5] = m * 1024 * a, m * 32 * b, m * c
    return cm.astype(bf), cf, daug.astype(bf)


def host_pcols(inp):
    pc = np.zeros((128, NPC), np.float32)
    p = np.arange(128)
    for l in range(4):
        pc[:, l * 16:l * 16 + 8] = inp["norm_mix_g"][l].reshape(8, 128).T
        pc[:, l * 16 + 8:l * 16 + 16] = inp["norm_ffn_g"][l].reshape(8, 128).T
    for j in range(2):
        b = 64 + 32 * j
        pc[:, b + 0] = inp["a_q_norm"][j][p % 64]
        pc[:, b + 1] = inp["a_k_norm"][j][p % 64]
        pc[:, b + 2:b + 4] = inp["b_cq_norm"][j].reshape(2, 128).T
        pc[:, b + 4] = inp["b_ckv_norm"][j]
        pc[:, b + 5] = inp["b_q_norm"][j][p % 64]
        pc[:, b + 6] = inp["b_q_norm"][j][64 + p % 32]
        pc[:, b + 7] = inp["b_k_norm"][j][p % 64]
        pc[:, b + 8] = inp["b_k_norm"][j][64 + p % 32]
        pc[:, b + 9:b + 17] = inp["a_sinks"][j][None, :]
        b = 128 + 32 * j
        pc[:, b + 0] = inp["d_q_norm"][j].reshape(128)
        pc[:, b + 1] = inp["d_k_norm"][j].reshape(128)
        pc[:, b + 2] = inp["d_subln"][j]
    dlam = np.broadcast_to(inp["d_lambda"].reshape(1, 2, 256), (128, 2, 256)).astype(np.float32)
    return pc, np.ascontiguousarray(dlam)


_CACHE = {}


def kernel(**inputs):
    inp = {k: np.asarray(v) for k, v in inputs.items()}
    x = inp["x"]
    B, S, D = x.shape
    if S not in _CACHE:
        _CACHE[S] = build(S)
    nc = _CACHE[S]
    cm, cf, daug = host_consts(S)
    pc, dlam = host_pcols(inp)
    shared = {
        "pcols": pc, "dlam": dlam, "cmat": cm, "cf32": cf, "daug": daug,
        "mlp_w_up": inp["mlp_w_up"], "mlp_w_down": inp["mlp_w_down"],
        "ev_w_in": inp["ev_w_in"], "ev_w_out": inp["ev_w_out"],
        "b_w_uq": inp["b_w_uq"], "b_w_ukv": inp["b_w_ukv"],
        "od_w_in": inp["od_w_in"], "od_w_out": inp["od_w_out"],
    }
    in_maps = []
    for b in range(B):
        m = dict(shared)
        m["xT"] = np.ascontiguousarray(x[b].T)
        m["posb"] = np.ascontiguousarray(np.broadcast_to(inp["positions"][b][None, :], (128, S))).astype(np.int32)
        in_maps.append(m)
    res = run_bass_kernel_spmd(nc, in_maps, core_ids=list(range(B)))
    out = np.stack([np.ascontiguousarray(r["yT"].T) for r in res.results], axis=0)
    return out.astype(np.float32)
```

```python
import math
import os
from contextlib import ExitStack

import numpy as np
import ml_dtypes
import concourse.bass as bass
import concourse.mybir as mybir
from concourse.bass_utils import run_bass_kernel_spmd

F32, BF16, I32 = mybir.dt.float32, mybir.dt.bfloat16, mybir.dt.int32
AF = mybir.ActivationFunctionType
ALU = mybir.AluOpType
AX = mybir.AxisListType
PE, ACT, DVE, POOL, SP = 0, 1, 2, 3, 4
EPS = 1e-6
DEPTH = 4
TWO_PI = 2.0 * math.pi

C_ONES, C_PERM, C_NEGU, C_NEGONES, C_DFB, C_DFC, C_DFD, C_SEL2, C_SEL4 = 0, 128, 256, 384, 512, 640, 768, 1280, 1312
NCM = 1328
F_SEL2T, F_SEL4T, F_KSELT, F_ABIAS, F_INVF, F_SHIFT, F_ONES = 0, 512, 768, 1024, 4096, 4097, 4161
NCF = 4289
NPC = 192


class Res:
    __slots__ = ("lw", "rd", "sem", "cnt", "ssem", "scnt")

    def __init__(self):
        self.lw = None
        self.rd = []
        self.sem = None
        self.cnt = 0
        self.ssem = None
        self.scnt = 0


class Sched:
    def __init__(self, nc, stack):
        self.nc = nc
        self.stack = stack
        self.q = [[] for _ in range(5)]
        self.seq = [0] * 5
        self.esem = [stack.enter_context(nc.semaphore(f"es{i}")) for i in range(5)]
        self.waited = [dict() for _ in range(5)]
        self.dsems = []
        self.free_dsems = []
        self.free_ssems = []
        self.ssems = []
        self.semcnt = {}
        self.nds = 0

    def _waits(self, e, deps):
        best = {}
        for (sem, val, src) in deps:
            if e == PE and src == PE:
                continue
            k = id(sem)
            if self.waited[e].get(k, 0) >= val:
                continue
            if k not in best or best[k][1] < val:
                best[k] = (sem, val)
        out = []
        for k, (sem, val) in best.items():
            self.waited[e][k] = val
            out.append((sem, val))
        return out

    def _deps(self, r, w, e=-9):
        deps = []
        for x in r:
            if x.lw is not None:
                deps.append(x.lw)
        for x in w:
            if x.lw is not None:
                deps.append(x.lw)
            deps.extend(x.rd)
        return deps

    def op(self, e, fn, r=(), w=()):
        waits = self._waits(e, self._deps(r, w, e))
        self.seq[e] += 1
        tok = (self.esem[e], self.seq[e], e)
        self.q[e].append((waits, fn, (self.esem[e], 1), True))
        for x in r:
            x.rd.append(tok)
        for x in w:
            x.lw = tok
            x.rd = []

    def dma(self, qe, fn, r=(), w=(), semres=None):
        waits = self._waits(qe, self._deps(r, w))
        sr = semres if semres is not None else (w[0] if w else r[0])
        sw = qe == POOL
        sa, ca = ("ssem", "scnt") if sw else ("sem", "cnt")
        free = self.free_ssems if sw else self.free_dsems
        if getattr(sr, sa) is None:
            if free:
                sem = free.pop()
            else:
                self.nds += 1
                sem = self.stack.enter_context(self.nc.semaphore(f"ds{self.nds}"))
            setattr(sr, sa, sem)
            setattr(sr, ca, self.semcnt.get(id(sem), 0))
            (self.ssems if sw else self.dsems).append(sr)
        sem = getattr(sr, sa)
        cnt = getattr(sr, ca) + 16
        setattr(sr, ca, cnt)
        self.semcnt[id(sem)] = cnt
        tok = (sem, cnt, -1)
        self.q[qe].append((waits, fn, (sem, 16), False))
        for x in r:
            x.rd.append(tok)
        for x in w:
            x.lw = tok
            x.rd = []

    def barrier(self):
        toks = [(self.esem[i], self.seq[i], i) for i in range(5) if self.seq[i] > 0]
        toks += [(sr.sem, sr.cnt, -1) for sr in self.dsems]
        toks += [(sr.ssem, sr.scnt, -1) for sr in self.ssems]
        for e in range(5):
            deps = [t for t in toks if t[2] != e]
            waits = self._waits(e, [(s, v, -2) for (s, v, _) in deps])
            if waits:
                self.q[e].append((waits, None, None, False))
        for sr in self.dsems:
            self.free_dsems.append(sr.sem)
            sr.sem = None
        self.dsems = []
        for sr in self.ssems:
            self.free_ssems.append(sr.ssem)
            sr.ssem = None
        self.ssems = []

    def emit(self, e, eng):
        for (waits, fn, inc, attach) in self.q[e]:
            if fn is None:
                for (sem, val) in waits:
                    eng.wait_ge(sem, val)
                continue
            if attach and waits:
                for (sem, val) in waits[:-1]:
                    eng.wait_ge(sem, val)
                ins = fn(eng)
                ins._wait_ge(*waits[-1])
            else:
                for (sem, val) in waits:
                    eng.wait_ge(sem, val)
                ins = fn(eng)
            ins.then_inc(inc[0], inc[1])


def pipeline(tiles, skews):
    T = len(tiles)
    mx = max(skews)
    for i in range(T + mx):
        for s, sk in enumerate(skews):
            t = i - sk
            if 0 <= t < T and tiles[t][s] is not None:
                tiles[t][s]()


class T:
    __slots__ = ("h", "res")

    def __init__(self, h):
        self.h = h
        self.res = Res()

    def __getitem__(self, k):
        return self.h[k]


class PsView:
    __slots__ = ("h", "off", "w", "res")

    def __init__(self, h, off, w):
        self.h, self.off, self.w = h, off, w
        self.res = Res()

    def __getitem__(self, k):
        rs, cs = k
        a = 0 if cs.start is None else cs.start
        b = self.w if cs.stop is None else cs.stop
        return self.h[rs, self.off + a:self.off + b]


def build(S):
    NG = S // 512
    NB = S // 128
    nc = bass.Bass("TRN2", target_bir_lowering=False)

    def din(name, shape, dt=F32):
        return nc.dram_tensor(name, list(shape), dt, kind="ExternalInput").ap()

    def dscr(name, shape, dt=BF16):
        return nc.dram_tensor(name, list(shape), dt).ap()

    xT = din("xT", [1024, S])
    posb = din("posb", [128, S], I32)
    pcols_d = din("pcols", [128, NPC])
    dlam_d = din("dlam", [128, 2, 256])
    cmat_d = din("cmat", [128, NCM], BF16)
    cf32_d = din("cf32", [128, NCF])
    daug_d = din("daug", [4, 2, 6, S], BF16)
    w_up_d = din("mlp_w_up", [4, 1024, 4096])
    w_down_d = din("mlp_w_down", [4, 4096, 1024])
    ev_in_d = din("ev_w_in", [2, 1024, 1184])
    ev_out_d = din("ev_w_out", [2, 1024, 1024])
    uq_d = din("b_w_uq", [2, 256, 768])
    ukv_d = din("b_w_ukv", [2, 128, 1024])
    od_in_d = din("od_w_in", [2, 1024, 3072])
    od_out_d = din("od_w_out", [2, 1024, 1024])
    yT = nc.dram_tensor("yT", [1024, S], F32, kind="ExternalOutput").ap()

    w_up = dscr("s_w_up", [4, 1024, 4096])
    w_down = dscr("s_w_down", [4, 4096, 1024])
    ev_in = dscr("s_ev_in", [2, 1024, 1184])
    ev_out = dscr("s_ev_out", [2, 1024, 1024])
    uq = dscr("s_uq", [2, 256, 768])
    ukv = dscr("s_ukv", [2, 128, 1024])
    od_in = dscr("s_od_in", [2, 1024, 3072])
    od_out = dscr("s_od_out", [2, 1024, 1024])
    xres = dscr("s_xres", [1024, S], F32)
    mixT = dscr("s_mix", [1024, S])
    costab = dscr("s_cos", [128, S], F32)
    sintab = dscr("s_sin", [128, S], F32)
    qa = dscr("s_qa", [8, 64, S])
    ka = dscr("s_ka", [2, 64, S])
    va = dscr("s_va", [2, 128, NB, 64])
    qb = dscr("s_qb", [8, 96, S])
    kb_ = dscr("s_kb", [8, 96, S])
    vb = dscr("s_vb", [8, 128, NB, 64])
    qc = dscr("s_qc", [8, 64, S])
    kc = dscr("s_kc", [8, 64, S])
    vc = dscr("s_vc", [8, 128, NB, 64])
    qd = dscr("s_qd", [4, 2, 70, S])
    kd = dscr("s_kd", [4, 2, 70, S])
    vd = dscr("s_vd", [4, 128, NB, 128])

    stack = ExitStack()
    sch = Sched(nc, stack)
    op, dma = sch.op, sch.dma

    nctr = [0]

    def sb(st, name, shape, dt):
        nctr[0] += 1
        return T(st.enter_context(nc.sbuf_tensor(f"t{nctr[0]}_{name}", list(shape), dt)))

    psall = stack.enter_context(nc.psum_tensor("psall", [128, 4096], F32))
    ps = [PsView(psall, 512 * i, 512) for i in range(8)]
    cmat = sb(stack, "cmat", [128, NCM], BF16)
    cf32 = sb(stack, "cf32", [128, NCF], F32)
    pcols = sb(stack, "pcols", [128, NPC], F32)
    dcol = sb(stack, "dcol", [128, 32], F32)
    epsc = sb(stack, "epsc", [128, 1], F32)
    onec = sb(stack, "onec", [128, 1], F32)

    ONES = lambda k=128, m=128: cmat[0:k, C_ONES:C_ONES + m]

    def act(out, in_, func, scale=1.0, bias=None):
        if bias is None:
            return lambda e: e.activation(out=out, in_=in_, func=func, scale=scale)
        return lambda e: e.activation(out=out, in_=in_, func=func, scale=scale, bias=bias)

    def mm(out, lhsT, rhs, start, stop):
        return lambda e: e.matmul(out, lhsT=lhsT, rhs=rhs, start=start, stop=stop, skip_group_check=True)

    def tt(out, in0, in1, o):
        return lambda e: e.tensor_tensor(out=out, in0=in0, in1=in1, op=o)

    def ts(out, in0, s1, s2, o0, o1=None):
        if o1 is None:
            return lambda e: e.tensor_scalar(out=out, in0=in0, scalar1=s1, scalar2=None, op0=o0)
        return lambda e: e.tensor_scalar(out=out, in0=in0, scalar1=s1, scalar2=s2, op0=o0, op1=o1)

    def stt(out, in0, s, in1, o0, o1):
        return lambda e: e.scalar_tensor_tensor(out=out, in0=in0, scalar=s, in1=in1, op0=o0, op1=o1)

    def cp(out, in_):
        return lambda e: e.tensor_copy(out=out, in_=in_)

    def dm(out, in_):
        return lambda e: e.dma_start(out=out, in_=in_)

    def rstd_ops(dst, src_ps, n, d):
        op(ACT, act(dst[0:n, :], src_ps[0:n, :], AF.Ln, scale=1.0 / d, bias=epsc[0:n, :]), r=[src_ps.res, epsc.res], w=[dst.res])
        op(ACT, act(dst[0:n, :], dst[0:n, :], AF.Exp, scale=-0.5), r=[dst.res], w=[dst.res])

    with ExitStack() as st:
        dma(SP, dm(cmat[:, :], cmat_d), w=[cmat.res])
        dma(SP, dm(cf32[:, :], cf32_d), w=[cf32.res])
        dma(SP, dm(pcols[:, :], pcols_d), w=[pcols.res])
        op(DVE, lambda e: e.memset(epsc[:, :], EPS), w=[epsc.res])
        op(DVE, lambda e: e.memset(onec[:, :], 1.0), w=[onec.res])
        castres = Res()

        def cast2d(dst, src, rows, step=128):
            for r0 in range(0, rows, step):
                r1 = min(rows, r0 + step)
                dma(POOL, dm(dst[r0:r1, :], src[r0:r1, :]), r=[castres], semres=castres)

        for l in range(4):
            cast2d(w_up[l], w_up_d[l], 1024)
            cast2d(w_down[l], w_down_d[l], 4096, 256)
        for j in range(2):
            cast2d(ev_in[j], ev_in_d[j], 1024)
            cast2d(ev_out[j], ev_out_d[j], 1024)
            cast2d(od_in[j], od_in_d[j], 1024)
            cast2d(od_out[j], od_out_d[j], 1024)
            cast2d(uq[j], uq_d[j], 256)
            cast2d(ukv[j], ukv_d[j], 128)
        augres = Res()
        for h in range(4):
            for m in range(2):
                dma(SP, dm(qd[h, m, 64:70, :], daug_d[h, 0]), r=[augres], semres=augres)
                dma(SP, dm(kd[h, m, 64:70, :], daug_d[h, 1]), r=[augres], semres=augres)
        dl = sb(st, "dl", [128, 2, 256], F32)
        dlp = sb(st, "dlp", [128, 64], F32)
        dma(SP, dm(dl[:, :, :], dlam_d), w=[dl.res])
        for j in range(2):
            layer = 2 * j + 1
            lam_init = 0.8 - 0.6 * math.exp(-0.3 * layer)
            for t in range(2):
                op(DVE, tt(dlp[:, :], dl[:, j, 128 * t:128 * t + 64], dl[:, j, 128 * t + 64:128 * t + 128], ALU.mult), r=[dl.res], w=[dlp.res])
                op(DVE, lambda e, t=t, j=j: e.reduce_sum(out=dcol[:, 20 + t:21 + t], in_=dlp[:, :], axis=AX.X), r=[dlp.res], w=[dcol.res])
                op(ACT, act(dcol[:, 20 + t:21 + t], dcol[:, 20 + t:21 + t], AF.Exp), r=[dcol.res], w=[dcol.res])
            op(DVE, tt(dcol[:, 22:23], dcol[:, 21:22], dcol[:, 20:21], ALU.subtract), r=[dcol.res], w=[dcol.res])
            op(DVE, ts(dcol[:, j:j + 1], dcol[:, 22:23], -lam_init, None, ALU.add), r=[dcol.res], w=[dcol.res])
            op(DVE, ts(dcol[:, 2 + j:3 + j], pcols[:, 128 + 32 * j + 2:128 + 32 * j + 3], 1.0 - lam_init, None, ALU.mult), r=[pcols.res, dcol.res], w=[dcol.res])
        for j in range(2):
            op(ACT, act(dcol[:, 4 + 8 * j:12 + 8 * j], pcols[:, 64 + 32 * j + 9:64 + 32 * j + 17], AF.Exp), r=[pcols.res, dcol.res], w=[dcol.res])
        CH = min(S, 2048)
        posi = sb(st, "posi", [128, CH], I32)
        ang = sb(st, "ang", [128, CH], F32)
        tq = sb(st, "tq", [128, CH], F32)
        ki = sb(st, "ki", [128, CH], I32)
        kf = sb(st, "kf", [128, CH], F32)
        rr = sb(st, "rr", [128, CH], F32)
        mk = sb(st, "mk", [128, CH], F32)
        C1 = 6.28125
        C2 = TWO_PI - C1
        for c0 in range(0, S, CH):
            dma(SP, dm(posi[:, :], posb[:, c0:c0 + CH]), w=[posi.res])
            op(DVE, cp(ang[:, :], posi[:, :]), r=[posi.res], w=[ang.res])
            op(DVE, ts(ang[:, :], ang[:, :], cf32[:, F_INVF:F_INVF + 1], None, ALU.mult), r=[ang.res, cf32.res], w=[ang.res])
            for which, tab in ((0, sintab), (1, costab)):
                src = ang
                if which == 1:
                    op(DVE, ts(rr[:, :], ang[:, :], math.pi / 2, None, ALU.add), r=[ang.res], w=[rr.res])
                    src = rr
                op(DVE, ts(tq[:, :], src[:, :], 1.0 / TWO_PI, None, ALU.mult), r=[src.res], w=[tq.res])
                op(DVE, cp(ki[:, :], tq[:, :]), r=[tq.res], w=[ki.res])
                op(DVE, cp(kf[:, :], ki[:, :]), r=[ki.res], w=[kf.res])
                op(DVE, stt(rr[:, :], kf[:, :], -C1, src[:, :], ALU.mult, ALU.add), r=[kf.res, src.res], w=[rr.res])
                op(DVE, stt(rr[:, :], kf[:, :], -C2, rr[:, :], ALU.mult, ALU.add), r=[kf.res, rr.res], w=[rr.res])
                op(DVE, ts(mk[:, :], rr[:, :], math.pi, -TWO_PI, ALU.is_gt, ALU.mult), r=[rr.res], w=[mk.res])
                op(DVE, tt(rr[:, :], rr[:, :], mk[:, :], ALU.add), r=[rr.res, mk.res], w=[rr.res])
                op(DVE, ts(mk[:, :], rr[:, :], -math.pi, TWO_PI, ALU.is_lt, ALU.mult), r=[rr.res], w=[mk.res])
                op(DVE, tt(rr[:, :], rr[:, :], mk[:, :], ALU.add), r=[rr.res, mk.res], w=[rr.res])
                op(DVE, ts(rr[:, :], rr[:, :], math.pi, -math.pi, ALU.min, ALU.max), r=[rr.res], w=[rr.res])
                op(ACT, act(tq[:, :], rr[:, :], AF.Sin), r=[rr.res], w=[tq.res])
                dma(SP, dm(tab[:, c0:c0 + CH], tq[:, :]), r=[tq.res])
        sch.barrier()

    def wview(w2d, c0, ncol, nchunk=8):
        return w2d.rearrange("(i p) c -> p i c", p=128)[:, 0:nchunk, c0:c0 + ncol]

    class Ctx:
        pass

    def proj_phase(l):
        with ExitStack() as st:
            xs = sb(st, "xs", [128, 8, 512], F32)
            xn = sb(st, "xn", [128, 8, 512], BF16)
            sq = sb(st, "sq", [128, 8, 512], BF16)
            sqr = [Res() for _ in range(8)]
            xsr = [Res() for _ in range(8)]
            xnr = [Res() for _ in range(8)]
            rstd = sb(st, "rstd", [128, 512], F32)
            wts = [sb(st, f"wt{i}", [128, 8, 512], BF16) for i in range(4)]
            wctr = [0]
            if l > 0:
                H = sb(st, "H", [128, 32, 512], BF16)
                mx = sb(st, "mx", [128, 8, 512], BF16)
                rl = [sb(st, f"rl{i}", [128, 512], BF16) for i in range(2)]
            if l < DEPTH:
                raws = [sb(st, f"raw{i}", [128, 512], F32) for i in range(8)]
                sqs = [sb(st, f"sqs{i}", [128, 512], BF16) for i in range(8)]
                outs = [sb(st, f"ob{i}", [128, 512], BF16) for i in range(4)]
                rc = sb(st, "rc", [8, 512], F32)
                vt = sb(st, "vt", [128, 4, 512], BF16)
                octr = [0]
                if l % 2 == 0:
                    cqn = sb(st, "cqn", [128, 2, 512], BF16)
                    ckvn = sb(st, "ckvn", [128, 512], BF16)
                    cost = sb(st, "cost", [128, 512], F32)
                    sint = sb(st, "sint", [128, 512], F32)
                    t1 = sb(st, "t1", [128, 512], F32)
                    t2 = sb(st, "t2", [128, 512], F32)
                    krr = sb(st, "krr", [32, 512], F32)
                    wsm = sb(st, "wsm", [128, 2, 768], BF16)
                    wkv = sb(st, "wkv", [128, 1024], BF16)
            pctr = [0]
            STQ = ACT

            def nps():
                pctr[0] += 1
                return ps[pctr[0] % 4]

            def load_w(view, nchunk=8, ncol=512):
                wt = wts[wctr[0] % 4]
                wctr[0] += 1
                dma(SP, dm(wt[:, 0:nchunk, 0:ncol], view), w=[wt.res])
                return wt

            def square(c):
                op(POOL if c % 2 == 0 else DVE, tt(sq[:, c, :], xs[:, c, :], xs[:, c, :], ALU.mult), r=[xsr[c]], w=[sqr[c]])

            def norm(gbase, do_sq=True):
                if do_sq:
                    for c in range(8):
                        square(c)
                p = nps()
                for c in range(8):
                    op(PE, mm(p[:, :], ONES(), sq[:, c, :], c == 0, c == 7), r=[sqr[c], cmat.res], w=[p.res])
                rstd_ops(rstd, p, 128, 1024.0)
                for c in range(8):
                    op(DVE, stt(xn[:, c, :], xs[:, c, :], pcols[:, gbase + c:gbase + c + 1], rstd[:, :], ALU.mult, ALU.mult),
                       r=[xsr[c], pcols.res, rstd.res], w=[xnr[c]])

            def outbuf():
                o = outs[octr[0] % 4]
                octr[0] += 1
                return o

            def proj_chunk(wt, col0, ncol, src, nchunk=8, srcidx=None):
                p = nps()
                for i in range(nchunk):
                    rhs = src[:, i, :] if srcidx is None else srcidx(i)
                    op(PE, mm(p[0:ncol, :], wt[:, i, col0:col0 + ncol], rhs, i == 0, i == nchunk - 1), r=[wt.res, xnr[i] if src is xn else src.res], w=[p.res])
                return p

            def vproj(wt, col0, ncol, src, nchunk, dst_fn, srcidx=None):
                for tb in range(4):
                    p = nps()
                    for i in range(nchunk):
                        lhs = src[:, i, tb * 128:(tb + 1) * 128] if srcidx is None else srcidx(i, tb)
                        op(PE, mm(p[:, 0:ncol], lhs, wt[:, i, col0:col0 + ncol], i == 0, i == nchunk - 1), r=[wt.res, xnr[i] if src is xn else src.res], w=[p.res])
                    op(ACT, act(vt[:, tb, 0:ncol], p[:, 0:ncol], AF.Copy), r=[p.res], w=[vt.res])
                dst_fn()

            for g in range(NG):
                gs = slice(g * 512, (g + 1) * 512)
                src_x = xT if l == 0 else xres
                dma(SP, dm(xs[:, :, :], src_x.rearrange("(c p) s -> p c s", p=128)[:, :, gs]), w=xsr)
                if l > 0:
                    lp = l - 1
                    jp = lp // 2
                    dma(SP, dm(mx[:, :, :], mixT.rearrange("(c p) s -> p c s", p=128)[:, :, gs]), w=[mx.res])
                    wout = (ev_out if lp % 2 == 0 else od_out)[jp]
                    for half in range(2):
                        wt = load_w(wview(wout, half * 512, 512))
                        for oc4 in range(4):
                            oc = half * 4 + oc4
                            p = proj_chunk(wt, oc4 * 128, 128, mx)
                            op(DVE, tt(xs[:, oc, :], p[:, :], xs[:, oc, :], ALU.add), r=[p.res, xsr[oc]], w=[xsr[oc]])
                            square(oc)
                    norm(lp * 16 + 8, do_sq=False)
                    for fb in range(8):
                        wt = load_w(wview(w_up[lp], fb * 512, 512))
                        for f4 in range(4):
                            fc = fb * 4 + f4
                            p = proj_chunk(wt, f4 * 128, 128, xn)
                            r_ = rl[fc % 2]
                            op(DVE, ts(r_[:, :], p[:, :], 0.0, None, ALU.max), r=[p.res], w=[r_.res])
                            op(POOL, tt(H[:, fc, :], r_[:, :], r_[:, :], ALU.mult), r=[r_.res], w=[H.res])
                    for half in range(2):
                        accs = [ps[4 + i] for i in range(4)]
                        for fs in range(4):
                            view = w_down[lp].rearrange("(i p) c -> p i c", p=128)[:, fs * 8:(fs + 1) * 8, half * 512:(half + 1) * 512]
                            wt = load_w(view)
                            for oc4 in range(4):
                                for i in range(8):
                                    fc = fs * 8 + i
                                    op(PE, mm(accs[oc4][:, :], wt[:, i, oc4 * 128:(oc4 + 1) * 128], H[:, fc, :], fc == 0, fc == 31),
                                       r=[wt.res, H.res], w=[accs[oc4].res])
                        for oc4 in range(4):
                            oc = half * 4 + oc4
                            op(DVE, tt(xs[:, oc, :], accs[oc4][:, :], xs[:, oc, :], ALU.add), r=[accs[oc4].res, xsr[oc]], w=[xsr[oc]])
                            if l < DEPTH:
                                square(oc)
                if l == DEPTH:
                    dma(STQ, dm(yT.rearrange("(c p) s -> p c s", p=128)[:, :, gs], xs[:, :, :]), r=xsr)
                    continue
                dma(STQ, dm(xres.rearrange("(c p) s -> p c s", p=128)[:, :, gs], xs[:, :, :]), r=xsr)
                norm(l * 16, do_sq=(l == 0))
                j = l // 2
                if l % 2 == 1:
                    pb = 128 + 32 * j
                    win = od_in[j]
                    for part, dst, scale in ((0, qc, 0.125), (1, kc, 1.0)):
                        wt = load_w(wview(win, part * 512, 512))
                        for c in range(4):
                            p = proj_chunk(wt, c * 128, 128, xn)
                            o = outbuf()
                            op(ACT, act(o[:, :], p[:, :], AF.Copy, scale=scale), r=[p.res], w=[o.res])
                            for hh in range(2):
                                dma(STQ, dm(dst[2 * c + hh, :, gs], o[64 * hh:64 * hh + 64, :]), r=[o.res])
                    wt = load_w(wview(win, 1024, 512))

                    def st_cv():
                        for h in range(8):
                            dma(STQ, dm(vc[h, :, 4 * g:4 * g + 4, :], vt[:, :, 64 * h:64 * h + 64]), r=[vt.res])
                    vproj(wt, 0, 512, xn, 8, st_cv)
                    for part, dst, gcol in ((3, qd, pb + 0), (4, kd, pb + 1)):
                        wt = load_w(wview(win, part * 512, 512))
                        pc = nps()
                        for c in range(4):
                            p = proj_chunk(wt, c * 128, 128, xn)
                            op(ACT, act(raws[c][:, :], p[:, :], AF.Copy), r=[p.res], w=[raws[c].res])
                            op(POOL, tt(sqs[c][:, :], raws[c][:, :], raws[c][:, :], ALU.mult), r=[raws[c].res], w=[sqs[c].res])
                        for c in range(4):
                            op(PE, mm(pc[0:8, :], cmat[:, C_SEL2 + 8 * c:C_SEL2 + 8 * c + 8], sqs[c][:, :], c == 0, c == 3), r=[cmat.res, sqs[c].res], w=[pc.res])
                        rstd_ops(rc, pc, 8, 64.0)
                        for c in range(4):
                            pbc = nps()
                            op(PE, mm(pbc[:, :], cf32[0:8, F_SEL2T + 128 * c:F_SEL2T + 128 * c + 128], rc[0:8, :], True, True), r=[cf32.res, rc.res], w=[pbc.res])
                            o = outbuf()
                            op(DVE, stt(o[:, :], raws[c][:, :], pcols[:, gcol:gcol + 1], pbc[:, :], ALU.mult, ALU.mult), r=[raws[c].res, pcols.res, pbc.res], w=[o.res])
                            for m in range(2):
                                dma(STQ, dm(dst[c, m, 0:64, gs], o[64 * m:64 * m + 64, :]), r=[o.res])
                    wt = load_w(wview(win, 2560, 512))

                    def st_dv():
                        for h in range(4):
                            dma(STQ, dm(vd[h, :, 4 * g:4 * g + 4, :], vt[:, :, 128 * h:128 * h + 128]), r=[vt.res])
                    vproj(wt, 0, 512, xn, 8, st_dv)
                else:
                    pb = 64 + 32 * j
                    win = ev_in[j]
                    wt = load_w(wview(win, 0, 512))
                    pc = nps()
                    for c in range(4):
                        p = proj_chunk(wt, c * 128, 128, xn)
                        op(ACT, act(raws[c][:, :], p[:, :], AF.Copy), r=[p.res], w=[raws[c].res])
                        op(POOL, tt(sqs[c][:, :], raws[c][:, :], raws[c][:, :], ALU.mult), r=[raws[c].res], w=[sqs[c].res])
                    for c in range(4):
                        op(PE, mm(pc[0:8, :], cmat[:, C_SEL2 + 8 * c:C_SEL2 + 8 * c + 8], sqs[c][:, :], c == 0, c == 3), r=[cmat.res, sqs[c].res], w=[pc.res])
                    rstd_ops(rc, pc, 8, 64.0)
                    for c in range(4):
                        pbc = nps()
                        op(PE, mm(pbc[:, :], cf32[0:8, F_SEL2T + 128 * c:F_SEL2T + 128 * c + 128], rc[0:8, :], True, True), r=[cf32.res, rc.res], w=[pbc.res])
                        o = outbuf()
                        op(DVE, stt(o[:, :], raws[c][:, :], pcols[:, pb:pb + 1], pbc[:, :], ALU.mult, ALU.mult), r=[raws[c].res, pcols.res, pbc.res], w=[o.res])
                        for hh in range(2):
                            dma(STQ, dm(qa[2 * c + hh, :, gs], o[64 * hh:64 * hh + 64, :]), r=[o.res])
                    wt = load_w(wview(win, 512, 512))
                    p = proj_chunk(wt, 0, 128, xn)
                    op(ACT, act(raws[0][:, :], p[:, :], AF.Copy), r=[p.res], w=[raws[0].res])
                    op(POOL, tt(sqs[0][:, :], raws[0][:, :], raws[0][:, :], ALU.mult), r=[raws[0].res], w=[sqs[0].res])
                    pc = nps()
                    op(PE, mm(pc[0:8, :], cmat[:, C_SEL2:C_SEL2 + 8], sqs[0][:, :], True, True), r=[cmat.res, sqs[0].res], w=[pc.res])
                    rstd_ops(rc, pc, 8, 64.0)
                    pbc = nps()
                    op(PE, mm(pbc[:, :], cf32[0:8, F_SEL2T:F_SEL2T + 128], rc[0:8, :], True, True), r=[cf32.res, rc.res], w=[pbc.res])
                    o = outbuf()
                    op(DVE, stt(o[:, :], raws[0][:, :], pcols[:, pb + 1:pb + 2], pbc[:, :], ALU.mult, ALU.mult), r=[raws[0].res, pcols.res, pbc.res], w=[o.res])
                    for hh in range(2):
                        dma(STQ, dm(ka[hh, :, gs], o[64 * hh:64 * hh + 64, :]), r=[o.res])

                    def st_av():
                        for h in range(2):
                            dma(STQ, dm(va[h, :, 4 * g:4 * g + 4, :], vt[:, :, 64 * h:64 * h + 64]), r=[vt.res])
                    vproj(wt, 128, 128, xn, 8, st_av)
                    for c in range(2):
                        p = proj_chunk(wt, 256 + c * 128, 128, xn)
                        op(ACT, act(raws[c][:, :], p[:, :], AF.Copy), r=[p.res], w=[raws[c].res])
                        op(POOL, tt(sqs[c][:, :], raws[c][:, :], raws[c][:, :], ALU.mult), r=[raws[c].res], w=[sqs[c].res])
                    pc = nps()
                    for c in range(2):
                        op(PE, mm(pc[:, :], ONES(), sqs[c][:, :], c == 0, c == 1), r=[cmat.res, sqs[c].res], w=[pc.res])
                    rstd_ops(rstd, pc, 128, 256.0)
                    for c in range(2):
                        op(DVE, stt(cqn[:, c, :], raws[c][:, :], pcols[:, pb + 2 + c:pb + 3 + c], rstd[:, :], ALU.mult, ALU.mult), r=[raws[c].res, pcols.res, rstd.res], w=[cqn.res])
                    wt = load_w(wview(win, 1024, 160), 8, 160)
                    p = proj_chunk(wt, 0, 128, xn)
                    op(ACT, act(raws[0][:, :], p[:, :], AF.Copy), r=[p.res], w=[raws[0].res])
                    op(POOL, tt(sqs[0][:, :], raws[0][:, :], raws[0][:, :], ALU.mult), r=[raws[0].res], w=[sqs[0].res])
                    pc = nps()
                    op(PE, mm(pc[:, :], ONES(), sqs[0][:, :], True, True), r=[cmat.res, sqs[0].res], w=[pc.res])
                    rstd_ops(rstd, pc, 128, 128.0)
                    op(DVE, stt(ckvn[:, :], raws[0][:, :], pcols[:, pb + 4:pb + 5], rstd[:, :], ALU.mult, ALU.mult), r=[raws[0].res, pcols.res, rstd.res], w=[ckvn.res])
                    pkr = proj_chunk(wt, 128, 32, xn)
                    op(ACT, act(raws[6][0:32, :], pkr[0:32, :], AF.Copy), r=[pkr.res], w=[raws[6].res])
                    op(POOL, tt(sqs[6][0:32, :], raws[6][0:32, :], raws[6][0:32, :], ALU.mult), r=[raws[6].res], w=[sqs[6].res])
                    dma(SP, dm(cost[:, :], costab[:, gs]), w=[cost.res])
                    dma(SP, dm(sint[:, :], sintab[:, gs]), w=[sint.res])
                    op(DVE, ts(t1[0:32, :], raws[6][0:32, :], pcols[0:32, pb + 8:pb + 9], None, ALU.mult), r=[raws[6].res, pcols.res], w=[t1.res])
                    op(DVE, cp(sqs[7][0:32, :], t1[0:32, :]), r=[t1.res], w=[sqs[7].res])
                    prot = nps()
                    op(PE, mm(prot[0:32, :], cmat[0:32, C_PERM:C_PERM + 32], sqs[7][0:32, :], True, True), r=[cmat.res, sqs[7].res], w=[prot.res])
                    op(DVE, tt(t2[0:32, :], prot[0:32, :], sint[0:32, :], ALU.mult), r=[prot.res, sint.res], w=[t2.res])
                    op(DVE, tt(t1[0:32, :], t1[0:32, :], cost[0:32, :], ALU.mult), r=[t1.res, cost.res], w=[t1.res])
                    op(DVE, tt(krr[0:32, :], t1[0:32, :], t2[0:32, :], ALU.add), r=[t1.res, t2.res], w=[krr.res])
                    uqv = uq[j].rearrange("(i p) (h d) -> p i h d", p=128, d=96)
                    for i in range(2):
                        dma(SP, dm(wsm[:, i, 0:512].rearrange("p (h d) -> p h d", d=64), uqv[:, i, :, 0:64]), w=[wsm.res])
                        dma(SP, dm(wsm[:, i, 512:768].rearrange("p (h d) -> p h d", d=32), uqv[:, i, :, 64:96]), w=[wsm.res])
                    ukvv = ukv[j].rearrange("p (h d) -> p h d", d=128)
                    dma(SP, dm(wkv[:, 0:512].rearrange("p (h d) -> p h d", d=64), ukvv[:, :, 0:64]), w=[wkv.res])
                    dma(SP, dm(wkv[:, 512:1024].rearrange("p (h d) -> p h d", d=64), ukvv[:, :, 64:128]), w=[wkv.res])
                    for c in range(6):
                        p = nps()
                        for i in range(2):
                            op(PE, mm(p[:, :], wsm[:, i, c * 128:(c + 1) * 128], cqn[:, i, :], i == 0, i == 1), r=[wsm.res, cqn.res], w=[p.res])
                        op(ACT, act(raws[c][:, :], p[:, :], AF.Copy), r=[p.res], w=[raws[c].res])
                        op(POOL, tt(sqs[c][:, :], raws[c][:, :], raws[c][:, :], ALU.mult), r=[raws[c].res], w=[sqs[c].res])
                    pc = nps()
                    for c in range(6):
                        sel = cmat[:, C_SEL2 + 8 * c:C_SEL2 + 8 * c + 8] if c < 4 else cmat[:, C_SEL4 + 8 * (c - 4):C_SEL4 + 8 * (c - 4) + 8]
                        op(PE, mm(pc[0:8, :], sel, sqs[c][:, :], c == 0, c == 5), r=[cmat.res, sqs[c].res], w=[pc.res])
                    rstd_ops(rc, pc, 8, 96.0)
                    for c in range(6):
                        pbc = nps()
                        selT = cf32[0:8, F_SEL2T + 128 * c:F_SEL2T + 128 * c + 128] if c < 4 else cf32[0:8, F_SEL4T + 128 * (c - 4):F_SEL4T + 128 * (c - 4) + 128]
                        op(PE, mm(pbc[:, :], selT, rc[0:8, :], True, True), r=[cf32.res, rc.res], w=[pbc.res])
                        if c < 4:
                            o = outbuf()
                            op(DVE, stt(o[:, :], raws[c][:, :], pcols[:, pb + 5:pb + 6], pbc[:, :], ALU.mult, ALU.mult), r=[raws[c].res, pcols.res, pbc.res], w=[o.res])
                            for hh in range(2):
                                dma(STQ, dm(qb[2 * c + hh, 0:64, gs], o[64 * hh:64 * hh + 64, :]), r=[o.res])
                        else:
                            op(DVE, stt(t1[:, :], raws[c][:, :], pcols[:, pb + 6:pb + 7], pbc[:, :], ALU.mult, ALU.mult), r=[raws[c].res, pcols.res, pbc.res], w=[t1.res])
                            op(POOL, cp(sqs[7][:, :], t1[:, :]), r=[t1.res], w=[sqs[7].res])
                            prot = nps()
                            op(PE, mm(prot[:, :], cmat[:, C_PERM:C_PERM + 128], sqs[7][:, :], True, True), r=[cmat.res, sqs[7].res], w=[prot.res])
                            op(DVE, tt(t2[:, :], prot[:, :], sint[:, :], ALU.mult), r=[prot.res, sint.res], w=[t2.res])
                            op(DVE, tt(t1[:, :], t1[:, :], cost[:, :], ALU.mult), r=[t1.res, cost.res], w=[t1.res])
                            o = outbuf()
                            op(DVE, tt(o[:, :], t1[:, :], t2[:, :], ALU.add), r=[t1.res, t2.res], w=[o.res])
                            for hh in range(4):
                                dma(STQ, dm(qb[4 * (c - 4) + hh, 64:96, gs], o[32 * hh:32 * hh + 32, :]), r=[o.res])
                    for c in range(4):
                        p = nps()
                        op(PE, mm(p[:, :], wkv[:, c * 128:(c + 1) * 128], ckvn[:, :], True, True), r=[wkv.res, ckvn.res], w=[p.res])
                        op(ACT, act(raws[c][:, :], p[:, :], AF.Copy), r=[p.res], w=[raws[c].res])
                        op(POOL, tt(sqs[c][:, :], raws[c][:, :], raws[c][:, :], ALU.mult), r=[raws[c].res], w=[sqs[c].res])
                    pc = nps()
                    for c in range(4):
                        op(PE, mm(pc[0:8, :], cmat[:, C_SEL2 + 8 * c:C_SEL2 + 8 * c + 8], sqs[c][:, :], c == 0, False), r=[cmat.res, sqs[c].res], w=[pc.res])
                    op(PE, mm(pc[0:8, :], cmat[0:32, C_ONES:C_ONES + 8], sqs[6][0:32, :], False, True), r=[cmat.res, sqs[6].res], w=[pc.res])
                    rstd_ops(rc, pc, 8, 96.0)
                    for c in range(4):
                        pbc = nps()
                        op(PE, mm(pbc[:, :], cf32[0:8, F_SEL2T + 128 * c:F_SEL2T + 128 * c + 128], rc[0:8, :], True, True), r=[cf32.res, rc.res], w=[pbc.res])
                        o = outbuf()
                        op(DVE, stt(o[:, :], raws[c][:, :], pcols[:, pb + 7:pb + 8], pbc[:, :], ALU.mult, ALU.mult), r=[raws[c].res, pcols.res, pbc.res], w=[o.res])
                        for hh in range(2):
                            dma(STQ, dm(kb_[2 * c + hh, 0:64, gs], o[64 * hh:64 * hh + 64, :]), r=[o.res])
                    for h in range(8):
                        pbc = nps()
                        op(PE, mm(pbc[0:32, :], cf32[0:8, F_KSELT + 32 * h:F_KSELT + 32 * h + 32], rc[0:8, :], True, True), r=[cf32.res, rc.res], w=[pbc.res])
                        o = outbuf()
                        op(DVE, tt(o[0:32, :], krr[0:32, :], pbc[0:32, :], ALU.mult), r=[krr.res, pbc.res], w=[o.res])
                        dma(STQ, dm(kb_[h, 64:96, gs], o[0:32, :]), r=[o.res])
                    for tb in range(4):
                        p = nps()
                        op(PE, mm(p[:, :], ckvn[:, tb * 128:(tb + 1) * 128], wkv[:, 512:1024], True, True), r=[wkv.res, ckvn.res], w=[p.res])
                        op(ACT, act(vt[:, tb, :], p[:, :], AF.Copy), r=[p.res], w=[vt.res])
                    for h in range(8):
                        dma(STQ, dm(vb[h, :, 4 * g:4 * g + 4, :], vt[:, :, 64 * h:64 * h + 64]), r=[vt.res])
        sch.barrier()

    def attn_even(j):
        pb = 64 + 32 * j
        with ExitStack() as st:
            KT = [sb(st, f"KT{i}", [128, S], BF16) for i in range(2)]
            V = [sb(st, f"V{i}", [128, NB, 128], BF16) for i in range(2)]
            for v_ in V:
                op(POOL, lambda e, v_=v_: e.memset(v_[:, :, 64:128], 1.0), w=[v_.res])
            tmpf = [sb(st, f"tmpf{i}", [128, 512], F32) for i in range(2)]
            STQA = POOL
            Q = [sb(st, f"Q{i}", [128, 512], BF16) for i in range(3)]
            Pb = [sb(st, f"P{i}", [128, 512], BF16) for i in range(6)]
            sbb = [sb(st, f"sbb{i}", [128, 256], F32) for i in range(4)]
            rec = sb(st, "rec", [64, 512], F32)
            ob = [sb(st, f"oo{i}", [64, 512], BF16) for i in range(2)]
            O, L = ps[6], ps[7]
            qctr = [0]
            octr = [0]
            tctr = [0]
            for kvh in range(2):
                kt, v = KT[kvh], V[kvh]
                dma(SP, dm(kt[0:64, :], ka[kvh]), w=[kt.res])
                dma(SP, dm(v[:, :, 0:64], va[kvh]), w=[v.res])
                for hq in range(4):
                    h = 4 * kvh + hq
                    for g in range(NG):
                        q = Q[qctr[0] % 3]
                        qctr[0] += 1
                        dma(SP, dm(q[0:64, :], qa[h, :, g * 512:(g + 1) * 512]), w=[q.res])
                        tiles = []
                        rels = list(range(-1, 4)) if g > 0 else list(range(0, 4))
                        for idx, rel in enumerate(rels):
                            kbi = 4 * g + rel
                            if rel < 0:
                                q0, n, boff = 0, 128, 0
                            else:
                                q0 = 128 * rel
                                n = min(256, 512 - q0)
                                boff = 128
                            t = tctr[0]
                            tctr[0] += 1
                            sp_, P_, sb_ = ps[t % 4], Pb[t % 6], sbb[t % 4]
                            first, last = idx == 0, idx == len(rels) - 1

                            def s1(kbi=kbi, q0=q0, n=n, sp_=sp_, q=q, kt=kt):
                                op(PE, mm(sp_[:, 0:n], kt[0:64, kbi * 128:(kbi + 1) * 128], q[0:64, q0:q0 + n], True, True), r=[kt.res, q.res], w=[sp_.res])

                            def s2(n=n, sp_=sp_, sb_=sb_, P_=P_, boff=boff, h=h):
                                op(DVE, stt(sb_[:, 0:n], sp_[:, 0:n], 0.125, cf32[:, F_ABIAS + 384 * h + boff:F_ABIAS + 384 * h + boff + n], ALU.mult, ALU.add),
                                   r=[sp_.res, cf32.res], w=[sb_.res])
                                op(ACT, act(P_[:, 0:n], sb_[:, 0:n], AF.Exp), r=[sb_.res], w=[P_.res])

                            def s3(kbi=kbi, q0=q0, n=n, P_=P_, v=v, first=first, last=last):
                                op(PE, mm(O[0:64, q0:q0 + n], v[:, kbi, 0:64], P_[:, 0:n], first, last), r=[v.res, P_.res], w=[O.res])
                                op(PE, mm(L[0:64, q0:q0 + n], ONES(128, 64), P_[:, 0:n], first, last), r=[cmat.res, P_.res], w=[L.res])
                            tiles.append([s1, s2, s3])
                        pipeline(tiles, [0, 2, 4])
                        o = ob[octr[0] % 2]
                        octr[0] += 1
                        op(DVE, ts(rec[:, :], L[0:64, :], dcol[0:64, 4 + 8 * j + h:5 + 8 * j + h], None, ALU.add), r=[L.res, dcol.res], w=[rec.res])
                        op(DVE, lambda e: e.reciprocal(out=rec[:, :], in_=rec[:, :]), r=[rec.res], w=[rec.res])
                        op(DVE, tt(o[:, :], O[0:64, :], rec[:, :], ALU.mult), r=[O.res, rec.res], w=[o.res])
                        dma(STQA, dm(mixT[64 * h:64 * h + 64, g * 512:(g + 1) * 512], o[:, :]), r=[o.res])
            scaleB = 96.0 ** -0.5
            for h in range(8):
                kt, v = KT[h % 2], V[h % 2]
                dma(SP, dm(kt[0:96, :], kb_[h]), w=[kt.res])
                dma(SP, dm(v[:, :, 0:64], vb[h]), w=[v.res])
                for g in range(NG):
                    q = Q[qctr[0] % 3]
                    qctr[0] += 1
                    dma(SP, dm(q[0:96, :], qb[h, :, g * 512:(g + 1) * 512]), w=[q.res])
                    OL = ps[6 + (octr[0] % 2)]
                    tiles = []
                    nkb = 4 * g + 4
                    for kbi in range(nkb):
                        rel = kbi - 4 * g
                        q0 = 128 * rel if rel > 0 else 0
                        n = 512 - q0
                        t = tctr[0]
                        tctr[0] += 1
                        sp_, P_ = ps[t % 4], Pb[t % 6]
                        first, last = kbi == 0, kbi == nkb - 1

                        def s1(kbi=kbi, q0=q0, n=n, sp_=sp_, q=q, kt=kt):
                            op(PE, mm(sp_[:, 0:n], kt[0:96, kbi * 128:(kbi + 1) * 128], q[0:96, q0:q0 + n], True, True), r=[kt.res, q.res], w=[sp_.res])

                        def s2(n=n, sp_=sp_, P_=P_, rel=rel):
                            op(ACT, act(P_[:, 0:n], sp_[:, 0:n], AF.Exp, scale=scaleB), r=[sp_.res], w=[P_.res])
                            if rel >= 0:
                                op(DVE, tt(P_[:, 0:128], P_[:, 0:128], cmat[:, C_DFB:C_DFB + 128], ALU.mult), r=[P_.res, cmat.res], w=[P_.res])

                        def s3(kbi=kbi, q0=q0, n=n, P_=P_, v=v, first=first, last=last, OL=OL):
                            op(PE, mm(OL[:, q0:q0 + n], v[:, kbi, :], P_[:, 0:n], first, last), r=[v.res, P_.res], w=[OL.res])
                        tiles.append([s1, s2, s3])
                    pipeline(tiles, [0, 2, 4])
                    o = ob[octr[0] % 2]
                    tf = tmpf[octr[0] % 2]
                    octr[0] += 1
                    op(DVE, cp(tf[:, :], OL[:, :]), r=[OL.res], w=[tf.res])
                    op(PE, mm(ps[5][0:64, :], cf32[:, F_SHIFT:F_SHIFT + 64], tf[:, :], True, True), r=[cf32.res, tf.res], w=[ps[5].res])
                    op(DVE, lambda e: e.reciprocal(out=rec[:, :], in_=ps[5][0:64, :]), r=[ps[5].res], w=[rec.res])
                    op(DVE, tt(o[:, :], tf[0:64, :], rec[:, :], ALU.mult), r=[tf.res, rec.res], w=[o.res])
                    dma(STQA, dm(mixT[512 + 64 * h:512 + 64 * h + 64, g * 512:(g + 1) * 512], o[:, :]), r=[o.res])
        sch.barrier()

    def attn_odd(j):
        with ExitStack() as st:
            KT = [sb(st, f"KT{i}", [128, S], BF16) for i in range(3)]
            V = [sb(st, f"V{i}", [128, NB, 128], BF16) for i in range(2)]
            Q = [sb(st, f"Q{i}", [128, 512], BF16) for i in range(4)]
            Pb = [sb(st, f"P{i}", [128, 512], BF16) for i in range(6)]
            ef = [sb(st, f"ef{i}", [128, 512], F32) for i in range(3)]
            spb = [sb(st, f"spb{i}", [128, 512], BF16) for i in range(3)]
            R32 = sb(st, "R32", [128, 512], F32)
            Rb = [sb(st, f"Rb{i}", [128, 512], BF16) for i in range(2)]
            ob = [sb(st, f"oo{i}", [128, 512], BF16) for i in range(2)]
            f1 = sb(st, "f1", [128, 512], F32)
            f2 = sb(st, "f2", [128, 512], F32)
            f3 = sb(st, "f3", [128, 512], F32)
            fsq = sb(st, "fsq", [128, 512], BF16)
            qctr = [0]
            octr = [0]
            tctr = [0]
            zw = [PsView(psall, 1024 * k, 1024) for k in range(3)]
            Ow = PsView(psall, 3072, 1024)
            efw = [sb(st, f"efw{i}", [128, 1024], F32) for i in range(2)]
            spw = [sb(st, f"spw{i}", [128, 1024], BF16) for i in range(3)]
            aw = [sb(st, f"aw{i}", [128, 1024], BF16) for i in range(3)]
            R32w = sb(st, "R32w", [128, 1024], F32)
            Rbw = [sb(st, f"Rbw{i}", [128, 1024], BF16) for i in range(2)]
            qws = [sb(st, f"qw{i}", [64, 1024], BF16) for i in range(2)]
            obw = [sb(st, f"obw{i}", [64, 1024], BF16) for i in range(2)]
            for h in range(8):
                kt, v = KT[h % 2], V[h % 2]
                dma(SP, dm(kt[0:64, :], kc[h]), w=[kt.res])
                dma(SP, dm(v[:, :, 0:64], vc[h]), w=[v.res])
                for G in range(NG // 2):
                    g = 2 * G
                    q = qws[qctr[0] % 2]
                    qctr[0] += 1
                    dma(SP, dm(q[0:64, :], qc[h, :, g * 512:(g + 2) * 512]), w=[q.res])
                    tiles = []
                    nkb = 4 * g + 8
                    order = list(range(nkb - 1, -1, -1))
                    for idx, kbi in enumerate(order):
                        if kbi >= 4 * g + 4:
                            c0, diag = 512 + 128 * (kbi - 4 * g - 4), True
                        elif kbi >= 4 * g:
                            c0, diag = 128 * (kbi - 4 * g), True
                        else:
                            c0, diag = 0, False
                        pieces = ([(c0, 512)] if c0 < 512 else []) + [(max(c0, 512), 1024)]
                        t = tctr[0]
                        tctr[0] += 1
                        zp, e_, s_, a_ = zw[t % 3], efw[t % 2], spw[t % 3], aw[t % 3]
                        rb_r, rb_w = Rbw[idx % 2], Rbw[(idx + 1) % 2]
                        first, last = idx == 0, idx == len(order) - 1
                        firstA = kbi == 4 * g + 3

                        def s1(kbi=kbi, pieces=pieces, zp=zp, q=q, kt=kt):
                            for (a_c, b_c) in pieces:
                                op(PE, mm(zp[:, a_c:b_c], kt[0:64, kbi * 128:(kbi + 1) * 128], q[0:64, a_c:b_c], True, False), r=[kt.res, q.res], w=[zp.res])

                        def s2a(c0=c0, zp=zp, e_=e_):
                            op(ACT, act(e_[:, c0:1024], zp[:, c0:1024], AF.Exp), r=[zp.res], w=[e_.res])

                        def s2(c0=c0, e_=e_, s_=s_, diag=diag):
                            op(ACT, act(s_[:, c0:1024], e_[:, c0:1024], AF.Ln, bias=onec[:, :]), r=[e_.res, onec.res], w=[s_.res])
                            if diag:
                                op(DVE, tt(s_[:, c0:c0 + 128], s_[:, c0:c0 + 128], cmat[:, C_DFC:C_DFC + 128], ALU.mult), r=[s_.res, cmat.res], w=[s_.res])

                        def s3(c0=c0, pieces=pieces, zp=zp, s_=s_, rb_r=rb_r, rb_w=rb_w, first=first, last=last):
                            for (a_c, b_c) in pieces:
                                op(PE, mm(zp[:, a_c:b_c], cmat[:, C_NEGU:C_NEGU + 128], s_[:, a_c:b_c], False, first), r=[cmat.res, s_.res], w=[zp.res])
                                if not first:
                                    op(PE, mm(zp[:, a_c:b_c], cmat[:, C_NEGONES:C_NEGONES + 128], rb_r[:, a_c:b_c], False, True), r=[cmat.res, rb_r.res], w=[zp.res])
                            if not last:
                                if first:
                                    op(POOL, lambda e: e.memset(R32w[:, :], 0.0), w=[R32w.res])
                                op(POOL, tt(R32w[:, c0:1024], R32w[:, c0:1024], s_[:, c0:1024], ALU.add), r=[R32w.res, s_.res], w=[R32w.res])
                                op(DVE, cp(rb_w[:, :], R32w[:, :]), r=[R32w.res], w=[rb_w.res])

                        def s4(c0=c0, zp=zp, a_=a_, diag=diag):
                            op(ACT, act(a_[:, c0:1024], zp[:, c0:1024], AF.Exp), r=[zp.res], w=[a_.res])
                            if diag:
                                op(DVE, tt(a_[:, c0:c0 + 128], a_[:, c0:c0 + 128], cmat[:, C_DFC:C_DFC + 128], ALU.mult), r=[a_.res, cmat.res], w=[a_.res])

                        def s5(kbi=kbi, pieces=pieces, a_=a_, v=v, first=first, firstA=firstA, last=last):
                            for (a_c, b_c) in pieces:
                                st_ = firstA if a_c < 512 else first
                                op(PE, mm(Ow[0:64, a_c:b_c], v[:, kbi, 0:64], a_[:, a_c:b_c], st_, last), r=[v.res, a_.res], w=[Ow.res])
                        tiles.append([s1, s2a, s2, s3, s4, s5])
                    pipeline(tiles, [0, 1, 1, 2, 2, 3])
                    o = obw[octr[0] % 2]
                    octr[0] += 1
                    op(DVE, cp(o[0:64, :], Ow[0:64, :]), r=[Ow.res], w=[o.res])
                    dma(POOL, dm(mixT[64 * h:64 * h + 64, g * 512:(g + 2) * 512], o[0:64, :]), r=[o.res])
            sch.barrier()
            O1, O2 = ps[6], ps[7]
            Lacc = [sb(st, f"Lacc{i}", [128, 512], F32) for i in range(2)]
            for h in range(4):
                k1, k2, v = KT[0], KT[1], V[h % 2]
                dma(SP, dm(k1[0:70, :], kd[h, 0]), w=[k1.res])
                dma(SP, dm(k2[0:70, :], kd[h, 1]), w=[k2.res])
                dma(SP, dm(v[:, :, :], vd[h]), w=[v.res])
                for g in range(NG):
                    q1 = Q[qctr[0] % 4]
                    q2 = Q[(qctr[0] + 1) % 4]
                    qctr[0] += 2
                    dma(SP, dm(q1[0:70, :], qd[h, 0, :, g * 512:(g + 1) * 512]), w=[q1.res])
                    dma(SP, dm(q2[0:70, :], qd[h, 1, :, g * 512:(g + 1) * 512]), w=[q2.res])
                    op(DVE, lambda e: e.memset(Lacc[0][:, :], 0.0), w=[Lacc[0].res])
                    op(POOL, lambda e: e.memset(Lacc[1][:, :], 0.0), w=[Lacc[1].res])
                    tiles = []
                    nkb = 4 * g + 4
                    for kbi in range(nkb):
                        rel = kbi - 4 * g
                        q0 = 128 * rel if rel > 0 else 0
                        n = 512 - q0
                        t = tctr[0]
                        tctr[0] += 1
                        sa, sb2 = ps[(2 * t) % 6], ps[(2 * t + 1) % 6]
                        Pa, Pb2 = Pb[(2 * t) % 6], Pb[(2 * t + 1) % 6]
                        first, last = kbi == 0, kbi == nkb - 1

                        def s1(kbi=kbi, q0=q0, n=n, sa=sa, sb2=sb2, q1=q1, q2=q2):
                            op(PE, mm(sa[:, 0:n], k1[0:70, kbi * 128:(kbi + 1) * 128], q1[0:70, q0:q0 + n], True, True), r=[k1.res, q1.res], w=[sa.res])
                            op(PE, mm(sb2[:, 0:n], k2[0:70, kbi * 128:(kbi + 1) * 128], q2[0:70, q0:q0 + n], True, True), r=[k2.res, q2.res], w=[sb2.res])

                        def s2(n=n, q0=q0, sa=sa, sb2=sb2, Pa=Pa, Pb2=Pb2, rel=rel, h=h):
                            for s_, p_, eng, la in ((sa, Pa, DVE, Lacc[0]), (sb2, Pb2, POOL, Lacc[1])):
                                op(ACT, act(p_[:, 0:n], s_[:, 0:n], AF.Exp, scale=0.125), r=[s_.res], w=[p_.res])
                                if rel >= 0:
                                    op(DVE, tt(p_[:, 0:128], p_[:, 0:128], cmat[:, C_DFD + 128 * h:C_DFD + 128 * h + 128], ALU.mult), r=[p_.res, cmat.res], w=[p_.res])
                                op(eng, tt(la[:, q0:q0 + n], la[:, q0:q0 + n], p_[:, 0:n], ALU.add), r=[la.res, p_.res], w=[la.res])

                        def s3(kbi=kbi, q0=q0, n=n, Pa=Pa, Pb2=Pb2, v=v, first=first, last=last):
                            op(PE, mm(O1[:, q0:q0 + n], v[:, kbi, :], Pa[:, 0:n], first, last), r=[v.res, Pa.res], w=[O1.res])
                            op(PE, mm(O2[:, q0:q0 + n], v[:, kbi, :], Pb2[:, 0:n], first, last), r=[v.res, Pb2.res], w=[O2.res])
                        tiles.append([s1, s2, s3])
                    pipeline(tiles, [0, 1, 2])
                    L1, L2 = ps[(2 * tctr[0]) % 6], ps[(2 * tctr[0] + 1) % 6]
                    op(PE, mm(L1[:, :], cf32[:, F_ONES:F_ONES + 128], Lacc[0][:, :], True, True), r=[cf32.res, Lacc[0].res], w=[L1.res])
                    op(PE, mm(L2[:, :], cf32[:, F_ONES:F_ONES + 128], Lacc[1][:, :], True, True), r=[cf32.res, Lacc[1].res], w=[L2.res])
                    op(DVE, lambda e, L1=L1: e.reciprocal(out=f1[:, :], in_=L1[:, :]), r=[L1.res], w=[f1.res])
                    op(DVE, tt(f1[:, :], O1[:, :], f1[:, :], ALU.mult), r=[O1.res, f1.res], w=[f1.res])
                    op(DVE, lambda e, L2=L2: e.reciprocal(out=f2[:, :], in_=L2[:, :]), r=[L2.res], w=[f2.res])
                    op(DVE, tt(f2[:, :], O2[:, :], f2[:, :], ALU.mult), r=[O2.res, f2.res], w=[f2.res])
                    op(DVE, stt(f3[:, :], f2[:, :], dcol[:, j:j + 1], f1[:, :], ALU.mult, ALU.add), r=[f2.res, dcol.res, f1.res], w=[f3.res])
                    op(POOL, tt(fsq[:, :], f3[:, :], f3[:, :], ALU.mult), r=[f3.res], w=[fsq.res])
                    pn = ps[(2 * tctr[0] + 2) % 6]
                    op(PE, mm(pn[:, :], ONES(), fsq[:, :], True, True), r=[cmat.res, fsq.res], w=[pn.res])
                    rstd_ops(f1, pn, 128, 128.0)
                    o = ob[octr[0] % 2]
                    octr[0] += 1
                    op(DVE, stt(o[:, :], f3[:, :], dcol[:, 2 + j:3 + j], f1[:, :], ALU.mult, ALU.mult), r=[f3.res, dcol.res, f1.res], w=[o.res])
                    dma(POOL, dm(mixT[512 + 128 * h:512 + 128 * h + 128, g * 512:(g + 1) * 512], o[:, :]), r=[o.res])
        sch.barrier()

    stop = int(os.environ.get("KSTOP", "99"))
    cnt = 0
    for l in range(DEPTH + 1):
        if cnt >= stop:
            break
        proj_phase(l)
        cnt += 1
        if l < DEPTH:
            if cnt >= stop:
                break
            if l % 2 == 0:
                attn_even(l // 2)
            else:
                attn_odd(l // 2)
            cnt += 1
    sch.barrier()

    with nc.Block() as block:
        @block.tensor
        def _(e):
            sch.emit(PE, e)

        @block.scalar
        def _(e):
            sch.emit(ACT, e)

        @block.vector
        def _(e):
            sch.emit(DVE, e)

        @block.gpsimd
        def _(e):
            sch.emit(POOL, e)

        @block.sync
        def _(e):
            sch.emit(SP, e)
    stack.close()
    return nc


def host_consts(S):
    bf = ml_dtypes.bfloat16
    cm = np.zeros((128, NCM), np.float32)
    cm[:, C_ONES:C_ONES + 128] = 1.0
    for m in range(128):
        if m % 32 < 16:
            cm[m + 16, C_PERM + m] = -1.0
        else:
            cm[m - 16, C_PERM + m] = 1.0
    jj, kk = np.meshgrid(np.arange(128), np.arange(128), indexing="ij")
    cm[:, C_NEGU:C_NEGU + 128] = -(jj >= kk).astype(np.float32)
    cm[:, C_NEGONES:C_NEGONES + 128] = -1.0
    k_, q_ = jj, kk
    cm[:, C_DFB:C_DFB + 128] = ((k_ // 64) <= (q_ // 64)).astype(np.float32)
    cm[:, C_DFC:C_DFC + 128] = (k_ < q_).astype(np.float32)
    for h in range(4):
        m = 2.0 ** (-8.0 * (h + 1) / 4)
        d = np.where(k_ <= q_, 1.0, np.where((k_ // 64) == (q_ // 64), np.exp(-2.0 * m * (k_ - q_)), 0.0))
        cm[:, C_DFD + 128 * h:C_DFD + 128 * h + 128] = d
    p = np.arange(128)
    for c in range(4):
        cm[p, C_SEL2 + 8 * c + 2 * c + p // 64] = 1.0
    for r in range(2):
        cm[p, C_SEL4 + 8 * r + 4 * r + p // 32] = 1.0
    cf = np.zeros((128, NCF), np.float32)
    for c in range(4):
        cf[2 * c + p // 64, F_SEL2T + 128 * c + p] = 1.0
    for r in range(2):
        cf[4 * r + p // 32, F_SEL4T + 128 * r + p] = 1.0
    for h in range(8):
        cf[h, F_KSELT + 32 * h:F_KSELT + 32 * h + 32] = 1.0
    for h in range(8):
        m = 2.0 ** (-8.0 * (h + 1) / 8)
        kpos = np.arange(128)[:, None] - 128
        qpos = np.arange(128)[None, :]
        dch = qpos // 64 - np.floor_divide(kpos, 64)
        ok = (dch >= 0) & (dch <= 2)
        cf[:, F_ABIAS + 384 * h:F_ABIAS + 384 * h + 128] = np.where(ok, -m * np.abs(qpos - kpos), -30000.0)
        kpos = np.arange(128)[:, None]
        qpos = np.arange(256)[None, :]
        dch = qpos // 64 - kpos // 64
        ok = (dch >= 0) & (dch <= 2)
        cf[:, F_ABIAS + 384 * h + 128:F_ABIAS + 384 * h + 384] = np.where(ok, -m * np.abs(qpos - kpos), -30000.0)
    half = 16
    inv = (np.float32(10000.0) ** (-np.arange(half, dtype=np.float32) / np.float32(half))).astype(np.float32)
    cf[:, F_INVF] = inv[p % 16]
    for m in range(64):
        cf[64 + m, F_SHIFT + m] = 1.0
    cf[:, F_ONES:F_ONES + 128] = 1.0
    pos = np.arange(S)
    a, b, c = pos // 1024, (pos % 1024) // 32, pos % 32
    daug = np.zeros((4, 2, 6, S), np.float32)
    for h in range(4):
        m = 2.0 ** (-8.0 * (h + 1) / 4) * 8.0
        daug[h, 0, 0], daug[h, 0, 1], daug[h, 0, 2] = -m * 1024 * a, -m * 32 * b, -m * c
        daug[h, 0, 3:6] = 1.0
        daug[h, 1, 0:3] = 1.0
        daug[h, 1, 3], daug[h, 1, 4], daug[h, 1, 5] = m * 1024 * a, m * 32 * b, m * c
    return cm.astype(bf), cf, daug.astype(bf)


def host_pcols(inp):
    pc = np.zeros((128, NPC), np.float32)
    p = np.arange(128)
    for l in range(4):
        pc[:, l * 16:l * 16 + 8] = inp["norm_mix_g"][l].reshape(8, 128).T
        pc[:, l * 16 + 8:l * 16 + 16] = inp["norm_ffn_g"][l].reshape(8, 128).T
    for j in range(2):
        b = 64 + 32 * j
        pc[:, b + 0] = inp["a_q_norm"][j][p % 64]
        pc[:, b + 1] = inp["a_k_norm"][j][p % 64]
        pc[:, b + 2:b + 4] = inp["b_cq_norm"][j].reshape(2, 128).T
        pc[:, b + 4] = inp["b_ckv_norm"][j]
        pc[:, b + 5] = inp["b_q_norm"][j][p % 64]
        pc[:, b + 6] = inp["b_q_norm"][j][64 + p % 32]
        pc[:, b + 7] = inp["b_k_norm"][j][p % 64]
        pc[:, b + 8] = inp["b_k_norm"][j][64 + p % 32]
        pc[:, b + 9:b + 17] = inp["a_sinks"][j][None, :]
        b = 128 + 32 * j
        pc[:, b + 0] = inp["d_q_norm"][j].reshape(128)
        pc[:, b + 1] = inp["d_k_norm"][j].reshape(128)
        pc[:, b + 2] = inp["d_subln"][j]
    dlam = np.broadcast_to(inp["d_lambda"].reshape(1, 2, 256), (128, 2, 256)).astype(np.float32)
    return pc, np.ascontiguousarray(dlam)


_CACHE = {}


def kernel(**inputs):
    inp = {k: np.asarray(v) for k, v in inputs.items()}
    x = inp["x"]
    B, S, D = x.shape
    if S not in _CACHE:
        _CACHE[S] = build(S)
    nc = _CACHE[S]
    cm, cf, daug = host_consts(S)
    pc, dlam = host_pcols(inp)
    shared = {
        "pcols": pc, "dlam": dlam, "cmat": cm, "cf32": cf, "daug": daug,
        "mlp_w_up": inp["mlp_w_up"], "mlp_w_down": inp["mlp_w_down"],
        "ev_w_in": inp["ev_w_in"], "ev_w_out": inp["ev_w_out"],
        "b_w_uq": inp["b_w_uq"], "b_w_ukv": inp["b_w_ukv"],
        "od_w_in": inp["od_w_in"], "od_w_out": inp["od_w_out"],
    }
    in_maps = []
    for b in range(B):
        m = dict(shared)
        m["xT"] = np.ascontiguousarray(x[b].T)
        m["posb"] = np.ascontiguousarray(np.broadcast_to(inp["positions"][b][None, :], (128, S))).astype(np.int32)
        in_maps.append(m)
    res = run_bass_kernel_spmd(nc, in_maps, core_ids=list(range(B)))
    out = np.stack([np.ascontiguousarray(r["yT"].T) for r in res.results], axis=0)
    return out.astype(np.float32)
```

```python
import math
import os
from contextlib import ExitStack

import numpy as np
import ml_dtypes
import concourse.bass as bass
import concourse.mybir as mybir
from concourse.bass_utils import run_bass_kernel_spmd

F32, BF16, I32 = mybir.dt.float32, mybir.dt.bfloat16, mybir.dt.int32
AF = mybir.ActivationFunctionType
ALU = mybir.AluOpType
AX = mybir.AxisListType
PE, ACT, DVE, POOL, SP = 0, 1, 2, 3, 4
EPS = 1e-6
DEPTH = 4
TWO_PI = 2.0 * math.pi

C_ONES, C_PERM, C_NEGU, C_NEGONES, C_DFB, C_DFC, C_DFD, C_SEL2, C_SEL4 = 0, 128, 256, 384, 512, 640, 768, 1280, 1312
NCM = 1328
F_SEL2T, F_SEL4T, F_KSELT, F_ABIAS, F_INVF, F_SHIFT, F_ONES = 0, 512, 768, 1024, 4096, 4097, 4161
NCF = 4289
NPC = 192


class Res:
    __slots__ = ("lw", "rd", "sem", "cnt", "ssem", "scnt")

    def __init__(self):
        self.lw = None
        self.rd = []
        self.sem = None
        self.cnt = 0
        self.ssem = None
        self.scnt = 0


class Sched:
    def __init__(self, nc, stack):
        self.nc = nc
        self.stack = stack
        self.q = [[] for _ in range(5)]
        self.seq = [0] * 5
        self.esem = [stack.enter_context(nc.semaphore(f"es{i}")) for i in range(5)]
        self.waited = [dict() for _ in range(5)]
        self.dsems = []
        self.free_dsems = []
        self.free_ssems = []
        self.ssems = []
        self.semcnt = {}
        self.nds = 0

    def _waits(self, e, deps):
        best = {}
        for (sem, val, src) in deps:
            if e == PE and src == PE:
                continue
            k = id(sem)
            if self.waited[e].get(k, 0) >= val:
                continue
            if k not in best or best[k][1] < val:
                best[k] = (sem, val)
        out = []
        for k, (sem, val) in best.items():
            self.waited[e][k] = val
            out.append((sem, val))
        return out

    def _deps(self, r, w, e=-9):
        deps = []
        for x in r:
            if x.lw is not None:
                deps.append(x.lw)
        for x in w:
            if x.lw is not None and not x.rd:
                deps.append(x.lw)
            deps.extend(x.rd)
        return deps

    def op(self, e, fn, r=(), w=()):
        waits = self._waits(e, self._deps(r, w, e))
        self.seq[e] += 1
        tok = (self.esem[e], self.seq[e], e)
        self.q[e].append((waits, fn, (self.esem[e], 1), True))
        for x in r:
            x.rd.append(tok)
        for x in w:
            x.lw = tok
            x.rd = []

    def dma(self, qe, fn, r=(), w=(), semres=None):
        waits = self._waits(qe, self._deps(r, w))
        sr = semres if semres is not None else (w[0] if w else r[0])
        sw = qe == POOL
        sa, ca = ("ssem", "scnt") if sw else ("sem", "cnt")
        free = self.free_ssems if sw else self.free_dsems
        if getattr(sr, sa) is None:
            if free:
                sem = free.pop()
            else:
                self.nds += 1
                sem = self.stack.enter_context(self.nc.semaphore(f"ds{self.nds}"))
            setattr(sr, sa, sem)
            setattr(sr, ca, self.semcnt.get(id(sem), 0))
            (self.ssems if sw else self.dsems).append(sr)
        sem = getattr(sr, sa)
        cnt = getattr(sr, ca) + 16
        setattr(sr, ca, cnt)
        self.semcnt[id(sem)] = cnt
        tok = (sem, cnt, -1)
        self.q[qe].append((waits, fn, (sem, 16), False))
        for x in r:
            x.rd.append(tok)
        for x in w:
            x.lw = tok
            x.rd = []

    def barrier(self):
        toks = [(self.esem[i], self.seq[i], i) for i in range(5) if self.seq[i] > 0]
        toks += [(sr.sem, sr.cnt, -1) for sr in self.dsems]
        toks += [(sr.ssem, sr.scnt, -1) for sr in self.ssems]
        for e in range(5):
            deps = [t for t in toks if t[2] != e]
            waits = self._waits(e, [(s, v, -2) for (s, v, _) in deps])
            if waits:
                self.q[e].append((waits, None, None, False))
        for sr in self.dsems:
            self.free_dsems.append(sr.sem)
            sr.sem = None
        self.dsems = []
        for sr in self.ssems:
            self.free_ssems.append(sr.ssem)
            sr.ssem = None
        self.ssems = []

    def emit(self, e, eng):
        for (waits, fn, inc, attach) in self.q[e]:
            if fn is None:
                for (sem, val) in waits:
                    eng.wait_ge(sem, val)
                continue
            if attach and waits:
                for (sem, val) in waits[:-1]:
                    eng.wait_ge(sem, val)
                ins = fn(eng)
                ins._wait_ge(*waits[-1])
            else:
                for (sem, val) in waits:
                    eng.wait_ge(sem, val)
                ins = fn(eng)
            ins.then_inc(inc[0], inc[1])


def pipeline(tiles, skews):
    T = len(tiles)
    mx = max(skews)
    for i in range(T + mx):
        for s, sk in enumerate(skews):
            t = i - sk
            if 0 <= t < T and tiles[t][s] is not None:
                tiles[t][s]()


class T:
    __slots__ = ("h", "res")

    def __init__(self, h):
        self.h = h
        self.res = Res()

    def __getitem__(self, k):
        return self.h[k]


class PsView:
    __slots__ = ("h", "off", "w", "res")

    def __init__(self, h, off, w):
        self.h, self.off, self.w = h, off, w
        self.res = Res()

    def __getitem__(self, k):
        rs, cs = k
        a = 0 if cs.start is None else cs.start
        b = self.w if cs.stop is None else cs.stop
        return self.h[rs, self.off + a:self.off + b]


def build(S):
    NG = S // 512
    NB = S // 128
    nc = bass.Bass("TRN2", target_bir_lowering=False)

    def din(name, shape, dt=F32):
        return nc.dram_tensor(name, list(shape), dt, kind="ExternalInput").ap()

    def dscr(name, shape, dt=BF16):
        return nc.dram_tensor(name, list(shape), dt).ap()

    xT = din("xT", [1024, S])
    posb = din("posb", [128, S], I32)
    pcols_d = din("pcols", [128, NPC])
    dlam_d = din("dlam", [128, 2, 256])
    cmat_d = din("cmat", [128, NCM], BF16)
    cf32_d = din("cf32", [128, NCF])
    daug_d = din("daug", [4, 2, 6, S], BF16)
    w_up_d = din("mlp_w_up", [4, 1024, 4096])
    w_down_d = din("mlp_w_down", [4, 4096, 1024])
    ev_in_d = din("ev_w_in", [2, 1024, 1184])
    ev_out_d = din("ev_w_out", [2, 1024, 1024])
    uq_d = din("b_w_uq", [2, 256, 768])
    ukv_d = din("b_w_ukv", [2, 128, 1024])
    od_in_d = din("od_w_in", [2, 1024, 3072])
    od_out_d = din("od_w_out", [2, 1024, 1024])
    yT = nc.dram_tensor("yT", [1024, S], F32, kind="ExternalOutput").ap()

    w_up = dscr("s_w_up", [4, 1024, 4096])
    w_down = dscr("s_w_down", [4, 4096, 1024])
    ev_in = dscr("s_ev_in", [2, 1024, 1184])
    ev_out = dscr("s_ev_out", [2, 1024, 1024])
    uq = dscr("s_uq", [2, 256, 768])
    ukv = dscr("s_ukv", [2, 128, 1024])
    od_in = dscr("s_od_in", [2, 1024, 3072])
    od_out = dscr("s_od_out", [2, 1024, 1024])
    xres = dscr("s_xres", [1024, S], F32)
    mixT = dscr("s_mix", [1024, S])
    costab = dscr("s_cos", [128, S], F32)
    sintab = dscr("s_sin", [128, S], F32)
    qa = dscr("s_qa", [8, 64, S])
    ka = dscr("s_ka", [2, 64, S])
    va = dscr("s_va", [2, 128, NB, 64])
    qb = dscr("s_qb", [8, 96, S])
    kb_ = dscr("s_kb", [8, 96, S])
    vb = dscr("s_vb", [8, 128, NB, 64])
    qc = dscr("s_qc", [8, 64, S])
    kc = dscr("s_kc", [8, 64, S])
    vc = dscr("s_vc", [8, 128, NB, 64])
    qd = dscr("s_qd", [4, 2, 70, S])
    kd = dscr("s_kd", [4, 2, 70, S])
    vd = dscr("s_vd", [4, 128, NB, 128])

    stack = ExitStack()
    sch = Sched(nc, stack)
    op, dma = sch.op, sch.dma

    nctr = [0]

    def sb(st, name, shape, dt):
        nctr[0] += 1
        return T(st.enter_context(nc.sbuf_tensor(f"t{nctr[0]}_{name}", list(shape), dt)))

    psall = stack.enter_context(nc.psum_tensor("psall", [128, 4096], F32))
    ps = [PsView(psall, 512 * i, 512) for i in range(8)]
    cmat = sb(stack, "cmat", [128, NCM], BF16)
    cf32 = sb(stack, "cf32", [128, NCF], F32)
    pcols = sb(stack, "pcols", [128, NPC], F32)
    dcol = sb(stack, "dcol", [128, 32], F32)
    epsc = sb(stack, "epsc", [128, 1], F32)
    onec = sb(stack, "onec", [128, 1], F32)

    ONES = lambda k=128, m=128: cmat[0:k, C_ONES:C_ONES + m]

    def act(out, in_, func, scale=1.0, bias=None):
        if bias is None:
            return lambda e: e.activation(out=out, in_=in_, func=func, scale=scale)
        return lambda e: e.activation(out=out, in_=in_, func=func, scale=scale, bias=bias)

    def mm(out, lhsT, rhs, start, stop):
        return lambda e: e.matmul(out, lhsT=lhsT, rhs=rhs, start=start, stop=stop, skip_group_check=True)

    def tt(out, in0, in1, o):
        return lambda e: e.tensor_tensor(out=out, in0=in0, in1=in1, op=o)

    def ts(out, in0, s1, s2, o0, o1=None):
        if o1 is None:
            return lambda e: e.tensor_scalar(out=out, in0=in0, scalar1=s1, scalar2=None, op0=o0)
        return lambda e: e.tensor_scalar(out=out, in0=in0, scalar1=s1, scalar2=s2, op0=o0, op1=o1)

    def stt(out, in0, s, in1, o0, o1):
        return lambda e: e.scalar_tensor_tensor(out=out, in0=in0, scalar=s, in1=in1, op0=o0, op1=o1)

    def cp(out, in_):
        return lambda e: e.tensor_copy(out=out, in_=in_)

    def dm(out, in_):
        return lambda e: e.dma_start(out=out, in_=in_)

    def rstd_ops(dst, src_ps, n, d):
        op(ACT, act(dst[0:n, :], src_ps[0:n, :], AF.Ln, scale=1.0 / d, bias=epsc[0:n, :]), r=[src_ps.res, epsc.res], w=[dst.res])
        op(ACT, act(dst[0:n, :], dst[0:n, :], AF.Exp, scale=-0.5), r=[dst.res], w=[dst.res])

    with ExitStack() as st:
        dma(SP, dm(cmat[:, :], cmat_d), w=[cmat.res])
        dma(SP, dm(cf32[:, :], cf32_d), w=[cf32.res])
        dma(SP, dm(pcols[:, :], pcols_d), w=[pcols.res])
        op(DVE, lambda e: e.memset(epsc[:, :], EPS), w=[epsc.res])
        op(DVE, lambda e: e.memset(onec[:, :], 1.0), w=[onec.res])
        castres = Res()

        def cast2d(dst, src, rows, step=128):
            for r0 in range(0, rows, step):
                r1 = min(rows, r0 + step)
                dma(POOL, dm(dst[r0:r1, :], src[r0:r1, :]), r=[castres], semres=castres)

        for l in range(4):
            cast2d(w_up[l], w_up_d[l], 1024)
            cast2d(w_down[l], w_down_d[l], 4096, 256)
        for j in range(2):
            cast2d(ev_in[j], ev_in_d[j], 1024)
            cast2d(ev_out[j], ev_out_d[j], 1024)
            cast2d(od_in[j], od_in_d[j], 1024)
            cast2d(od_out[j], od_out_d[j], 1024)
            cast2d(uq[j], uq_d[j], 256)
            cast2d(ukv[j], ukv_d[j], 128)
        augres = Res()
        for h in range(4):
            for m in range(2):
                dma(SP, dm(qd[h, m, 64:70, :], daug_d[h, 0]), r=[augres], semres=augres)
                dma(SP, dm(kd[h, m, 64:70, :], daug_d[h, 1]), r=[augres], semres=augres)
        dl = sb(st, "dl", [128, 2, 256], F32)
        dlp = sb(st, "dlp", [128, 64], F32)
        dma(SP, dm(dl[:, :, :], dlam_d), w=[dl.res])
        for j in range(2):
            layer = 2 * j + 1
            lam_init = 0.8 - 0.6 * math.exp(-0.3 * layer)
            for t in range(2):
                op(DVE, tt(dlp[:, :], dl[:, j, 128 * t:128 * t + 64], dl[:, j, 128 * t + 64:128 * t + 128], ALU.mult), r=[dl.res], w=[dlp.res])
                op(DVE, lambda e, t=t, j=j: e.reduce_sum(out=dcol[:, 20 + t:21 + t], in_=dlp[:, :], axis=AX.X), r=[dlp.res], w=[dcol.res])
                op(ACT, act(dcol[:, 20 + t:21 + t], dcol[:, 20 + t:21 + t], AF.Exp), r=[dcol.res], w=[dcol.res])
            op(DVE, tt(dcol[:, 22:23], dcol[:, 21:22], dcol[:, 20:21], ALU.subtract), r=[dcol.res], w=[dcol.res])
            op(DVE, ts(dcol[:, j:j + 1], dcol[:, 22:23], -lam_init, None, ALU.add), r=[dcol.res], w=[dcol.res])
            op(DVE, ts(dcol[:, 2 + j:3 + j], pcols[:, 128 + 32 * j + 2:128 + 32 * j + 3], 1.0 - lam_init, None, ALU.mult), r=[pcols.res, dcol.res], w=[dcol.res])
        for j in range(2):
            op(ACT, act(dcol[:, 4 + 8 * j:12 + 8 * j], pcols[:, 64 + 32 * j + 9:64 + 32 * j + 17], AF.Exp), r=[pcols.res, dcol.res], w=[dcol.res])
        CH = min(S, 2048)
        posi = sb(st, "posi", [128, CH], I32)
        ang = sb(st, "ang", [128, CH], F32)
        tq = sb(st, "tq", [128, CH], F32)
        ki = sb(st, "ki", [128, CH], I32)
        kf = sb(st, "kf", [128, CH], F32)
        rr = sb(st, "rr", [128, CH], F32)
        mk = sb(st, "mk", [128, CH], F32)
        C1 = 6.28125
        C2 = TWO_PI - C1
        for c0 in range(0, S, CH):
            dma(SP, dm(posi[:, :], posb[:, c0:c0 + CH]), w=[posi.res])
            op(DVE, cp(ang[:, :], posi[:, :]), r=[posi.res], w=[ang.res])
            op(DVE, ts(ang[:, :], ang[:, :], cf32[:, F_INVF:F_INVF + 1], None, ALU.mult), r=[ang.res, cf32.res], w=[ang.res])
            for which, tab in ((0, sintab), (1, costab)):
                src = ang
                if which == 1:
                    op(DVE, ts(rr[:, :], ang[:, :], math.pi / 2, None, ALU.add), r=[ang.res], w=[rr.res])
                    src = rr
                op(DVE, ts(tq[:, :], src[:, :], 1.0 / TWO_PI, None, ALU.mult), r=[src.res], w=[tq.res])
                op(DVE, cp(ki[:, :], tq[:, :]), r=[tq.res], w=[ki.res])
                op(DVE, cp(kf[:, :], ki[:, :]), r=[ki.res], w=[kf.res])
                op(DVE, stt(rr[:, :], kf[:, :], -C1, src[:, :], ALU.mult, ALU.add), r=[kf.res, src.res], w=[rr.res])
                op(DVE, stt(rr[:, :], kf[:, :], -C2, rr[:, :], ALU.mult, ALU.add), r=[kf.res, rr.res], w=[rr.res])
                op(DVE, ts(mk[:, :], rr[:, :], math.pi, -TWO_PI, ALU.is_gt, ALU.mult), r=[rr.res], w=[mk.res])
                op(DVE, tt(rr[:, :], rr[:, :], mk[:, :], ALU.add), r=[rr.res, mk.res], w=[rr.res])
                op(DVE, ts(mk[:, :], rr[:, :], -math.pi, TWO_PI, ALU.is_lt, ALU.mult), r=[rr.res], w=[mk.res])
                op(DVE, tt(rr[:, :], rr[:, :], mk[:, :], ALU.add), r=[rr.res, mk.res], w=[rr.res])
                op(DVE, ts(rr[:, :], rr[:, :], math.pi, -math.pi, ALU.min, ALU.max), r=[rr.res], w=[rr.res])
                op(ACT, act(tq[:, :], rr[:, :], AF.Sin), r=[rr.res], w=[tq.res])
                dma(SP, dm(tab[:, c0:c0 + CH], tq[:, :]), r=[tq.res])
        sch.barrier()

    def wview(w2d, c0, ncol, nchunk=8):
        return w2d.rearrange("(i p) c -> p i c", p=128)[:, 0:nchunk, c0:c0 + ncol]

    class Ctx:
        pass

    def proj_phase(l):
        with ExitStack() as st:
            xs = sb(st, "xs", [128, 8, 512], F32)
            xn = sb(st, "xn", [128, 8, 512], BF16)
            sq = sb(st, "sq", [128, 8, 512], BF16)
            sqr = [Res() for _ in range(8)]
            xsr = [Res() for _ in range(8)]
            xnr = [Res() for _ in range(8)]
            rstd = sb(st, "rstd", [128, 512], F32)
            wts = [sb(st, f"wt{i}", [128, 8, 512], BF16) for i in range(4)]
            wctr = [0]
            if l > 0:
                H = sb(st, "H", [128, 32, 512], BF16)
                mx = sb(st, "mx", [128, 8, 512], BF16)
                rl = [sb(st, f"rl{i}", [128, 512], BF16) for i in range(2)]
            if l < DEPTH:
                raws = [sb(st, f"raw{i}", [128, 512], F32) for i in range(8)]
                sqs = [sb(st, f"sqs{i}", [128, 512], BF16) for i in range(8)]
                outs = [sb(st, f"ob{i}", [128, 512], BF16) for i in range(4)]
                rc = sb(st, "rc", [8, 512], F32)
                vt = sb(st, "vt", [128, 4, 512], BF16)
                octr = [0]
                if l % 2 == 0:
                    cqn = sb(st, "cqn", [128, 2, 512], BF16)
                    ckvn = sb(st, "ckvn", [128, 512], BF16)
                    cost = sb(st, "cost", [128, 512], F32)
                    sint = sb(st, "sint", [128, 512], F32)
                    t1 = sb(st, "t1", [128, 512], F32)
                    t2 = sb(st, "t2", [128, 512], F32)
                    krr = sb(st, "krr", [32, 512], F32)
                    wsm = sb(st, "wsm", [128, 2, 768], BF16)
                    wkv = sb(st, "wkv", [128, 1024], BF16)
            pctr = [0]
            STQ = ACT

            def nps():
                pctr[0] += 1
                return ps[pctr[0] % 4]

            def load_w(view, nchunk=8, ncol=512):
                wt = wts[wctr[0] % 4]
                wctr[0] += 1
                dma(SP, dm(wt[:, 0:nchunk, 0:ncol], view), w=[wt.res])
                return wt

            def square(c):
                op(POOL if c % 2 == 0 else DVE, tt(sq[:, c, :], xs[:, c, :], xs[:, c, :], ALU.mult), r=[xsr[c]], w=[sqr[c]])

            def norm(gbase, do_sq=True):
                if do_sq:
                    for c in range(8):
                        square(c)
                p = nps()
                for c in range(8):
                    op(PE, mm(p[:, :], ONES(), sq[:, c, :], c == 0, c == 7), r=[sqr[c], cmat.res], w=[p.res])
                rstd_ops(rstd, p, 128, 1024.0)
                for c in range(8):
                    op(DVE, stt(xn[:, c, :], xs[:, c, :], pcols[:, gbase + c:gbase + c + 1], rstd[:, :], ALU.mult, ALU.mult),
                       r=[xsr[c], pcols.res, rstd.res], w=[xnr[c]])

            def outbuf():
                o = outs[octr[0] % 4]
                octr[0] += 1
                return o

            def proj_chunk(wt, col0, ncol, src, nchunk=8, srcidx=None):
                p = nps()
                for i in range(nchunk):
                    rhs = src[:, i, :] if srcidx is None else srcidx(i)
                    op(PE, mm(p[0:ncol, :], wt[:, i, col0:col0 + ncol], rhs, i == 0, i == nchunk - 1), r=[wt.res, xnr[i] if src is xn else src.res], w=[p.res])
                return p

            def vproj(wt, col0, ncol, src, nchunk, dst_fn, srcidx=None):
                for tb in range(4):
                    p = nps()
                    for i in range(nchunk):
                        lhs = src[:, i, tb * 128:(tb + 1) * 128] if srcidx is None else srcidx(i, tb)
                        op(PE, mm(p[:, 0:ncol], lhs, wt[:, i, col0:col0 + ncol], i == 0, i == nchunk - 1), r=[wt.res, xnr[i] if src is xn else src.res], w=[p.res])
                    op(ACT, act(vt[:, tb, 0:ncol], p[:, 0:ncol], AF.Copy), r=[p.res], w=[vt.res])
                dst_fn()

            for g in range(NG):
                gs = slice(g * 512, (g + 1) * 512)
                src_x = xT if l == 0 else xres
                dma(SP, dm(xs[:, :, :], src_x.rearrange("(c p) s -> p c s", p=128)[:, :, gs]), w=xsr)
                if l > 0:
                    lp = l - 1
                    jp = lp // 2
                    dma(SP, dm(mx[:, :, :], mixT.rearrange("(c p) s -> p c s", p=128)[:, :, gs]), w=[mx.res])
                    wout = (ev_out if lp % 2 == 0 else od_out)[jp]
                    for half in range(2):
                        wt = load_w(wview(wout, half * 512, 512))
                        for oc4 in range(4):
                            oc = half * 4 + oc4
                            p = proj_chunk(wt, oc4 * 128, 128, mx)
                            op(DVE, tt(xs[:, oc, :], p[:, :], xs[:, oc, :], ALU.add), r=[p.res, xsr[oc]], w=[xsr[oc]])
                            square(oc)
                    norm(lp * 16 + 8, do_sq=False)
                    for fb in range(8):
                        wt = load_w(wview(w_up[lp], fb * 512, 512))
                        for f4 in range(4):
                            fc = fb * 4 + f4
                            p = proj_chunk(wt, f4 * 128, 128, xn)
                            r_ = rl[fc % 2]
                            op(DVE, ts(r_[:, :], p[:, :], 0.0, None, ALU.max), r=[p.res], w=[r_.res])
                            op(POOL, tt(H[:, fc, :], r_[:, :], r_[:, :], ALU.mult), r=[r_.res], w=[H.res])
                    for half in range(2):
                        accs = [ps[4 + i] for i in range(4)]
                        for fs in range(4):
                            view = w_down[lp].rearrange("(i p) c -> p i c", p=128)[:, fs * 8:(fs + 1) * 8, half * 512:(half + 1) * 512]
                            wt = load_w(view)
                            for oc4 in range(4):
                                for i in range(8):
                                    fc = fs * 8 + i
                                    op(PE, mm(accs[oc4][:, :], wt[:, i, oc4 * 128:(oc4 + 1) * 128], H[:, fc, :], fc == 0, fc == 31),
                                       r=[wt.res, H.res], w=[accs[oc4].res])
                        for oc4 in range(4):
                            oc = half * 4 + oc4
                            op(DVE, tt(xs[:, oc, :], accs[oc4][:, :], xs[:, oc, :], ALU.add), r=[accs[oc4].res, xsr[oc]], w=[xsr[oc]])
                            if l < DEPTH:
                                square(oc)
                if l == DEPTH:
                    dma(STQ, dm(yT.rearrange("(c p) s -> p c s", p=128)[:, :, gs], xs[:, :, :]), r=xsr)
                    continue
                dma(STQ, dm(xres.rearrange("(c p) s -> p c s", p=128)[:, :, gs], xs[:, :, :]), r=xsr)
                norm(l * 16, do_sq=(l == 0))
                j = l // 2
                if l % 2 == 1:
                    pb = 128 + 32 * j
                    win = od_in[j]
                    for part, dst, scale in ((0, qc, 0.125), (1, kc, 1.0)):
                        wt = load_w(wview(win, part * 512, 512))
                        for c in range(4):
                            p = proj_chunk(wt, c * 128, 128, xn)
                            o = outbuf()
                            op(ACT, act(o[:, :], p[:, :], AF.Copy, scale=scale), r=[p.res], w=[o.res])
                            for hh in range(2):
                                dma(STQ, dm(dst[2 * c + hh, :, gs], o[64 * hh:64 * hh + 64, :]), r=[o.res])
                    wt = load_w(wview(win, 1024, 512))

                    def st_cv():
                        for h in range(8):
                            dma(STQ, dm(vc[h, :, 4 * g:4 * g + 4, :], vt[:, :, 64 * h:64 * h + 64]), r=[vt.res])
                    vproj(wt, 0, 512, xn, 8, st_cv)
                    for part, dst, gcol in ((3, qd, pb + 0), (4, kd, pb + 1)):
                        wt = load_w(wview(win, part * 512, 512))
                        pc = nps()
                        for c in range(4):
                            p = proj_chunk(wt, c * 128, 128, xn)
                            op(ACT, act(raws[c][:, :], p[:, :], AF.Copy), r=[p.res], w=[raws[c].res])
                            op(POOL, tt(sqs[c][:, :], raws[c][:, :], raws[c][:, :], ALU.mult), r=[raws[c].res], w=[sqs[c].res])
                        for c in range(4):
                            op(PE, mm(pc[0:8, :], cmat[:, C_SEL2 + 8 * c:C_SEL2 + 8 * c + 8], sqs[c][:, :], c == 0, c == 3), r=[cmat.res, sqs[c].res], w=[pc.res])
                        rstd_ops(rc, pc, 8, 64.0)
                        for c in range(4):
                            pbc = nps()
                            op(PE, mm(pbc[:, :], cf32[0:8, F_SEL2T + 128 * c:F_SEL2T + 128 * c + 128], rc[0:8, :], True, True), r=[cf32.res, rc.res], w=[pbc.res])
                            o = outbuf()
                            op(DVE, stt(o[:, :], raws[c][:, :], pcols[:, gcol:gcol + 1], pbc[:, :], ALU.mult, ALU.mult), r=[raws[c].res, pcols.res, pbc.res], w=[o.res])
                            for m in range(2):
                                dma(STQ, dm(dst[c, m, 0:64, gs], o[64 * m:64 * m + 64, :]), r=[o.res])
                    wt = load_w(wview(win, 2560, 512))

                    def st_dv():
                        for h in range(4):
                            dma(STQ, dm(vd[h, :, 4 * g:4 * g + 4, :], vt[:, :, 128 * h:128 * h + 128]), r=[vt.res])
                    vproj(wt, 0, 512, xn, 8, st_dv)
                else:
                    pb = 64 + 32 * j
                    win = ev_in[j]
                    wt = load_w(wview(win, 0, 512))
                    pc = nps()
                    for c in range(4):
                        p = proj_chunk(wt, c * 128, 128, xn)
                        op(ACT, act(raws[c][:, :], p[:, :], AF.Copy), r=[p.res], w=[raws[c].res])
                        op(POOL, tt(sqs[c][:, :], raws[c][:, :], raws[c][:, :], ALU.mult), r=[raws[c].res], w=[sqs[c].res])
                    for c in range(4):
                        op(PE, mm(pc[0:8, :], cmat[:, C_SEL2 + 8 * c:C_SEL2 + 8 * c + 8], sqs[c][:, :], c == 0, c == 3), r=[cmat.res, sqs[c].res], w=[pc.res])
                    rstd_ops(rc, pc, 8, 64.0)
                    for c in range(4):
                        pbc = nps()
                        op(PE, mm(pbc[:, :], cf32[0:8, F_SEL2T + 128 * c:F_SEL2T + 128 * c + 128], rc[0:8, :], True, True), r=[cf32.res, rc.res], w=[pbc.res])
                        o = outbuf()
                        op(DVE, stt(o[:, :], raws[c][:, :], pcols[:, pb:pb + 1], pbc[:, :], ALU.mult, ALU.mult), r=[raws[c].res, pcols.res, pbc.res], w=[o.res])
                        for hh in range(2):
                            dma(STQ, dm(qa[2 * c + hh, :, gs], o[64 * hh:64 * hh + 64, :]), r=[o.res])
                    wt = load_w(wview(win, 512, 512))
                    p = proj_chunk(wt, 0, 128, xn)
                    op(ACT, act(raws[0][:, :], p[:, :], AF.Copy), r=[p.res], w=[raws[0].res])
                    op(POOL, tt(sqs[0][:, :], raws[0][:, :], raws[0][:, :], ALU.mult), r=[raws[0].res], w=[sqs[0].res])
                    pc = nps()
                    op(PE, mm(pc[0:8, :], cmat[:, C_SEL2:C_SEL2 + 8], sqs[0][:, :], True, True), r=[cmat.res, sqs[0].res], w=[pc.res])
                    rstd_ops(rc, pc, 8, 64.0)
                    pbc = nps()
                    op(PE, mm(pbc[:, :], cf32[0:8, F_SEL2T:F_SEL2T + 128], rc[0:8, :], True, True), r=[cf32.res, rc.res], w=[pbc.res])
                    o = outbuf()
                    op(DVE, stt(o[:, :], raws[0][:, :], pcols[:, pb + 1:pb + 2], pbc[:, :], ALU.mult, ALU.mult), r=[raws[0].res, pcols.res, pbc.res], w=[o.res])
                    for hh in range(2):
                        dma(STQ, dm(ka[hh, :, gs], o[64 * hh:64 * hh + 64, :]), r=[o.res])

                    def st_av():
                        for h in range(2):
                            dma(STQ, dm(va[h, :, 4 * g:4 * g + 4, :], vt[:, :, 64 * h:64 * h + 64]), r=[vt.res])
                    vproj(wt, 128, 128, xn, 8, st_av)
                    for c in range(2):
                        p = proj_chunk(wt, 256 + c * 128, 128, xn)
                        op(ACT, act(raws[c][:, :], p[:, :], AF.Copy), r=[p.res], w=[raws[c].res])
                        op(POOL, tt(sqs[c][:, :], raws[c][:, :], raws[c][:, :], ALU.mult), r=[raws[c].res], w=[sqs[c].res])
                    pc = nps()
                    for c in range(2):
                        op(PE, mm(pc[:, :], ONES(), sqs[c][:, :], c == 0, c == 1), r=[cmat.res, sqs[c].res], w=[pc.res])
                    rstd_ops(rstd, pc, 128, 256.0)
                    for c in range(2):
                        op(DVE, stt(cqn[:, c, :], raws[c][:, :], pcols[:, pb + 2 + c:pb + 3 + c], rstd[:, :], ALU.mult, ALU.mult), r=[raws[c].res, pcols.res, rstd.res], w=[cqn.res])
                    wt = load_w(wview(win, 1024, 160), 8, 160)
                    p = proj_chunk(wt, 0, 128, xn)
                    op(ACT, act(raws[0][:, :], p[:, :], AF.Copy), r=[p.res], w=[raws[0].res])
                    op(POOL, tt(sqs[0][:, :], raws[0][:, :], raws[0][:, :], ALU.mult), r=[raws[0].res], w=[sqs[0].res])
                    pc = nps()
                    op(PE, mm(pc[:, :], ONES(), sqs[0][:, :], True, True), r=[cmat.res, sqs[0].res], w=[pc.res])
                    rstd_ops(rstd, pc, 128, 128.0)
                    op(DVE, stt(ckvn[:, :], raws[0][:, :], pcols[:, pb + 4:pb + 5], rstd[:, :], ALU.mult, ALU.mult), r=[raws[0].res, pcols.res, rstd.res], w=[ckvn.res])
                    pkr = proj_chunk(wt, 128, 32, xn)
                    op(ACT, act(raws[6][0:32, :], pkr[0:32, :], AF.Copy), r=[pkr.res], w=[raws[6].res])
                    op(POOL, tt(sqs[6][0:32, :], raws[6][0:32, :], raws[6][0:32, :], ALU.mult), r=[raws[6].res], w=[sqs[6].res])
                    dma(SP, dm(cost[:, :], costab[:, gs]), w=[cost.res])
                    dma(SP, dm(sint[:, :], sintab[:, gs]), w=[sint.res])
                    op(DVE, ts(t1[0:32, :], raws[6][0:32, :], pcols[0:32, pb + 8:pb + 9], None, ALU.mult), r=[raws[6].res, pcols.res], w=[t1.res])
                    op(DVE, cp(sqs[7][0:32, :], t1[0:32, :]), r=[t1.res], w=[sqs[7].res])
                    prot = nps()
                    op(PE, mm(prot[0:32, :], cmat[0:32, C_PERM:C_PERM + 32], sqs[7][0:32, :], True, True), r=[cmat.res, sqs[7].res], w=[prot.res])
                    op(DVE, tt(t2[0:32, :], prot[0:32, :], sint[0:32, :], ALU.mult), r=[prot.res, sint.res], w=[t2.res])
                    op(DVE, tt(t1[0:32, :], t1[0:32, :], cost[0:32, :], ALU.mult), r=[t1.res, cost.res], w=[t1.res])
                    op(DVE, tt(krr[0:32, :], t1[0:32, :], t2[0:32, :], ALU.add), r=[t1.res, t2.res], w=[krr.res])
                    uqv = uq[j].rearrange("(i p) (h d) -> p i h d", p=128, d=96)
                    for i in range(2):
                        dma(SP, dm(wsm[:, i, 0:512].rearrange("p (h d) -> p h d", d=64), uqv[:, i, :, 0:64]), w=[wsm.res])
                        dma(SP, dm(wsm[:, i, 512:768].rearrange("p (h d) -> p h d", d=32), uqv[:, i, :, 64:96]), w=[wsm.res])
                    ukvv = ukv[j].rearrange("p (h d) -> p h d", d=128)
                    dma(SP, dm(wkv[:, 0:512].rearrange("p (h d) -> p h d", d=64), ukvv[:, :, 0:64]), w=[wkv.res])
                    dma(SP, dm(wkv[:, 512:1024].rearrange("p (h d) -> p h d", d=64), ukvv[:, :, 64:128]), w=[wkv.res])
                    for c in range(6):
                        p = nps()
                        for i in range(2):
                            op(PE, mm(p[:, :], wsm[:, i, c * 128:(c + 1) * 128], cqn[:, i, :], i == 0, i == 1), r=[wsm.res, cqn.res], w=[p.res])
                        op(ACT, act(raws[c][:, :], p[:, :], AF.Copy), r=[p.res], w=[raws[c].res])
                        op(POOL, tt(sqs[c][:, :], raws[c][:, :], raws[c][:, :], ALU.mult), r=[raws[c].res], w=[sqs[c].res])
                    pc = nps()
                    for c in range(6):
                        sel = cmat[:, C_SEL2 + 8 * c:C_SEL2 + 8 * c + 8] if c < 4 else cmat[:, C_SEL4 + 8 * (c - 4):C_SEL4 + 8 * (c - 4) + 8]
                        op(PE, mm(pc[0:8, :], sel, sqs[c][:, :], c == 0, c == 5), r=[cmat.res, sqs[c].res], w=[pc.res])
                    rstd_ops(rc, pc, 8, 96.0)
                    for c in range(6):
                        pbc = nps()
                        selT = cf32[0:8, F_SEL2T + 128 * c:F_SEL2T + 128 * c + 128] if c < 4 else cf32[0:8, F_SEL4T + 128 * (c - 4):F_SEL4T + 128 * (c - 4) + 128]
                        op(PE, mm(pbc[:, :], selT, rc[0:8, :], True, True), r=[cf32.res, rc.res], w=[pbc.res])
                        if c < 4:
                            o = outbuf()
                            op(DVE, stt(o[:, :], raws[c][:, :], pcols[:, pb + 5:pb + 6], pbc[:, :], ALU.mult, ALU.mult), r=[raws[c].res, pcols.res, pbc.res], w=[o.res])
                            for hh in range(2):
                                dma(STQ, dm(qb[2 * c + hh, 0:64, gs], o[64 * hh:64 * hh + 64, :]), r=[o.res])
                        else:
                            op(DVE, stt(t1[:, :], raws[c][:, :], pcols[:, pb + 6:pb + 7], pbc[:, :], ALU.mult, ALU.mult), r=[raws[c].res, pcols.res, pbc.res], w=[t1.res])
                            op(POOL, cp(sqs[7][:, :], t1[:, :]), r=[t1.res], w=[sqs[7].res])
                            prot = nps()
                            op(PE, mm(prot[:, :], cmat[:, C_PERM:C_PERM + 128], sqs[7][:, :], True, True), r=[cmat.res, sqs[7].res], w=[prot.res])
                            op(DVE, tt(t2[:, :], prot[:, :], sint[:, :], ALU.mult), r=[prot.res, sint.res], w=[t2.res])
                            op(DVE, tt(t1[:, :], t1[:, :], cost[:, :], ALU.mult), r=[t1.res, cost.res], w=[t1.res])
                            o = outbuf()
                            op(DVE, tt(o[:, :], t1[:, :], t2[:, :], ALU.add), r=[t1.res, t2.res], w=[o.res])
                            for hh in range(4):
                                dma(STQ, dm(qb[4 * (c - 4) + hh, 64:96, gs], o[32 * hh:32 * hh + 32, :]), r=[o.res])
                    for c in range(4):
                        p = nps()
                        op(PE, mm(p[:, :], wkv[:, c * 128:(c + 1) * 128], ckvn[:, :], True, True), r=[wkv.res, ckvn.res], w=[p.res])
                        op(ACT, act(raws[c][:, :], p[:, :], AF.Copy), r=[p.res], w=[raws[c].res])
                        op(POOL, tt(sqs[c][:, :], raws[c][:, :], raws[c][:, :], ALU.mult), r=[raws[c].res], w=[sqs[c].res])
                    pc = nps()
                    for c in range(4):
                        op(PE, mm(pc[0:8, :], cmat[:, C_SEL2 + 8 * c:C_SEL2 + 8 * c + 8], sqs[c][:, :], c == 0, False), r=[cmat.res, sqs[c].res], w=[pc.res])
                    op(PE, mm(pc[0:8, :], cmat[0:32, C_ONES:C_ONES + 8], sqs[6][0:32, :], False, True), r=[cmat.res, sqs[6].res], w=[pc.res])
                    rstd_ops(rc, pc, 8, 96.0)
                    for c in range(4):
                        pbc = nps()
                        op(PE, mm(pbc[:, :], cf32[0:8, F_SEL2T + 128 * c:F_SEL2T + 128 * c + 128], rc[0:8, :], True, True), r=[cf32.res, rc.res], w=[pbc.res])
                        o = outbuf()
                        op(DVE, stt(o[:, :], raws[c][:, :], pcols[:, pb + 7:pb + 8], pbc[:, :], ALU.mult, ALU.mult), r=[raws[c].res, pcols.res, pbc.res], w=[o.res])
                        for hh in range(2):
                            dma(STQ, dm(kb_[2 * c + hh, 0:64, gs], o[64 * hh:64 * hh + 64, :]), r=[o.res])
                    for h in range(8):
                        pbc = nps()
                        op(PE, mm(pbc[0:32, :], cf32[0:8, F_KSELT + 32 * h:F_KSELT + 32 * h + 32], rc[0:8, :], True, True), r=[cf32.res, rc.res], w=[pbc.res])
                        o = outbuf()
                        op(DVE, tt(o[0:32, :], krr[0:32, :], pbc[0:32, :], ALU.mult), r=[krr.res, pbc.res], w=[o.res])
                        dma(STQ, dm(kb_[h, 64:96, gs], o[0:32, :]), r=[o.res])
                    for tb in range(4):
                        p = nps()
                        op(PE, mm(p[:, :], ckvn[:, tb * 128:(tb + 1) * 128], wkv[:, 512:1024], True, True), r=[wkv.res, ckvn.res], w=[p.res])
                        op(ACT, act(vt[:, tb, :], p[:, :], AF.Copy), r=[p.res], w=[vt.res])
                    for h in range(8):
                        dma(STQ, dm(vb[h, :, 4 * g:4 * g + 4, :], vt[:, :, 64 * h:64 * h + 64]), r=[vt.res])
        sch.barrier()

    def attn_even(j):
        pb = 64 + 32 * j
        with ExitStack() as st:
            KT = [sb(st, f"KT{i}", [128, S], BF16) for i in range(2)]
            V = [sb(st, f"V{i}", [128, NB, 128], BF16) for i in range(2)]
            for v_ in V:
                op(POOL, lambda e, v_=v_: e.memset(v_[:, :, 64:128], 1.0), w=[v_.res])
            tmpf = [sb(st, f"tmpf{i}", [128, 512], F32) for i in range(2)]
            STQA = POOL
            Q = [sb(st, f"Q{i}", [128, 512], BF16) for i in range(3)]
            Pb = [sb(st, f"P{i}", [128, 512], BF16) for i in range(6)]
            sbb = [sb(st, f"sbb{i}", [128, 256], F32) for i in range(4)]
            rec = sb(st, "rec", [64, 512], F32)
            ob = [sb(st, f"oo{i}", [64, 512], BF16) for i in range(2)]
            O, L = ps[6], ps[7]
            qctr = [0]
            octr = [0]
            tctr = [0]
            for kvh in range(2):
                kt, v = KT[kvh], V[kvh]
                dma(SP, dm(kt[0:64, :], ka[kvh]), w=[kt.res])
                dma(SP, dm(v[:, :, 0:64], va[kvh]), w=[v.res])
                for hq in range(4):
                    h = 4 * kvh + hq
                    for g in range(NG):
                        q = Q[qctr[0] % 3]
                        qctr[0] += 1
                        dma(SP, dm(q[0:64, :], qa[h, :, g * 512:(g + 1) * 512]), w=[q.res])
                        tiles = []
                        rels = list(range(-1, 4)) if g > 0 else list(range(0, 4))
                        for idx, rel in enumerate(rels):
                            kbi = 4 * g + rel
                            if rel < 0:
                                q0, n, boff = 0, 128, 0
                            else:
                                q0 = 128 * rel
                                n = min(256, 512 - q0)
                                boff = 128
                            t = tctr[0]
                            tctr[0] += 1
                            sp_, P_, sb_ = ps[t % 4], Pb[t % 6], sbb[t % 4]
                            first, last = idx == 0, idx == len(rels) - 1

                            def s1(kbi=kbi, q0=q0, n=n, sp_=sp_, q=q, kt=kt):
                                op(PE, mm(sp_[:, 0:n], kt[0:64, kbi * 128:(kbi + 1) * 128], q[0:64, q0:q0 + n], True, True), r=[kt.res, q.res], w=[sp_.res])

                            def s2(n=n, sp_=sp_, sb_=sb_, P_=P_, boff=boff, h=h):
                                op(DVE, stt(sb_[:, 0:n], sp_[:, 0:n], 0.125, cf32[:, F_ABIAS + 384 * h + boff:F_ABIAS + 384 * h + boff + n], ALU.mult, ALU.add),
                                   r=[sp_.res, cf32.res], w=[sb_.res])
                                op(ACT, act(P_[:, 0:n], sb_[:, 0:n], AF.Exp), r=[sb_.res], w=[P_.res])

                            def s3(kbi=kbi, q0=q0, n=n, P_=P_, v=v, first=first, last=last):
                                op(PE, mm(O[0:64, q0:q0 + n], v[:, kbi, 0:64], P_[:, 0:n], first, last), r=[v.res, P_.res], w=[O.res])
                                op(PE, mm(L[0:64, q0:q0 + n], ONES(128, 64), P_[:, 0:n], first, last), r=[cmat.res, P_.res], w=[L.res])
                            tiles.append([s1, s2, s3])
                        pipeline(tiles, [0, 2, 4])
                        o = ob[octr[0] % 2]
                        octr[0] += 1
                        op(DVE, ts(rec[:, :], L[0:64, :], dcol[0:64, 4 + 8 * j + h:5 + 8 * j + h], None, ALU.add), r=[L.res, dcol.res], w=[rec.res])
                        op(DVE, lambda e: e.reciprocal(out=rec[:, :], in_=rec[:, :]), r=[rec.res], w=[rec.res])
                        op(DVE, tt(o[:, :], O[0:64, :], rec[:, :], ALU.mult), r=[O.res, rec.res], w=[o.res])
                        dma(STQA, dm(mixT[64 * h:64 * h + 64, g * 512:(g + 1) * 512], o[:, :]), r=[o.res])
            scaleB = 96.0 ** -0.5
            for h in range(8):
                kt, v = KT[h % 2], V[h % 2]
                dma(SP, dm(kt[0:96, :], kb_[h]), w=[kt.res])
                dma(SP, dm(v[:, :, 0:64], vb[h]), w=[v.res])
                for g in range(NG):
                    q = Q[qctr[0] % 3]
                    qctr[0] += 1
                    dma(SP, dm(q[0:96, :], qb[h, :, g * 512:(g + 1) * 512]), w=[q.res])
                    OL = ps[6 + (octr[0] % 2)]
                    tiles = []
                    nkb = 4 * g + 4
                    for kbi in range(nkb):
                        rel = kbi - 4 * g
                        q0 = 128 * rel if rel > 0 else 0
                        n = 512 - q0
                        t = tctr[0]
                        tctr[0] += 1
                        sp_, P_ = ps[t % 4], Pb[t % 6]
                        first, last = kbi == 0, kbi == nkb - 1

                        def s1(kbi=kbi, q0=q0, n=n, sp_=sp_, q=q, kt=kt):
                            op(PE, mm(sp_[:, 0:n], kt[0:96, kbi * 128:(kbi + 1) * 128], q[0:96, q0:q0 + n], True, True), r=[kt.res, q.res], w=[sp_.res])

                        def s2(n=n, sp_=sp_, P_=P_, rel=rel):
                            op(ACT, act(P_[:, 0:n], sp_[:, 0:n], AF.Exp, scale=scaleB), r=[sp_.res], w=[P_.res])
                            if rel >= 0:
                                op(DVE, tt(P_[:, 0:128], P_[:, 0:128], cmat[:, C_DFB:C_DFB + 128], ALU.mult), r=[P_.res, cmat.res], w=[P_.res])

                        def s3(kbi=kbi, q0=q0, n=n, P_=P_, v=v, first=first, last=last, OL=OL):
                            op(PE, mm(OL[:, q0:q0 + n], v[:, kbi, :], P_[:, 0:n], first, last), r=[v.res, P_.res], w=[OL.res])
                        tiles.append([s1, s2, s3])
                    pipeline(tiles, [0, 2, 4])
                    o = ob[octr[0] % 2]
                    tf = tmpf[octr[0] % 2]
                    octr[0] += 1
                    op(DVE, cp(tf[:, :], OL[:, :]), r=[OL.res], w=[tf.res])
                    op(PE, mm(ps[5][0:64, :], cf32[:, F_SHIFT:F_SHIFT + 64], tf[:, :], True, True), r=[cf32.res, tf.res], w=[ps[5].res])
                    op(DVE, lambda e: e.reciprocal(out=rec[:, :], in_=ps[5][0:64, :]), r=[ps[5].res], w=[rec.res])
                    op(DVE, tt(o[:, :], tf[0:64, :], rec[:, :], ALU.mult), r=[tf.res, rec.res], w=[o.res])
                    dma(STQA, dm(mixT[512 + 64 * h:512 + 64 * h + 64, g * 512:(g + 1) * 512], o[:, :]), r=[o.res])
        sch.barrier()

    def attn_odd(j):
        with ExitStack() as st:
            KT = [sb(st, f"KT{i}", [128, S], BF16) for i in range(3)]
            V = [sb(st, f"V{i}", [128, NB, 128], BF16) for i in range(2)]
            Q = [sb(st, f"Q{i}", [128, 512], BF16) for i in range(4)]
            Pb = [sb(st, f"P{i}", [128, 512], BF16) for i in range(6)]
            ef = [sb(st, f"ef{i}", [128, 512], F32) for i in range(3)]
            spb = [sb(st, f"spb{i}", [128, 512], BF16) for i in range(3)]
            R32 = sb(st, "R32", [128, 512], F32)
            Rb = [sb(st, f"Rb{i}", [128, 512], BF16) for i in range(2)]
            ob = [sb(st, f"oo{i}", [128, 512], BF16) for i in range(2)]
            f1 = sb(st, "f1", [128, 512], F32)
            f2 = sb(st, "f2", [128, 512], F32)
            f3 = sb(st, "f3", [128, 512], F32)
            fsq = sb(st, "fsq", [128, 512], BF16)
            qctr = [0]
            octr = [0]
            tctr = [0]
            zw = [PsView(psall, 1024 * k, 1024) for k in range(3)]
            Ow = PsView(psall, 3072, 1024)
            efw = [sb(st, f"efw{i}", [128, 1024], F32) for i in range(2)]
            spw = [sb(st, f"spw{i}", [128, 1024], BF16) for i in range(3)]
            aw = [sb(st, f"aw{i}", [128, 1024], BF16) for i in range(3)]
            R32w = sb(st, "R32w", [128, 1024], F32)
            Rbw = [sb(st, f"Rbw{i}", [128, 1024], BF16) for i in range(2)]
            qws = [sb(st, f"qw{i}", [64, 1024], BF16) for i in range(2)]
            obw = [sb(st, f"obw{i}", [64, 1024], BF16) for i in range(2)]
            for h in range(8):
                kt, v = KT[h % 2], V[h % 2]
                dma(SP, dm(kt[0:64, :], kc[h]), w=[kt.res])
                dma(SP, dm(v[:, :, 0:64], vc[h]), w=[v.res])
                for G in range(NG // 2):
                    g = 2 * G
                    q = qws[qctr[0] % 2]
                    qctr[0] += 1
                    dma(SP, dm(q[0:64, :], qc[h, :, g * 512:(g + 2) * 512]), w=[q.res])
                    tiles = []
                    nkb = 4 * g + 8
                    order = list(range(nkb - 1, -1, -1))
                    for idx, kbi in enumerate(order):
                        if kbi >= 4 * g + 4:
                            c0, diag = 512 + 128 * (kbi - 4 * g - 4), True
                        elif kbi >= 4 * g:
                            c0, diag = 128 * (kbi - 4 * g), True
                        else:
                            c0, diag = 0, False
                        pieces = ([(c0, 512)] if c0 < 512 else []) + [(max(c0, 512), 1024)]
                        t = tctr[0]
                        tctr[0] += 1
                        zp, e_, s_, a_ = zw[t % 3], efw[t % 2], spw[t % 3], aw[t % 3]
                        rb_r, rb_w = Rbw[idx % 2], Rbw[(idx + 1) % 2]
                        first, last = idx == 0, idx == len(order) - 1
                        firstA = kbi == 4 * g + 3

                        def s1(kbi=kbi, pieces=pieces, zp=zp, q=q, kt=kt):
                            for (a_c, b_c) in pieces:
                                op(PE, mm(zp[:, a_c:b_c], kt[0:64, kbi * 128:(kbi + 1) * 128], q[0:64, a_c:b_c], True, False), r=[kt.res, q.res], w=[zp.res])

                        def s2a(c0=c0, zp=zp, e_=e_):
                            op(ACT, act(e_[:, c0:1024], zp[:, c0:1024], AF.Exp), r=[zp.res], w=[e_.res])

                        def s2(c0=c0, e_=e_, s_=s_, diag=diag):
                            op(ACT, act(s_[:, c0:1024], e_[:, c0:1024], AF.Ln, bias=onec[:, :]), r=[e_.res, onec.res], w=[s_.res])
                            if diag:
                                op(DVE, tt(s_[:, c0:c0 + 128], s_[:, c0:c0 + 128], cmat[:, C_DFC:C_DFC + 128], ALU.mult), r=[s_.res, cmat.res], w=[s_.res])

                        def s3(c0=c0, pieces=pieces, zp=zp, s_=s_, rb_r=rb_r, rb_w=rb_w, first=first, last=last):
                            for (a_c, b_c) in pieces:
                                op(PE, mm(zp[:, a_c:b_c], cmat[:, C_NEGU:C_NEGU + 128], s_[:, a_c:b_c], False, first), r=[cmat.res, s_.res], w=[zp.res])
                                if not first:
                                    op(PE, mm(zp[:, a_c:b_c], cmat[:, C_NEGONES:C_NEGONES + 128], rb_r[:, a_c:b_c], False, True), r=[cmat.res, rb_r.res], w=[zp.res])
                            if not last:
                                if first:
                                    op(POOL, lambda e: e.memset(R32w[:, :], 0.0), w=[R32w.res])
                                op(POOL, tt(R32w[:, c0:1024], R32w[:, c0:1024], s_[:, c0:1024], ALU.add), r=[R32w.res, s_.res], w=[R32w.res])
                                op(DVE, cp(rb_w[:, :], R32w[:, :]), r=[R32w.res], w=[rb_w.res])

                        def s4(c0=c0, zp=zp, a_=a_, diag=diag):
                            op(ACT, act(a_[:, c0:1024], zp[:, c0:1024], AF.Exp), r=[zp.res], w=[a_.res])
                            if diag:
                                op(DVE, tt(a_[:, c0:c0 + 128], a_[:, c0:c0 + 128], cmat[:, C_DFC:C_DFC + 128], ALU.mult), r=[a_.res, cmat.res], w=[a_.res])

                        def s5(kbi=kbi, pieces=pieces, a_=a_, v=v, first=first, firstA=firstA, last=last):
                            for (a_c, b_c) in pieces:
                                st_ = firstA if a_c < 512 else first
                                op(PE, mm(Ow[0:64, a_c:b_c], v[:, kbi, 0:64], a_[:, a_c:b_c], st_, last), r=[v.res, a_.res], w=[Ow.res])
                        tiles.append([s1, s2a, s2, s3, s4, s5])
                    pipeline(tiles, [0, 1, 1, 2, 2, 3])
                    o = obw[octr[0] % 2]
                    octr[0] += 1
                    op(DVE, cp(o[0:64, :], Ow[0:64, :]), r=[Ow.res], w=[o.res])
                    dma(POOL, dm(mixT[64 * h:64 * h + 64, g * 512:(g + 2) * 512], o[0:64, :]), r=[o.res])
            sch.barrier()
            O1, O2 = ps[6], ps[7]
            Lacc = [sb(st, f"Lacc{i}", [128, 512], F32) for i in range(2)]
            for h in range(4):
                k1, k2, v = KT[0], KT[1], V[h % 2]
                dma(SP, dm(k1[0:70, :], kd[h, 0]), w=[k1.res])
                dma(SP, dm(k2[0:70, :], kd[h, 1]), w=[k2.res])
                dma(SP, dm(v[:, :, :], vd[h]), w=[v.res])
                for g in range(NG):
                    q1 = Q[qctr[0] % 4]
                    q2 = Q[(qctr[0] + 1) % 4]
                    qctr[0] += 2
                    dma(SP, dm(q1[0:70, :], qd[h, 0, :, g * 512:(g + 1) * 512]), w=[q1.res])
                    dma(SP, dm(q2[0:70, :], qd[h, 1, :, g * 512:(g + 1) * 512]), w=[q2.res])
                    op(DVE, lambda e: e.memset(Lacc[0][:, :], 0.0), w=[Lacc[0].res])
                    op(POOL, lambda e: e.memset(Lacc[1][:, :], 0.0), w=[Lacc[1].res])
                    tiles = []
                    nkb = 4 * g + 4
                    for kbi in range(nkb):
                        rel = kbi - 4 * g
                        q0 = 128 * rel if rel > 0 else 0
                        n = 512 - q0
                        t = tctr[0]
                        tctr[0] += 1
                        sa, sb2 = ps[(2 * t) % 6], ps[(2 * t + 1) % 6]
                        Pa, Pb2 = Pb[(2 * t) % 6], Pb[(2 * t + 1) % 6]
                        first, last = kbi == 0, kbi == nkb - 1

                        def s1(kbi=kbi, q0=q0, n=n, sa=sa, sb2=sb2, q1=q1, q2=q2):
                            op(PE, mm(sa[:, 0:n], k1[0:70, kbi * 128:(kbi + 1) * 128], q1[0:70, q0:q0 + n], True, True), r=[k1.res, q1.res], w=[sa.res])
                            op(PE, mm(sb2[:, 0:n], k2[0:70, kbi * 128:(kbi + 1) * 128], q2[0:70, q0:q0 + n], True, True), r=[k2.res, q2.res], w=[sb2.res])

                        def s2(n=n, q0=q0, sa=sa, sb2=sb2, Pa=Pa, Pb2=Pb2, rel=rel, h=h):
                            for s_, p_, eng, la in ((sa, Pa, DVE, Lacc[0]), (sb2, Pb2, POOL, Lacc[1])):
                                op(ACT, act(p_[:, 0:n], s_[:, 0:n], AF.Exp, scale=0.125), r=[s_.res], w=[p_.res])
                                if rel >= 0:
                                    op(DVE, tt(p_[:, 0:128], p_[:, 0:128], cmat[:, C_DFD + 128 * h:C_DFD + 128 * h + 128], ALU.mult), r=[p_.res, cmat.res], w=[p_.res])
                                op(eng, tt(la[:, q0:q0 + n], la[:, q0:q0 + n], p_[:, 0:n], ALU.add), r=[la.res, p_.res], w=[la.res])

                        def s3(kbi=kbi, q0=q0, n=n, Pa=Pa, Pb2=Pb2, v=v, first=first, last=last):
                            op(PE, mm(O1[:, q0:q0 + n], v[:, kbi, :], Pa[:, 0:n], first, last), r=[v.res, Pa.res], w=[O1.res])
                            op(PE, mm(O2[:, q0:q0 + n], v[:, kbi, :], Pb2[:, 0:n], first, last), r=[v.res, Pb2.res], w=[O2.res])
                        tiles.append([s1, s2, s3])
                    pipeline(tiles, [0, 1, 2])
                    L1, L2 = ps[(2 * tctr[0]) % 6], ps[(2 * tctr[0] + 1) % 6]
                    op(PE, mm(L1[:, :], cf32[:, F_ONES:F_ONES + 128], Lacc[0][:, :], True, True), r=[cf32.res, Lacc[0].res], w=[L1.res])
                    op(PE, mm(L2[:, :], cf32[:, F_ONES:F_ONES + 128], Lacc[1][:, :], True, True), r=[cf32.res, Lacc[1].res], w=[L2.res])
                    op(DVE, lambda e, L1=L1: e.reciprocal(out=f1[:, :], in_=L1[:, :]), r=[L1.res], w=[f1.res])
                    op(DVE, tt(f1[:, :], O1[:, :], f1[:, :], ALU.mult), r=[O1.res, f1.res], w=[f1.res])
                    op(DVE, lambda e, L2=L2: e.reciprocal(out=f2[:, :], in_=L2[:, :]), r=[L2.res], w=[f2.res])
                    op(DVE, tt(f2[:, :], O2[:, :], f2[:, :], ALU.mult), r=[O2.res, f2.res], w=[f2.res])
                    op(DVE, stt(f3[:, :], f2[:, :], dcol[:, j:j + 1], f1[:, :], ALU.mult, ALU.add), r=[f2.res, dcol.res, f1.res], w=[f3.res])
                    op(POOL, tt(fsq[:, :], f3[:, :], f3[:, :], ALU.mult), r=[f3.res], w=[fsq.res])
                    pn = ps[(2 * tctr[0] + 2) % 6]
                    op(PE, mm(pn[:, :], ONES(), fsq[:, :], True, True), r=[cmat.res, fsq.res], w=[pn.res])
                    rstd_ops(f1, pn, 128, 128.0)
                    o = ob[octr[0] % 2]
                    octr[0] += 1
                    op(DVE, stt(o[:, :], f3[:, :], dcol[:, 2 + j:3 + j], f1[:, :], ALU.mult, ALU.mult), r=[f3.res, dcol.res, f1.res], w=[o.res])
                    dma(POOL, dm(mixT[512 + 128 * h:512 + 128 * h + 128, g * 512:(g + 1) * 512], o[:, :]), r=[o.res])
        sch.barrier()

    stop = int(os.environ.get("KSTOP", "99"))
    cnt = 0
    for l in range(DEPTH + 1):
        if cnt >= stop:
            break
        proj_phase(l)
        cnt += 1
        if l < DEPTH:
            if cnt >= stop:
                break
            if l % 2 == 0:
                attn_even(l // 2)
            else:
                attn_odd(l // 2)
            cnt += 1
    sch.barrier()

    with nc.Block() as block:
        @block.tensor
        def _(e):
            sch.emit(PE, e)

        @block.scalar
        def _(e):
            sch.emit(ACT, e)

        @block.vector
        def _(e):
            sch.emit(DVE, e)

        @block.gpsimd
        def _(e):
            sch.emit(POOL, e)

        @block.sync
        def _(e):
            sch.emit(SP, e)
    stack.close()
    return nc


def host_consts(S):
    bf = ml_dtypes.bfloat16
    cm = np.zeros((128, NCM), np.float32)
    cm[:, C_ONES:C_ONES + 128] = 1.0
    for m in range(128):
        if m % 32 < 16:
            cm[m + 16, C_PERM + m] = -1.0
        else:
            cm[m - 16, C_PERM + m] = 1.0
    jj, kk = np.meshgrid(np.arange(128), np.arange(128), indexing="ij")
    cm[:, C_NEGU:C_NEGU + 128] = -(jj >= kk).astype(np.float32)
    cm[:, C_NEGONES:C_NEGONES + 128] = -1.0
    k_, q_ = jj, kk
    cm[:, C_DFB:C_DFB + 128] = ((k_ // 64) <= (q_ // 64)).astype(np.float32)
    cm[:, C_DFC:C_DFC + 128] = (k_ < q_).astype(np.float32)
    for h in range(4):
        m = 2.0 ** (-8.0 * (h + 1) / 4)
        d = np.where(k_ <= q_, 1.0, np.where((k_ // 64) == (q_ // 64), np.exp(-2.0 * m * (k_ - q_)), 0.0))
        cm[:, C_DFD + 128 * h:C_DFD + 128 * h + 128] = d
    p = np.arange(128)
    for c in range(4):
        cm[p, C_SEL2 + 8 * c + 2 * c + p // 64] = 1.0
    for r in range(2):
        cm[p, C_SEL4 + 8 * r + 4 * r + p // 32] = 1.0
    cf = np.zeros((128, NCF), np.float32)
    for c in range(4):
        cf[2 * c + p // 64, F_SEL2T + 128 * c + p] = 1.0
    for r in range(2):
        cf[4 * r + p // 32, F_SEL4T + 128 * r + p] = 1.0
    for h in range(8):
        cf[h, F_KSELT + 32 * h:F_KSELT + 32 * h + 32] = 1.0
    for h in range(8):
        m = 2.0 ** (-8.0 * (h + 1) / 8)
        kpos = np.arange(128)[:, None] - 128
        qpos = np.arange(128)[None, :]
        dch = qpos // 64 - np.floor_divide(kpos, 64)
        ok = (dch >= 0) & (dch <= 2)
        cf[:, F_ABIAS + 384 * h:F_ABIAS + 384 * h + 128] = np.where(ok, -m * np.abs(qpos - kpos), -30000.0)
        kpos = np.arange(128)[:, None]
        qpos = np.arange(256)[None, :]
        dch = qpos // 64 - kpos // 64
        ok = (dch >= 0) & (dch <= 2)
        cf[:, F_ABIAS + 384 * h + 128:F_ABIAS + 384 * h + 384] = np.where(ok, -m * np.abs(qpos - kpos), -30000.0)
    half = 16
    inv = (np.float32(10000.0) ** (-np.arange(half, dtype=np.float32) / np.float32(half))).astype(np.float32)
    cf[:, F_INVF] = inv[p % 16]
    for m in range(64):
        cf[64 + m, F_SHIFT + m] = 1.0
    cf[:, F_ONES:F_ONES + 128] = 1.0
    pos = np.arange(S)
    a, b, c = pos // 1024, (pos % 1024) // 32, pos % 32
    daug = np.zeros((4, 2, 6, S), np.float32)
    for h in range(4):
        m = 2.0 ** (-8.0 * (h + 1) / 4) * 8.0
        daug[h, 0, 0], daug[h, 0, 1], daug[h, 0, 2] = -m * 1024 * a, -m * 32 * b, -m * c
        daug[h, 0, 3:6] = 1.0
        daug[h, 1, 0:3] = 1.0
        daug[h, 1, 3], daug[h, 1, 4], daug[h, 1, 5] = m * 1024 * a, m * 32 * b, m * c
    return cm.astype(bf), cf, daug.astype(bf)


def host_pcols(inp):
    pc = np.zeros((128, NPC), np.float32)
    p = np.arange(128)
    for l in range(4):
        pc[:, l * 16:l * 16 + 8] = inp["norm_mix_g"][l].reshape(8, 128).T
        pc[:, l * 16 + 8:l * 16 + 16] = inp["norm_ffn_g"][l].reshape(8, 128).T
    for j in range(2):
        b = 64 + 32 * j
        pc[:, b + 0] = inp["a_q_norm"][j][p % 64]
        pc[:, b + 1] = inp["a_k_norm"][j][p % 64]
        pc[:, b + 2:b + 4] = inp["b_cq_norm"][j].reshape(2, 128).T
        pc[:, b + 4] = inp["b_ckv_norm"][j]
        pc[:, b + 5] = inp["b_q_norm"][j][p % 64]
        pc[:, b + 6] = inp["b_q_norm"][j][64 + p % 32]
        pc[:, b + 7] = inp["b_k_norm"][j][p % 64]
        pc[:, b + 8] = inp["b_k_norm"][j][64 + p % 32]
        pc[:, b + 9:b + 17] = inp["a_sinks"][j][None, :]
        b = 128 + 32 * j
        pc[:, b + 0] = inp["d_q_norm"][j].reshape(128)
        pc[:, b + 1] = inp["d_k_norm"][j].reshape(128)
        pc[:, b + 2] = inp["d_subln"][j]
    dlam = np.broadcast_to(inp["d_lambda"].reshape(1, 2, 256), (128, 2, 256)).astype(np.float32)
    return pc, np.ascontiguousarray(dlam)


_CACHE = {}


def kernel(**inputs):
    inp = {k: np.asarray(v) for k, v in inputs.items()}
    x = inp["x"]
    B, S, D = x.shape
    if S not in _CACHE:
        _CACHE[S] = build(S)
    nc = _CACHE[S]
    cm, cf, daug = host_consts(S)
    pc, dlam = host_pcols(inp)
    shared = {
        "pcols": pc, "dlam": dlam, "cmat": cm, "cf32": cf, "daug": daug,
        "mlp_w_up": inp["mlp_w_up"], "mlp_w_down": inp["mlp_w_down"],
        "ev_w_in": inp["ev_w_in"], "ev_w_out": inp["ev_w_out"],
        "b_w_uq": inp["b_w_uq"], "b_w_ukv": inp["b_w_ukv"],
        "od_w_in": inp["od_w_in"], "od_w_out": inp["od_w_out"],
    }
    in_maps = []
    for b in range(B):
        m = dict(shared)
        m["xT"] = np.ascontiguousarray(x[b].T)
        m["posb"] = np.ascontiguousarray(np.broadcast_to(inp["positions"][b][None, :], (128, S))).astype(np.int32)
        in_maps.append(m)
    res = run_bass_kernel_spmd(nc, in_maps, core_ids=list(range(B)))
    out = np.stack([np.ascontiguousarray(r["yT"].T) for r in res.results], axis=0)
    return out.astype(np.float32)
```
